# Optimizing a Trainium2 kernel written in Bass

```python
import math
import jax, jax.numpy as jnp
from jax import lax
import numpy as np

D_MODEL = 1024
BATCH = 8
SEQ = 2048
DEPTH = 2

CHUNK = 64
Q_BLOCK = 128
ROPE_THETA = 10000.0
LN_EPS = 1e-5
RMS_EPS = 1e-6
DN_ALPHA = (2 * DEPTH) ** 0.25
DN_BETA = (8 * DEPTH) ** -0.25
N_MIXERS = 2
N_MLA = (DEPTH + 1) // 2
N_DSA = DEPTH // 2

MLA_HEADS = 8
MLA_NOPE = 128
MLA_ROPE = 64
MLA_V = 128
MLA_Q_RANK = 384
MLA_KV_RANK = 256
MLA_IN = MLA_Q_RANK + MLA_KV_RANK + MLA_ROPE

DSA_HEADS = 8
DSA_HEAD_DIM = D_MODEL // DSA_HEADS
IDX_HEADS = 8
IDX_DIM = 64
DSA_TOPK_MAX = 256
DSA_HD = DSA_HEADS * DSA_HEAD_DIM
DSA_IN = 3 * DSA_HD + IDX_HEADS * IDX_DIM + IDX_DIM + IDX_HEADS

PEER_HEADS = 8
PEER_NKEYS = 128
PEER_EXPERTS = PEER_NKEYS * PEER_NKEYS
PEER_QDIM = 256
PEER_TOPK = 16
PEER_TOK_BLOCK = 128

kernel_name = "hybrid_mla_dsa_peer_chunk_causal"


def _normal(key, shape, scale):
    return jax.random.normal(key, shape, jnp.float32) * scale


def _layernorm(x, g, b):
    xf = x.astype(jnp.float32)
    mu = jnp.mean(xf, axis=-1, keepdims=True)
    var = jnp.mean(jnp.square(xf - mu), axis=-1, keepdims=True)
    return ((xf - mu) * lax.rsqrt(var + LN_EPS) * g + b).astype(x.dtype)


def _rmsnorm(x, g):
    xf = x.astype(jnp.float32)
    ms = jnp.mean(jnp.square(xf), axis=-1, keepdims=True)
    return (xf * lax.rsqrt(ms + RMS_EPS) * g).astype(x.dtype)


def _rope_tables(seq, dim):
    inv = ROPE_THETA ** (-jnp.arange(0, dim, 2, dtype=jnp.float32) / dim)
    ang = jnp.arange(seq, dtype=jnp.float32)[:, None] * inv[None, :]
    return jnp.cos(ang), jnp.sin(ang)


def _apply_rope(x, cos, sin):
    x1, x2 = jnp.split(x.astype(jnp.float32), 2, axis=-1)
    c = cos[:, None, :]
    s = sin[:, None, :]
    return jnp.concatenate([x1 * c - x2 * s, x2 * c + x1 * s], axis=-1).astype(x.dtype)


def _chunk_causal_attention(q, k, v, scale):
    B, S, H, Dq = q.shape
    nqb = S // Q_BLOCK
    qb = q.reshape(B, nqb, Q_BLOCK, H, Dq).transpose(1, 0, 2, 3, 4)
    key_chunk = jnp.arange(S) // CHUNK

    def one_block(args):
        q_blk, blk = args
        q_chunk = (blk * Q_BLOCK + jnp.arange(Q_BLOCK)) // CHUNK
        s = jnp.einsum('bqhd,bkhd->bhqk', q_blk, k).astype(jnp.float32) * scale
        mask = key_chunk[None, :] <= q_chunk[:, None]
        s = jnp.where(mask[None, None], s, -jnp.inf)
        p = jax.nn.softmax(s, axis=-1).astype(v.dtype)
        return jnp.einsum('bhqk,bkhd->bqhd', p, v)

    out = lax.map(one_block, (qb, jnp.arange(nqb)))
    return out.transpose(1, 0, 2, 3, 4).reshape(B, S, H, v.shape[-1])


def _mla_mixer(x, w_in, q_norm, kv_norm, w_uq, w_ukv, w_o):
    B, S, _ = x.shape
    cos, sin = _rope_tables(S, MLA_ROPE)
    h = x @ w_in
    cq, ckv, k_rope = jnp.split(h, [MLA_Q_RANK, MLA_Q_RANK + MLA_KV_RANK], axis=-1)
    q = jnp.einsum('bsr,rhd->bshd', _rmsnorm(cq, q_norm), w_uq)
    kv = jnp.einsum('bsr,rhd->bshd', _rmsnorm(ckv, kv_norm), w_ukv)
    q_nope, q_rope = jnp.split(q, [MLA_NOPE], axis=-1)
    k_nope, v = jnp.split(kv, [MLA_NOPE], axis=-1)
    q_rope = _apply_rope(q_rope, cos, sin)
    k_rope = _apply_rope(k_rope[:, :, None, :], cos, sin)
    q = jnp.concatenate([q_nope, q_rope], axis=-1)
    k = jnp.concatenate([k_nope, jnp.broadcast_to(k_rope, (B, S, MLA_HEADS, MLA_ROPE))], axis=-1)
    o = _chunk_causal_attention(q, k, v, (MLA_NOPE + MLA_ROPE) ** -0.5)
    return o.reshape(B, S, MLA_HEADS * MLA_V) @ w_o


def _dsa_mixer(x, w_in, w_o):
    B, S, _ = x.shape
    cos_h, sin_h = _rope_tables(S, DSA_HEAD_DIM)
    cos_i, sin_i = _rope_tables(S, IDX_DIM)
    h = x @ w_in
    o1 = 3 * DSA_HD + IDX_HEADS * IDX_DIM
    q, k, v, q_idx, k_idx, w_idx = jnp.split(
        h, [DSA_HD, 2 * DSA_HD, 3 * DSA_HD, o1, o1 + IDX_DIM], axis=-1)
    q = _apply_rope(q.reshape(B, S, DSA_HEADS, DSA_HEAD_DIM), cos_h, sin_h)
    k = _apply_rope(k.reshape(B, S, DSA_HEADS, DSA_HEAD_DIM), cos_h, sin_h)
    v = v.reshape(B, S, DSA_HEADS, DSA_HEAD_DIM)
    q_idx = _apply_rope(q_idx.reshape(B, S, IDX_HEADS, IDX_DIM), cos_i, sin_i)
    k_idx = _apply_rope(k_idx[:, :, None, :], cos_i, sin_i)[:, :, 0, :]
    w_idx = w_idx * (IDX_HEADS ** -0.5 * IDX_DIM ** -0.5)
    topk = min(DSA_TOPK_MAX, S // 4)
    nqb = S // Q_BLOCK
    key_chunk = jnp.arange(S) // CHUNK
    scale = DSA_HEAD_DIM ** -0.5

    def per_batch(args):
        q_b, k_b, v_b, qi_b, ki_b, wi_b = args
        qb = q_b.reshape(nqb, Q_BLOCK, DSA_HEADS, DSA_HEAD_DIM)
        qib = qi_b.reshape(nqb, Q_BLOCK, IDX_HEADS, IDX_DIM)
        wib = wi_b.reshape(nqb, Q_BLOCK, IDX_HEADS)

        def per_block(bargs):
            q_blk, qi_blk, wi_blk, blk = bargs
            q_chunk = (blk * Q_BLOCK + jnp.arange(Q_BLOCK)) // CHUNK
            logits = jnp.einsum('qhd,kd->qhk', qi_blk, ki_b).astype(jnp.float32)
            idx_score = jnp.einsum('qh,qhk->qk', wi_blk.astype(jnp.float32), jax.nn.relu(logits))
            admissible = key_chunk[None, :] <= q_chunk[:, None]
            idx_score = jnp.where(admissible, idx_score, -jnp.inf)
            _, sel = lax.top_k(idx_score, topk)
            k_sel = k_b[sel]
            v_sel = v_b[sel]
            valid = key_chunk[sel] <= q_chunk[:, None]
            s = jnp.einsum('qhd,qkhd->qhk', q_blk, k_sel).astype(jnp.float32) * scale
            s = jnp.where(valid[:, None, :], s, -jnp.inf)
            p = jax.nn.softmax(s, axis=-1).astype(v_sel.dtype)
            return jnp.einsum('qhk,qkhd->qhd', p, v_sel)

        ob = lax.map(per_block, (qb, qib, wib, jnp.arange(nqb)))
        return ob.reshape(S, DSA_HD)

    o = lax.map(per_batch, (q, k, v, q_idx, k_idx, w_idx))
    return o @ w_o


def _peer(x, w_q, sub_keys, w_down, w_up):
    B, S, D = x.shape
    T = B * S
    xt = x.reshape(T, D)
    q = (xt @ w_q).reshape(T, PEER_HEADS, 2, PEER_QDIM // 2)
    s1 = jnp.einsum('thd,hnd->thn', q[:, :, 0], sub_keys[0]).astype(jnp.float32)
    s2 = jnp.einsum('thd,hnd->thn', q[:, :, 1], sub_keys[1]).astype(jnp.float32)
    v1, i1 = lax.top_k(s1, PEER_TOPK)
    v2, i2 = lax.top_k(s2, PEER_TOPK)
    cand = (v1[..., :, None] + v2[..., None, :]).reshape(T, PEER_HEADS, PEER_TOPK * PEER_TOPK)
    cand_idx = (i1[..., :, None] * PEER_NKEYS + i2[..., None, :]).reshape(
        T, PEER_HEADS, PEER_TOPK * PEER_TOPK)
    best, pos = lax.top_k(cand, PEER_TOPK)
    expert = jnp.take_along_axis(cand_idx, pos, axis=-1)
    gate = jax.nn.softmax(best, axis=-1)
    n_e = PEER_HEADS * PEER_TOPK
    nb = T // PEER_TOK_BLOCK
    xb = xt.reshape(nb, PEER_TOK_BLOCK, D)
    eb = expert.reshape(nb, PEER_TOK_BLOCK, n_e)
    gb = gate.reshape(nb, PEER_TOK_BLOCK, n_e)

    def per_block(args):
        x_blk, e_blk, g_blk = args
        u = w_down[e_blk]
        a = jnp.einsum('td,ted->te', x_blk, u).astype(jnp.float32)
        hg = (jax.nn.gelu(a, approximate=False) * g_blk).astype(w_up.dtype)
        return jnp.einsum('te,ted->td', hg, w_up[e_blk]).astype(x_blk.dtype)

    y = lax.map(per_block, (xb, eb, gb))
    return y.reshape(B, S, D)


def setup_inputs(seed: int = 0) -> dict:
    key = jax.random.key(seed)
    ks = jax.random.split(key, 20)
    D = D_MODEL
    return {
        "x": _normal(ks[0], (BATCH, SEQ, D), 1.0),
        "mla_w_in": _normal(ks[1], (N_MLA, D, MLA_IN), D ** -0.5),
        "mla_q_norm": 1.0 + _normal(ks[2], (N_MLA, MLA_Q_RANK), 0.02),
        "mla_kv_norm": 1.0 + _normal(ks[3], (N_MLA, MLA_KV_RANK), 0.02),
        "mla_w_uq": _normal(ks[4], (N_MLA, MLA_Q_RANK, MLA_HEADS, MLA_NOPE + MLA_ROPE), MLA_Q_RANK ** -0.5),
        "mla_w_ukv": _normal(ks[5], (N_MLA, MLA_KV_RANK, MLA_HEADS, MLA_NOPE + MLA_V), MLA_KV_RANK ** -0.5),
        "mla_w_o": _normal(ks[6], (N_MLA, MLA_HEADS * MLA_V, D), DN_BETA * (MLA_HEADS * MLA_V) ** -0.5),
        "dsa_w_in": _normal(ks[7], (N_DSA, D, DSA_IN), D ** -0.5),
        "dsa_w_o": _normal(ks[8], (N_DSA, DSA_HD, D), DN_BETA * DSA_HD ** -0.5),
        "peer_w_q": _normal(ks[9], (DEPTH, D, PEER_HEADS * PEER_QDIM), D ** -0.5),
        "peer_sub_keys": _normal(ks[10], (DEPTH, 2, PEER_HEADS, PEER_NKEYS, PEER_QDIM // 2), (PEER_QDIM // 2) ** -0.5),
        "peer_w_down": _normal(ks[11], (DEPTH, PEER_EXPERTS, D), D ** -0.5),
        "peer_w_up": _normal(ks[12], (DEPTH, PEER_EXPERTS, D), DN_BETA * PEER_HEADS ** -0.5),
        "ln_gain": 1.0 + _normal(ks[13], (DEPTH, 2, D), 0.02),
        "ln_bias": _normal(ks[14], (DEPTH, 2, D), 0.02),
    }


def reference(x, mla_w_in, mla_q_norm, mla_kv_norm, mla_w_uq, mla_w_ukv, mla_w_o,
              dsa_w_in, dsa_w_o, peer_w_q, peer_sub_keys, peer_w_down, peer_w_up,
              ln_gain, ln_bias):
    for i in range(DEPTH):
        j = i // N_MIXERS
        if i % N_MIXERS == 0:
            m = _mla_mixer(x, mla_w_in[j], mla_q_norm[j], mla_kv_norm[j],
                           mla_w_uq[j], mla_w_ukv[j], mla_w_o[j])
        else:
            m = _dsa_mixer(x, dsa_w_in[j], dsa_w_o[j])
        x = _layernorm(DN_ALPHA * x + m, ln_gain[i, 0], ln_bias[i, 0])
        f = _peer(x, peer_w_q[i], peer_sub_keys[i], peer_w_down[i], peer_w_up[i])
        x = _layernorm(DN_ALPHA * x + f, ln_gain[i, 1], ln_bias[i, 1])
    return x
```

```python
from contextlib import ExitStack
import math
import numpy as np
import concourse.bass as bass
import concourse.mybir as mybir
from concourse.bass_utils import run_bass_kernel_spmd

F32 = mybir.dt.float32
BF16 = mybir.dt.bfloat16
U32 = mybir.dt.uint32
AF = mybir.ActivationFunctionType
ALU = mybir.AluOpType

S = 2048
D = 1024
NT = 16
P = 128
ALPHA = float((2 * 2) ** 0.25)
LN_EPS = 1e-5
RMS_EPS = 1e-6
NEG = -1.0e30


class Tn:
    def __init__(self, h, nslots=1, name=""):
        self.h = h
        self.n = nslots
        self.name = name
        self.lw = [None] * nslots
        self.rd = [dict() for _ in range(nslots)]

    def __getitem__(self, key):
        return self.h[key]

    def s(self, *slots):
        return (self, slots)


def _norm(acc):
    if isinstance(acc, Tn):
        return acc, range(acc.n)
    return acc


class KB:
    def __init__(self, nc, es):
        self.nc = nc
        self.es = es
        self.eng = {"pe": nc.tensor, "act": nc.scalar, "dve": nc.vector, "pool": nc.gpsimd, "sp": nc.sync}
        self.sem = {}
        self.cnt = {}
        for e in ("pe", "act", "dve", "pool"):
            self.sem[e] = es.enter_context(nc.semaphore("c_" + e))
            self.cnt[e] = 0
        self.known = {e: {} for e in self.eng}
        self.rings = {}
        self.dma_n = {}
        for q, n in {"sp": 24, "pool": 8, "act": 8}.items():
            self.rings[q] = [es.enter_context(nc.semaphore(f"r_{q}{i}")) for i in range(n)]
            self.dma_n[q] = 0

    def wait(self, e, ev):
        sem, val = ev
        kk = id(sem)
        if self.known[e].get(kk, 0) >= val:
            return
        self.eng[e].wait_ge(sem, val)
        self.known[e][kk] = val

    def _deps(self, reads, writes):
        evs = []
        for acc in reads:
            t, sl = _norm(acc)
            for s in sl:
                if t.lw[s] is not None:
                    evs.append(t.lw[s])
        for acc in writes:
            t, sl = _norm(acc)
            for s in sl:
                if t.lw[s] is not None:
                    evs.append(t.lw[s])
                evs.extend(t.rd[s].values())
        return evs

    def _record(self, ev, reads, writes):
        sem, val = ev
        kk = id(sem)
        for acc in reads:
            t, sl = _norm(acc)
            for s in sl:
                d = t.rd[s]
                if kk not in d or d[kk][1] < val:
                    d[kk] = ev
        for acc in writes:
            t, sl = _norm(acc)
            for s in sl:
                t.lw[s] = ev
                t.rd[s] = dict()

    def op(self, e, fn, reads=(), writes=(), inc=True):
        own = self.sem[e]
        for ev in self._deps(reads, writes):
            if ev[0] is own and e == "pe":
                continue
            self.wait(e, ev)
        ins = fn()
        if inc:
            self.cnt[e] += 1
            ins.then_inc(own, 1)
            ev = (own, self.cnt[e])
        else:
            assert e == "pe"
            ev = (own, self.cnt[e] + 1)
        self._record(ev, reads, writes)
        return ev

    def dma(self, q, out, in_, reads=(), writes=(), **kw):
        n = self.dma_n[q]
        ring = self.rings[q]
        K = len(ring)
        slot = n % K
        if n >= K:
            self.wait(q, (ring[slot], 16 * (n // K)))
        for ev in self._deps(reads, writes):
            self.wait(q, ev)
        self.eng[q].dma_start(out=out, in_=in_, **kw).then_inc(ring[slot], 16)
        self.dma_n[q] = n + 1
        ev = (ring[slot], 16 * (n // K + 1))
        self._record(ev, reads, writes)
        return ev

    def all_events(self):
        evs = []
        for e in ("pe", "act", "dve", "pool"):
            if self.cnt[e] > 0:
                evs.append((self.sem[e], self.cnt[e]))
        for q, ring in self.rings.items():
            n = self.dma_n[q]
            K = len(ring)
            for slot in range(min(n, K)):
                last = ((n - 1 - slot) // K) * K + slot
                evs.append((ring[slot], 16 * (last // K + 1)))
        return evs

    def barrier(self):
        evs = self.all_events()
        for e in self.eng:
            for ev in evs:
                self.wait(e, ev)

    def sbuf(self, name, shape, dt, nslots=1, es=None):
        self.uid = getattr(self, "uid", 0) + 1
        name = f"{name}_{self.uid}"
        h = (es or self.es).enter_context(self.nc.sbuf_tensor(name, list(shape), dt))
        return Tn(h, nslots, name)

    def psum(self, name, shape, dt, nslots=1, es=None):
        h = (es or self.es).enter_context(self.nc.psum_tensor(name, list(shape), dt))
        return Tn(h, nslots, name)

    def dram(self, name, shape, dt, nslots=1):
        h = self.nc.dram_tensor(name, list(shape), dt).ap()
        return Tn(h, nslots, name)


class Prog:
    def __init__(self, debug=()):
        self.debug = set(debug)
        self.nc = bass.Bass("TRN2", target_bir_lowering=False)
        self.dbg_out = {}

    def din(self, name, shape, dt=F32):
        return self.nc.dram_tensor(name, list(shape), dt, kind="ExternalInput").ap()

    def build(self, stages=("mla", "peer0", "dsa", "peer1")):
        nc = self.nc
        I = {}
        I["x"] = self.din("x", [S, D])
        I["ident"] = self.din("ident", [P, P])
        I["cos64"] = self.din("cos64", [P, S])
        I["sin64"] = self.din("sin64", [P, S])
        I["cos128"] = self.din("cos128", [P, S])
        I["sin128"] = self.din("sin128", [P, S])
        I["maskT"] = self.din("maskT", [P, P])
        I["negm"] = self.din("negm", [P, P])
        I["iota"] = self.din("iota", [P, P])
        I["m_win"] = self.din("m_win", [D, 896])
        I["m_wuq"] = self.din("m_wuq", [384, 2048])
        I["m_wukv"] = self.din("m_wukv", [256, 2048])
        I["m_wo"] = self.din("m_wo", [D, D])
        I["m_qn"] = self.din("m_qn", [P, 3])
        I["m_kvn"] = self.din("m_kvn", [P, 2])
        I["d_win"] = self.din("d_win", [D, 6408])
        I["d_wo"] = self.din("d_wo", [D, D])
        for l in range(2):
            I[f"p_wq{l}"] = self.din(f"p_wq{l}", [D, 2048])
            I[f"p_skT{l}"] = self.din(f"p_skT{l}", [16, P, P])
            I[f"p_wdT{l}"] = self.din(f"p_wdT{l}", [P, P, 8 * P])
            I[f"p_wu{l}"] = self.din(f"p_wu{l}", [P * P, D])
        I["ln_g"] = self.din("ln_g", [4, D])
        I["ln_b"] = self.din("ln_b", [4, D])
        self.I = I
        self.out = nc.dram_tensor("out", [S, D], F32, kind="ExternalOutput").ap()

        with ExitStack() as es:
            k = KB(nc, es)
            self.k = k
            self.setup_common(es)
            self.load_x()
            li = 0
            for st in stages:
                if st == "mla":
                    self.mla_layer()
                    self.mixer_out_ln(I["m_wo"], 0)
                elif st == "dsa":
                    self.dsa_layer()
                    self.mixer_out_ln(I["d_wo"], 2)
                    self.dsa_es.close()
                elif st.startswith("peer"):
                    l = int(st[4:])
                    self.peer_layer(l, 2 * l + 1)
            self.store_out()
            k.barrier()
        return nc

    def setup_common(self, es):
        k, nc, I = self.k, self.nc, self.I
        self.X = k.sbuf("X", [P, NT, D], F32, nslots=NT)
        self.xT = k.sbuf("xT", [P, 8, S], BF16, nslots=NT)
        self.ident_f = k.sbuf("ident_f", [P, P], F32)
        self.ident_b = k.sbuf("ident_b", [P, P], BF16)
        self.ones_b = k.sbuf("ones_b", [P, P], BF16)
        self.maskT_b = k.sbuf("maskT_b", [P, P], BF16)
        self.negm = k.sbuf("negm_s", [P, P], F32)
        self.iota_b = k.sbuf("iota_b", [P, P], BF16)
        self.iota_f = k.sbuf("iota_f", [P, P], F32)
        self.eps_ln = k.sbuf("eps_ln", [P, 1], F32)
        self.eps_rms = k.sbuf("eps_rms", [P, 1], F32)
        self.thr_c = k.sbuf("thr_c", [P, 1], F32)
        self.stg = [k.sbuf(f"stg{i}", [P, 2048], F32) for i in range(2)]
        self.stg_i = 0
        self.pb = [k.psum(f"pb{i}", [P, 512], F32) for i in range(7)]
        self.pb_i = 0
        self.ptb = k.psum("ptb", [P, 1024], BF16)
        k.dma("sp", self.ident_f[:], I["ident"], writes=[self.ident_f])
        k.op("dve", lambda: nc.vector.tensor_copy(self.ident_b[:], self.ident_f[:]), reads=[self.ident_f], writes=[self.ident_b])
        k.op("pool", lambda: nc.gpsimd.memset(self.ones_b[:], 1.0), writes=[self.ones_b])
        k.op("pool", lambda: nc.gpsimd.memset(self.eps_ln[:], LN_EPS), writes=[self.eps_ln])
        k.op("pool", lambda: nc.gpsimd.memset(self.eps_rms[:], RMS_EPS), writes=[self.eps_rms])
        k.op("pool", lambda: nc.gpsimd.memset(self.thr_c[:], -1.0e29), writes=[self.thr_c])
        st = self.stage()
        k.dma("sp", st[:, 0:P], I["maskT"], writes=[st])
        k.op("dve", lambda: nc.vector.tensor_copy(self.maskT_b[:], st[:, 0:P]), reads=[st], writes=[self.maskT_b])
        k.dma("sp", self.negm[:], I["negm"], writes=[self.negm])
        k.dma("sp", self.iota_f[:], I["iota"], writes=[self.iota_f])
        k.op("dve", lambda: nc.vector.tensor_copy(self.iota_b[:], self.iota_f[:]), reads=[self.iota_f], writes=[self.iota_b])
        self.ln_st = k.sbuf("ln_st", [P, NT, 12], F32, nslots=NT)
        self.ln_mv = k.sbuf("ln_mv", [P, NT, 4], F32, nslots=NT)
        self.xbf = [k.sbuf(f"xbf{i}", [P, D], BF16) for i in range(2)]
        self.qnD = k.dram("qnD", [8, P, S], BF16, nslots=8)
        self.knD = k.dram("knD", [8, P, S], BF16, nslots=8)
        self.qrD = k.dram("qrD", [8, 64, S], BF16, nslots=8)
        self.krD = k.dram("krD", [P, S], BF16)
        self.VD = k.dram("VD", [NT, P, D], BF16, nslots=NT)
        self.qiD = k.dram("qiD", [4, P, S], BF16, nslots=4)
        self.kiD = k.dram("kiD", [P, S], BF16)

    def stage(self):
        s = self.stg[self.stg_i % len(self.stg)]
        self.stg_i += 1
        return s

    def bank(self):
        b = self.pb[self.pb_i % len(self.pb)]
        self.pb_i += 1
        return b

    def dbg(self, name, t, shape, dt=F32, ap=None):
        if name not in self.debug:
            return
        o = self.nc.dram_tensor("dbg_" + name, list(shape), dt, kind="ExternalOutput").ap()
        self.k.dma("sp", o, ap if ap is not None else t[:], reads=[t])
        self.dbg_out[name] = (shape, dt)

    def make_xT(self, tt):
        k, nc = self.k, self.nc
        xb = self.xbf[tt % 2]
        k.op("act", lambda: nc.scalar.copy(xb[:], self.X[:, tt, :]), reads=[self.X.s(tt)], writes=[xb])
        for c in range(8):
            k.op("pe", lambda c=c: nc.tensor.transpose(self.ptb[:, c * P:(c + 1) * P], xb[:, c * P:(c + 1) * P], self.ident_b[:]),
                 reads=[xb, self.ident_b], writes=[self.ptb], inc=(c == 7))
        k.op("dve", lambda: nc.vector.tensor_copy(self.xT[:, :, tt * P:(tt + 1) * P],
                                                  self.ptb[:].rearrange("p (c t) -> p c t", c=8)),
             reads=[self.ptb], writes=[self.xT.s(tt)])

    def load_x(self):
        k, nc = self.k, self.nc
        for tt in range(NT):
            k.dma("sp", self.X[:, tt, :], self.I["x"][tt * P:(tt + 1) * P, :], writes=[self.X.s(tt)])
            self.make_xT(tt)

    def store_out(self):
        k = self.k
        for tt in range(NT):
            k.dma("sp", self.out[tt * P:(tt + 1) * P, :], self.X[:, tt, :], reads=[self.X.s(tt)])

    def ln_params(self, li, es):
        k, I = self.k, self.I
        self.lng = k.sbuf("lng", [P, D], F32, es=es)
        self.lnb = k.sbuf("lnb", [P, D], F32, es=es)
        k.dma("sp", self.lng[:], I["ln_g"][li:li + 1, :].partition_broadcast(P), writes=[self.lng])
        k.dma("sp", self.lnb[:], I["ln_b"][li:li + 1, :].partition_broadcast(P), writes=[self.lnb])

    def ln_tile(self, tt, li):
        k, nc = self.k, self.nc
        X = self.X
        xt = X[:, tt, :]
        st = self.ln_st
        mv = self.ln_mv
        k.op("dve", lambda: nc.vector.bn_stats(st[:, tt, 0:6], X[:, tt, 0:512]), reads=[X.s(tt)], writes=[st.s(tt)])
        k.op("dve", lambda: nc.vector.bn_stats(st[:, tt, 6:12], X[:, tt, 512:1024]), reads=[X.s(tt)], writes=[st.s(tt)])
        k.op("dve", lambda: nc.vector.bn_aggr(mv[:, tt, 0:2], st[:, tt, :].rearrange("p (a b) -> p a b", a=2)),
             reads=[st.s(tt)], writes=[mv.s(tt)])
        k.op("act", lambda: nc.scalar.activation(mv[:, tt, 2:3], mv[:, tt, 1:2], AF.Sqrt, bias=self.eps_ln[:, 0:1], scale=1.0),
             reads=[mv.s(tt), self.eps_ln], writes=[mv.s(tt)])
        k.op("dve", lambda: nc.vector.reciprocal(mv[:, tt, 2:3], mv[:, tt, 2:3]), reads=[mv.s(tt)], writes=[mv.s(tt)])
        k.op("dve", lambda: nc.vector.scalar_tensor_tensor(mv[:, tt, 3:4], mv[:, tt, 0:1], -1.0, mv[:, tt, 2:3], op0=ALU.mult, op1=ALU.mult),
             reads=[mv.s(tt)], writes=[mv.s(tt)])
        k.op("act", lambda: nc.scalar.activation(xt, xt, AF.Identity, bias=mv[:, tt, 3:4], scale=mv[:, tt, 2:3]),
             reads=[X.s(tt), mv.s(tt)], writes=[X.s(tt)])
        k.op("pool", lambda: nc.gpsimd.tensor_tensor(xt, xt, self.lng[:], op=ALU.mult), reads=[X.s(tt), self.lng], writes=[X.s(tt)])
        k.op("pool", lambda: nc.gpsimd.tensor_tensor(xt, xt, self.lnb[:], op=ALU.add), reads=[X.s(tt), self.lnb], writes=[X.s(tt)])
        self.make_xT(tt)

    def load_w(self, dst, dst_ap, src_ap, ncols, eng="pool", scale=None, dst_slots=None):
        k, nc = self.k, self.nc
        st = self.stage()
        w = [dst] if dst_slots is None else [dst.s(*dst_slots)]
        k.dma("sp", st[:, 0:ncols], src_ap, writes=[st])
        if scale is not None:
            k.op("dve", lambda: nc.vector.tensor_scalar(dst_ap, st[:, 0:ncols], scale[1], None, op0=ALU.mult), reads=[st, scale[0]], writes=w)
        elif eng == "pool":
            k.op("pool", lambda: nc.gpsimd.tensor_copy(dst_ap, st[:, 0:ncols]), reads=[st], writes=w)
        elif eng == "act":
            k.op("act", lambda: nc.scalar.copy(dst_ap, st[:, 0:ncols]), reads=[st], writes=w)
        else:
            k.op("dve", lambda: nc.vector.tensor_copy(dst_ap, st[:, 0:ncols]), reads=[st], writes=w)

    def rope(self, ps_n, ps_s, cos, sin, tb, out_t, out_ap, es_tmp):
        k, nc = self.k, self.nc
        t1, t2 = es_tmp
        sl = slice(tb * 512, (tb + 1) * 512)
        k.op("dve", lambda: nc.vector.tensor_tensor(t1[:], ps_n[:], cos[:, sl], op=ALU.mult), reads=[ps_n, cos], writes=[t1])
        k.op("dve", lambda: nc.vector.tensor_tensor(t2[:], ps_s[:], sin[:, sl], op=ALU.mult), reads=[ps_s, sin], writes=[t2])
        k.op("pool", lambda: nc.gpsimd.tensor_tensor(out_ap, t1[:], t2[:], op=ALU.add), reads=[t1, t2], writes=[out_t])

    def mla_layer(self):
        k, nc, I = self.k, self.nc, self.I
        xT = self.xT
        with ExitStack() as es:
            win = k.sbuf("m_win_b", [P, 8, 896], BF16, es=es)
            wuq = k.sbuf("m_wuq_b", [P, 3, 2048], BF16, es=es)
            wukv = k.sbuf("m_wukv_b", [P, 2, 2048], BF16, es=es)
            qn = k.sbuf("m_qn_s", [P, 3], F32, es=es)
            kvn = k.sbuf("m_kvn_s", [P, 2], F32, es=es)
            cos = k.sbuf("cos64_s", [P, S], F32, es=es)
            sin = k.sbuf("sin64_s", [P, S], F32, es=es)
            k.dma("sp", qn[:], I["m_qn"], writes=[qn])
            k.dma("sp", kvn[:], I["m_kvn"], writes=[kvn])
            k.dma("sp", cos[:], I["cos64"], writes=[cos])
            k.dma("sp", sin[:], I["sin64"], writes=[sin])
            for c in range(8):
                self.load_w(win, win[:, c, :], I["m_win"][c * P:(c + 1) * P, :], 896, eng=("pool", "act")[c % 2])
            for c in range(3):
                self.load_w(wuq, wuq[:, c, :], I["m_wuq"][c * P:(c + 1) * P, :], 2048, scale=(qn, qn[:, c:c + 1]))
            for c in range(2):
                self.load_w(wukv, wukv[:, c, :], I["m_wukv"][c * P:(c + 1) * P, :], 2048, scale=(kvn, kvn[:, c:c + 1]))
            lat_f = k.sbuf("lat_f", [P, 5, 512], F32, es=es)
            sq_b = k.sbuf("sq_b", [P, 5, 512], BF16, es=es)
            rs = k.sbuf("rs", [P, 2, 512], F32, es=es)
            lat_n = [k.sbuf(f"lat_n{i}", [P, 5, 512], BF16, es=es) for i in range(1)]
            t1 = k.sbuf("rp_t1", [P, 512], F32, es=es)
            t2 = k.sbuf("rp_t2", [P, 512], F32, es=es)
            ob = [k.sbuf(f"m_ob{i}", [P, 512], BF16, es=es) for i in range(4)]
            vb = [k.sbuf(f"m_vb{i}", [P, D], BF16, es=es) for i in range(2)]
            obi = 0
            for tb in range(4):
                tsl = slice(tb * 512, (tb + 1) * 512)
                tslots = tuple(range(tb * 4, tb * 4 + 4))
                ln = lat_n[0]
                for c in range(5):
                    ps = self.bank()
                    for dc in range(8):
                        k.op("pe", lambda dc=dc, c=c, ps=ps: nc.tensor.matmul(ps[:], win[:, dc, c * P:(c + 1) * P], xT[:, dc, tsl],
                                                                               start=(dc == 0), stop=(dc == 7)),
                             reads=[win, xT.s(*tslots)], writes=[ps], inc=(dc == 7))
                    k.op("act", lambda c=c, ps=ps: nc.scalar.copy(lat_f[:, c, :], ps[:]), reads=[ps], writes=[lat_f])
                    k.op("act", lambda c=c, ps=ps: nc.scalar.activation(sq_b[:, c, :], ps[:], AF.Square), reads=[ps], writes=[sq_b])
                for gi, (c0, nch, width) in enumerate(((0, 3, 384), (3, 2, 256))):
                    ps = self.bank()
                    for c in range(nch):
                        k.op("pe", lambda c=c, ps=ps, c0=c0, nch=nch: nc.tensor.matmul(ps[:], self.ones_b[:], sq_b[:, c0 + c, :],
                                                                                        start=(c == 0), stop=(c == nch - 1)),
                             reads=[self.ones_b, sq_b], writes=[ps], inc=(c == nch - 1))
                    k.op("act", lambda ps=ps, gi=gi, width=width: nc.scalar.activation(rs[:, gi, :], ps[:], AF.Sqrt, bias=self.eps_rms[:, 0:1],
                                                                                         scale=1.0 / width),
                         reads=[ps, self.eps_rms], writes=[rs])
                    k.op("dve", lambda gi=gi: nc.vector.reciprocal(rs[:, gi, :], rs[:, gi, :]), reads=[rs], writes=[rs])
                    k.op("dve", lambda gi=gi, c0=c0, nch=nch, ln=ln: nc.vector.tensor_tensor(
                        ln[:, c0:c0 + nch, :], lat_f[:, c0:c0 + nch, :], rs[:, gi, :].unsqueeze(1).to_broadcast([P, nch, 512]), op=ALU.mult),
                         reads=[lat_f, rs], writes=[ln])
                psn = self.bank()
                pss = self.bank()
                for (ps, c0) in ((psn, 640), (pss, 768)):
                    for dc in range(8):
                        k.op("pe", lambda dc=dc, ps=ps, c0=c0: nc.tensor.matmul(ps[:], win[:, dc, c0:c0 + P], xT[:, dc, tsl],
                                                                                 start=(dc == 0), stop=(dc == 7)),
                             reads=[win, xT.s(*tslots)], writes=[ps], inc=(dc == 7))
                o = ob[obi % 4]; obi += 1
                self.rope(psn, pss, cos, sin, tb, o, o[:], (t1, t2))
                k.dma("sp", self.krD[:, tsl], o[:], reads=[o], writes=[self.krD])
                for h in range(8):
                    ps = self.bank()
                    for rc in range(3):
                        k.op("pe", lambda rc=rc, ps=ps, h=h: nc.tensor.matmul(ps[:], wuq[:, rc, h * P:(h + 1) * P], ln[:, rc, :],
                                                                               start=(rc == 0), stop=(rc == 2)),
                             reads=[wuq, ln], writes=[ps], inc=(rc == 2))
                    o = ob[obi % 4]; obi += 1
                    k.op("act", lambda ps=ps, o=o: nc.scalar.copy(o[:], ps[:]), reads=[ps], writes=[o])
                    k.dma("sp", self.qnD[h, :, tsl], o[:], reads=[o], writes=[self.qnD.s(h)])
                for pr in range(4):
                    psn = self.bank()
                    pss = self.bank()
                    for (ps, c0) in ((psn, 1024 + pr * P), (pss, 1536 + pr * P)):
                        for rc in range(3):
                            k.op("pe", lambda rc=rc, ps=ps, c0=c0: nc.tensor.matmul(ps[:], wuq[:, rc, c0:c0 + P], ln[:, rc, :],
                                                                                     start=(rc == 0), stop=(rc == 2)),
                                 reads=[wuq, ln], writes=[ps], inc=(rc == 2))
                    o = ob[obi % 4]; obi += 1
                    self.rope(psn, pss, cos, sin, tb, o, o[:], (t1, t2))
                    k.dma("sp", self.qrD[2 * pr, :, tsl], o[0:64, :], reads=[o], writes=[self.qrD.s(2 * pr)])
                    k.dma("sp", self.qrD[2 * pr + 1, :, tsl], o[64:128, :], reads=[o], writes=[self.qrD.s(2 * pr + 1)])
                for h in range(8):
                    ps = self.bank()
                    for rc in range(2):
                        k.op("pe", lambda rc=rc, ps=ps, h=h: nc.tensor.matmul(ps[:], wukv[:, rc, h * P:(h + 1) * P], ln[:, 3 + rc, :],
                                                                               start=(rc == 0), stop=(rc == 1)),
                             reads=[wukv, ln], writes=[ps], inc=(rc == 1))
                    o = ob[obi % 4]; obi += 1
                    k.op("dve", lambda ps=ps, o=o: nc.vector.tensor_copy(o[:], ps[:]), reads=[ps], writes=[o])
                    k.dma("sp", self.knD[h, :, tsl], o[:], reads=[o], writes=[self.knD.s(h)])
                for t4 in range(4):
                    tt = tb * 4 + t4
                    v = vb[tt % 2]
                    for half in range(2):
                        ps = self.bank()
                        for rc in range(2):
                            k.op("pe", lambda rc=rc, ps=ps, half=half, t4=t4: nc.tensor.matmul(
                                ps[:], ln[:, 3 + rc, t4 * P:(t4 + 1) * P], wukv[:, rc, 1024 + half * 512:1024 + (half + 1) * 512],
                                start=(rc == 0), stop=(rc == 1)),
                                 reads=[wukv, ln], writes=[ps], inc=(rc == 1))
                        if half == 0:
                            k.op("act", lambda ps=ps, v=v: nc.scalar.copy(v[:, 0:512], ps[:]), reads=[ps], writes=[v])
                        else:
                            k.op("dve", lambda ps=ps, v=v: nc.vector.tensor_copy(v[:, 512:1024], ps[:]), reads=[ps], writes=[v])
                    k.dma("sp", self.VD[tt], v[:], reads=[v], writes=[self.VD.s(tt)])
            k.barrier()
        self.attention(mla=True)

    def attention(self, mla, selT=None, selT_t=None):
        k, nc = self.k, self.nc
        scale = (192.0 if mla else 128.0) ** -0.5
        self.att_es = ExitStack()
        es = self.att_es
        OT = k.sbuf("OT", [P, 8, S], BF16, nslots=8, es=es)
        self.OT = OT
        with ExitStack() as es2:
            kn = [Tn(self.stg[i][:].bitcast(BF16), 1, f"a_kn{i}") for i in range(2)]
            qn = [k.sbuf(f"a_qn{i}", [P, S], BF16, es=es2) for i in range(2)]
            vh = [k.sbuf(f"a_vh{i}", [P, NT, P], BF16, es=es2) for i in range(2)]
            pt = [k.sbuf(f"a_pt{i}", [P, 512], BF16, es=es2) for i in range(3)]
            rec = k.sbuf("a_rec", [P, 512], F32, es=es2)
            if mla:
                qr = [k.sbuf(f"a_qr{i}", [64, S], BF16, es=es2) for i in range(2)]
                kr = k.sbuf("a_kr", [64, S], BF16, es=es2)
                k.dma("sp", kr[:], self.krD[0:64, :], reads=[self.krD], writes=[kr])
            def load_head(h):
                b = h % 2
                k.dma("sp", kn[b][:, 0:S], self.knD[h], reads=[self.knD.s(h)], writes=[kn[b]])
                k.dma("sp", qn[b][:], self.qnD[h], reads=[self.qnD.s(h)], writes=[qn[b]])
                k.dma("sp", vh[b][:], self.VD[:, :, h * P:(h + 1) * P].rearrange("t p v -> p t v"), reads=[self.VD], writes=[vh[b]])
                if mla:
                    k.dma("sp", qr[b][:], self.qrD[h], reads=[self.qrD.s(h)], writes=[qr[b]])

            pairs = []
            for h in range(8):
                for QB in range(4):
                    for kc in range(4 * QB + 4):
                        pairs.append((h, QB, kc))
            info = {}

            def emit_qk(i):
                h, QB, kc = pairs[i]
                b = h % 2
                if QB == 0 and kc == 0:
                    if h == 0:
                        load_head(0)
                    if h + 1 < 8:
                        load_head(h + 1)
                qlo = max(kc, 4 * QB)
                c0 = (qlo - 4 * QB) * P
                qs = slice(QB * 512 + c0, (QB + 1) * 512)
                ks = slice(kc * P, (kc + 1) * P)
                st = self.pb[i % 3]
                k.op("pe", lambda: nc.tensor.matmul(st[:, c0:512], kn[b][:, ks], qn[b][:, qs], start=True, stop=(not mla)),
                     reads=[kn[b], qn[b]], writes=[st], inc=(not mla))
                if mla:
                    k.op("pe", lambda: nc.tensor.matmul(st[:, c0:512], kr[:, ks], qr[b][:, qs], start=False, stop=True),
                         reads=[kr, qr[b]], writes=[st])
                info[i] = (st, c0, qs)

            def emit_soft(i):
                h, QB, kc = pairs[i]
                st, c0, qs = info[i]
                p_ = pt[i % 3]
                k.op("act", lambda: nc.scalar.activation(p_[:, c0:512], st[:, c0:512], AF.Exp, scale=scale), reads=[st], writes=[p_])
                if mla:
                    if kc >= 4 * QB:
                        k.op("dve", lambda: nc.vector.tensor_tensor(p_[:, c0:c0 + P], p_[:, c0:c0 + P], self.maskT_b[:], op=ALU.mult),
                             reads=[p_, self.maskT_b], writes=[p_])
                else:
                    k.op("dve", lambda: nc.vector.tensor_tensor(p_[:, c0:512], p_[:, c0:512], selT(kc, qs.start, qs.stop), op=ALU.mult),
                         reads=[p_, selT_t], writes=[p_])

            def emit_pv(i):
                h, QB, kc = pairs[i]
                b = h % 2
                st, c0, qs = info.pop(i)
                p_ = pt[i % 3]
                oT = self.pb[3 + QB % 2]
                sm = self.pb[5 + QB % 2]
                last = 4 * QB + 3
                k.op("pe", lambda: nc.tensor.matmul(oT[:, c0:512], vh[b][:, kc, :], p_[:, c0:512], start=(kc == 0), stop=(kc == last), skip_group_check=True),
                     reads=[vh[b], p_], writes=[oT], inc=False)
                k.op("pe", lambda: nc.tensor.matmul(sm[:, c0:512], self.ones_b[:], p_[:, c0:512], start=(kc == 0), stop=(kc == last), skip_group_check=True),
                     reads=[self.ones_b, p_], writes=[sm])
                if kc == last:
                    k.op("dve", lambda: nc.vector.reciprocal(rec[:], sm[:]), reads=[sm], writes=[rec])
                    k.op("dve", lambda: nc.vector.tensor_tensor(OT[:, h, QB * 512:(QB + 1) * 512], oT[:], rec[:], op=ALU.mult),
                         reads=[oT, rec], writes=[OT.s(h)])

            npairs = len(pairs)
            LA = 2
            for i in range(min(LA, npairs)):
                emit_qk(i)
            for i in range(npairs):
                emit_soft(i)
                if i + LA < npairs:
                    emit_qk(i + LA)
                emit_pv(i)
            k.barrier()

    def mixer_out_ln(self, wo_ap, li):
        k, nc = self.k, self.nc
        OT = self.OT
        with ExitStack() as es:
            wo = k.sbuf("wo_b", [P, 8, 512], BF16, es=es)
            self.ln_params(li, es)
            for half in range(2):
                for c in range(8):
                    self.load_w(wo, wo[:, c, :], wo_ap[c * P:(c + 1) * P, half * 512:(half + 1) * 512], 512, eng=("pool", "act")[c % 2])
                for tt in range(NT):
                    ps = self.bank()
                    for h in range(8):
                        k.op("pe", lambda ps=ps, h=h, tt=tt: nc.tensor.matmul(
                            ps[:], OT[:, h, tt * P:(tt + 1) * P], wo[:, h, :], start=(h == 0), stop=(h == 7)),
                             reads=[OT, wo], writes=[ps], inc=(h == 7))
                    xs = self.X[:, tt, half * 512:(half + 1) * 512]
                    k.op("dve", lambda ps=ps, xs=xs: nc.vector.scalar_tensor_tensor(xs, xs, ALPHA, ps[:], op0=ALU.mult, op1=ALU.add),
                         reads=[ps, self.X.s(tt)], writes=[self.X.s(tt)])
            for tt in range(NT):
                self.ln_tile(tt, li)
            k.barrier()
        self.att_es.close()

    def dsa_layer(self):
        k, nc, I = self.k, self.nc, self.I
        xT = self.xT
        WI = I["d_win"].rearrange("(c p) n -> p c n", p=P)
        self.dsa_es = ExitStack()
        esD = self.dsa_es
        selT = k.sbuf("d_selT", [P, 136 * P], BF16, es=esD)
        wtok = k.sbuf("d_wtok", [P, NT, 8], F32, es=esD)

        def soff(kc):
            return P * (16 * kc - kc * (kc - 1) // 2)

        with ExitStack() as es:
            wb = [k.sbuf(f"d_wb{i}", [P, 8, 512], BF16, es=es) for i in range(2)]
            t1 = k.sbuf("d_t1", [P, 512], F32, es=es)
            t2 = k.sbuf("d_t2", [P, 512], F32, es=es)
            ob = [k.sbuf(f"d_ob{i}", [P, 512], BF16, es=es) for i in range(4)]
            cos = k.sbuf("d_cos", [P, S], F32, es=es)
            sin = k.sbuf("d_sin", [P, S], F32, es=es)
            gi = 0
            obi = 0

            def load_group(c0, ncols):
                nonlocal gi
                wb_ = wb[gi % 2]
                gi += 1
                for hf in range(2):
                    st_ = self.stage()
                    sv = st_[:].rearrange("p (c n) -> p c n", c=4)
                    k.dma("sp", sv[:, :, 0:ncols], WI[:, hf * 4:(hf + 1) * 4, c0:c0 + ncols], writes=[st_])
                    k.op("pool", lambda sv=sv, hf=hf: nc.gpsimd.tensor_copy(wb_[:, hf * 4:(hf + 1) * 4, 0:ncols], sv[:, :, 0:ncols]), reads=[st_], writes=[wb_])
                return wb_

            def rope_pairs(c0, npairs, dst_fn):
                nonlocal obi
                wb_ = load_group(c0, npairs * 256)
                for tb in range(4):
                    tsl = slice(tb * 512, (tb + 1) * 512)
                    tslots = tuple(range(tb * 4, tb * 4 + 4))
                    for pi in range(npairs):
                        psn = self.bank()
                        pss = self.bank()
                        for (ps, cc) in ((psn, pi * 256), (pss, pi * 256 + P)):
                            for dc in range(8):
                                k.op("pe", lambda dc=dc, ps=ps, cc=cc: nc.tensor.matmul(ps[:], wb_[:, dc, cc:cc + P], xT[:, dc, tsl],
                                                                                         start=(dc == 0), stop=(dc == 7)),
                                     reads=[wb_, xT.s(*tslots)], writes=[ps], inc=(dc == 7))
                        o = ob[obi % 4]; obi += 1
                        self.rope(psn, pss, cos, sin, tb, o, o[:], (t1, t2))
                        dt_, dap = dst_fn(pi, tsl)
                        k.dma("sp", dap, o[:], reads=[o], writes=[dt_])

            k.dma("sp", cos[:], I["cos128"], writes=[cos])
            k.dma("sp", sin[:], I["sin128"], writes=[sin])
            for g in range(4):
                rope_pairs(g * 512, 2, lambda pi, tsl, g=g: (self.qnD.s(2 * g + pi), self.qnD[2 * g + pi, :, tsl]))
            for g in range(4):
                rope_pairs(2048 + g * 512, 2, lambda pi, tsl, g=g: (self.knD.s(2 * g + pi), self.knD[2 * g + pi, :, tsl]))
            k.dma("sp", cos[:], I["cos64"], writes=[cos])
            k.dma("sp", sin[:], I["sin64"], writes=[sin])
            for g in range(2):
                rope_pairs(4096 + g * 512, 2, lambda pi, tsl, g=g: (self.qiD.s(2 * g + pi), self.qiD[2 * g + pi, :, tsl]))
            rope_pairs(5120, 1, lambda pi, tsl: (self.kiD, self.kiD[:, tsl]))
            vb = [k.sbuf(f"d_vb{i}", [P, 512], BF16, es=es) for i in range(2)]
            vi = 0
            for half in range(2):
                wb_ = load_group(5376 + half * 512, 512)
                for tt in range(NT):
                    ps = self.bank()
                    for dc in range(8):
                        k.op("pe", lambda dc=dc, ps=ps, tt=tt: nc.tensor.matmul(ps[:], xT[:, dc, tt * P:(tt + 1) * P], wb_[:, dc, :],
                                                                                 start=(dc == 0), stop=(dc == 7)),
                             reads=[wb_, xT.s(tt)], writes=[ps], inc=(dc == 7))
                    v = vb[vi % 2]; vi += 1
                    if tt % 2 == 0:
                        k.op("act", lambda ps=ps, v=v: nc.scalar.copy(v[:], ps[:]), reads=[ps], writes=[v])
                    else:
                        k.op("dve", lambda ps=ps, v=v: nc.vector.tensor_copy(v[:], ps[:]), reads=[ps], writes=[v])
                    k.dma("sp", self.VD[tt, :, half * 512:(half + 1) * 512], v[:], reads=[v], writes=[self.VD.s(tt)])
            wb_ = load_group(6400, 8)
            wscale = float(8 ** -0.5 * 64 ** -0.5)
            for tt in range(NT):
                ps = self.bank()
                for dc in range(8):
                    k.op("pe", lambda dc=dc, ps=ps, tt=tt: nc.tensor.matmul(ps[:, 0:8], xT[:, dc, tt * P:(tt + 1) * P], wb_[:, dc, 0:8],
                                                                             start=(dc == 0), stop=(dc == 7)),
                         reads=[wb_, xT.s(tt)], writes=[ps], inc=(dc == 7))
                k.op("act", lambda ps=ps, tt=tt: nc.scalar.mul(wtok[:, tt, :], ps[:, 0:8], wscale), reads=[ps], writes=[wtok])
            k.barrier()
        with ExitStack() as es:
            kiT = k.sbuf("d_kiT", [P, S], BF16, es=es)
            qiT = k.sbuf("d_qiT", [P, 4, S], BF16, es=es)
            accs = [k.sbuf(f"d_acc{i}", [P, S], F32, es=es) for i in range(2)]
            rls = [k.sbuf(f"d_rl{i}", [P, 512], F32, es=es) for i in range(2)]
            sels = [k.sbuf(f"d_sel{i}", [P, S], BF16, es=es) for i in range(1)]
            mxs = [k.sbuf(f"d_mx{i}", [P, 8], F32, es=es) for i in range(4)]
            k.dma("sp", kiT[:], self.kiD[:], reads=[self.kiD], writes=[kiT])
            for pr in range(4):
                k.dma("sp", qiT[:, pr, :], self.qiD[pr], reads=[self.qiD.s(pr)], writes=[qiT])
            rli = 0
            tbi = 0
            sel2 = k.sbuf("d_sel2", [P, S], BF16, es=es)

            def q_chain(qi):
                nonlocal rli, tbi
                n = (qi + 1) * P
                acc = accs[qi % 2]
                mxs_ = mxs[2 * (qi % 2):2 * (qi % 2) + 2]
                qsl = slice(qi * P, (qi + 1) * P)
                for h in range(8):
                    pr, hp = h // 2, h % 2
                    prt = slice(hp * 64, (hp + 1) * 64)
                    for k0 in range(0, n, 512):
                        kw = min(512, n - k0)
                        ps = self.bank()
                        k.op("pe", lambda ps=ps, kw=kw, k0=k0, pr=pr, prt=prt, qsl=qsl: nc.tensor.matmul(
                            ps[:, 0:kw], qiT[prt, pr, qsl], kiT[prt, k0:k0 + kw], start=True, stop=True),
                             reads=[qiT, kiT], writes=[ps])
                        rl = rls[rli % 2]; rli += 1
                        k.op("act", lambda ps=ps, rl=rl, kw=kw: nc.scalar.activation(rl[:, 0:kw], ps[:, 0:kw], AF.Relu), reads=[ps], writes=[rl])
                        if h == 0:
                            k.op("dve", lambda rl=rl, kw=kw, k0=k0, acc=acc, qi=qi: nc.vector.tensor_scalar(
                                acc[:, k0:k0 + kw], rl[:, 0:kw], wtok[:, qi, 0:1], None, op0=ALU.mult), reads=[rl, wtok], writes=[acc])
                        else:
                            k.op("dve", lambda rl=rl, kw=kw, k0=k0, acc=acc, qi=qi, h=h: nc.vector.scalar_tensor_tensor(
                                acc[:, k0:k0 + kw], rl[:, 0:kw], wtok[:, qi, h:h + 1], acc[:, k0:k0 + kw], op0=ALU.mult, op1=ALU.add),
                                 reads=[rl, wtok, acc], writes=[acc])
                        yield
                k.op("pool", lambda acc=acc, n=n: nc.gpsimd.tensor_tensor(acc[:, n - P:n], acc[:, n - P:n], self.negm[:], op=ALU.add),
                     reads=[acc, self.negm], writes=[acc])
                sel = sels[0] if qi % 2 == 0 else sel2
                if qi >= 2:
                    for r in range(32):
                        mx = mxs_[r % 2]
                        k.op("dve", lambda mx=mx, acc=acc, n=n: nc.vector.max(out=mx[:], in_=acc[:, 0:n]), reads=[acc], writes=[mx])
                        yield
                        if r < 31:
                            k.op("dve", lambda mx=mx, acc=acc, n=n: nc.vector.match_replace(
                                out=acc[:, 0:n], in_to_replace=mx[:], in_values=acc[:, 0:n], imm_value=-3.0e30),
                                 reads=[acc, mx], writes=[acc])
                            yield
                    thr_t, thr = mxs_[31 % 2], mxs_[31 % 2][:, 7:8]
                    k.op("dve", lambda sel=sel, acc=acc, n=n, thr=thr: nc.vector.tensor_scalar(sel[:, 0:n], acc[:, 0:n], thr, None, op0=ALU.is_ge),
                         reads=[acc, thr_t], writes=[sel])
                    yield
                    k.op("dve", lambda sel=sel, acc=acc, n=n: nc.vector.scalar_tensor_tensor(
                        sel[:, 0:n], acc[:, 0:n], -2.0e30, sel[:, 0:n], op0=ALU.is_le, op1=ALU.add),
                         reads=[acc, sel], writes=[sel])
                    yield
                else:
                    thr_t, thr = self.thr_c, self.thr_c[:, 0:1]
                    k.op("dve", lambda sel=sel, acc=acc, n=n, thr=thr: nc.vector.tensor_scalar(sel[:, 0:n], acc[:, 0:n], thr, None, op0=ALU.is_ge),
                         reads=[acc, thr_t], writes=[sel])
                    yield
                for kc in range(qi + 1):
                    blk = tbi % 8; tbi += 1
                    k.op("pe", lambda sel=sel, kc=kc, blk=blk: nc.tensor.transpose(self.ptb[:, blk * P:(blk + 1) * P], sel[:, kc * P:(kc + 1) * P], self.ident_b[:]),
                         reads=[sel, self.ident_b], writes=[self.ptb])
                    o_ = soff(kc) + (qi - kc) * P
                    k.op("act", lambda blk=blk, o_=o_: nc.scalar.copy(selT[:, o_:o_ + P], self.ptb[:, blk * P:(blk + 1) * P]),
                         reads=[self.ptb], writes=[selT])

            for pa in range(8):
                gens = [q_chain(2 * pa + 1), q_chain(2 * pa)]
                while gens:
                    for g_ in list(gens):
                        try:
                            next(g_)
                        except StopIteration:
                            gens.remove(g_)
            k.barrier()
        self.dbg("selT", selT, [P, 136 * P], BF16)
        self.attention(mla=False, selT=lambda kc, q0, q1: selT[:, soff(kc) + q0 - kc * P: soff(kc) + q1 - kc * P], selT_t=selT)

    def peer_layer(self, l, li):
        k, nc, I = self.k, self.nc, self.I
        xT, X = self.xT, self.X
        AXX = mybir.AxisListType.X
        if not hasattr(self, "gateD"):
            self.gateD = k.dram("gateD", [P, P, S], BF16, nslots=P)
        gateD = self.gateD
        with ExitStack() as esL:
            LT = k.sbuf("p_LT", [P, 3, S], BF16, nslots=NT, es=esL)
            with ExitStack() as es:
                wq = k.sbuf("p_wq_b", [P, 8, 2048], BF16, es=es)
                skT = k.sbuf("p_skT_b", [P, 16, P], BF16, es=es)
                for c in range(8):
                    self.load_w(wq, wq[:, c, :], I[f"p_wq{l}"][c * P:(c + 1) * P, :], 2048, eng=("pool", "act")[c % 2])
                st = self.stage()
                k.dma("sp", st[:].rearrange("p (g n) -> p g n", g=16), I[f"p_skT{l}"].rearrange("g d n -> d g n"), writes=[st])
                k.op("dve", lambda: nc.vector.tensor_copy(skT[:].rearrange("p g n -> p (g n)"), st[:]), reads=[st], writes=[skT])
                qTb = k.sbuf("p_qTb", [P, 16, 256], BF16, es=es)
                s_sb = Tn(self.stg[0].h, 16, "p_s")
                work = Tn(self.stg[1].h, 16, "p_work")
                v16 = k.sbuf("p_v16", [P, 256], F32, nslots=16, es=es)
                i16 = k.sbuf("p_i16", [P, 256], U32, nslots=16, es=es)
                i16f = k.sbuf("p_i16f", [P, 256], F32, es=es)
                best = k.sbuf("p_best", [P, 128], F32, nslots=8, es=es)
                pos = k.sbuf("p_pos", [P, 128], U32, nslots=8, es=es)
                pab = k.sbuf("p_pab", [P, 2, 128], U32, es=es)
                pabf = k.sbuf("p_pabf", [P, 2, 128], F32, es=es)
                sm8 = k.sbuf("p_sm8", [P, 3, 8], F32, es=es)
                ex = k.sbuf("p_ex", [P, 128], F32, es=es)
                Lt = k.sbuf("p_Lt", [P, 3, 128], F32, es=es)
                k.barrier()
                cand = s_sb

                def selfsync():
                    if k.cnt["dve"] > 0:
                        k.wait("dve", (k.sem["dve"], k.cnt["dve"]))

                def tile_chain(tt, t2):
                    for bq in range(4):
                        ps = self.bank()
                        for gg in range(4):
                            g = bq * 4 + gg
                            k.op("pe", lambda ps=ps, g=g, gg=gg: nc.tensor.matmul(
                                ps[:, gg * P:(gg + 1) * P], qTb[:, g, t2 * P:(t2 + 1) * P], skT[:, g, :], start=True, stop=True),
                                 reads=[qTb, skT], writes=[ps], inc=(gg == 3))
                        k.op("act", lambda ps=ps, bq=bq: nc.scalar.copy(s_sb[:, bq * 512:(bq + 1) * 512], ps[:]),
                             reads=[ps], writes=[s_sb.s(*range(bq * 4, bq * 4 + 4))])
                    G16 = range(16)
                    sg = lambda g: s_sb[:, g * P:(g + 1) * P]
                    va = lambda g: v16[:, g * 16:g * 16 + 8]
                    vb_ = lambda g: v16[:, g * 16 + 8:g * 16 + 16]
                    wk = lambda g: work[:, g * P:(g + 1) * P]
                    for g in G16:
                        k.op("dve", lambda g=g: nc.vector.max(out=va(g), in_=sg(g)), reads=[s_sb.s(g)], writes=[v16.s(g)])
                    selfsync()
                    for g in G16:
                        k.op("dve", lambda g=g: nc.vector.max_index(i16[:, g * 16:g * 16 + 8], va(g), sg(g)), reads=[s_sb.s(g), v16.s(g)], writes=[i16.s(g)])
                    for g in G16:
                        k.op("dve", lambda g=g: nc.vector.match_replace(out=wk(g), in_to_replace=va(g), in_values=sg(g), imm_value=NEG),
                             reads=[s_sb.s(g), v16.s(g)], writes=[work.s(g)])
                    selfsync()
                    for g in G16:
                        k.op("dve", lambda g=g: nc.vector.max(out=vb_(g), in_=wk(g)), reads=[work.s(g)], writes=[v16.s(g)])
                    selfsync()
                    for g in G16:
                        k.op("dve", lambda g=g: nc.vector.max_index(i16[:, g * 16 + 8:g * 16 + 16], vb_(g), sg(g)), reads=[s_sb.s(g), v16.s(g)], writes=[i16.s(g)])
                    k.op("dve", lambda: nc.vector.tensor_copy(i16f[:], i16[:]), reads=[i16], writes=[i16f])
                    v4 = v16[:].rearrange("p (h c a) -> p h c a", h=8, c=2)
                    i4 = i16f[:].rearrange("p (h c a) -> p h c a", h=8, c=2)
                    c4 = cand[:].rearrange("p (h a b) -> p h a b", h=8, a=16)
                    k.op("dve", lambda: nc.vector.tensor_tensor(c4, v4[:, :, 0, :].unsqueeze(3).to_broadcast([P, 8, 16, 16]),
                                                                v4[:, :, 1, :].unsqueeze(2).to_broadcast([P, 8, 16, 16]), op=ALU.add),
                         reads=[v16], writes=[cand])
                    H8 = range(8)
                    ch = lambda h: cand[:, h * 256:(h + 1) * 256]
                    cs = lambda h: cand.s(2 * h, 2 * h + 1)
                    ws = lambda h: work.s(2 * h, 2 * h + 1)
                    wh = lambda h: work[:, h * 256:(h + 1) * 256]
                    ba = lambda h: best[:, h * 16:h * 16 + 8]
                    bb = lambda h: best[:, h * 16 + 8:h * 16 + 16]
                    selfsync()
                    for h in H8:
                        k.op("dve", lambda h=h: nc.vector.max(out=ba(h), in_=ch(h)), reads=[cs(h)], writes=[best.s(h)])
                    selfsync()
                    for h in H8:
                        k.op("dve", lambda h=h: nc.vector.max_index(pos[:, h * 16:h * 16 + 8], ba(h), ch(h)), reads=[cs(h), best.s(h)], writes=[pos.s(h)])
                    for h in H8:
                        k.op("dve", lambda h=h: nc.vector.match_replace(out=wh(h), in_to_replace=ba(h), in_values=ch(h), imm_value=NEG),
                             reads=[cs(h), best.s(h)], writes=[ws(h)])
                    selfsync()
                    for h in H8:
                        k.op("dve", lambda h=h: nc.vector.max(out=bb(h), in_=wh(h)), reads=[ws(h)], writes=[best.s(h)])
                    selfsync()
                    for h in H8:
                        k.op("dve", lambda h=h: nc.vector.max_index(pos[:, h * 16 + 8:h * 16 + 16], bb(h), ch(h)), reads=[cs(h), best.s(h)], writes=[pos.s(h)])
                    b3 = best[:].rearrange("p (h r) -> p h r", h=8)
                    e3 = ex[:].rearrange("p (h r) -> p h r", h=8)
                    k.op("dve", lambda: nc.vector.tensor_tensor(e3, b3, b3[:, :, 0:1].to_broadcast([P, 8, 16]), op=ALU.subtract),
                         reads=[best], writes=[ex])
                    k.op("act", lambda: nc.scalar.activation(ex[:], ex[:], AF.Exp), reads=[ex], writes=[ex])
                    k.op("dve", lambda: nc.vector.tensor_single_scalar(pab[:, 0, :], pos[:], 4, op=ALU.logical_shift_right), reads=[pos], writes=[pab])
                    k.op("dve", lambda: nc.vector.tensor_single_scalar(pab[:, 1, :], pos[:], 15, op=ALU.bitwise_and), reads=[pos], writes=[pab])
                    k.op("dve", lambda: nc.vector.tensor_copy(pabf[:], pab[:]), reads=[pab], writes=[pabf])
                    k.op("dve", lambda: nc.vector.reduce_sum(sm8[:, 0, :], e3, axis=AXX), reads=[ex], writes=[sm8])
                    k.op("dve", lambda: nc.vector.reciprocal(sm8[:, 1, :], sm8[:, 0, :]), reads=[sm8], writes=[sm8])
                    k.op("dve", lambda: nc.vector.tensor_tensor(Lt[:, 0, :].rearrange("p (h r) -> p h r", h=8), e3,
                                                                sm8[:, 1, :].unsqueeze(2).to_broadcast([P, 8, 16]), op=ALU.mult),
                         reads=[ex, sm8], writes=[Lt])
                    for w_ in range(2):
                        abf = pabf[:, w_, :].rearrange("p (h r) -> p h r", h=8)
                        k.op("dve", lambda abf=abf: nc.vector.tensor_tensor(
                            c4, self.iota_f[:, 0:16].unsqueeze(1).unsqueeze(1).to_broadcast([P, 8, 16, 16]),
                            abf.unsqueeze(3).to_broadcast([P, 8, 16, 16]), op=ALU.is_equal),
                             reads=[self.iota_f, pabf], writes=[cand])
                        k.op("dve", lambda w_=w_: nc.vector.tensor_tensor(c4, c4, i4[:, :, w_, :].unsqueeze(2).to_broadcast([P, 8, 16, 16]), op=ALU.mult),
                             reads=[i16f, cand], writes=[cand])
                        k.op("dve", lambda w_=w_: nc.vector.reduce_sum(Lt[:, 1 + w_, :].rearrange("p (h r) -> p h r", h=8), c4, axis=AXX),
                             reads=[cand], writes=[Lt])
                    ps = self.bank()
                    for q3 in range(3):
                        k.op("pe", lambda q3=q3, ps=ps: nc.tensor.transpose(ps[:, q3 * P:(q3 + 1) * P], Lt[:, q3, :], self.ident_f[:]),
                             reads=[Lt, self.ident_f], writes=[ps], inc=(q3 == 2))
                    k.op("act", lambda ps=ps, tt=tt: nc.scalar.copy(LT[:, :, tt * P:(tt + 1) * P], ps[:, 0:384].rearrange("p (q t) -> p q t", q=3)),
                         reads=[ps], writes=[LT.s(tt)])

                for tb in range(8):
                    tsl = slice(tb * 256, (tb + 1) * 256)
                    tslots = (2 * tb, 2 * tb + 1)
                    for g in range(16):
                        ps = self.bank()
                        for dc in range(8):
                            k.op("pe", lambda dc=dc, g=g, ps=ps: nc.tensor.matmul(ps[:, 0:256], wq[:, dc, g * P:(g + 1) * P], xT[:, dc, tsl],
                                                                                   start=(dc == 0), stop=(dc == 7)),
                                 reads=[wq, xT.s(*tslots)], writes=[ps], inc=(dc == 7))
                        k.op("act", lambda g=g, ps=ps: nc.scalar.copy(qTb[:, g, :], ps[:, 0:256]), reads=[ps], writes=[qTb])
                    for t2 in range(2):
                        tile_chain(2 * tb + t2, t2)
                k.barrier()
            self.dbg("LT", LT, [P, 3, S], BF16)
            with ExitStack() as es:
                TBK = 8
                Ap = [k.sbuf(f"p_Ap{i}", [P, TBK, P], BF16, es=es) for i in range(2)]
                Bp = [k.sbuf(f"p_Bp{i}", [P, TBK, P], BF16, es=es) for i in range(2)]
                gt = [k.sbuf(f"p_gt{i}", [P, P, P], BF16, es=es) for i in range(2)]
                ev_i = 0
                for tt in range(NT):
                    g_ = gt[tt % 2]
                    for sub in range(P // TBK):
                        t0 = tt * P + sub * TBK
                        A_ = Ap[sub % 2]
                        B_ = Bp[sub % 2]
                        for tl_ in range(TBK):
                            t_g = t0 + tl_
                            k.op("dve", lambda A_=A_, tl_=tl_, t_g=t_g: nc.vector.tensor_scalar(
                                A_[:, tl_, :], self.iota_b[:], LT[:, 1, t_g:t_g + 1], LT[:, 0, t_g:t_g + 1], op0=ALU.is_equal, op1=ALU.mult),
                                 reads=[self.iota_b, LT.s(tt)], writes=[A_])
                            k.op("pool", lambda B_=B_, tl_=tl_, t_g=t_g: nc.gpsimd.tensor_scalar(
                                B_[:, tl_, :], self.iota_b[:], LT[:, 2, t_g:t_g + 1], None, op0=ALU.is_equal),
                                 reads=[self.iota_b, LT.s(tt)], writes=[B_])
                        for q4 in range(TBK // 4):
                            ps = self.bank()
                            for tl in range(4):
                                t_ = q4 * 4 + tl
                                k.op("pe", lambda ps=ps, tl=tl, t_=t_, A_=A_, B_=B_: nc.tensor.matmul(
                                    ps[:, tl * P:(tl + 1) * P], B_[:, t_, :], A_[:, t_, :], start=True, stop=True),
                                     reads=[A_, B_], writes=[ps], inc=(tl == 3))
                            c_ = sub * TBK + q4 * 4
                            dst = g_[:, :, c_:c_ + 4]
                            src = ps[:].rearrange("p (t i) -> p i t", t=4)
                            k.op("act", lambda dst=dst, src=src: nc.scalar.copy(dst, src), reads=[ps], writes=[g_])
                            ev_i += 1
                    for i8 in range(8):
                        k.dma("sp", gateD[i8 * 16:(i8 + 1) * 16, :, tt * P:(tt + 1) * P].rearrange("i j t -> j i t"),
                              g_[:, i8 * 16:(i8 + 1) * 16, :], reads=[g_], writes=[gateD.s(*range(i8 * 16, (i8 + 1) * 16))])
                k.barrier()
        with ExitStack() as es:
            G = 4
            hg = [k.sbuf(f"p_hg{i}", [P, S], BF16, es=es) for i in range(2 * G)]
            wub = [k.sbuf(f"p_wub{i}", [P, D], BF16, es=es) for i in range(2 * G)]
            wdb = [k.sbuf(f"p_wdb{i}", [P, D], BF16, es=es) for i in range(2)]
            wst = [k.sbuf(f"p_wst{i}", [P, D], F32, es=es) for i in range(4)]
            ge = [k.sbuf(f"p_ge{i}", [P, 512], BF16, es=es) for i in range(3)]
            for tt in range(NT):
                k.op("pool", lambda tt=tt: nc.gpsimd.tensor_scalar(X[:, tt, :], X[:, tt, :], ALPHA, None, op0=ALU.mult),
                     reads=[X.s(tt)], writes=[X.s(tt)])
            gei = 0
            abank = 0
            ybank = 0
            wsi = 0
            for i in range(P):
                slot = i % (2 * G)
                h_ = hg[slot]
                wu_ = wub[slot]
                wd_ = wdb[i % 2]
                s1 = wst[wsi % 4]; wsi += 1
                s2 = wst[wsi % 4]; wsi += 1
                k.dma("sp", s1[:], I[f"p_wdT{l}"][i], writes=[s1])
                k.dma("sp", s2[:], I[f"p_wu{l}"][i * P:(i + 1) * P, :], writes=[s2])
                k.dma("sp", h_[:], gateD[i], reads=[gateD.s(i)], writes=[h_])
                k.op("pool", lambda wd_=wd_, s1=s1: nc.gpsimd.tensor_copy(wd_[:], s1[:]), reads=[s1], writes=[wd_])
                k.op("pool", lambda wu_=wu_, s2=s2: nc.gpsimd.tensor_copy(wu_[:], s2[:]), reads=[s2], writes=[wu_])
                for nb in range(4):
                    ps = self.pb[abank % 4]; abank += 1
                    for dc in range(8):
                        k.op("pe", lambda ps=ps, dc=dc, nb=nb, wd_=wd_: nc.tensor.matmul(
                            ps[:], wd_[:, dc * P:(dc + 1) * P], xT[:, dc, nb * 512:(nb + 1) * 512], start=(dc == 0), stop=(dc == 7)),
                             reads=[wd_, xT.s(*range(nb * 4, nb * 4 + 4))], writes=[ps], inc=(dc == 7))
                    g_ = ge[gei % 3]; gei += 1
                    k.op("act", lambda ps=ps, g_=g_: nc.scalar.activation(g_[:], ps[:], AF.Gelu), reads=[ps], writes=[g_])
                    k.op("dve", lambda g_=g_, h_=h_, nb=nb: nc.vector.tensor_tensor(
                        h_[:, nb * 512:(nb + 1) * 512], h_[:, nb * 512:(nb + 1) * 512], g_[:], op=ALU.mult),
                         reads=[g_, h_], writes=[h_])
                if i % G == G - 1:
                    base = slot - (G - 1)
                    for tt in range(NT):
                        for half in range(2):
                            ps = self.pb[4 + ybank % 3]; ybank += 1
                            for gg in range(G):
                                k.op("pe", lambda ps=ps, gg=gg, tt=tt, half=half, base=base: nc.tensor.matmul(
                                    ps[:], hg[base + gg][:, tt * P:(tt + 1) * P], wub[base + gg][:, half * 512:(half + 1) * 512],
                                    start=(gg == 0), stop=(gg == G - 1)),
                                     reads=[hg[base + gg], wub[base + gg]], writes=[ps], inc=(gg == G - 1))
                            xs = X[:, tt, half * 512:(half + 1) * 512]
                            k.op("dve", lambda ps=ps, xs=xs: nc.vector.tensor_tensor(xs, xs, ps[:], op=ALU.add),
                                 reads=[ps, X.s(tt)], writes=[X.s(tt)])
            self.ln_params(li, es)
            for tt in range(NT):
                self.ln_tile(tt, li)
            k.barrier()

def _rope_tab(dim):
    inv = (np.float32(10000.0) ** (-np.arange(0, dim, 2, dtype=np.float32) / np.float32(dim))).astype(np.float32)
    ang = (np.arange(S, dtype=np.float32)[:, None] * inv[None, :]).astype(np.float32)
    c = np.cos(ang).astype(np.float32).T
    s = np.sin(ang).astype(np.float32).T
    half = dim // 2
    reps = P // dim
    cos = np.concatenate([c, c] * reps, axis=0)
    sin = np.concatenate([-s, s] * reps, axis=0)
    return np.ascontiguousarray(cos), np.ascontiguousarray(sin)


def prep_shared(inp):
    f = np.float32
    sh = {}
    sh["ident"] = np.eye(P, dtype=f)
    sh["cos64"], sh["sin64"] = _rope_tab(64)
    sh["cos128"], sh["sin128"] = _rope_tab(128)
    kk = np.arange(P)[:, None]
    qq = np.arange(P)[None, :]
    sh["maskT"] = np.where((kk >= 64) & (qq < 64), 0.0, 1.0).astype(f)
    sh["negm"] = np.where((kk < 64) & (qq >= 64), NEG, 0.0).astype(f)
    sh["iota"] = np.broadcast_to(np.arange(P, dtype=f)[None, :], (P, P)).copy()
    w_in = inp["mla_w_in"][0]
    kr = w_in[:, 640:704]
    kr_sw = np.concatenate([kr[:, 32:64], kr[:, 0:32]], axis=1)
    sh["m_win"] = np.ascontiguousarray(np.concatenate([w_in[:, :640], kr, kr, kr_sw, kr_sw], axis=1))
    wuq = inp["mla_w_uq"][0]
    nope = wuq[:, :, :128].reshape(384, 1024)
    rp = wuq[:, :, 128:192]
    rp_sw = np.concatenate([rp[:, :, 32:64], rp[:, :, 0:32]], axis=2)
    sh["m_wuq"] = np.ascontiguousarray(np.concatenate([nope, rp.reshape(384, 512), rp_sw.reshape(384, 512)], axis=1))
    wukv = inp["mla_w_ukv"][0]
    sh["m_wukv"] = np.ascontiguousarray(np.concatenate([wukv[:, :, :128].reshape(256, 1024), wukv[:, :, 128:].reshape(256, 1024)], axis=1))
    sh["m_wo"] = np.ascontiguousarray(inp["mla_w_o"][0])
    sh["m_qn"] = np.ascontiguousarray(inp["mla_q_norm"][0].reshape(3, P).T)
    sh["m_kvn"] = np.ascontiguousarray(inp["mla_kv_norm"][0].reshape(2, P).T)
    dw = inp["dsa_w_in"][0]

    def sw(w, hd):
        n = w.shape[1] // hd
        w3 = w.reshape(w.shape[0], n, hd)
        return np.concatenate([w3[:, :, hd // 2:], w3[:, :, :hd // 2]], axis=2).reshape(w.shape[0], n * hd)

    q, kx, v = dw[:, 0:1024], dw[:, 1024:2048], dw[:, 2048:3072]
    qi, ki, wi = dw[:, 3072:3584], dw[:, 3584:3648], dw[:, 3648:3656]
    cols = []
    for h in range(8):
        cols += [q[:, h * P:(h + 1) * P], sw(q[:, h * P:(h + 1) * P], P)]
    for h in range(8):
        cols += [kx[:, h * P:(h + 1) * P], sw(kx[:, h * P:(h + 1) * P], P)]
    qis = sw(qi, 64)
    for pr in range(4):
        cols += [qi[:, pr * P:(pr + 1) * P], qis[:, pr * P:(pr + 1) * P]]
    kis = sw(ki, 64)
    cols += [ki, ki, kis, kis]
    cols += [v, wi]
    sh["d_win"] = np.ascontiguousarray(np.concatenate(cols, axis=1))
    assert sh["d_win"].shape[1] == 6408
    sh["d_wo"] = np.ascontiguousarray(inp["dsa_w_o"][0])
    for l in range(2):
        sh[f"p_wq{l}"] = np.ascontiguousarray(inp["peer_w_q"][l])
        sk = inp["peer_sub_keys"][l]
        sh[f"p_skT{l}"] = np.ascontiguousarray(sk.transpose(1, 0, 3, 2).reshape(16, P, P))
        wd = inp["peer_w_down"][l]
        sh[f"p_wdT{l}"] = np.ascontiguousarray(wd.reshape(P, P, 8, P).transpose(0, 3, 2, 1).reshape(P, P, 8 * P))
        sh[f"p_wu{l}"] = np.ascontiguousarray(inp["peer_w_up"][l])
    sh["ln_g"] = np.ascontiguousarray(inp["ln_gain"].reshape(4, D))
    sh["ln_b"] = np.ascontiguousarray(inp["ln_bias"].reshape(4, D))
    return sh


def kernel(**inputs):
    inp = {k_: np.asarray(v) for k_, v in inputs.items()}
    sh = prep_shared(inp)
    prog = Prog()
    nc = prog.build()
    x = np.ascontiguousarray(inp["x"], dtype=np.float32)
    in_maps = []
    for c in range(8):
        m = dict(sh)
        m["x"] = x[c]
        in_maps.append(m)
    res = run_bass_kernel_spmd(nc, in_maps, core_ids=list(range(8)))
    return np.stack([res.results[c]["out"] for c in range(8)], axis=0).astype(np.float32)
```

```python
from contextlib import ExitStack
import math
import numpy as np
import concourse.bass as bass
import concourse.mybir as mybir
from concourse.bass_utils import run_bass_kernel_spmd

F32 = mybir.dt.float32
BF16 = mybir.dt.bfloat16
U32 = mybir.dt.uint32
AF = mybir.ActivationFunctionType
ALU = mybir.AluOpType

S = 2048
D = 1024
NT = 16
P = 128
ALPHA = float((2 * 2) ** 0.25)
LN_EPS = 1e-5
RMS_EPS = 1e-6
NEG = -1.0e30


class Tn:
    def __init__(self, h, nslots=1, name=""):
        self.h = h
        self.n = nslots
        self.name = name
        self.lw = [None] * nslots
        self.rd = [dict() for _ in range(nslots)]

    def __getitem__(self, key):
        return self.h[key]

    def s(self, *slots):
        return (self, slots)


def _norm(acc):
    if isinstance(acc, Tn):
        return acc, range(acc.n)
    return acc


class KB:
    def __init__(self, nc, es):
        self.nc = nc
        self.es = es
        self.eng = {"pe": nc.tensor, "act": nc.scalar, "dve": nc.vector, "pool": nc.gpsimd, "sp": nc.sync}
        self.sem = {}
        self.cnt = {}
        for e in ("pe", "act", "dve", "pool"):
            self.sem[e] = es.enter_context(nc.semaphore("c_" + e))
            self.cnt[e] = 0
        self.known = {e: {} for e in self.eng}
        self.rings = {}
        self.dma_n = {}
        for q, n in {"sp": 24, "pool": 8, "act": 8}.items():
            self.rings[q] = [es.enter_context(nc.semaphore(f"r_{q}{i}")) for i in range(n)]
            self.dma_n[q] = 0

    def wait(self, e, ev):
        sem, val = ev
        kk = id(sem)
        if self.known[e].get(kk, 0) >= val:
            return
        self.eng[e].wait_ge(sem, val)
        self.known[e][kk] = val

    def _deps(self, reads, writes):
        evs = []
        for acc in reads:
            t, sl = _norm(acc)
            for s in sl:
                if t.lw[s] is not None:
                    evs.append(t.lw[s])
        for acc in writes:
            t, sl = _norm(acc)
            for s in sl:
                if t.lw[s] is not None:
                    evs.append(t.lw[s])
                evs.extend(t.rd[s].values())
        return evs

    def _record(self, ev, reads, writes):
        sem, val = ev
        kk = id(sem)
        for acc in reads:
            t, sl = _norm(acc)
            for s in sl:
                d = t.rd[s]
                if kk not in d or d[kk][1] < val:
                    d[kk] = ev
        for acc in writes:
            t, sl = _norm(acc)
            for s in sl:
                t.lw[s] = ev
                t.rd[s] = dict()

    def op(self, e, fn, reads=(), writes=(), inc=True):
        own = self.sem[e]
        for ev in self._deps(reads, writes):
            if ev[0] is own and e == "pe":
                continue
            self.wait(e, ev)
        ins = fn()
        if inc:
            self.cnt[e] += 1
            ins.then_inc(own, 1)
            ev = (own, self.cnt[e])
        else:
            assert e == "pe"
            ev = (own, self.cnt[e] + 1)
        self._record(ev, reads, writes)
        return ev

    def dma(self, q, out, in_, reads=(), writes=(), **kw):
        n = self.dma_n[q]
        ring = self.rings[q]
        K = len(ring)
        slot = n % K
        if n >= K:
            self.wait(q, (ring[slot], 16 * (n // K)))
        for ev in self._deps(reads, writes):
            self.wait(q, ev)
        self.eng[q].dma_start(out=out, in_=in_, **kw).then_inc(ring[slot], 16)
        self.dma_n[q] = n + 1
        ev = (ring[slot], 16 * (n // K + 1))
        self._record(ev, reads, writes)
        return ev

    def all_events(self):
        evs = []
        for e in ("pe", "act", "dve", "pool"):
            if self.cnt[e] > 0:
                evs.append((self.sem[e], self.cnt[e]))
        for q, ring in self.rings.items():
            n = self.dma_n[q]
            K = len(ring)
            for slot in range(min(n, K)):
                last = ((n - 1 - slot) // K) * K + slot
                evs.append((ring[slot], 16 * (last // K + 1)))
        return evs

    def barrier(self):
        evs = self.all_events()
        for e in self.eng:
            for ev in evs:
                self.wait(e, ev)

    def sbuf(self, name, shape, dt, nslots=1, es=None):
        self.uid = getattr(self, "uid", 0) + 1
        name = f"{name}_{self.uid}"
        h = (es or self.es).enter_context(self.nc.sbuf_tensor(name, list(shape), dt))
        return Tn(h, nslots, name)

    def psum(self, name, shape, dt, nslots=1, es=None):
        h = (es or self.es).enter_context(self.nc.psum_tensor(name, list(shape), dt))
        return Tn(h, nslots, name)

    def dram(self, name, shape, dt, nslots=1):
        h = self.nc.dram_tensor(name, list(shape), dt).ap()
        return Tn(h, nslots, name)


class Prog:
    def __init__(self, debug=()):
        self.debug = set(debug)
        self.nc = bass.Bass("TRN2", target_bir_lowering=False)
        self.dbg_out = {}

    def din(self, name, shape, dt=F32):
        return self.nc.dram_tensor(name, list(shape), dt, kind="ExternalInput").ap()

    def build(self, stages=("mla", "peer0", "dsa", "peer1")):
        nc = self.nc
        I = {}
        I["x"] = self.din("x", [S, D])
        I["ident"] = self.din("ident", [P, P])
        I["cos64"] = self.din("cos64", [P, S])
        I["sin64"] = self.din("sin64", [P, S])
        I["cos128"] = self.din("cos128", [P, S])
        I["sin128"] = self.din("sin128", [P, S])
        I["maskT"] = self.din("maskT", [P, P])
        I["negm"] = self.din("negm", [P, P])
        I["iota"] = self.din("iota", [P, P])
        I["m_win"] = self.din("m_win", [D, 896])
        I["m_wuq"] = self.din("m_wuq", [384, 2048])
        I["m_wukv"] = self.din("m_wukv", [256, 2048])
        I["m_wo"] = self.din("m_wo", [D, D])
        I["m_qn"] = self.din("m_qn", [P, 3])
        I["m_kvn"] = self.din("m_kvn", [P, 2])
        I["d_win"] = self.din("d_win", [D, 6408])
        I["d_wo"] = self.din("d_wo", [D, D])
        for l in range(2):
            I[f"p_wq{l}"] = self.din(f"p_wq{l}", [D, 2048])
            I[f"p_skT{l}"] = self.din(f"p_skT{l}", [16, P, P])
            I[f"p_wdT{l}"] = self.din(f"p_wdT{l}", [P, P, 8 * P])
            I[f"p_wu{l}"] = self.din(f"p_wu{l}", [P * P, D])
        I["ln_g"] = self.din("ln_g", [4, D])
        I["ln_b"] = self.din("ln_b", [4, D])
        self.I = I
        self.out = nc.dram_tensor("out", [S, D], F32, kind="ExternalOutput").ap()

        with ExitStack() as es:
            k = KB(nc, es)
            self.k = k
            self.setup_common(es)
            self.load_x()
            li = 0
            for st in stages:
                if st == "mla":
                    self.mla_layer()
                    self.mixer_out_ln(I["m_wo"], 0)
                elif st == "dsa":
                    self.dsa_layer()
                    self.mixer_out_ln(I["d_wo"], 2)
                    self.dsa_es.close()
                elif st.startswith("peer"):
                    l = int(st[4:])
                    self.peer_layer(l, 2 * l + 1)
            self.store_out()
            k.barrier()
        return nc

    def setup_common(self, es):
        k, nc, I = self.k, self.nc, self.I
        self.X = k.sbuf("X", [P, NT, D], F32, nslots=NT)
        self.xT = k.sbuf("xT", [P, 8, S], BF16, nslots=NT)
        self.ident_f = k.sbuf("ident_f", [P, P], F32)
        self.ident_b = k.sbuf("ident_b", [P, P], BF16)
        self.ones_b = k.sbuf("ones_b", [P, P], BF16)
        self.maskT_b = k.sbuf("maskT_b", [P, P], BF16)
        self.negm = k.sbuf("negm_s", [P, P], F32)
        self.iota_b = k.sbuf("iota_b", [P, P], BF16)
        self.iota_f = k.sbuf("iota_f", [P, P], F32)
        self.eps_ln = k.sbuf("eps_ln", [P, 1], F32)
        self.eps_rms = k.sbuf("eps_rms", [P, 1], F32)
        self.thr_c = k.sbuf("thr_c", [P, 1], F32)
        self.stg = [k.sbuf(f"stg{i}", [P, 2048], F32) for i in range(2)]
        self.stg_i = 0
        self.pb = [k.psum(f"pb{i}", [P, 512], F32) for i in range(7)]
        self.pb_i = 0
        self.ptb = k.psum("ptb", [P, 1024], BF16)
        k.dma("sp", self.ident_f[:], I["ident"], writes=[self.ident_f])
        k.op("dve", lambda: nc.vector.tensor_copy(self.ident_b[:], self.ident_f[:]), reads=[self.ident_f], writes=[self.ident_b])
        k.op("pool", lambda: nc.gpsimd.memset(self.ones_b[:], 1.0), writes=[self.ones_b])
        k.op("pool", lambda: nc.gpsimd.memset(self.eps_ln[:], LN_EPS), writes=[self.eps_ln])
        k.op("pool", lambda: nc.gpsimd.memset(self.eps_rms[:], RMS_EPS), writes=[self.eps_rms])
        k.op("pool", lambda: nc.gpsimd.memset(self.thr_c[:], -1.0e29), writes=[self.thr_c])
        st = self.stage()
        k.dma("sp", st[:, 0:P], I["maskT"], writes=[st])
        k.op("dve", lambda: nc.vector.tensor_copy(self.maskT_b[:], st[:, 0:P]), reads=[st], writes=[self.maskT_b])
        k.dma("sp", self.negm[:], I["negm"], writes=[self.negm])
        k.dma("sp", self.iota_f[:], I["iota"], writes=[self.iota_f])
        k.op("dve", lambda: nc.vector.tensor_copy(self.iota_b[:], self.iota_f[:]), reads=[self.iota_f], writes=[self.iota_b])
        self.ln_st = k.sbuf("ln_st", [P, NT, 12], F32, nslots=NT)
        self.ln_mv = k.sbuf("ln_mv", [P, NT, 4], F32, nslots=NT)
        self.xbf = [k.sbuf(f"xbf{i}", [P, D], BF16) for i in range(2)]
        self.qnD = k.dram("qnD", [8, P, S], BF16, nslots=8)
        self.knD = k.dram("knD", [8, P, S], BF16, nslots=8)
        self.qrD = k.dram("qrD", [8, 64, S], BF16, nslots=8)
        self.krD = k.dram("krD", [P, S], BF16)
        self.VD = k.dram("VD", [NT, P, D], BF16, nslots=NT)
        self.qiD = k.dram("qiD", [4, P, S], BF16, nslots=4)
        self.kiD = k.dram("kiD", [P, S], BF16)

    def stage(self):
        s = self.stg[self.stg_i % len(self.stg)]
        self.stg_i += 1
        return s

    def bank(self):
        b = self.pb[self.pb_i % len(self.pb)]
        self.pb_i += 1
        return b

    def dbg(self, name, t, shape, dt=F32, ap=None):
        if name not in self.debug:
            return
        o = self.nc.dram_tensor("dbg_" + name, list(shape), dt, kind="ExternalOutput").ap()
        self.k.dma("sp", o, ap if ap is not None else t[:], reads=[t])
        self.dbg_out[name] = (shape, dt)

    def make_xT(self, tt):
        k, nc = self.k, self.nc
        xb = self.xbf[tt % 2]
        k.op("act", lambda: nc.scalar.copy(xb[:], self.X[:, tt, :]), reads=[self.X.s(tt)], writes=[xb])
        for c in range(8):
            k.op("pe", lambda c=c: nc.tensor.transpose(self.ptb[:, c * P:(c + 1) * P], xb[:, c * P:(c + 1) * P], self.ident_b[:]),
                 reads=[xb, self.ident_b], writes=[self.ptb], inc=(c == 7))
        k.op("dve", lambda: nc.vector.tensor_copy(self.xT[:, :, tt * P:(tt + 1) * P],
                                                  self.ptb[:].rearrange("p (c t) -> p c t", c=8)),
             reads=[self.ptb], writes=[self.xT.s(tt)])

    def load_x(self):
        k, nc = self.k, self.nc
        for tt in range(NT):
            k.dma("sp", self.X[:, tt, :], self.I["x"][tt * P:(tt + 1) * P, :], writes=[self.X.s(tt)])
            self.make_xT(tt)

    def store_out(self):
        k = self.k
        for tt in range(NT):
            k.dma("sp", self.out[tt * P:(tt + 1) * P, :], self.X[:, tt, :], reads=[self.X.s(tt)])

    def ln_params(self, li, es):
        k, I = self.k, self.I
        self.lng = k.sbuf("lng", [P, D], F32, es=es)
        self.lnb = k.sbuf("lnb", [P, D], F32, es=es)
        k.dma("sp", self.lng[:], I["ln_g"][li:li + 1, :].partition_broadcast(P), writes=[self.lng])
        k.dma("sp", self.lnb[:], I["ln_b"][li:li + 1, :].partition_broadcast(P), writes=[self.lnb])

    def ln_tile(self, tt, li):
        k, nc = self.k, self.nc
        X = self.X
        xt = X[:, tt, :]
        st = self.ln_st
        mv = self.ln_mv
        k.op("dve", lambda: nc.vector.bn_stats(st[:, tt, 0:6], X[:, tt, 0:512]), reads=[X.s(tt)], writes=[st.s(tt)])
        k.op("dve", lambda: nc.vector.bn_stats(st[:, tt, 6:12], X[:, tt, 512:1024]), reads=[X.s(tt)], writes=[st.s(tt)])
        k.op("dve", lambda: nc.vector.bn_aggr(mv[:, tt, 0:2], st[:, tt, :].rearrange("p (a b) -> p a b", a=2)),
             reads=[st.s(tt)], writes=[mv.s(tt)])
        k.op("act", lambda: nc.scalar.activation(mv[:, tt, 2:3], mv[:, tt, 1:2], AF.Sqrt, bias=self.eps_ln[:, 0:1], scale=1.0),
             reads=[mv.s(tt), self.eps_ln], writes=[mv.s(tt)])
        k.op("dve", lambda: nc.vector.reciprocal(mv[:, tt, 2:3], mv[:, tt, 2:3]), reads=[mv.s(tt)], writes=[mv.s(tt)])
        k.op("dve", lambda: nc.vector.scalar_tensor_tensor(mv[:, tt, 3:4], mv[:, tt, 0:1], -1.0, mv[:, tt, 2:3], op0=ALU.mult, op1=ALU.mult),
             reads=[mv.s(tt)], writes=[mv.s(tt)])
        k.op("act", lambda: nc.scalar.activation(xt, xt, AF.Identity, bias=mv[:, tt, 3:4], scale=mv[:, tt, 2:3]),
             reads=[X.s(tt), mv.s(tt)], writes=[X.s(tt)])
        k.op("pool", lambda: nc.gpsimd.tensor_tensor(xt, xt, self.lng[:], op=ALU.mult), reads=[X.s(tt), self.lng], writes=[X.s(tt)])
        k.op("pool", lambda: nc.gpsimd.tensor_tensor(xt, xt, self.lnb[:], op=ALU.add), reads=[X.s(tt), self.lnb], writes=[X.s(tt)])
        self.make_xT(tt)

    def load_w(self, dst, dst_ap, src_ap, ncols, eng="pool", scale=None, dst_slots=None):
        k, nc = self.k, self.nc
        st = self.stage()
        w = [dst] if dst_slots is None else [dst.s(*dst_slots)]
        k.dma("sp", st[:, 0:ncols], src_ap, writes=[st])
        if scale is not None:
            k.op("dve", lambda: nc.vector.tensor_scalar(dst_ap, st[:, 0:ncols], scale[1], None, op0=ALU.mult), reads=[st, scale[0]], writes=w)
        elif eng == "pool":
            k.op("pool", lambda: nc.gpsimd.tensor_copy(dst_ap, st[:, 0:ncols]), reads=[st], writes=w)
        elif eng == "act":
            k.op("act", lambda: nc.scalar.copy(dst_ap, st[:, 0:ncols]), reads=[st], writes=w)
        else:
            k.op("dve", lambda: nc.vector.tensor_copy(dst_ap, st[:, 0:ncols]), reads=[st], writes=w)

    def rope(self, ps_n, ps_s, cos, sin, tb, out_t, out_ap, es_tmp):
        k, nc = self.k, self.nc
        t1, t2 = es_tmp
        sl = slice(tb * 512, (tb + 1) * 512)
        k.op("dve", lambda: nc.vector.tensor_tensor(t1[:], ps_n[:], cos[:, sl], op=ALU.mult), reads=[ps_n, cos], writes=[t1])
        k.op("dve", lambda: nc.vector.tensor_tensor(t2[:], ps_s[:], sin[:, sl], op=ALU.mult), reads=[ps_s, sin], writes=[t2])
        k.op("pool", lambda: nc.gpsimd.tensor_tensor(out_ap, t1[:], t2[:], op=ALU.add), reads=[t1, t2], writes=[out_t])

    def mla_layer(self):
        k, nc, I = self.k, self.nc, self.I
        xT = self.xT
        with ExitStack() as es:
            win = k.sbuf("m_win_b", [P, 8, 896], BF16, es=es)
            wuq = k.sbuf("m_wuq_b", [P, 3, 2048], BF16, es=es)
            wukv = k.sbuf("m_wukv_b", [P, 2, 2048], BF16, es=es)
            qn = k.sbuf("m_qn_s", [P, 3], F32, es=es)
            kvn = k.sbuf("m_kvn_s", [P, 2], F32, es=es)
            cos = k.sbuf("cos64_s", [P, S], F32, es=es)
            sin = k.sbuf("sin64_s", [P, S], F32, es=es)
            k.dma("sp", qn[:], I["m_qn"], writes=[qn])
            k.dma("sp", kvn[:], I["m_kvn"], writes=[kvn])
            k.dma("sp", cos[:], I["cos64"], writes=[cos])
            k.dma("sp", sin[:], I["sin64"], writes=[sin])
            for c in range(8):
                self.load_w(win, win[:, c, :], I["m_win"][c * P:(c + 1) * P, :], 896, eng=("pool", "act")[c % 2])
            for c in range(3):
                self.load_w(wuq, wuq[:, c, :], I["m_wuq"][c * P:(c + 1) * P, :], 2048, scale=(qn, qn[:, c:c + 1]))
            for c in range(2):
                self.load_w(wukv, wukv[:, c, :], I["m_wukv"][c * P:(c + 1) * P, :], 2048, scale=(kvn, kvn[:, c:c + 1]))
            lat_f = k.sbuf("lat_f", [P, 5, 512], F32, es=es)
            sq_b = k.sbuf("sq_b", [P, 5, 512], BF16, es=es)
            rs = k.sbuf("rs", [P, 2, 512], F32, es=es)
            lat_n = [k.sbuf(f"lat_n{i}", [P, 5, 512], BF16, es=es) for i in range(1)]
            t1 = k.sbuf("rp_t1", [P, 512], F32, es=es)
            t2 = k.sbuf("rp_t2", [P, 512], F32, es=es)
            ob = [k.sbuf(f"m_ob{i}", [P, 512], BF16, es=es) for i in range(4)]
            vb = [k.sbuf(f"m_vb{i}", [P, D], BF16, es=es) for i in range(2)]
            obi = 0
            for tb in range(4):
                tsl = slice(tb * 512, (tb + 1) * 512)
                tslots = tuple(range(tb * 4, tb * 4 + 4))
                ln = lat_n[0]
                for c in range(5):
                    ps = self.bank()
                    for dc in range(8):
                        k.op("pe", lambda dc=dc, c=c, ps=ps: nc.tensor.matmul(ps[:], win[:, dc, c * P:(c + 1) * P], xT[:, dc, tsl],
                                                                               start=(dc == 0), stop=(dc == 7)),
                             reads=[win, xT.s(*tslots)], writes=[ps], inc=(dc == 7))
                    k.op("act", lambda c=c, ps=ps: nc.scalar.copy(lat_f[:, c, :], ps[:]), reads=[ps], writes=[lat_f])
                    k.op("act", lambda c=c, ps=ps: nc.scalar.activation(sq_b[:, c, :], ps[:], AF.Square), reads=[ps], writes=[sq_b])
                for gi, (c0, nch, width) in enumerate(((0, 3, 384), (3, 2, 256))):
                    ps = self.bank()
                    for c in range(nch):
                        k.op("pe", lambda c=c, ps=ps, c0=c0, nch=nch: nc.tensor.matmul(ps[:], self.ones_b[:], sq_b[:, c0 + c, :],
                                                                                        start=(c == 0), stop=(c == nch - 1)),
                             reads=[self.ones_b, sq_b], writes=[ps], inc=(c == nch - 1))
                    k.op("act", lambda ps=ps, gi=gi, width=width: nc.scalar.activation(rs[:, gi, :], ps[:], AF.Sqrt, bias=self.eps_rms[:, 0:1],
                                                                                         scale=1.0 / width),
                         reads=[ps, self.eps_rms], writes=[rs])
                    k.op("dve", lambda gi=gi: nc.vector.reciprocal(rs[:, gi, :], rs[:, gi, :]), reads=[rs], writes=[rs])
                    k.op("dve", lambda gi=gi, c0=c0, nch=nch, ln=ln: nc.vector.tensor_tensor(
                        ln[:, c0:c0 + nch, :], lat_f[:, c0:c0 + nch, :], rs[:, gi, :].unsqueeze(1).to_broadcast([P, nch, 512]), op=ALU.mult),
                         reads=[lat_f, rs], writes=[ln])
                psn = self.bank()
                pss = self.bank()
                for (ps, c0) in ((psn, 640), (pss, 768)):
                    for dc in range(8):
                        k.op("pe", lambda dc=dc, ps=ps, c0=c0: nc.tensor.matmul(ps[:], win[:, dc, c0:c0 + P], xT[:, dc, tsl],
                                                                                 start=(dc == 0), stop=(dc == 7)),
                             reads=[win, xT.s(*tslots)], writes=[ps], inc=(dc == 7))
                o = ob[obi % 4]; obi += 1
                self.rope(psn, pss, cos, sin, tb, o, o[:], (t1, t2))
                k.dma("sp", self.krD[:, tsl], o[:], reads=[o], writes=[self.krD])
                for h in range(8):
                    ps = self.bank()
                    for rc in range(3):
                        k.op("pe", lambda rc=rc, ps=ps, h=h: nc.tensor.matmul(ps[:], wuq[:, rc, h * P:(h + 1) * P], ln[:, rc, :],
                                                                               start=(rc == 0), stop=(rc == 2)),
                             reads=[wuq, ln], writes=[ps], inc=(rc == 2))
                    o = ob[obi % 4]; obi += 1
                    k.op("act", lambda ps=ps, o=o: nc.scalar.copy(o[:], ps[:]), reads=[ps], writes=[o])
                    k.dma("sp", self.qnD[h, :, tsl], o[:], reads=[o], writes=[self.qnD.s(h)])
                for pr in range(4):
                    psn = self.bank()
                    pss = self.bank()
                    for (ps, c0) in ((psn, 1024 + pr * P), (pss, 1536 + pr * P)):
                        for rc in range(3):
                            k.op("pe", lambda rc=rc, ps=ps, c0=c0: nc.tensor.matmul(ps[:], wuq[:, rc, c0:c0 + P], ln[:, rc, :],
                                                                                     start=(rc == 0), stop=(rc == 2)),
                                 reads=[wuq, ln], writes=[ps], inc=(rc == 2))
                    o = ob[obi % 4]; obi += 1
                    self.rope(psn, pss, cos, sin, tb, o, o[:], (t1, t2))
                    k.dma("sp", self.qrD[2 * pr, :, tsl], o[0:64, :], reads=[o], writes=[self.qrD.s(2 * pr)])
                    k.dma("sp", self.qrD[2 * pr + 1, :, tsl], o[64:128, :], reads=[o], writes=[self.qrD.s(2 * pr + 1)])
                for h in range(8):
                    ps = self.bank()
                    for rc in range(2):
                        k.op("pe", lambda rc=rc, ps=ps, h=h: nc.tensor.matmul(ps[:], wukv[:, rc, h * P:(h + 1) * P], ln[:, 3 + rc, :],
                                                                               start=(rc == 0), stop=(rc == 1)),
                             reads=[wukv, ln], writes=[ps], inc=(rc == 1))
                    o = ob[obi % 4]; obi += 1
                    k.op("dve", lambda ps=ps, o=o: nc.vector.tensor_copy(o[:], ps[:]), reads=[ps], writes=[o])
                    k.dma("sp", self.knD[h, :, tsl], o[:], reads=[o], writes=[self.knD.s(h)])
                for t4 in range(4):
                    tt = tb * 4 + t4
                    v = vb[tt % 2]
                    for half in range(2):
                        ps = self.bank()
                        for rc in range(2):
                            k.op("pe", lambda rc=rc, ps=ps, half=half, t4=t4: nc.tensor.matmul(
                                ps[:], ln[:, 3 + rc, t4 * P:(t4 + 1) * P], wukv[:, rc, 1024 + half * 512:1024 + (half + 1) * 512],
                                start=(rc == 0), stop=(rc == 1)),
                                 reads=[wukv, ln], writes=[ps], inc=(rc == 1))
                        if half == 0:
                            k.op("act", lambda ps=ps, v=v: nc.scalar.copy(v[:, 0:512], ps[:]), reads=[ps], writes=[v])
                        else:
                            k.op("dve", lambda ps=ps, v=v: nc.vector.tensor_copy(v[:, 512:1024], ps[:]), reads=[ps], writes=[v])
                    k.dma("sp", self.VD[tt], v[:], reads=[v], writes=[self.VD.s(tt)])
            k.barrier()
        self.attention(mla=True)

    def attention(self, mla, selT=None, selT_t=None):
        k, nc = self.k, self.nc
        scale = (192.0 if mla else 128.0) ** -0.5
        self.att_es = ExitStack()
        es = self.att_es
        OT = k.sbuf("OT", [P, 8, S], BF16, nslots=8, es=es)
        self.OT = OT
        with ExitStack() as es2:
            kn = [Tn(self.stg[i][:].bitcast(BF16), 1, f"a_kn{i}") for i in range(2)]
            qn = [k.sbuf(f"a_qn{i}", [P, S], BF16, es=es2) for i in range(2)]
            vh = [k.sbuf(f"a_vh{i}", [P, NT, P], BF16, es=es2) for i in range(2)]
            pt = [k.sbuf(f"a_pt{i}", [P, 512], BF16, es=es2) for i in range(3)]
            rec = k.sbuf("a_rec", [P, 512], F32, es=es2)
            if mla:
                qr = [k.sbuf(f"a_qr{i}", [64, S], BF16, es=es2) for i in range(2)]
                kr = k.sbuf("a_kr", [64, S], BF16, es=es2)
                k.dma("sp", kr[:], self.krD[0:64, :], reads=[self.krD], writes=[kr])
            def load_head(h):
                b = h % 2
                k.dma("sp", kn[b][:, 0:S], self.knD[h], reads=[self.knD.s(h)], writes=[kn[b]])
                k.dma("sp", qn[b][:], self.qnD[h], reads=[self.qnD.s(h)], writes=[qn[b]])
                k.dma("sp", vh[b][:], self.VD[:, :, h * P:(h + 1) * P].rearrange("t p v -> p t v"), reads=[self.VD], writes=[vh[b]])
                if mla:
                    k.dma("sp", qr[b][:], self.qrD[h], reads=[self.qrD.s(h)], writes=[qr[b]])

            pairs = []
            for h in range(8):
                for QB in range(4):
                    for kc in range(4 * QB + 4):
                        pairs.append((h, QB, kc))
            info = {}

            def emit_qk(i):
                h, QB, kc = pairs[i]
                b = h % 2
                if QB == 0 and kc == 0:
                    if h == 0:
                        load_head(0)
                    if h + 1 < 8:
                        load_head(h + 1)
                qlo = max(kc, 4 * QB)
                c0 = (qlo - 4 * QB) * P
                qs = slice(QB * 512 + c0, (QB + 1) * 512)
                ks = slice(kc * P, (kc + 1) * P)
                st = self.pb[i % 3]
                k.op("pe", lambda: nc.tensor.matmul(st[:, c0:512], kn[b][:, ks], qn[b][:, qs], start=True, stop=(not mla)),
                     reads=[kn[b], qn[b]], writes=[st], inc=(not mla))
                if mla:
                    k.op("pe", lambda: nc.tensor.matmul(st[:, c0:512], kr[:, ks], qr[b][:, qs], start=False, stop=True),
                         reads=[kr, qr[b]], writes=[st])
                info[i] = (st, c0, qs)

            def emit_soft(i):
                h, QB, kc = pairs[i]
                st, c0, qs = info[i]
                p_ = pt[i % 3]
                k.op("act", lambda: nc.scalar.activation(p_[:, c0:512], st[:, c0:512], AF.Exp, scale=scale), reads=[st], writes=[p_])
                if mla:
                    if kc >= 4 * QB:
                        k.op("dve", lambda: nc.vector.tensor_tensor(p_[:, c0:c0 + P], p_[:, c0:c0 + P], self.maskT_b[:], op=ALU.mult),
                             reads=[p_, self.maskT_b], writes=[p_])
                else:
                    k.op("dve", lambda: nc.vector.tensor_tensor(p_[:, c0:512], p_[:, c0:512], selT(kc, qs.start, qs.stop), op=ALU.mult),
                         reads=[p_, selT_t], writes=[p_])

            def emit_pv(i):
                h, QB, kc = pairs[i]
                b = h % 2
                st, c0, qs = info.pop(i)
                p_ = pt[i % 3]
                oT = self.pb[3 + QB % 2]
                sm = self.pb[5 + QB % 2]
                last = 4 * QB + 3
                k.op("pe", lambda: nc.tensor.matmul(oT[:, c0:512], vh[b][:, kc, :], p_[:, c0:512], start=(kc == 0), stop=(kc == last), skip_group_check=True),
                     reads=[vh[b], p_], writes=[oT], inc=False)
                k.op("pe", lambda: nc.tensor.matmul(sm[:, c0:512], self.ones_b[:], p_[:, c0:512], start=(kc == 0), stop=(kc == last), skip_group_check=True),
                     reads=[self.ones_b, p_], writes=[sm])
                if kc == last:
                    k.op("dve", lambda: nc.vector.reciprocal(rec[:], sm[:]), reads=[sm], writes=[rec])
                    k.op("dve", lambda: nc.vector.tensor_tensor(OT[:, h, QB * 512:(QB + 1) * 512], oT[:], rec[:], op=ALU.mult),
                         reads=[oT, rec], writes=[OT.s(h)])

            npairs = len(pairs)
            LA = 2
            for i in range(min(LA, npairs)):
                emit_qk(i)
            for i in range(npairs):
                emit_soft(i)
                if i + LA < npairs:
                    emit_qk(i + LA)
                emit_pv(i)
            k.barrier()

    def mixer_out_ln(self, wo_ap, li):
        k, nc = self.k, self.nc
        OT = self.OT
        with ExitStack() as es:
            wo = k.sbuf("wo_b", [P, 8, 512], BF16, es=es)
            self.ln_params(li, es)
            for half in range(2):
                for c in range(8):
                    self.load_w(wo, wo[:, c, :], wo_ap[c * P:(c + 1) * P, half * 512:(half + 1) * 512], 512, eng=("pool", "act")[c % 2])
                for tt in range(NT):
                    ps = self.bank()
                    for h in range(8):
                        k.op("pe", lambda ps=ps, h=h, tt=tt: nc.tensor.matmul(
                            ps[:], OT[:, h, tt * P:(tt + 1) * P], wo[:, h, :], start=(h == 0), stop=(h == 7)),
                             reads=[OT, wo], writes=[ps], inc=(h == 7))
                    xs = self.X[:, tt, half * 512:(half + 1) * 512]
                    k.op("dve", lambda ps=ps, xs=xs: nc.vector.scalar_tensor_tensor(xs, xs, ALPHA, ps[:], op0=ALU.mult, op1=ALU.add),
                         reads=[ps, self.X.s(tt)], writes=[self.X.s(tt)])
            for tt in range(NT):
                self.ln_tile(tt, li)
            k.barrier()
        self.att_es.close()

    def dsa_layer(self):
        k, nc, I = self.k, self.nc, self.I
        xT = self.xT
        WI = I["d_win"].rearrange("(c p) n -> p c n", p=P)
        self.dsa_es = ExitStack()
        esD = self.dsa_es
        selT = k.sbuf("d_selT", [P, 136 * P], BF16, es=esD)
        wtok = k.sbuf("d_wtok", [P, NT, 8], F32, es=esD)

        def soff(kc):
            return P * (16 * kc - kc * (kc - 1) // 2)

        with ExitStack() as es:
            wb = [k.sbuf(f"d_wb{i}", [P, 8, 512], BF16, es=es) for i in range(2)]
            t1 = k.sbuf("d_t1", [P, 512], F32, es=es)
            t2 = k.sbuf("d_t2", [P, 512], F32, es=es)
            ob = [k.sbuf(f"d_ob{i}", [P, 512], BF16, es=es) for i in range(4)]
            cos = k.sbuf("d_cos", [P, S], F32, es=es)
            sin = k.sbuf("d_sin", [P, S], F32, es=es)
            gi = 0
            obi = 0

            def load_group(c0, ncols):
                nonlocal gi
                wb_ = wb[gi % 2]
                gi += 1
                for hf in range(2):
                    st_ = self.stage()
                    sv = st_[:].rearrange("p (c n) -> p c n", c=4)
                    k.dma("sp", sv[:, :, 0:ncols], WI[:, hf * 4:(hf + 1) * 4, c0:c0 + ncols], writes=[st_])
                    if hf == 0:
                        k.op("act", lambda sv=sv, hf=hf: nc.scalar.copy(wb_[:, hf * 4:(hf + 1) * 4, 0:ncols], sv[:, :, 0:ncols]), reads=[st_], writes=[wb_])
                    else:
                        k.op("pool", lambda sv=sv, hf=hf: nc.gpsimd.tensor_copy(wb_[:, hf * 4:(hf + 1) * 4, 0:ncols], sv[:, :, 0:ncols]), reads=[st_], writes=[wb_])
                return wb_

            def rope_pairs(c0, npairs, dst_fn):
                nonlocal obi
                wb_ = load_group(c0, npairs * 256)
                for tb in range(4):
                    tsl = slice(tb * 512, (tb + 1) * 512)
                    tslots = tuple(range(tb * 4, tb * 4 + 4))
                    for pi in range(npairs):
                        psn = self.bank()
                        pss = self.bank()
                        for (ps, cc) in ((psn, pi * 256), (pss, pi * 256 + P)):
                            for dc in range(8):
                                k.op("pe", lambda dc=dc, ps=ps, cc=cc: nc.tensor.matmul(ps[:], wb_[:, dc, cc:cc + P], xT[:, dc, tsl],
                                                                                         start=(dc == 0), stop=(dc == 7)),
                                     reads=[wb_, xT.s(*tslots)], writes=[ps], inc=(dc == 7))
                        o = ob[obi % 4]; obi += 1
                        self.rope(psn, pss, cos, sin, tb, o, o[:], (t1, t2))
                        dt_, dap = dst_fn(pi, tsl)
                        k.dma("sp", dap, o[:], reads=[o], writes=[dt_])

            k.dma("sp", cos[:], I["cos128"], writes=[cos])
            k.dma("sp", sin[:], I["sin128"], writes=[sin])
            for g in range(4):
                rope_pairs(g * 512, 2, lambda pi, tsl, g=g: (self.qnD.s(2 * g + pi), self.qnD[2 * g + pi, :, tsl]))
            for g in range(4):
                rope_pairs(2048 + g * 512, 2, lambda pi, tsl, g=g: (self.knD.s(2 * g + pi), self.knD[2 * g + pi, :, tsl]))
            k.dma("sp", cos[:], I["cos64"], writes=[cos])
            k.dma("sp", sin[:], I["sin64"], writes=[sin])
            for g in range(2):
                rope_pairs(4096 + g * 512, 2, lambda pi, tsl, g=g: (self.qiD.s(2 * g + pi), self.qiD[2 * g + pi, :, tsl]))
            rope_pairs(5120, 1, lambda pi, tsl: (self.kiD, self.kiD[:, tsl]))
            vb = [k.sbuf(f"d_vb{i}", [P, 512], BF16, es=es) for i in range(2)]
            vi = 0
            for half in range(2):
                wb_ = load_group(5376 + half * 512, 512)
                for tt in range(NT):
                    ps = self.bank()
                    for dc in range(8):
                        k.op("pe", lambda dc=dc, ps=ps, tt=tt: nc.tensor.matmul(ps[:], xT[:, dc, tt * P:(tt + 1) * P], wb_[:, dc, :],
                                                                                 start=(dc == 0), stop=(dc == 7)),
                             reads=[wb_, xT.s(tt)], writes=[ps], inc=(dc == 7))
                    v = vb[vi % 2]; vi += 1
                    if tt % 2 == 0:
                        k.op("act", lambda ps=ps, v=v: nc.scalar.copy(v[:], ps[:]), reads=[ps], writes=[v])
                    else:
                        k.op("dve", lambda ps=ps, v=v: nc.vector.tensor_copy(v[:], ps[:]), reads=[ps], writes=[v])
                    k.dma("sp", self.VD[tt, :, half * 512:(half + 1) * 512], v[:], reads=[v], writes=[self.VD.s(tt)])
            wb_ = load_group(6400, 8)
            wscale = float(8 ** -0.5 * 64 ** -0.5)
            for tt in range(NT):
                ps = self.bank()
                for dc in range(8):
                    k.op("pe", lambda dc=dc, ps=ps, tt=tt: nc.tensor.matmul(ps[:, 0:8], xT[:, dc, tt * P:(tt + 1) * P], wb_[:, dc, 0:8],
                                                                             start=(dc == 0), stop=(dc == 7)),
                         reads=[wb_, xT.s(tt)], writes=[ps], inc=(dc == 7))
                k.op("act", lambda ps=ps, tt=tt: nc.scalar.mul(wtok[:, tt, :], ps[:, 0:8], wscale), reads=[ps], writes=[wtok])
            k.barrier()
        with ExitStack() as es:
            kiT = k.sbuf("d_kiT", [P, S], BF16, es=es)
            qiT = k.sbuf("d_qiT", [P, 4, S], BF16, es=es)
            accs = [k.sbuf(f"d_acc{i}", [P, S], F32, es=es) for i in range(2)]
            rls = [k.sbuf(f"d_rl{i}", [P, 512], F32, es=es) for i in range(2)]
            sels = [k.sbuf(f"d_sel{i}", [P, S], BF16, es=es) for i in range(1)]
            mxs = [k.sbuf(f"d_mx{i}", [P, 8], F32, es=es) for i in range(4)]
            k.dma("sp", kiT[:], self.kiD[:], reads=[self.kiD], writes=[kiT])
            for pr in range(4):
                k.dma("sp", qiT[:, pr, :], self.qiD[pr], reads=[self.qiD.s(pr)], writes=[qiT])
            rli = 0
            tbi = 0
            sel2 = k.sbuf("d_sel2", [P, S], BF16, es=es)

            def q_chain(qi):
                nonlocal rli, tbi
                n = (qi + 1) * P
                acc = accs[qi % 2]
                mxs_ = mxs[2 * (qi % 2):2 * (qi % 2) + 2]
                qsl = slice(qi * P, (qi + 1) * P)
                for h in range(8):
                    pr, hp = h // 2, h % 2
                    prt = slice(hp * 64, (hp + 1) * 64)
                    for k0 in range(0, n, 512):
                        kw = min(512, n - k0)
                        ps = self.bank()
                        k.op("pe", lambda ps=ps, kw=kw, k0=k0, pr=pr, prt=prt, qsl=qsl: nc.tensor.matmul(
                            ps[:, 0:kw], qiT[prt, pr, qsl], kiT[prt, k0:k0 + kw], start=True, stop=True),
                             reads=[qiT, kiT], writes=[ps])
                        rl = rls[rli % 2]; rli += 1
                        k.op("act", lambda ps=ps, rl=rl, kw=kw: nc.scalar.activation(rl[:, 0:kw], ps[:, 0:kw], AF.Relu), reads=[ps], writes=[rl])
                        if h == 0:
                            k.op("dve", lambda rl=rl, kw=kw, k0=k0, acc=acc, qi=qi: nc.vector.tensor_scalar(
                                acc[:, k0:k0 + kw], rl[:, 0:kw], wtok[:, qi, 0:1], None, op0=ALU.mult), reads=[rl, wtok], writes=[acc])
                        else:
                            k.op("dve", lambda rl=rl, kw=kw, k0=k0, acc=acc, qi=qi, h=h: nc.vector.scalar_tensor_tensor(
                                acc[:, k0:k0 + kw], rl[:, 0:kw], wtok[:, qi, h:h + 1], acc[:, k0:k0 + kw], op0=ALU.mult, op1=ALU.add),
                                 reads=[rl, wtok, acc], writes=[acc])
                        yield
                k.op("pool", lambda acc=acc, n=n: nc.gpsimd.tensor_tensor(acc[:, n - P:n], acc[:, n - P:n], self.negm[:], op=ALU.add),
                     reads=[acc, self.negm], writes=[acc])
                sel = sels[0] if qi % 2 == 0 else sel2
                if qi >= 2:
                    for r in range(32):
                        mx = mxs_[r % 2]
                        k.op("dve", lambda mx=mx, acc=acc, n=n: nc.vector.max(out=mx[:], in_=acc[:, 0:n]), reads=[acc], writes=[mx])
                        yield
                        if r < 31:
                            k.op("dve", lambda mx=mx, acc=acc, n=n: nc.vector.match_replace(
                                out=acc[:, 0:n], in_to_replace=mx[:], in_values=acc[:, 0:n], imm_value=-3.0e30),
                                 reads=[acc, mx], writes=[acc])
                            yield
                    thr_t, thr = mxs_[31 % 2], mxs_[31 % 2][:, 7:8]
                    k.op("dve", lambda sel=sel, acc=acc, n=n, thr=thr: nc.vector.tensor_scalar(sel[:, 0:n], acc[:, 0:n], thr, None, op0=ALU.is_ge),
                         reads=[acc, thr_t], writes=[sel])
                    yield
                    k.op("dve", lambda sel=sel, acc=acc, n=n: nc.vector.scalar_tensor_tensor(
                        sel[:, 0:n], acc[:, 0:n], -2.0e30, sel[:, 0:n], op0=ALU.is_le, op1=ALU.add),
                         reads=[acc, sel], writes=[sel])
                    yield
                else:
                    thr_t, thr = self.thr_c, self.thr_c[:, 0:1]
                    k.op("dve", lambda sel=sel, acc=acc, n=n, thr=thr: nc.vector.tensor_scalar(sel[:, 0:n], acc[:, 0:n], thr, None, op0=ALU.is_ge),
                         reads=[acc, thr_t], writes=[sel])
                    yield
                for kc in range(qi + 1):
                    blk = tbi % 8; tbi += 1
                    k.op("pe", lambda sel=sel, kc=kc, blk=blk: nc.tensor.transpose(self.ptb[:, blk * P:(blk + 1) * P], sel[:, kc * P:(kc + 1) * P], self.ident_b[:]),
                         reads=[sel, self.ident_b], writes=[self.ptb])
                    o_ = soff(kc) + (qi - kc) * P
                    k.op("act", lambda blk=blk, o_=o_: nc.scalar.copy(selT[:, o_:o_ + P], self.ptb[:, blk * P:(blk + 1) * P]),
                         reads=[self.ptb], writes=[selT])

            for pa in range(8):
                gens = [q_chain(2 * pa + 1), q_chain(2 * pa)]
                while gens:
                    for g_ in list(gens):
                        try:
                            next(g_)
                        except StopIteration:
                            gens.remove(g_)
            k.barrier()
        self.dbg("selT", selT, [P, 136 * P], BF16)
        self.attention(mla=False, selT=lambda kc, q0, q1: selT[:, soff(kc) + q0 - kc * P: soff(kc) + q1 - kc * P], selT_t=selT)

    def peer_layer(self, l, li):
        k, nc, I = self.k, self.nc, self.I
        xT, X = self.xT, self.X
        AXX = mybir.AxisListType.X
        if not hasattr(self, "gateD"):
            self.gateD = k.dram("gateD", [P, P, S], BF16, nslots=P)
        gateD = self.gateD
        with ExitStack() as esL:
            LT = k.sbuf("p_LT", [P, 3, S], BF16, nslots=NT, es=esL)
            with ExitStack() as es:
                wq = k.sbuf("p_wq_b", [P, 8, 2048], BF16, es=es)
                skT = k.sbuf("p_skT_b", [P, 16, P], BF16, es=es)
                for c in range(8):
                    self.load_w(wq, wq[:, c, :], I[f"p_wq{l}"][c * P:(c + 1) * P, :], 2048, eng=("dve", "act")[c % 2])
                st = self.stage()
                k.dma("sp", st[:].rearrange("p (g n) -> p g n", g=16), I[f"p_skT{l}"].rearrange("g d n -> d g n"), writes=[st])
                k.op("dve", lambda: nc.vector.tensor_copy(skT[:].rearrange("p g n -> p (g n)"), st[:]), reads=[st], writes=[skT])
                qTb = k.sbuf("p_qTb", [P, 16, 256], BF16, es=es)
                s_sbs = [Tn(self.stg[z].h, 16, f"p_s{z}") for z in range(2)]
                v16 = k.sbuf("p_v16", [P, 256], F32, nslots=16, es=es)
                i16 = k.sbuf("p_i16", [P, 256], U32, nslots=16, es=es)
                i16f = k.sbuf("p_i16f", [P, 256], F32, es=es)
                best = k.sbuf("p_best", [P, 128], F32, nslots=8, es=es)
                pos = k.sbuf("p_pos", [P, 128], U32, nslots=8, es=es)
                pab = k.sbuf("p_pab", [P, 2, 128], U32, es=es)
                pabf = k.sbuf("p_pabf", [P, 2, 128], F32, es=es)
                sm8 = k.sbuf("p_sm8", [P, 3, 8], F32, es=es)
                ex = k.sbuf("p_ex", [P, 128], F32, es=es)
                Lt = k.sbuf("p_Lt", [P, 3, 128], F32, es=es)
                k.barrier()

                def selfsync():
                    if k.cnt["dve"] > 0:
                        k.wait("dve", (k.sem["dve"], k.cnt["dve"]))

                def tile_chain(tt, t2):
                    s_sb = s_sbs[tt % 2]
                    cand = s_sb
                    work = s_sb
                    for bq in range(4):
                        ps = self.bank()
                        for gg in range(4):
                            g = bq * 4 + gg
                            k.op("pe", lambda ps=ps, g=g, gg=gg: nc.tensor.matmul(
                                ps[:, gg * P:(gg + 1) * P], qTb[:, g, t2 * P:(t2 + 1) * P], skT[:, g, :], start=True, stop=True),
                                 reads=[qTb, skT], writes=[ps], inc=(gg == 3))
                        k.op("act", lambda ps=ps, bq=bq: nc.scalar.copy(s_sb[:, bq * 512:(bq + 1) * 512], ps[:]),
                             reads=[ps], writes=[s_sb.s(*range(bq * 4, bq * 4 + 4))])
                    G16 = range(16)
                    sg = lambda g: s_sb[:, g * P:(g + 1) * P]
                    va = lambda g: v16[:, g * 16:g * 16 + 8]
                    vb_ = lambda g: v16[:, g * 16 + 8:g * 16 + 16]
                    wk = lambda g: work[:, g * P:(g + 1) * P]
                    for g in G16:
                        k.op("dve", lambda g=g: nc.vector.max(out=va(g), in_=sg(g)), reads=[s_sb.s(g)], writes=[v16.s(g)])
                    selfsync()
                    for g in G16:
                        k.op("dve", lambda g=g: nc.vector.max_index(i16[:, g * 16:g * 16 + 8], va(g), sg(g)), reads=[s_sb.s(g), v16.s(g)], writes=[i16.s(g)])
                    for g in G16:
                        k.op("dve", lambda g=g: nc.vector.match_replace(out=sg(g), in_to_replace=va(g), in_values=sg(g), imm_value=NEG),
                             reads=[s_sb.s(g), v16.s(g)], writes=[s_sb.s(g)])
                    selfsync()
                    for g in G16:
                        k.op("dve", lambda g=g: nc.vector.max(out=vb_(g), in_=sg(g)), reads=[s_sb.s(g)], writes=[v16.s(g)])
                    selfsync()
                    for g in G16:
                        k.op("dve", lambda g=g: nc.vector.max_index(i16[:, g * 16 + 8:g * 16 + 16], vb_(g), sg(g)), reads=[s_sb.s(g), v16.s(g)], writes=[i16.s(g)])
                    k.op("dve", lambda: nc.vector.tensor_copy(i16f[:], i16[:]), reads=[i16], writes=[i16f])
                    v4 = v16[:].rearrange("p (h c a) -> p h c a", h=8, c=2)
                    i4 = i16f[:].rearrange("p (h c a) -> p h c a", h=8, c=2)
                    c4 = cand[:].rearrange("p (h a b) -> p h a b", h=8, a=16)
                    k.op("dve", lambda: nc.vector.tensor_tensor(c4, v4[:, :, 0, :].unsqueeze(3).to_broadcast([P, 8, 16, 16]),
                                                                v4[:, :, 1, :].unsqueeze(2).to_broadcast([P, 8, 16, 16]), op=ALU.add),
                         reads=[v16], writes=[cand])
                    H8 = range(8)
                    ch = lambda h: cand[:, h * 256:(h + 1) * 256]
                    cs = lambda h: cand.s(2 * h, 2 * h + 1)
                    ws = lambda h: work.s(2 * h, 2 * h + 1)
                    wh = lambda h: work[:, h * 256:(h + 1) * 256]
                    ba = lambda h: best[:, h * 16:h * 16 + 8]
                    bb = lambda h: best[:, h * 16 + 8:h * 16 + 16]
                    selfsync()
                    for h in H8:
                        k.op("dve", lambda h=h: nc.vector.max(out=ba(h), in_=ch(h)), reads=[cs(h)], writes=[best.s(h)])
                    selfsync()
                    for h in H8:
                        k.op("dve", lambda h=h: nc.vector.max_index(pos[:, h * 16:h * 16 + 8], ba(h), ch(h)), reads=[cs(h), best.s(h)], writes=[pos.s(h)])
                    for h in H8:
                        k.op("dve", lambda h=h: nc.vector.match_replace(out=ch(h), in_to_replace=ba(h), in_values=ch(h), imm_value=NEG),
                             reads=[cs(h), best.s(h)], writes=[cs(h)])
                    selfsync()
                    for h in H8:
                        k.op("dve", lambda h=h: nc.vector.max(out=bb(h), in_=ch(h)), reads=[cs(h)], writes=[best.s(h)])
                    selfsync()
                    for h in H8:
                        k.op("dve", lambda h=h: nc.vector.max_index(pos[:, h * 16 + 8:h * 16 + 16], bb(h), ch(h)), reads=[cs(h), best.s(h)], writes=[pos.s(h)])
                    b3 = best[:].rearrange("p (h r) -> p h r", h=8)
                    e3 = ex[:].rearrange("p (h r) -> p h r", h=8)
                    k.op("dve", lambda: nc.vector.tensor_tensor(e3, b3, b3[:, :, 0:1].to_broadcast([P, 8, 16]), op=ALU.subtract),
                         reads=[best], writes=[ex])
                    k.op("act", lambda: nc.scalar.activation(ex[:], ex[:], AF.Exp), reads=[ex], writes=[ex])
                    k.op("dve", lambda: nc.vector.tensor_single_scalar(pab[:, 0, :], pos[:], 4, op=ALU.logical_shift_right), reads=[pos], writes=[pab])
                    k.op("dve", lambda: nc.vector.tensor_single_scalar(pab[:, 1, :], pos[:], 15, op=ALU.bitwise_and), reads=[pos], writes=[pab])
                    k.op("dve", lambda: nc.vector.tensor_copy(pabf[:], pab[:]), reads=[pab], writes=[pabf])
                    k.op("dve", lambda: nc.vector.reduce_sum(sm8[:, 0, :], e3, axis=AXX), reads=[ex], writes=[sm8])
                    k.op("dve", lambda: nc.vector.reciprocal(sm8[:, 1, :], sm8[:, 0, :]), reads=[sm8], writes=[sm8])
                    k.op("dve", lambda: nc.vector.tensor_tensor(Lt[:, 0, :].rearrange("p (h r) -> p h r", h=8), e3,
                                                                sm8[:, 1, :].unsqueeze(2).to_broadcast([P, 8, 16]), op=ALU.mult),
                         reads=[ex, sm8], writes=[Lt])
                    for w_ in range(2):
                        abf = pabf[:, w_, :].rearrange("p (h r) -> p h r", h=8)
                        k.op("dve", lambda abf=abf: nc.vector.tensor_tensor(
                            c4, self.iota_f[:, 0:16].unsqueeze(1).unsqueeze(1).to_broadcast([P, 8, 16, 16]),
                            abf.unsqueeze(3).to_broadcast([P, 8, 16, 16]), op=ALU.is_equal),
                             reads=[self.iota_f, pabf], writes=[cand])
                        k.op("dve", lambda w_=w_: nc.vector.tensor_tensor(c4, c4, i4[:, :, w_, :].unsqueeze(2).to_broadcast([P, 8, 16, 16]), op=ALU.mult),
                             reads=[i16f, cand], writes=[cand])
                        k.op("dve", lambda w_=w_: nc.vector.reduce_sum(Lt[:, 1 + w_, :].rearrange("p (h r) -> p h r", h=8), c4, axis=AXX),
                             reads=[cand], writes=[Lt])
                    ps = self.bank()
                    for q3 in range(3):
                        k.op("pe", lambda q3=q3, ps=ps: nc.tensor.transpose(ps[:, q3 * P:(q3 + 1) * P], Lt[:, q3, :], self.ident_f[:]),
                             reads=[Lt, self.ident_f], writes=[ps], inc=(q3 == 2))
                    k.op("act", lambda ps=ps, tt=tt: nc.scalar.copy(LT[:, :, tt * P:(tt + 1) * P], ps[:, 0:384].rearrange("p (q t) -> p q t", q=3)),
                         reads=[ps], writes=[LT.s(tt)])

                for tb in range(8):
                    tsl = slice(tb * 256, (tb + 1) * 256)
                    tslots = (2 * tb, 2 * tb + 1)
                    for g in range(16):
                        ps = self.bank()
                        for dc in range(8):
                            k.op("pe", lambda dc=dc, g=g, ps=ps: nc.tensor.matmul(ps[:, 0:256], wq[:, dc, g * P:(g + 1) * P], xT[:, dc, tsl],
                                                                                   start=(dc == 0), stop=(dc == 7)),
                                 reads=[wq, xT.s(*tslots)], writes=[ps], inc=(dc == 7))
                        k.op("act", lambda g=g, ps=ps: nc.scalar.copy(qTb[:, g, :], ps[:, 0:256]), reads=[ps], writes=[qTb])
                    for t2 in range(2):
                        tile_chain(2 * tb + t2, t2)
                k.barrier()
            self.dbg("LT", LT, [P, 3, S], BF16)
            with ExitStack() as es:
                TBK = 8
                Ap = [k.sbuf(f"p_Ap{i}", [P, TBK, P], BF16, es=es) for i in range(2)]
                Bp = [k.sbuf(f"p_Bp{i}", [P, TBK, P], BF16, es=es) for i in range(2)]
                gt = [k.sbuf(f"p_gt{i}", [P, P, P], BF16, es=es) for i in range(2)]
                ev_i = 0
                for tt in range(NT):
                    g_ = gt[tt % 2]
                    for sub in range(P // TBK):
                        t0 = tt * P + sub * TBK
                        A_ = Ap[sub % 2]
                        B_ = Bp[sub % 2]
                        for tl_ in range(TBK):
                            t_g = t0 + tl_
                            k.op("dve", lambda A_=A_, tl_=tl_, t_g=t_g: nc.vector.tensor_scalar(
                                A_[:, tl_, :], self.iota_b[:], LT[:, 1, t_g:t_g + 1], LT[:, 0, t_g:t_g + 1], op0=ALU.is_equal, op1=ALU.mult),
                                 reads=[self.iota_b, LT.s(tt)], writes=[A_])
                            k.op("dve", lambda B_=B_, tl_=tl_, t_g=t_g: nc.vector.tensor_scalar(
                                B_[:, tl_, :], self.iota_b[:], LT[:, 2, t_g:t_g + 1], None, op0=ALU.is_equal),
                                 reads=[self.iota_b, LT.s(tt)], writes=[B_])
                        for q4 in range(TBK // 4):
                            ps = self.bank()
                            for tl in range(4):
                                t_ = q4 * 4 + tl
                                k.op("pe", lambda ps=ps, tl=tl, t_=t_, A_=A_, B_=B_: nc.tensor.matmul(
                                    ps[:, tl * P:(tl + 1) * P], B_[:, t_, :], A_[:, t_, :], start=True, stop=True),
                                     reads=[A_, B_], writes=[ps], inc=(tl == 3))
                            c_ = sub * TBK + q4 * 4
                            dst = g_[:, :, c_:c_ + 4]
                            src = ps[:].rearrange("p (t i) -> p i t", t=4)
                            k.op("act", lambda dst=dst, src=src: nc.scalar.copy(dst, src), reads=[ps], writes=[g_])
                            ev_i += 1
                    for i8 in range(8):
                        k.dma("sp", gateD[i8 * 16:(i8 + 1) * 16, :, tt * P:(tt + 1) * P].rearrange("i j t -> j i t"),
                              g_[:, i8 * 16:(i8 + 1) * 16, :], reads=[g_], writes=[gateD.s(*range(i8 * 16, (i8 + 1) * 16))])
                k.barrier()
        with ExitStack() as es:
            G = 4
            hg = [k.sbuf(f"p_hg{i}", [P, S], BF16, es=es) for i in range(2 * G)]
            wub = [k.sbuf(f"p_wub{i}", [P, D], BF16, es=es) for i in range(2 * G)]
            wdb = [k.sbuf(f"p_wdb{i}", [P, D], BF16, es=es) for i in range(2)]
            wst = [k.sbuf(f"p_wst{i}", [P, D], F32, es=es) for i in range(4)]
            ge = [k.sbuf(f"p_ge{i}", [P, 512], BF16, es=es) for i in range(3)]
            for tt in range(NT):
                k.op("pool", lambda tt=tt: nc.gpsimd.tensor_scalar(X[:, tt, :], X[:, tt, :], ALPHA, None, op0=ALU.mult),
                     reads=[X.s(tt)], writes=[X.s(tt)])
            gei = 0
            abank = 0
            ybank = 0
            wsi = 0
            for i in range(P):
                slot = i % (2 * G)
                h_ = hg[slot]
                wu_ = wub[slot]
                wd_ = wdb[i % 2]
                s1 = wst[wsi % 4]; wsi += 1
                s2 = wst[wsi % 4]; wsi += 1
                k.dma("sp", s1[:], I[f"p_wdT{l}"][i], writes=[s1])
                k.dma("sp", s2[:], I[f"p_wu{l}"][i * P:(i + 1) * P, :], writes=[s2])
                k.dma("sp", h_[:], gateD[i], reads=[gateD.s(i)], writes=[h_])
                k.op("pool", lambda wd_=wd_, s1=s1: nc.gpsimd.tensor_copy(wd_[:], s1[:]), reads=[s1], writes=[wd_])
                k.op("pool", lambda wu_=wu_, s2=s2: nc.gpsimd.tensor_copy(wu_[:], s2[:]), reads=[s2], writes=[wu_])
                for nb in range(4):
                    ps = self.pb[abank % 4]; abank += 1
                    for dc in range(8):
                        k.op("pe", lambda ps=ps, dc=dc, nb=nb, wd_=wd_: nc.tensor.matmul(
                            ps[:], wd_[:, dc * P:(dc + 1) * P], xT[:, dc, nb * 512:(nb + 1) * 512], start=(dc == 0), stop=(dc == 7)),
                             reads=[wd_, xT.s(*range(nb * 4, nb * 4 + 4))], writes=[ps], inc=(dc == 7))
                    g_ = ge[gei % 3]; gei += 1
                    k.op("act", lambda ps=ps, g_=g_: nc.scalar.activation(g_[:], ps[:], AF.Gelu), reads=[ps], writes=[g_])
                    k.op("dve", lambda g_=g_, h_=h_, nb=nb: nc.vector.tensor_tensor(
                        h_[:, nb * 512:(nb + 1) * 512], h_[:, nb * 512:(nb + 1) * 512], g_[:], op=ALU.mult),
                         reads=[g_, h_], writes=[h_])
                if i % G == G - 1:
                    base = slot - (G - 1)
                    for tt in range(NT):
                        for half in range(2):
                            ps = self.pb[4 + ybank % 3]; ybank += 1
                            for gg in range(G):
                                k.op("pe", lambda ps=ps, gg=gg, tt=tt, half=half, base=base: nc.tensor.matmul(
                                    ps[:], hg[base + gg][:, tt * P:(tt + 1) * P], wub[base + gg][:, half * 512:(half + 1) * 512],
                                    start=(gg == 0), stop=(gg == G - 1)),
                                     reads=[hg[base + gg], wub[base + gg]], writes=[ps], inc=(gg == G - 1))
                            xs = X[:, tt, half * 512:(half + 1) * 512]
                            k.op("dve", lambda ps=ps, xs=xs: nc.vector.tensor_tensor(xs, xs, ps[:], op=ALU.add),
                                 reads=[ps, X.s(tt)], writes=[X.s(tt)])
            self.ln_params(li, es)
            for tt in range(NT):
                self.ln_tile(tt, li)
            k.barrier()

def _rope_tab(dim):
    inv = (np.float32(10000.0) ** (-np.arange(0, dim, 2, dtype=np.float32) / np.float32(dim))).astype(np.float32)
    ang = (np.arange(S, dtype=np.float32)[:, None] * inv[None, :]).astype(np.float32)
    c = np.cos(ang).astype(np.float32).T
    s = np.sin(ang).astype(np.float32).T
    half = dim // 2
    reps = P // dim
    cos = np.concatenate([c, c] * reps, axis=0)
    sin = np.concatenate([-s, s] * reps, axis=0)
    return np.ascontiguousarray(cos), np.ascontiguousarray(sin)


def prep_shared(inp):
    f = np.float32
    sh = {}
    sh["ident"] = np.eye(P, dtype=f)
    sh["cos64"], sh["sin64"] = _rope_tab(64)
    sh["cos128"], sh["sin128"] = _rope_tab(128)
    kk = np.arange(P)[:, None]
    qq = np.arange(P)[None, :]
    sh["maskT"] = np.where((kk >= 64) & (qq < 64), 0.0, 1.0).astype(f)
    sh["negm"] = np.where((kk < 64) & (qq >= 64), NEG, 0.0).astype(f)
    sh["iota"] = np.broadcast_to(np.arange(P, dtype=f)[None, :], (P, P)).copy()
    w_in = inp["mla_w_in"][0]
    kr = w_in[:, 640:704]
    kr_sw = np.concatenate([kr[:, 32:64], kr[:, 0:32]], axis=1)
    sh["m_win"] = np.ascontiguousarray(np.concatenate([w_in[:, :640], kr, kr, kr_sw, kr_sw], axis=1))
    wuq = inp["mla_w_uq"][0]
    nope = wuq[:, :, :128].reshape(384, 1024)
    rp = wuq[:, :, 128:192]
    rp_sw = np.concatenate([rp[:, :, 32:64], rp[:, :, 0:32]], axis=2)
    sh["m_wuq"] = np.ascontiguousarray(np.concatenate([nope, rp.reshape(384, 512), rp_sw.reshape(384, 512)], axis=1))
    wukv = inp["mla_w_ukv"][0]
    sh["m_wukv"] = np.ascontiguousarray(np.concatenate([wukv[:, :, :128].reshape(256, 1024), wukv[:, :, 128:].reshape(256, 1024)], axis=1))
    sh["m_wo"] = np.ascontiguousarray(inp["mla_w_o"][0])
    sh["m_qn"] = np.ascontiguousarray(inp["mla_q_norm"][0].reshape(3, P).T)
    sh["m_kvn"] = np.ascontiguousarray(inp["mla_kv_norm"][0].reshape(2, P).T)
    dw = inp["dsa_w_in"][0]

    def sw(w, hd):
        n = w.shape[1] // hd
        w3 = w.reshape(w.shape[0], n, hd)
        return np.concatenate([w3[:, :, hd // 2:], w3[:, :, :hd // 2]], axis=2).reshape(w.shape[0], n * hd)

    q, kx, v = dw[:, 0:1024], dw[:, 1024:2048], dw[:, 2048:3072]
    qi, ki, wi = dw[:, 3072:3584], dw[:, 3584:3648], dw[:, 3648:3656]
    cols = []
    for h in range(8):
        cols += [q[:, h * P:(h + 1) * P], sw(q[:, h * P:(h + 1) * P], P)]
    for h in range(8):
        cols += [kx[:, h * P:(h + 1) * P], sw(kx[:, h * P:(h + 1) * P], P)]
    qis = sw(qi, 64)
    for pr in range(4):
        cols += [qi[:, pr * P:(pr + 1) * P], qis[:, pr * P:(pr + 1) * P]]
    kis = sw(ki, 64)
    cols += [ki, ki, kis, kis]
    cols += [v, wi]
    sh["d_win"] = np.ascontiguousarray(np.concatenate(cols, axis=1))
    assert sh["d_win"].shape[1] == 6408
    sh["d_wo"] = np.ascontiguousarray(inp["dsa_w_o"][0])
    for l in range(2):
        sh[f"p_wq{l}"] = np.ascontiguousarray(inp["peer_w_q"][l])
        sk = inp["peer_sub_keys"][l]
        sh[f"p_skT{l}"] = np.ascontiguousarray(sk.transpose(1, 0, 3, 2).reshape(16, P, P))
        wd = inp["peer_w_down"][l]
        sh[f"p_wdT{l}"] = np.ascontiguousarray(wd.reshape(P, P, 8, P).transpose(0, 3, 2, 1).reshape(P, P, 8 * P))
        sh[f"p_wu{l}"] = np.ascontiguousarray(inp["peer_w_up"][l])
    sh["ln_g"] = np.ascontiguousarray(inp["ln_gain"].reshape(4, D))
    sh["ln_b"] = np.ascontiguousarray(inp["ln_bias"].reshape(4, D))
    return sh


def kernel(**inputs):
    inp = {k_: np.asarray(v) for k_, v in inputs.items()}
    sh = prep_shared(inp)
    prog = Prog()
    nc = prog.build()
    x = np.ascontiguousarray(inp["x"], dtype=np.float32)
    in_maps = []
    for c in range(8):
        m = dict(sh)
        m["x"] = x[c]
        in_maps.append(m)
    res = run_bass_kernel_spmd(nc, in_maps, core_ids=list(range(8)))
    return np.stack([res.results[c]["out"] for c in range(8)], axis=0).astype(np.float32)
```

```python
from contextlib import ExitStack
import math
import numpy as np
import concourse.bass as bass
import concourse.mybir as mybir
from concourse.bass_utils import run_bass_kernel_spmd

F32 = mybir.dt.float32
BF16 = mybir.dt.bfloat16
U32 = mybir.dt.uint32
AF = mybir.ActivationFunctionType
ALU = mybir.AluOpType

S = 2048
D = 1024
NT = 16
P = 128
ALPHA = float((2 * 2) ** 0.25)
LN_EPS = 1e-5
RMS_EPS = 1e-6
NEG = -1.0e30


class Tn:
    def __init__(self, h, nslots=1, name=""):
        self.h = h
        self.n = nslots
        self.name = name
        self.lw = [None] * nslots
        self.rd = [dict() for _ in range(nslots)]

    def __getitem__(self, key):
        return self.h[key]

    def s(self, *slots):
        return (self, slots)


def _norm(acc):
    if isinstance(acc, Tn):
        return acc, range(acc.n)
    return acc


class KB:
    def __init__(self, nc, es):
        self.nc = nc
        self.es = es
        self.eng = {"pe": nc.tensor, "act": nc.scalar, "dve": nc.vector, "pool": nc.gpsimd, "sp": nc.sync}
        self.sem = {}
        self.cnt = {}
        for e in ("pe", "act", "dve", "pool"):
            self.sem[e] = es.enter_context(nc.semaphore("c_" + e))
            self.cnt[e] = 0
        self.known = {e: {} for e in self.eng}
        self.rings = {}
        self.dma_n = {}
        for q, n in {"sp": 24, "pool": 8, "act": 8}.items():
            self.rings[q] = [es.enter_context(nc.semaphore(f"r_{q}{i}")) for i in range(n)]
            self.dma_n[q] = 0

    def wait(self, e, ev):
        sem, val = ev
        kk = id(sem)
        if self.known[e].get(kk, 0) >= val:
            return
        self.eng[e].wait_ge(sem, val)
        self.known[e][kk] = val

    def _deps(self, reads, writes):
        evs = []
        for acc in reads:
            t, sl = _norm(acc)
            for s in sl:
                if t.lw[s] is not None:
                    evs.append(t.lw[s])
        for acc in writes:
            t, sl = _norm(acc)
            for s in sl:
                if t.lw[s] is not None:
                    evs.append(t.lw[s])
                evs.extend(t.rd[s].values())
        return evs

    def _record(self, ev, reads, writes):
        sem, val = ev
        kk = id(sem)
        for acc in reads:
            t, sl = _norm(acc)
            for s in sl:
                d = t.rd[s]
                if kk not in d or d[kk][1] < val:
                    d[kk] = ev
        for acc in writes:
            t, sl = _norm(acc)
            for s in sl:
                t.lw[s] = ev
                t.rd[s] = dict()

    def op(self, e, fn, reads=(), writes=(), inc=True):
        own = self.sem[e]
        for ev in self._deps(reads, writes):
            if ev[0] is own and e == "pe":
                continue
            self.wait(e, ev)
        ins = fn()
        if inc:
            self.cnt[e] += 1
            ins.then_inc(own, 1)
            ev = (own, self.cnt[e])
        else:
            assert e == "pe"
            ev = (own, self.cnt[e] + 1)
        self._record(ev, reads, writes)
        return ev

    def dma(self, q, out, in_, reads=(), writes=(), **kw):
        n = self.dma_n[q]
        ring = self.rings[q]
        K = len(ring)
        slot = n % K
        if n >= K:
            self.wait(q, (ring[slot], 16 * (n // K)))
        for ev in self._deps(reads, writes):
            self.wait(q, ev)
        self.eng[q].dma_start(out=out, in_=in_, **kw).then_inc(ring[slot], 16)
        self.dma_n[q] = n + 1
        ev = (ring[slot], 16 * (n // K + 1))
        self._record(ev, reads, writes)
        return ev

    def all_events(self):
        evs = []
        for e in ("pe", "act", "dve", "pool"):
            if self.cnt[e] > 0:
                evs.append((self.sem[e], self.cnt[e]))
        for q, ring in self.rings.items():
            n = self.dma_n[q]
            K = len(ring)
            for slot in range(min(n, K)):
                last = ((n - 1 - slot) // K) * K + slot
                evs.append((ring[slot], 16 * (last // K + 1)))
        return evs

    def barrier(self):
        evs = self.all_events()
        for e in self.eng:
            for ev in evs:
                self.wait(e, ev)

    def sbuf(self, name, shape, dt, nslots=1, es=None):
        self.uid = getattr(self, "uid", 0) + 1
        name = f"{name}_{self.uid}"
        h = (es or self.es).enter_context(self.nc.sbuf_tensor(name, list(shape), dt))
        return Tn(h, nslots, name)

    def psum(self, name, shape, dt, nslots=1, es=None):
        h = (es or self.es).enter_context(self.nc.psum_tensor(name, list(shape), dt))
        return Tn(h, nslots, name)

    def dram(self, name, shape, dt, nslots=1):
        h = self.nc.dram_tensor(name, list(shape), dt).ap()
        return Tn(h, nslots, name)


class Prog:
    def __init__(self, debug=()):
        self.debug = set(debug)
        self.nc = bass.Bass("TRN2", target_bir_lowering=False)
        self.dbg_out = {}

    def din(self, name, shape, dt=F32):
        return self.nc.dram_tensor(name, list(shape), dt, kind="ExternalInput").ap()

    def build(self, stages=("mla", "peer0", "dsa", "peer1")):
        nc = self.nc
        I = {}
        I["x"] = self.din("x", [S, D])
        I["ident"] = self.din("ident", [P, P])
        I["cos64"] = self.din("cos64", [P, S])
        I["sin64"] = self.din("sin64", [P, S])
        I["cos128"] = self.din("cos128", [P, S])
        I["sin128"] = self.din("sin128", [P, S])
        I["maskT"] = self.din("maskT", [P, P])
        I["negm"] = self.din("negm", [P, P])
        I["iota"] = self.din("iota", [P, P])
        I["m_win"] = self.din("m_win", [D, 896])
        I["m_wuq"] = self.din("m_wuq", [384, 2048])
        I["m_wukv"] = self.din("m_wukv", [256, 2048])
        I["m_wo"] = self.din("m_wo", [D, D])
        I["m_qn"] = self.din("m_qn", [P, 3])
        I["m_kvn"] = self.din("m_kvn", [P, 2])
        I["d_win"] = self.din("d_win", [D, 6408])
        I["d_wo"] = self.din("d_wo", [D, D])
        for l in range(2):
            I[f"p_wq{l}"] = self.din(f"p_wq{l}", [D, 2048])
            I[f"p_skT{l}"] = self.din(f"p_skT{l}", [16, P, P])
            I[f"p_wdT{l}"] = self.din(f"p_wdT{l}", [P, P, 8 * P])
            I[f"p_wu{l}"] = self.din(f"p_wu{l}", [P * P, D])
        I["ln_g"] = self.din("ln_g", [4, D])
        I["ln_b"] = self.din("ln_b", [4, D])
        self.I = I
        self.out = nc.dram_tensor("out", [S, D], F32, kind="ExternalOutput").ap()

        with ExitStack() as es:
            k = KB(nc, es)
            self.k = k
            self.setup_common(es)
            self.load_x()
            li = 0
            for st in stages:
                if st == "mla":
                    self.mla_layer()
                    self.mixer_out_ln(I["m_wo"], 0)
                elif st == "dsa":
                    self.dsa_layer()
                    self.mixer_out_ln(I["d_wo"], 2)
                    self.dsa_es.close()
                elif st.startswith("peer"):
                    l = int(st[4:])
                    self.peer_layer(l, 2 * l + 1)
            self.store_out()
            k.barrier()
        return nc

    def setup_common(self, es):
        k, nc, I = self.k, self.nc, self.I
        self.X = k.sbuf("X", [P, NT, D], F32, nslots=NT)
        self.xT = k.sbuf("xT", [P, 8, S], BF16, nslots=NT)
        self.ident_f = k.sbuf("ident_f", [P, P], F32)
        self.ident_b = k.sbuf("ident_b", [P, P], BF16)
        self.ones_b = k.sbuf("ones_b", [P, P], BF16)
        self.maskT_b = k.sbuf("maskT_b", [P, P], BF16)
        self.negm = k.sbuf("negm_s", [P, P], F32)
        self.iota_b = k.sbuf("iota_b", [P, P], BF16)
        self.iota_f = k.sbuf("iota_f", [P, P], F32)
        self.eps_ln = k.sbuf("eps_ln", [P, 1], F32)
        self.eps_rms = k.sbuf("eps_rms", [P, 1], F32)
        self.thr_c = k.sbuf("thr_c", [P, 1], F32)
        self.stg = [k.sbuf(f"stg{i}", [P, 2048], F32) for i in range(2)]
        self.stg_i = 0
        self.pb = [k.psum(f"pb{i}", [P, 512], F32) for i in range(7)]
        self.pb_i = 0
        self.ptb = k.psum("ptb", [P, 1024], BF16)
        k.dma("sp", self.ident_f[:], I["ident"], writes=[self.ident_f])
        k.op("dve", lambda: nc.vector.tensor_copy(self.ident_b[:], self.ident_f[:]), reads=[self.ident_f], writes=[self.ident_b])
        k.op("pool", lambda: nc.gpsimd.memset(self.ones_b[:], 1.0), writes=[self.ones_b])
        k.op("pool", lambda: nc.gpsimd.memset(self.eps_ln[:], LN_EPS), writes=[self.eps_ln])
        k.op("pool", lambda: nc.gpsimd.memset(self.eps_rms[:], RMS_EPS), writes=[self.eps_rms])
        k.op("pool", lambda: nc.gpsimd.memset(self.thr_c[:], -1.0e29), writes=[self.thr_c])
        st = self.stage()
        k.dma("sp", st[:, 0:P], I["maskT"], writes=[st])
        k.op("dve", lambda: nc.vector.tensor_copy(self.maskT_b[:], st[:, 0:P]), reads=[st], writes=[self.maskT_b])
        k.dma("sp", self.negm[:], I["negm"], writes=[self.negm])
        k.dma("sp", self.iota_f[:], I["iota"], writes=[self.iota_f])
        k.op("dve", lambda: nc.vector.tensor_copy(self.iota_b[:], self.iota_f[:]), reads=[self.iota_f], writes=[self.iota_b])
        self.ln_st = k.sbuf("ln_st", [P, NT, 12], F32, nslots=NT)
        self.ln_mv = k.sbuf("ln_mv", [P, NT, 4], F32, nslots=NT)
        self.xbf = [k.sbuf(f"xbf{i}", [P, D], BF16) for i in range(2)]
        self.qnD = k.dram("qnD", [8, P, S], BF16, nslots=8)
        self.knD = k.dram("knD", [8, P, S], BF16, nslots=8)
        self.qrD = k.dram("qrD", [8, 64, S], BF16, nslots=8)
        self.krD = k.dram("krD", [P, S], BF16)
        self.VD = k.dram("VD", [NT, P, D], BF16, nslots=NT)
        self.qiD = k.dram("qiD", [4, P, S], BF16, nslots=4)
        self.kiD = k.dram("kiD", [P, S], BF16)

    def stage(self):
        s = self.stg[self.stg_i % len(self.stg)]
        self.stg_i += 1
        return s

    def bank(self):
        b = self.pb[self.pb_i % len(self.pb)]
        self.pb_i += 1
        return b

    def dbg(self, name, t, shape, dt=F32, ap=None):
        if name not in self.debug:
            return
        o = self.nc.dram_tensor("dbg_" + name, list(shape), dt, kind="ExternalOutput").ap()
        self.k.dma("sp", o, ap if ap is not None else t[:], reads=[t])
        self.dbg_out[name] = (shape, dt)

    def make_xT(self, tt):
        k, nc = self.k, self.nc
        xb = self.xbf[tt % 2]
        k.op("act", lambda: nc.scalar.copy(xb[:], self.X[:, tt, :]), reads=[self.X.s(tt)], writes=[xb])
        for c in range(8):
            k.op("pe", lambda c=c: nc.tensor.transpose(self.ptb[:, c * P:(c + 1) * P], xb[:, c * P:(c + 1) * P], self.ident_b[:]),
                 reads=[xb, self.ident_b], writes=[self.ptb], inc=(c == 7))
        k.op("dve", lambda: nc.vector.tensor_copy(self.xT[:, :, tt * P:(tt + 1) * P],
                                                  self.ptb[:].rearrange("p (c t) -> p c t", c=8)),
             reads=[self.ptb], writes=[self.xT.s(tt)])

    def load_x(self):
        k, nc = self.k, self.nc
        for tt in range(NT):
            k.dma("sp", self.X[:, tt, :], self.I["x"][tt * P:(tt + 1) * P, :], writes=[self.X.s(tt)])
            self.make_xT(tt)

    def store_out(self):
        k = self.k
        for tt in range(NT):
            k.dma("sp", self.out[tt * P:(tt + 1) * P, :], self.X[:, tt, :], reads=[self.X.s(tt)])

    def ln_params(self, li, es):
        k, I = self.k, self.I
        self.lng = k.sbuf("lng", [P, D], F32, es=es)
        self.lnb = k.sbuf("lnb", [P, D], F32, es=es)
        k.dma("sp", self.lng[:], I["ln_g"][li:li + 1, :].partition_broadcast(P), writes=[self.lng])
        k.dma("sp", self.lnb[:], I["ln_b"][li:li + 1, :].partition_broadcast(P), writes=[self.lnb])

    def ln_tile(self, tt, li):
        k, nc = self.k, self.nc
        X = self.X
        xt = X[:, tt, :]
        st = self.ln_st
        mv = self.ln_mv
        k.op("dve", lambda: nc.vector.bn_stats(st[:, tt, 0:6], X[:, tt, 0:512]), reads=[X.s(tt)], writes=[st.s(tt)])
        k.op("dve", lambda: nc.vector.bn_stats(st[:, tt, 6:12], X[:, tt, 512:1024]), reads=[X.s(tt)], writes=[st.s(tt)])
        k.op("dve", lambda: nc.vector.bn_aggr(mv[:, tt, 0:2], st[:, tt, :].rearrange("p (a b) -> p a b", a=2)),
             reads=[st.s(tt)], writes=[mv.s(tt)])
        k.op("act", lambda: nc.scalar.activation(mv[:, tt, 2:3], mv[:, tt, 1:2], AF.Sqrt, bias=self.eps_ln[:, 0:1], scale=1.0),
             reads=[mv.s(tt), self.eps_ln], writes=[mv.s(tt)])
        k.op("dve", lambda: nc.vector.reciprocal(mv[:, tt, 2:3], mv[:, tt, 2:3]), reads=[mv.s(tt)], writes=[mv.s(tt)])
        k.op("dve", lambda: nc.vector.scalar_tensor_tensor(mv[:, tt, 3:4], mv[:, tt, 0:1], -1.0, mv[:, tt, 2:3], op0=ALU.mult, op1=ALU.mult),
             reads=[mv.s(tt)], writes=[mv.s(tt)])
        k.op("act", lambda: nc.scalar.activation(xt, xt, AF.Identity, bias=mv[:, tt, 3:4], scale=mv[:, tt, 2:3]),
             reads=[X.s(tt), mv.s(tt)], writes=[X.s(tt)])
        k.op("pool", lambda: nc.gpsimd.tensor_tensor(xt, xt, self.lng[:], op=ALU.mult), reads=[X.s(tt), self.lng], writes=[X.s(tt)])
        k.op("pool", lambda: nc.gpsimd.tensor_tensor(xt, xt, self.lnb[:], op=ALU.add), reads=[X.s(tt), self.lnb], writes=[X.s(tt)])
        self.make_xT(tt)

    def load_w(self, dst, dst_ap, src_ap, ncols, eng="pool", scale=None, dst_slots=None):
        k, nc = self.k, self.nc
        st = self.stage()
        w = [dst] if dst_slots is None else [dst.s(*dst_slots)]
        k.dma("sp", st[:, 0:ncols], src_ap, writes=[st])
        if scale is not None:
            k.op("dve", lambda: nc.vector.tensor_scalar(dst_ap, st[:, 0:ncols], scale[1], None, op0=ALU.mult), reads=[st, scale[0]], writes=w)
        elif eng == "pool":
            k.op("pool", lambda: nc.gpsimd.tensor_copy(dst_ap, st[:, 0:ncols]), reads=[st], writes=w)
        elif eng == "act":
            k.op("act", lambda: nc.scalar.copy(dst_ap, st[:, 0:ncols]), reads=[st], writes=w)
        else:
            k.op("dve", lambda: nc.vector.tensor_copy(dst_ap, st[:, 0:ncols]), reads=[st], writes=w)

    def rope(self, ps_n, ps_s, cos, sin, tb, out_t, out_ap, es_tmp):
        k, nc = self.k, self.nc
        t1, t2 = es_tmp
        sl = slice(tb * 512, (tb + 1) * 512)
        k.op("dve", lambda: nc.vector.tensor_tensor(t1[:], ps_n[:], cos[:, sl], op=ALU.mult), reads=[ps_n, cos], writes=[t1])
        k.op("dve", lambda: nc.vector.tensor_tensor(t2[:], ps_s[:], sin[:, sl], op=ALU.mult), reads=[ps_s, sin], writes=[t2])
        k.op("pool", lambda: nc.gpsimd.tensor_tensor(out_ap, t1[:], t2[:], op=ALU.add), reads=[t1, t2], writes=[out_t])

    def mla_layer(self):
        k, nc, I = self.k, self.nc, self.I
        xT = self.xT
        with ExitStack() as es:
            win = k.sbuf("m_win_b", [P, 8, 896], BF16, es=es)
            wuq = k.sbuf("m_wuq_b", [P, 3, 2048], BF16, es=es)
            wukv = k.sbuf("m_wukv_b", [P, 2, 2048], BF16, es=es)
            qn = k.sbuf("m_qn_s", [P, 3], F32, es=es)
            kvn = k.sbuf("m_kvn_s", [P, 2], F32, es=es)
            cos = k.sbuf("cos64_s", [P, S], F32, es=es)
            sin = k.sbuf("sin64_s", [P, S], F32, es=es)
            k.dma("sp", qn[:], I["m_qn"], writes=[qn])
            k.dma("sp", kvn[:], I["m_kvn"], writes=[kvn])
            k.dma("sp", cos[:], I["cos64"], writes=[cos])
            k.dma("sp", sin[:], I["sin64"], writes=[sin])
            for c in range(8):
                self.load_w(win, win[:, c, :], I["m_win"][c * P:(c + 1) * P, :], 896, eng=("pool", "act")[c % 2])
            for c in range(3):
                self.load_w(wuq, wuq[:, c, :], I["m_wuq"][c * P:(c + 1) * P, :], 2048, scale=(qn, qn[:, c:c + 1]))
            for c in range(2):
                self.load_w(wukv, wukv[:, c, :], I["m_wukv"][c * P:(c + 1) * P, :], 2048, scale=(kvn, kvn[:, c:c + 1]))
            lat_f = k.sbuf("lat_f", [P, 5, 512], F32, es=es)
            sq_b = k.sbuf("sq_b", [P, 5, 512], BF16, es=es)
            rs = k.sbuf("rs", [P, 2, 512], F32, es=es)
            lat_n = [k.sbuf(f"lat_n{i}", [P, 5, 512], BF16, es=es) for i in range(1)]
            t1 = k.sbuf("rp_t1", [P, 512], F32, es=es)
            t2 = k.sbuf("rp_t2", [P, 512], F32, es=es)
            ob = [k.sbuf(f"m_ob{i}", [P, 512], BF16, es=es) for i in range(4)]
            vb = [k.sbuf(f"m_vb{i}", [P, D], BF16, es=es) for i in range(2)]
            obi = 0
            for tb in range(4):
                tsl = slice(tb * 512, (tb + 1) * 512)
                tslots = tuple(range(tb * 4, tb * 4 + 4))
                ln = lat_n[0]
                for c in range(5):
                    ps = self.bank()
                    for dc in range(8):
                        k.op("pe", lambda dc=dc, c=c, ps=ps: nc.tensor.matmul(ps[:], win[:, dc, c * P:(c + 1) * P], xT[:, dc, tsl],
                                                                               start=(dc == 0), stop=(dc == 7)),
                             reads=[win, xT.s(*tslots)], writes=[ps], inc=(dc == 7))
                    k.op("act", lambda c=c, ps=ps: nc.scalar.copy(lat_f[:, c, :], ps[:]), reads=[ps], writes=[lat_f])
                    k.op("act", lambda c=c, ps=ps: nc.scalar.activation(sq_b[:, c, :], ps[:], AF.Square), reads=[ps], writes=[sq_b])
                for gi, (c0, nch, width) in enumerate(((0, 3, 384), (3, 2, 256))):
                    ps = self.bank()
                    for c in range(nch):
                        k.op("pe", lambda c=c, ps=ps, c0=c0, nch=nch: nc.tensor.matmul(ps[:], self.ones_b[:], sq_b[:, c0 + c, :],
                                                                                        start=(c == 0), stop=(c == nch - 1)),
                             reads=[self.ones_b, sq_b], writes=[ps], inc=(c == nch - 1))
                    k.op("act", lambda ps=ps, gi=gi, width=width: nc.scalar.activation(rs[:, gi, :], ps[:], AF.Sqrt, bias=self.eps_rms[:, 0:1],
                                                                                         scale=1.0 / width),
                         reads=[ps, self.eps_rms], writes=[rs])
                    k.op("dve", lambda gi=gi: nc.vector.reciprocal(rs[:, gi, :], rs[:, gi, :]), reads=[rs], writes=[rs])
                    k.op("dve", lambda gi=gi, c0=c0, nch=nch, ln=ln: nc.vector.tensor_tensor(
                        ln[:, c0:c0 + nch, :], lat_f[:, c0:c0 + nch, :], rs[:, gi, :].unsqueeze(1).to_broadcast([P, nch, 512]), op=ALU.mult),
                         reads=[lat_f, rs], writes=[ln])
                psn = self.bank()
                pss = self.bank()
                for (ps, c0) in ((psn, 640), (pss, 768)):
                    for dc in range(8):
                        k.op("pe", lambda dc=dc, ps=ps, c0=c0: nc.tensor.matmul(ps[:], win[:, dc, c0:c0 + P], xT[:, dc, tsl],
                                                                                 start=(dc == 0), stop=(dc == 7)),
                             reads=[win, xT.s(*tslots)], writes=[ps], inc=(dc == 7))
                o = ob[obi % 4]; obi += 1
                self.rope(psn, pss, cos, sin, tb, o, o[:], (t1, t2))
                k.dma("sp", self.krD[:, tsl], o[:], reads=[o], writes=[self.krD])
                for h in range(8):
                    ps = self.bank()
                    for rc in range(3):
                        k.op("pe", lambda rc=rc, ps=ps, h=h: nc.tensor.matmul(ps[:], wuq[:, rc, h * P:(h + 1) * P], ln[:, rc, :],
                                                                               start=(rc == 0), stop=(rc == 2)),
                             reads=[wuq, ln], writes=[ps], inc=(rc == 2))
                    o = ob[obi % 4]; obi += 1
                    k.op("act", lambda ps=ps, o=o: nc.scalar.copy(o[:], ps[:]), reads=[ps], writes=[o])
                    k.dma("sp", self.qnD[h, :, tsl], o[:], reads=[o], writes=[self.qnD.s(h)])
                for pr in range(4):
                    psn = self.bank()
                    pss = self.bank()
                    for (ps, c0) in ((psn, 1024 + pr * P), (pss, 1536 + pr * P)):
                        for rc in range(3):
                            k.op("pe", lambda rc=rc, ps=ps, c0=c0: nc.tensor.matmul(ps[:], wuq[:, rc, c0:c0 + P], ln[:, rc, :],
                                                                                     start=(rc == 0), stop=(rc == 2)),
                                 reads=[wuq, ln], writes=[ps], inc=(rc == 2))
                    o = ob[obi % 4]; obi += 1
                    self.rope(psn, pss, cos, sin, tb, o, o[:], (t1, t2))
                    k.dma("sp", self.qrD[2 * pr, :, tsl], o[0:64, :], reads=[o], writes=[self.qrD.s(2 * pr)])
                    k.dma("sp", self.qrD[2 * pr + 1, :, tsl], o[64:128, :], reads=[o], writes=[self.qrD.s(2 * pr + 1)])
                for h in range(8):
                    ps = self.bank()
                    for rc in range(2):
                        k.op("pe", lambda rc=rc, ps=ps, h=h: nc.tensor.matmul(ps[:], wukv[:, rc, h * P:(h + 1) * P], ln[:, 3 + rc, :],
                                                                               start=(rc == 0), stop=(rc == 1)),
                             reads=[wukv, ln], writes=[ps], inc=(rc == 1))
                    o = ob[obi % 4]; obi += 1
                    k.op("dve", lambda ps=ps, o=o: nc.vector.tensor_copy(o[:], ps[:]), reads=[ps], writes=[o])
                    k.dma("sp", self.knD[h, :, tsl], o[:], reads=[o], writes=[self.knD.s(h)])
                for t4 in range(4):
                    tt = tb * 4 + t4
                    v = vb[tt % 2]
                    for half in range(2):
                        ps = self.bank()
                        for rc in range(2):
                            k.op("pe", lambda rc=rc, ps=ps, half=half, t4=t4: nc.tensor.matmul(
                                ps[:], ln[:, 3 + rc, t4 * P:(t4 + 1) * P], wukv[:, rc, 1024 + half * 512:1024 + (half + 1) * 512],
                                start=(rc == 0), stop=(rc == 1)),
                                 reads=[wukv, ln], writes=[ps], inc=(rc == 1))
                        if half == 0:
                            k.op("act", lambda ps=ps, v=v: nc.scalar.copy(v[:, 0:512], ps[:]), reads=[ps], writes=[v])
                        else:
                            k.op("dve", lambda ps=ps, v=v: nc.vector.tensor_copy(v[:, 512:1024], ps[:]), reads=[ps], writes=[v])
                    k.dma("sp", self.VD[tt], v[:], reads=[v], writes=[self.VD.s(tt)])
            k.barrier()
        self.attention(mla=True)

    def attention(self, mla, selT=None, selT_t=None):
        k, nc = self.k, self.nc
        scale = (192.0 if mla else 128.0) ** -0.5
        self.att_es = ExitStack()
        es = self.att_es
        OT = k.sbuf("OT", [P, 8, S], BF16, nslots=8, es=es)
        self.OT = OT
        with ExitStack() as es2:
            kn = [Tn(self.stg[i][:].bitcast(BF16), 1, f"a_kn{i}") for i in range(2)]
            qn = [k.sbuf(f"a_qn{i}", [P, S], BF16, es=es2) for i in range(2)]
            vh = [k.sbuf(f"a_vh{i}", [P, NT, P], BF16, es=es2) for i in range(2)]
            pt = [k.sbuf(f"a_pt{i}", [P, 512], BF16, es=es2) for i in range(3)]
            rec = k.sbuf("a_rec", [P, 512], F32, es=es2)
            if mla:
                qr = [k.sbuf(f"a_qr{i}", [64, S], BF16, es=es2) for i in range(2)]
                kr = k.sbuf("a_kr", [64, S], BF16, es=es2)
                k.dma("sp", kr[:], self.krD[0:64, :], reads=[self.krD], writes=[kr])
            def load_head(h):
                b = h % 2
                k.dma("sp", kn[b][:, 0:S], self.knD[h], reads=[self.knD.s(h)], writes=[kn[b]])
                k.dma("sp", qn[b][:], self.qnD[h], reads=[self.qnD.s(h)], writes=[qn[b]])
                k.dma("sp", vh[b][:], self.VD[:, :, h * P:(h + 1) * P].rearrange("t p v -> p t v"), reads=[self.VD], writes=[vh[b]])
                if mla:
                    k.dma("sp", qr[b][:], self.qrD[h], reads=[self.qrD.s(h)], writes=[qr[b]])

            pairs = []
            for h in range(8):
                for QB in range(4):
                    for kc in range(4 * QB + 4):
                        pairs.append((h, QB, kc))
            info = {}

            def emit_qk(i):
                h, QB, kc = pairs[i]
                b = h % 2
                if QB == 0 and kc == 0 and h == 0:
                    load_head(0)
                qlo = max(kc, 4 * QB)
                c0 = (qlo - 4 * QB) * P
                qs = slice(QB * 512 + c0, (QB + 1) * 512)
                ks = slice(kc * P, (kc + 1) * P)
                st = self.pb[i % 3]
                k.op("pe", lambda: nc.tensor.matmul(st[:, c0:512], kn[b][:, ks], qn[b][:, qs], start=True, stop=(not mla)),
                     reads=[kn[b], qn[b]], writes=[st], inc=(not mla))
                if mla:
                    k.op("pe", lambda: nc.tensor.matmul(st[:, c0:512], kr[:, ks], qr[b][:, qs], start=False, stop=True),
                         reads=[kr, qr[b]], writes=[st])
                info[i] = (st, c0, qs)

            def emit_soft(i):
                h, QB, kc = pairs[i]
                st, c0, qs = info[i]
                p_ = pt[i % 3]
                k.op("act", lambda: nc.scalar.activation(p_[:, c0:512], st[:, c0:512], AF.Exp, scale=scale), reads=[st], writes=[p_])
                if mla:
                    if kc >= 4 * QB:
                        k.op("dve", lambda: nc.vector.tensor_tensor(p_[:, c0:c0 + P], p_[:, c0:c0 + P], self.maskT_b[:], op=ALU.mult),
                             reads=[p_, self.maskT_b], writes=[p_])
                else:
                    k.op("dve", lambda: nc.vector.tensor_tensor(p_[:, c0:512], p_[:, c0:512], selT(kc, qs.start, qs.stop), op=ALU.mult),
                         reads=[p_, selT_t], writes=[p_])

            def emit_pv(i):
                h, QB, kc = pairs[i]
                b = h % 2
                st, c0, qs = info.pop(i)
                p_ = pt[i % 3]
                oT = self.pb[3 + QB % 2]
                sm = self.pb[5 + QB % 2]
                last = 4 * QB + 3
                if QB == 0 and kc == 0 and h + 1 < 8:
                    load_head(h + 1)
                k.op("pe", lambda: nc.tensor.matmul(oT[:, c0:512], vh[b][:, kc, :], p_[:, c0:512], start=(kc == 0), stop=(kc == last), skip_group_check=True),
                     reads=[vh[b], p_], writes=[oT], inc=False)
                k.op("pe", lambda: nc.tensor.matmul(sm[:, c0:512], self.ones_b[:], p_[:, c0:512], start=(kc == 0), stop=(kc == last), skip_group_check=True),
                     reads=[self.ones_b, p_], writes=[sm])
                if kc == last:
                    k.op("dve", lambda: nc.vector.reciprocal(rec[:], sm[:]), reads=[sm], writes=[rec])
                    k.op("dve", lambda: nc.vector.tensor_tensor(OT[:, h, QB * 512:(QB + 1) * 512], oT[:], rec[:], op=ALU.mult),
                         reads=[oT, rec], writes=[OT.s(h)])

            npairs = len(pairs)
            LA = 2
            for i in range(min(LA, npairs)):
                emit_qk(i)
            for i in range(npairs):
                emit_soft(i)
                if i + LA < npairs:
                    emit_qk(i + LA)
                emit_pv(i)
            k.barrier()

    def mixer_out_ln(self, wo_ap, li):
        k, nc = self.k, self.nc
        OT = self.OT
        with ExitStack() as es:
            wo = k.sbuf("wo_b", [P, 8, 512], BF16, es=es)
            self.ln_params(li, es)
            for half in range(2):
                for c in range(8):
                    self.load_w(wo, wo[:, c, :], wo_ap[c * P:(c + 1) * P, half * 512:(half + 1) * 512], 512, eng=("pool", "act")[c % 2])
                for tt in range(NT):
                    ps = self.bank()
                    for h in range(8):
                        k.op("pe", lambda ps=ps, h=h, tt=tt: nc.tensor.matmul(
                            ps[:], OT[:, h, tt * P:(tt + 1) * P], wo[:, h, :], start=(h == 0), stop=(h == 7)),
                             reads=[OT, wo], writes=[ps], inc=(h == 7))
                    xs = self.X[:, tt, half * 512:(half + 1) * 512]
                    k.op("dve", lambda ps=ps, xs=xs: nc.vector.scalar_tensor_tensor(xs, xs, ALPHA, ps[:], op0=ALU.mult, op1=ALU.add),
                         reads=[ps, self.X.s(tt)], writes=[self.X.s(tt)])
            for tt in range(NT):
                self.ln_tile(tt, li)
            k.barrier()
        self.att_es.close()

    def dsa_layer(self):
        k, nc, I = self.k, self.nc, self.I
        xT = self.xT
        WI = I["d_win"].rearrange("(c p) n -> p c n", p=P)
        self.dsa_es = ExitStack()
        esD = self.dsa_es
        selT = k.sbuf("d_selT", [P, 136 * P], BF16, es=esD)
        wtok = k.sbuf("d_wtok", [P, NT, 8], F32, es=esD)

        def soff(kc):
            return P * (16 * kc - kc * (kc - 1) // 2)

        with ExitStack() as es:
            wb = [k.sbuf(f"d_wb{i}", [P, 8, 512], BF16, es=es) for i in range(2)]
            t1 = k.sbuf("d_t1", [P, 512], F32, es=es)
            t2 = k.sbuf("d_t2", [P, 512], F32, es=es)
            ob = [k.sbuf(f"d_ob{i}", [P, 512], BF16, es=es) for i in range(4)]
            cos = k.sbuf("d_cos", [P, S], F32, es=es)
            sin = k.sbuf("d_sin", [P, S], F32, es=es)
            gi = 0
            obi = 0

            def load_group(c0, ncols):
                nonlocal gi
                wb_ = wb[gi % 2]
                gi += 1
                for hf in range(2):
                    st_ = self.stage()
                    sv = st_[:].rearrange("p (c n) -> p c n", c=4)
                    k.dma("sp", sv[:, :, 0:ncols], WI[:, hf * 4:(hf + 1) * 4, c0:c0 + ncols], writes=[st_])
                    if hf == 0:
                        k.op("act", lambda sv=sv, hf=hf: nc.scalar.copy(wb_[:, hf * 4:(hf + 1) * 4, 0:ncols], sv[:, :, 0:ncols]), reads=[st_], writes=[wb_])
                    else:
                        k.op("pool", lambda sv=sv, hf=hf: nc.gpsimd.tensor_copy(wb_[:, hf * 4:(hf + 1) * 4, 0:ncols], sv[:, :, 0:ncols]), reads=[st_], writes=[wb_])
                return wb_

            def rope_pairs(c0, npairs, dst_fn):
                nonlocal obi
                wb_ = load_group(c0, npairs * 256)
                for tb in range(4):
                    tsl = slice(tb * 512, (tb + 1) * 512)
                    tslots = tuple(range(tb * 4, tb * 4 + 4))
                    for pi in range(npairs):
                        psn = self.bank()
                        pss = self.bank()
                        for (ps, cc) in ((psn, pi * 256), (pss, pi * 256 + P)):
                            for dc in range(8):
                                k.op("pe", lambda dc=dc, ps=ps, cc=cc: nc.tensor.matmul(ps[:], wb_[:, dc, cc:cc + P], xT[:, dc, tsl],
                                                                                         start=(dc == 0), stop=(dc == 7)),
                                     reads=[wb_, xT.s(*tslots)], writes=[ps], inc=(dc == 7))
                        o = ob[obi % 4]; obi += 1
                        self.rope(psn, pss, cos, sin, tb, o, o[:], (t1, t2))
                        dt_, dap = dst_fn(pi, tsl)
                        k.dma("sp", dap, o[:], reads=[o], writes=[dt_])

            k.dma("sp", cos[:], I["cos128"], writes=[cos])
            k.dma("sp", sin[:], I["sin128"], writes=[sin])
            for g in range(4):
                rope_pairs(g * 512, 2, lambda pi, tsl, g=g: (self.qnD.s(2 * g + pi), self.qnD[2 * g + pi, :, tsl]))
            for g in range(4):
                rope_pairs(2048 + g * 512, 2, lambda pi, tsl, g=g: (self.knD.s(2 * g + pi), self.knD[2 * g + pi, :, tsl]))
            k.dma("sp", cos[:], I["cos64"], writes=[cos])
            k.dma("sp", sin[:], I["sin64"], writes=[sin])
            for g in range(2):
                rope_pairs(4096 + g * 512, 2, lambda pi, tsl, g=g: (self.qiD.s(2 * g + pi), self.qiD[2 * g + pi, :, tsl]))
            rope_pairs(5120, 1, lambda pi, tsl: (self.kiD, self.kiD[:, tsl]))
            vb = [k.sbuf(f"d_vb{i}", [P, 512], BF16, es=es) for i in range(2)]
            vi = 0
            for half in range(2):
                wb_ = load_group(5376 + half * 512, 512)
                for tt in range(NT):
                    ps = self.bank()
                    for dc in range(8):
                        k.op("pe", lambda dc=dc, ps=ps, tt=tt: nc.tensor.matmul(ps[:], xT[:, dc, tt * P:(tt + 1) * P], wb_[:, dc, :],
                                                                                 start=(dc == 0), stop=(dc == 7)),
                             reads=[wb_, xT.s(tt)], writes=[ps], inc=(dc == 7))
                    v = vb[vi % 2]; vi += 1
                    if tt % 2 == 0:
                        k.op("act", lambda ps=ps, v=v: nc.scalar.copy(v[:], ps[:]), reads=[ps], writes=[v])
                    else:
                        k.op("dve", lambda ps=ps, v=v: nc.vector.tensor_copy(v[:], ps[:]), reads=[ps], writes=[v])
                    k.dma("sp", self.VD[tt, :, half * 512:(half + 1) * 512], v[:], reads=[v], writes=[self.VD.s(tt)])
            wb_ = load_group(6400, 8)
            wscale = float(8 ** -0.5 * 64 ** -0.5)
            for tt in range(NT):
                ps = self.bank()
                for dc in range(8):
                    k.op("pe", lambda dc=dc, ps=ps, tt=tt: nc.tensor.matmul(ps[:, 0:8], xT[:, dc, tt * P:(tt + 1) * P], wb_[:, dc, 0:8],
                                                                             start=(dc == 0), stop=(dc == 7)),
                         reads=[wb_, xT.s(tt)], writes=[ps], inc=(dc == 7))
                k.op("act", lambda ps=ps, tt=tt: nc.scalar.mul(wtok[:, tt, :], ps[:, 0:8], wscale), reads=[ps], writes=[wtok])
            k.barrier()
        with ExitStack() as es:
            kiT = k.sbuf("d_kiT", [P, S], BF16, es=es)
            qiT = k.sbuf("d_qiT", [P, 4, S], BF16, es=es)
            accs = [k.sbuf(f"d_acc{i}", [P, S], F32, es=es) for i in range(2)]
            rls = [k.sbuf(f"d_rl{i}", [P, 512], F32, es=es) for i in range(2)]
            sels = [k.sbuf(f"d_sel{i}", [P, S], BF16, es=es) for i in range(1)]
            mxs = [k.sbuf(f"d_mx{i}", [P, 8], F32, es=es) for i in range(4)]
            k.dma("sp", kiT[:], self.kiD[:], reads=[self.kiD], writes=[kiT])
            for pr in range(4):
                k.dma("sp", qiT[:, pr, :], self.qiD[pr], reads=[self.qiD.s(pr)], writes=[qiT])
            rli = 0
            tbi = 0
            sel2 = k.sbuf("d_sel2", [P, S], BF16, es=es)
            bss = [k.sbuf(f"d_bs{i}", [P, 8], F32, es=es) for i in range(2)]

            def q_chain(qi):
                nonlocal rli, tbi
                n = (qi + 1) * P
                acc = accs[qi % 2]
                mxs_ = mxs[2 * (qi % 2):2 * (qi % 2) + 2]
                qsl = slice(qi * P, (qi + 1) * P)
                for h in range(8):
                    pr, hp = h // 2, h % 2
                    prt = slice(hp * 64, (hp + 1) * 64)
                    for k0 in range(0, n, 512):
                        kw = min(512, n - k0)
                        ps = self.bank()
                        k.op("pe", lambda ps=ps, kw=kw, k0=k0, pr=pr, prt=prt, qsl=qsl: nc.tensor.matmul(
                            ps[:, 0:kw], qiT[prt, pr, qsl], kiT[prt, k0:k0 + kw], start=True, stop=True),
                             reads=[qiT, kiT], writes=[ps])
                        rl = rls[rli % 2]; rli += 1
                        k.op("act", lambda ps=ps, rl=rl, kw=kw: nc.scalar.activation(rl[:, 0:kw], ps[:, 0:kw], AF.Relu), reads=[ps], writes=[rl])
                        if h == 0:
                            k.op("dve", lambda rl=rl, kw=kw, k0=k0, acc=acc, qi=qi: nc.vector.tensor_scalar(
                                acc[:, k0:k0 + kw], rl[:, 0:kw], wtok[:, qi, 0:1], None, op0=ALU.mult), reads=[rl, wtok], writes=[acc])
                        else:
                            k.op("dve", lambda rl=rl, kw=kw, k0=k0, acc=acc, qi=qi, h=h: nc.vector.scalar_tensor_tensor(
                                acc[:, k0:k0 + kw], rl[:, 0:kw], wtok[:, qi, h:h + 1], acc[:, k0:k0 + kw], op0=ALU.mult, op1=ALU.add),
                                 reads=[rl, wtok, acc], writes=[acc])
                        yield
                sel = sels[0] if qi % 2 == 0 else sel2
                if qi >= 2:
                    bs = bss[qi % 2]
                    AXX_ = mybir.AxisListType.X
                    k.op("dve", lambda: nc.vector.tensor_reduce(bs[:, 0:1], acc[:, 0:n], AXX_, ALU.max), reads=[acc], writes=[bs])
                    yield
                    k.op("dve", lambda: nc.vector.tensor_reduce(bs[:, 1:2], acc[:, 0:n], AXX_, ALU.min), reads=[acc], writes=[bs])
                    yield
                    k.op("dve", lambda: nc.vector.scalar_tensor_tensor(bs[:, 2:3], bs[:, 0:1], 1.0, bs[:, 1:2], op0=ALU.mult, op1=ALU.subtract),
                         reads=[bs], writes=[bs])
                    yield
                    k.op("dve", lambda: nc.vector.tensor_scalar(bs[:, 2:3], bs[:, 2:3], 1.0009765625, 1e-20, op0=ALU.mult, op1=ALU.add),
                         reads=[bs], writes=[bs])
                    yield
                k.op("pool", lambda acc=acc, n=n: nc.gpsimd.tensor_tensor(acc[:, n - P:n], acc[:, n - P:n], self.negm[:], op=ALU.add),
                     reads=[acc, self.negm], writes=[acc])
                if qi >= 2:
                    bs = bss[qi % 2]
                    NIT = 20
                    k.op("dve", lambda: nc.vector.scalar_tensor_tensor(bs[:, 3:4], bs[:, 2:3], 0.5, bs[:, 1:2], op0=ALU.mult, op1=ALU.add),
                         reads=[bs], writes=[bs])
                    yield
                    for it in range(1, NIT + 1):
                        step = 0.5 ** it
                        k.op("dve", lambda: nc.vector.tensor_scalar(sel[:, 0:n], acc[:, 0:n], bs[:, 3:4], 0.0, op0=ALU.is_ge, op1=ALU.add,
                                                                    accum_out=bs[:, 4:5]),
                             reads=[acc, bs], writes=[sel, bs])
                        yield
                        k.op("dve", lambda step=step: nc.vector.tensor_scalar(bs[:, 5:6], bs[:, 4:5], 256.0, step, op0=ALU.is_ge, op1=ALU.mult),
                             reads=[bs], writes=[bs])
                        yield
                        k.op("dve", lambda: nc.vector.scalar_tensor_tensor(bs[:, 1:2], bs[:, 5:6], bs[:, 2:3], bs[:, 1:2], op0=ALU.mult, op1=ALU.add),
                             reads=[bs], writes=[bs])
                        yield
                        if it < NIT:
                            k.op("dve", lambda step=step: nc.vector.scalar_tensor_tensor(bs[:, 3:4], bs[:, 2:3], step * 0.5, bs[:, 1:2], op0=ALU.mult, op1=ALU.add),
                                 reads=[bs], writes=[bs])
                            yield
                    k.op("dve", lambda: nc.vector.tensor_scalar(sel[:, 0:n], acc[:, 0:n], bs[:, 1:2], None, op0=ALU.is_ge),
                         reads=[acc, bs], writes=[sel])
                    yield
                else:
                    thr_t, thr = self.thr_c, self.thr_c[:, 0:1]
                    k.op("dve", lambda sel=sel, acc=acc, n=n, thr=thr: nc.vector.tensor_scalar(sel[:, 0:n], acc[:, 0:n], thr, None, op0=ALU.is_ge),
                         reads=[acc, thr_t], writes=[sel])
                    yield
                for kc in range(qi + 1):
                    blk = tbi % 8; tbi += 1
                    k.op("pe", lambda sel=sel, kc=kc, blk=blk: nc.tensor.transpose(self.ptb[:, blk * P:(blk + 1) * P], sel[:, kc * P:(kc + 1) * P], self.ident_b[:]),
                         reads=[sel, self.ident_b], writes=[self.ptb])
                    o_ = soff(kc) + (qi - kc) * P
                    k.op("act", lambda blk=blk, o_=o_: nc.scalar.copy(selT[:, o_:o_ + P], self.ptb[:, blk * P:(blk + 1) * P]),
                         reads=[self.ptb], writes=[selT])

            for pa in range(8):
                gens = [q_chain(2 * pa + 1), q_chain(2 * pa)]
                while gens:
                    for g_ in list(gens):
                        try:
                            next(g_)
                        except StopIteration:
                            gens.remove(g_)
            self.dbg("bs1", bss[1], [P, 8])
            self.dbg("acc1", accs[1], [P, S])
            self.dbg("sel1", sel2, [P, S], BF16)
            k.barrier()
        self.dbg("selT", selT, [P, 136 * P], BF16)
        self.attention(mla=False, selT=lambda kc, q0, q1: selT[:, soff(kc) + q0 - kc * P: soff(kc) + q1 - kc * P], selT_t=selT)

    def peer_layer(self, l, li):
        k, nc, I = self.k, self.nc, self.I
        xT, X = self.xT, self.X
        AXX = mybir.AxisListType.X
        if not hasattr(self, "gateD"):
            self.gateD = k.dram("gateD", [P, P, S], BF16, nslots=P)
        gateD = self.gateD
        with ExitStack() as esL:
            LT = k.sbuf("p_LT", [P, 3, S], BF16, nslots=NT, es=esL)
            with ExitStack() as es:
                wq = k.sbuf("p_wq_b", [P, 8, 2048], BF16, es=es)
                skT = k.sbuf("p_skT_b", [P, 16, P], BF16, es=es)
                for c in range(8):
                    self.load_w(wq, wq[:, c, :], I[f"p_wq{l}"][c * P:(c + 1) * P, :], 2048, eng=("dve", "act")[c % 2])
                st = self.stage()
                k.dma("sp", st[:].rearrange("p (g n) -> p g n", g=16), I[f"p_skT{l}"].rearrange("g d n -> d g n"), writes=[st])
                k.op("dve", lambda: nc.vector.tensor_copy(skT[:].rearrange("p g n -> p (g n)"), st[:]), reads=[st], writes=[skT])
                qTbs = [k.sbuf(f"p_qTb{z}", [P, 16, 256], BF16, es=es) for z in range(2)]
                s_sbs = [Tn(self.stg[z].h, 16, f"p_s{z}") for z in range(2)]
                v16 = k.sbuf("p_v16", [P, 256], F32, nslots=16, es=es)
                i16 = k.sbuf("p_i16", [P, 256], U32, nslots=16, es=es)
                i16f = k.sbuf("p_i16f", [P, 256], F32, es=es)
                best = k.sbuf("p_best", [P, 128], F32, nslots=8, es=es)
                pos = k.sbuf("p_pos", [P, 128], U32, nslots=8, es=es)
                pab = k.sbuf("p_pab", [P, 2, 128], U32, es=es)
                pabf = k.sbuf("p_pabf", [P, 2, 128], F32, es=es)
                sm8 = k.sbuf("p_sm8", [P, 3, 8], F32, es=es)
                ex = k.sbuf("p_ex", [P, 128], F32, es=es)
                Lt = k.sbuf("p_Lt", [P, 3, 128], F32, es=es)
                k.barrier()

                def selfsync():
                    if k.cnt["dve"] > 0:
                        k.wait("dve", (k.sem["dve"], k.cnt["dve"]))

                def tile_chain(tt, t2, qTb):
                    s_sb = s_sbs[tt % 2]
                    cand = s_sb
                    work = s_sb
                    for bq in range(4):
                        ps = self.bank()
                        for gg in range(4):
                            g = bq * 4 + gg
                            k.op("pe", lambda ps=ps, g=g, gg=gg: nc.tensor.matmul(
                                ps[:, gg * P:(gg + 1) * P], qTb[:, g, t2 * P:(t2 + 1) * P], skT[:, g, :], start=True, stop=True),
                                 reads=[qTb, skT], writes=[ps], inc=(gg == 3))
                        k.op("act", lambda ps=ps, bq=bq: nc.scalar.copy(s_sb[:, bq * 512:(bq + 1) * 512], ps[:]),
                             reads=[ps], writes=[s_sb.s(*range(bq * 4, bq * 4 + 4))])
                    G16 = range(16)
                    sg = lambda g: s_sb[:, g * P:(g + 1) * P]
                    va = lambda g: v16[:, g * 16:g * 16 + 8]
                    vb_ = lambda g: v16[:, g * 16 + 8:g * 16 + 16]
                    wk = lambda g: work[:, g * P:(g + 1) * P]
                    for g in G16:
                        k.op("dve", lambda g=g: nc.vector.max(out=va(g), in_=sg(g)), reads=[s_sb.s(g)], writes=[v16.s(g)])
                    selfsync()
                    for g in G16:
                        k.op("dve", lambda g=g: nc.vector.max_index(i16[:, g * 16:g * 16 + 8], va(g), sg(g)), reads=[s_sb.s(g), v16.s(g)], writes=[i16.s(g)])
                    for g in G16:
                        k.op("dve", lambda g=g: nc.vector.match_replace(out=sg(g), in_to_replace=va(g), in_values=sg(g), imm_value=NEG),
                             reads=[s_sb.s(g), v16.s(g)], writes=[s_sb.s(g)])
                    selfsync()
                    for g in G16:
                        k.op("dve", lambda g=g: nc.vector.max(out=vb_(g), in_=sg(g)), reads=[s_sb.s(g)], writes=[v16.s(g)])
                    selfsync()
                    for g in G16:
                        k.op("dve", lambda g=g: nc.vector.max_index(i16[:, g * 16 + 8:g * 16 + 16], vb_(g), sg(g)), reads=[s_sb.s(g), v16.s(g)], writes=[i16.s(g)])
                    k.op("dve", lambda: nc.vector.tensor_copy(i16f[:], i16[:]), reads=[i16], writes=[i16f])
                    v4 = v16[:].rearrange("p (h c a) -> p h c a", h=8, c=2)
                    i4 = i16f[:].rearrange("p (h c a) -> p h c a", h=8, c=2)
                    c4 = cand[:].rearrange("p (h a b) -> p h a b", h=8, a=16)
                    k.op("dve", lambda: nc.vector.tensor_tensor(c4, v4[:, :, 0, :].unsqueeze(3).to_broadcast([P, 8, 16, 16]),
                                                                v4[:, :, 1, :].unsqueeze(2).to_broadcast([P, 8, 16, 16]), op=ALU.add),
                         reads=[v16], writes=[cand])
                    H8 = range(8)
                    ch = lambda h: cand[:, h * 256:(h + 1) * 256]
                    cs = lambda h: cand.s(2 * h, 2 * h + 1)
                    ws = lambda h: work.s(2 * h, 2 * h + 1)
                    wh = lambda h: work[:, h * 256:(h + 1) * 256]
                    ba = lambda h: best[:, h * 16:h * 16 + 8]
                    bb = lambda h: best[:, h * 16 + 8:h * 16 + 16]
                    selfsync()
                    for h in H8:
                        k.op("dve", lambda h=h: nc.vector.max(out=ba(h), in_=ch(h)), reads=[cs(h)], writes=[best.s(h)])
                    selfsync()
                    for h in H8:
                        k.op("dve", lambda h=h: nc.vector.max_index(pos[:, h * 16:h * 16 + 8], ba(h), ch(h)), reads=[cs(h), best.s(h)], writes=[pos.s(h)])
                    for h in H8:
                        k.op("dve", lambda h=h: nc.vector.match_replace(out=ch(h), in_to_replace=ba(h), in_values=ch(h), imm_value=NEG),
                             reads=[cs(h), best.s(h)], writes=[cs(h)])
                    selfsync()
                    for h in H8:
                        k.op("dve", lambda h=h: nc.vector.max(out=bb(h), in_=ch(h)), reads=[cs(h)], writes=[best.s(h)])
                    selfsync()
                    for h in H8:
                        k.op("dve", lambda h=h: nc.vector.max_index(pos[:, h * 16 + 8:h * 16 + 16], bb(h), ch(h)), reads=[cs(h), best.s(h)], writes=[pos.s(h)])
                    b3 = best[:].rearrange("p (h r) -> p h r", h=8)
                    e3 = ex[:].rearrange("p (h r) -> p h r", h=8)
                    k.op("dve", lambda: nc.vector.tensor_tensor(e3, b3, b3[:, :, 0:1].to_broadcast([P, 8, 16]), op=ALU.subtract),
                         reads=[best], writes=[ex])
                    k.op("act", lambda: nc.scalar.activation(ex[:], ex[:], AF.Exp), reads=[ex], writes=[ex])
                    k.op("dve", lambda: nc.vector.tensor_single_scalar(pab[:, 0, :], pos[:], 4, op=ALU.logical_shift_right), reads=[pos], writes=[pab])
                    k.op("dve", lambda: nc.vector.tensor_single_scalar(pab[:, 1, :], pos[:], 15, op=ALU.bitwise_and), reads=[pos], writes=[pab])
                    k.op("dve", lambda: nc.vector.tensor_copy(pabf[:], pab[:]), reads=[pab], writes=[pabf])
                    k.op("dve", lambda: nc.vector.reduce_sum(sm8[:, 0, :], e3, axis=AXX), reads=[ex], writes=[sm8])
                    k.op("dve", lambda: nc.vector.reciprocal(sm8[:, 1, :], sm8[:, 0, :]), reads=[sm8], writes=[sm8])
                    k.op("dve", lambda: nc.vector.tensor_tensor(Lt[:, 0, :].rearrange("p (h r) -> p h r", h=8), e3,
                                                                sm8[:, 1, :].unsqueeze(2).to_broadcast([P, 8, 16]), op=ALU.mult),
                         reads=[ex, sm8], writes=[Lt])
                    for w_ in range(2):
                        abf = pabf[:, w_, :].rearrange("p (h r) -> p h r", h=8)
                        k.op("dve", lambda abf=abf: nc.vector.tensor_tensor(
                            c4, self.iota_f[:, 0:16].unsqueeze(1).unsqueeze(1).to_broadcast([P, 8, 16, 16]),
                            abf.unsqueeze(3).to_broadcast([P, 8, 16, 16]), op=ALU.is_equal),
                             reads=[self.iota_f, pabf], writes=[cand])
                        k.op("dve", lambda w_=w_: nc.vector.tensor_tensor(c4, c4, i4[:, :, w_, :].unsqueeze(2).to_broadcast([P, 8, 16, 16]), op=ALU.mult),
                             reads=[i16f, cand], writes=[cand])
                        k.op("dve", lambda w_=w_: nc.vector.reduce_sum(Lt[:, 1 + w_, :].rearrange("p (h r) -> p h r", h=8), c4, axis=AXX),
                             reads=[cand], writes=[Lt])
                    ps = self.bank()
                    for q3 in range(3):
                        k.op("pe", lambda q3=q3, ps=ps: nc.tensor.transpose(ps[:, q3 * P:(q3 + 1) * P], Lt[:, q3, :], self.ident_f[:]),
                             reads=[Lt, self.ident_f], writes=[ps], inc=(q3 == 2))
                    k.op("act", lambda ps=ps, tt=tt: nc.scalar.copy(LT[:, :, tt * P:(tt + 1) * P], ps[:, 0:384].rearrange("p (q t) -> p q t", q=3)),
                         reads=[ps], writes=[LT.s(tt)])

                for tb in range(8):
                    tsl = slice(tb * 256, (tb + 1) * 256)
                    tslots = (2 * tb, 2 * tb + 1)
                    qTb = qTbs[tb % 2]
                    for g in range(16):
                        ps = self.bank()
                        for dc in range(8):
                            k.op("pe", lambda dc=dc, g=g, ps=ps: nc.tensor.matmul(ps[:, 0:256], wq[:, dc, g * P:(g + 1) * P], xT[:, dc, tsl],
                                                                                   start=(dc == 0), stop=(dc == 7)),
                                 reads=[wq, xT.s(*tslots)], writes=[ps], inc=(dc == 7))
                        k.op("act", lambda g=g, ps=ps, qTb=qTb: nc.scalar.copy(qTb[:, g, :], ps[:, 0:256]), reads=[ps], writes=[qTb])
                    for t2 in range(2):
                        tile_chain(2 * tb + t2, t2, qTb)
                k.barrier()
            self.dbg("LT", LT, [P, 3, S], BF16)
            with ExitStack() as es:
                TBK = 8
                Ap = [k.sbuf(f"p_Ap{i}", [P, TBK, P], BF16, es=es) for i in range(2)]
                Bp = [k.sbuf(f"p_Bp{i}", [P, TBK, P], BF16, es=es) for i in range(2)]
                gt = [k.sbuf(f"p_gt{i}", [P, P, P], BF16, es=es) for i in range(2)]
                ev_i = 0
                for tt in range(NT):
                    g_ = gt[tt % 2]
                    for sub in range(P // TBK):
                        t0 = tt * P + sub * TBK
                        A_ = Ap[sub % 2]
                        B_ = Bp[sub % 2]
                        for tl_ in range(TBK):
                            t_g = t0 + tl_
                            k.op("dve", lambda A_=A_, tl_=tl_, t_g=t_g: nc.vector.tensor_scalar(
                                A_[:, tl_, :], self.iota_b[:], LT[:, 1, t_g:t_g + 1], LT[:, 0, t_g:t_g + 1], op0=ALU.is_equal, op1=ALU.mult),
                                 reads=[self.iota_b, LT.s(tt)], writes=[A_])
                            k.op("dve", lambda B_=B_, tl_=tl_, t_g=t_g: nc.vector.tensor_scalar(
                                B_[:, tl_, :], self.iota_b[:], LT[:, 2, t_g:t_g + 1], None, op0=ALU.is_equal),
                                 reads=[self.iota_b, LT.s(tt)], writes=[B_])
                        for q4 in range(TBK // 4):
                            ps = self.bank()
                            for tl in range(4):
                                t_ = q4 * 4 + tl
                                k.op("pe", lambda ps=ps, tl=tl, t_=t_, A_=A_, B_=B_: nc.tensor.matmul(
                                    ps[:, tl * P:(tl + 1) * P], B_[:, t_, :], A_[:, t_, :], start=True, stop=True),
                                     reads=[A_, B_], writes=[ps], inc=(tl == 3))
                            c_ = sub * TBK + q4 * 4
                            dst = g_[:, :, c_:c_ + 4]
                            src = ps[:].rearrange("p (t i) -> p i t", t=4)
                            k.op("act", lambda dst=dst, src=src: nc.scalar.copy(dst, src), reads=[ps], writes=[g_])
                            ev_i += 1
                    for i8 in range(8):
                        k.dma("sp", gateD[i8 * 16:(i8 + 1) * 16, :, tt * P:(tt + 1) * P].rearrange("i j t -> j i t"),
                              g_[:, i8 * 16:(i8 + 1) * 16, :], reads=[g_], writes=[gateD.s(*range(i8 * 16, (i8 + 1) * 16))])
                k.barrier()
        with ExitStack() as es:
            G = 4
            hg = [k.sbuf(f"p_hg{i}", [P, S], BF16, es=es) for i in range(2 * G)]
            wub = [k.sbuf(f"p_wub{i}", [P, D], BF16, es=es) for i in range(2 * G)]
            wdb = [k.sbuf(f"p_wdb{i}", [P, D], BF16, es=es) for i in range(2)]
            wst = [k.sbuf(f"p_wst{i}", [P, D], F32, es=es) for i in range(4)]
            ge = [k.sbuf(f"p_ge{i}", [P, 512], BF16, es=es) for i in range(3)]
            for tt in range(NT):
                k.op("pool", lambda tt=tt: nc.gpsimd.tensor_scalar(X[:, tt, :], X[:, tt, :], ALPHA, None, op0=ALU.mult),
                     reads=[X.s(tt)], writes=[X.s(tt)])
            gei = 0
            abank = 0
            ybank = 0
            wsi = 0
            for i in range(P):
                slot = i % (2 * G)
                h_ = hg[slot]
                wu_ = wub[slot]
                wd_ = wdb[i % 2]
                s1 = wst[wsi % 4]; wsi += 1
                s2 = wst[wsi % 4]; wsi += 1
                k.dma("sp", s1[:], I[f"p_wdT{l}"][i], writes=[s1])
                k.dma("sp", s2[:], I[f"p_wu{l}"][i * P:(i + 1) * P, :], writes=[s2])
                k.dma("sp", h_[:], gateD[i], reads=[gateD.s(i)], writes=[h_])
                k.op("pool", lambda wd_=wd_, s1=s1: nc.gpsimd.tensor_copy(wd_[:], s1[:]), reads=[s1], writes=[wd_])
                k.op("pool", lambda wu_=wu_, s2=s2: nc.gpsimd.tensor_copy(wu_[:], s2[:]), reads=[s2], writes=[wu_])
                for nb in range(4):
                    ps = self.pb[abank % 4]; abank += 1
                    for dc in range(8):
                        k.op("pe", lambda ps=ps, dc=dc, nb=nb, wd_=wd_: nc.tensor.matmul(
                            ps[:], wd_[:, dc * P:(dc + 1) * P], xT[:, dc, nb * 512:(nb + 1) * 512], start=(dc == 0), stop=(dc == 7)),
                             reads=[wd_, xT.s(*range(nb * 4, nb * 4 + 4))], writes=[ps], inc=(dc == 7))
                    g_ = ge[gei % 3]; gei += 1
                    k.op("act", lambda ps=ps, g_=g_: nc.scalar.activation(g_[:], ps[:], AF.Gelu), reads=[ps], writes=[g_])
                    k.op("dve", lambda g_=g_, h_=h_, nb=nb: nc.vector.tensor_tensor(
                        h_[:, nb * 512:(nb + 1) * 512], h_[:, nb * 512:(nb + 1) * 512], g_[:], op=ALU.mult),
                         reads=[g_, h_], writes=[h_])
                if i % G == G - 1:
                    base = slot - (G - 1)
                    for tt in range(NT):
                        for half in range(2):
                            ps = self.pb[4 + ybank % 3]; ybank += 1
                            for gg in range(G):
                                k.op("pe", lambda ps=ps, gg=gg, tt=tt, half=half, base=base: nc.tensor.matmul(
                                    ps[:], hg[base + gg][:, tt * P:(tt + 1) * P], wub[base + gg][:, half * 512:(half + 1) * 512],
                                    start=(gg == 0), stop=(gg == G - 1)),
                                     reads=[hg[base + gg], wub[base + gg]], writes=[ps], inc=(gg == G - 1))
                            xs = X[:, tt, half * 512:(half + 1) * 512]
                            k.op("dve", lambda ps=ps, xs=xs: nc.vector.tensor_tensor(xs, xs, ps[:], op=ALU.add),
                                 reads=[ps, X.s(tt)], writes=[X.s(tt)])
            self.ln_params(li, es)
            for tt in range(NT):
                self.ln_tile(tt, li)
            k.barrier()

def _rope_tab(dim):
    inv = (np.float32(10000.0) ** (-np.arange(0, dim, 2, dtype=np.float32) / np.float32(dim))).astype(np.float32)
    ang = (np.arange(S, dtype=np.float32)[:, None] * inv[None, :]).astype(np.float32)
    c = np.cos(ang).astype(np.float32).T
    s = np.sin(ang).astype(np.float32).T
    half = dim // 2
    reps = P // dim
    cos = np.concatenate([c, c] * reps, axis=0)
    sin = np.concatenate([-s, s] * reps, axis=0)
    return np.ascontiguousarray(cos), np.ascontiguousarray(sin)


def prep_shared(inp):
    f = np.float32
    sh = {}
    sh["ident"] = np.eye(P, dtype=f)
    sh["cos64"], sh["sin64"] = _rope_tab(64)
    sh["cos128"], sh["sin128"] = _rope_tab(128)
    kk = np.arange(P)[:, None]
    qq = np.arange(P)[None, :]
    sh["maskT"] = np.where((kk >= 64) & (qq < 64), 0.0, 1.0).astype(f)
    sh["negm"] = np.where((kk < 64) & (qq >= 64), NEG, 0.0).astype(f)
    sh["iota"] = np.broadcast_to(np.arange(P, dtype=f)[None, :], (P, P)).copy()
    w_in = inp["mla_w_in"][0]
    kr = w_in[:, 640:704]
    kr_sw = np.concatenate([kr[:, 32:64], kr[:, 0:32]], axis=1)
    sh["m_win"] = np.ascontiguousarray(np.concatenate([w_in[:, :640], kr, kr, kr_sw, kr_sw], axis=1))
    wuq = inp["mla_w_uq"][0]
    nope = wuq[:, :, :128].reshape(384, 1024)
    rp = wuq[:, :, 128:192]
    rp_sw = np.concatenate([rp[:, :, 32:64], rp[:, :, 0:32]], axis=2)
    sh["m_wuq"] = np.ascontiguousarray(np.concatenate([nope, rp.reshape(384, 512), rp_sw.reshape(384, 512)], axis=1))
    wukv = inp["mla_w_ukv"][0]
    sh["m_wukv"] = np.ascontiguousarray(np.concatenate([wukv[:, :, :128].reshape(256, 1024), wukv[:, :, 128:].reshape(256, 1024)], axis=1))
    sh["m_wo"] = np.ascontiguousarray(inp["mla_w_o"][0])
    sh["m_qn"] = np.ascontiguousarray(inp["mla_q_norm"][0].reshape(3, P).T)
    sh["m_kvn"] = np.ascontiguousarray(inp["mla_kv_norm"][0].reshape(2, P).T)
    dw = inp["dsa_w_in"][0]

    def sw(w, hd):
        n = w.shape[1] // hd
        w3 = w.reshape(w.shape[0], n, hd)
        return np.concatenate([w3[:, :, hd // 2:], w3[:, :, :hd // 2]], axis=2).reshape(w.shape[0], n * hd)

    q, kx, v = dw[:, 0:1024], dw[:, 1024:2048], dw[:, 2048:3072]
    qi, ki, wi = dw[:, 3072:3584], dw[:, 3584:3648], dw[:, 3648:3656]
    cols = []
    for h in range(8):
        cols += [q[:, h * P:(h + 1) * P], sw(q[:, h * P:(h + 1) * P], P)]
    for h in range(8):
        cols += [kx[:, h * P:(h + 1) * P], sw(kx[:, h * P:(h + 1) * P], P)]
    qis = sw(qi, 64)
    for pr in range(4):
        cols += [qi[:, pr * P:(pr + 1) * P], qis[:, pr * P:(pr + 1) * P]]
    kis = sw(ki, 64)
    cols += [ki, ki, kis, kis]
    cols += [v, wi]
    sh["d_win"] = np.ascontiguousarray(np.concatenate(cols, axis=1))
    assert sh["d_win"].shape[1] == 6408
    sh["d_wo"] = np.ascontiguousarray(inp["dsa_w_o"][0])
    for l in range(2):
        sh[f"p_wq{l}"] = np.ascontiguousarray(inp["peer_w_q"][l])
        sk = inp["peer_sub_keys"][l]
        sh[f"p_skT{l}"] = np.ascontiguousarray(sk.transpose(1, 0, 3, 2).reshape(16, P, P))
        wd = inp["peer_w_down"][l]
        sh[f"p_wdT{l}"] = np.ascontiguousarray(wd.reshape(P, P, 8, P).transpose(0, 3, 2, 1).reshape(P, P, 8 * P))
        sh[f"p_wu{l}"] = np.ascontiguousarray(inp["peer_w_up"][l])
    sh["ln_g"] = np.ascontiguousarray(inp["ln_gain"].reshape(4, D))
    sh["ln_b"] = np.ascontiguousarray(inp["ln_bias"].reshape(4, D))
    return sh


def kernel(**inputs):
    inp = {k_: np.asarray(v) for k_, v in inputs.items()}
    sh = prep_shared(inp)
    prog = Prog()
    nc = prog.build()
    x = np.ascontiguousarray(inp["x"], dtype=np.float32)
    in_maps = []
    for c in range(8):
        m = dict(sh)
        m["x"] = x[c]
        in_maps.append(m)
    res = run_bass_kernel_spmd(nc, in_maps, core_ids=list(range(8)))
    return np.stack([res.results[c]["out"] for c in range(8)], axis=0).astype(np.float32)
```

```python
from contextlib import ExitStack
import math
import numpy as np
import concourse.bass as bass
import concourse.mybir as mybir
from concourse.bass_utils import run_bass_kernel_spmd

F32 = mybir.dt.float32
BF16 = mybir.dt.bfloat16
U32 = mybir.dt.uint32
AF = mybir.ActivationFunctionType
ALU = mybir.AluOpType

S = 2048
D = 1024
NT = 16
P = 128
ALPHA = float((2 * 2) ** 0.25)
LN_EPS = 1e-5
RMS_EPS = 1e-6
NEG = -1.0e30


class Tn:
    def __init__(self, h, nslots=1, name=""):
        self.h = h
        self.n = nslots
        self.name = name
        self.lw = [None] * nslots
        self.rd = [dict() for _ in range(nslots)]

    def __getitem__(self, key):
        return self.h[key]

    def s(self, *slots):
        return (self, slots)


def _norm(acc):
    if isinstance(acc, Tn):
        return acc, range(acc.n)
    return acc


class KB:
    def __init__(self, nc, es):
        self.nc = nc
        self.es = es
        self.eng = {"pe": nc.tensor, "act": nc.scalar, "dve": nc.vector, "pool": nc.gpsimd, "sp": nc.sync}
        self.sem = {}
        self.cnt = {}
        for e in ("pe", "act", "dve", "pool"):
            self.sem[e] = es.enter_context(nc.semaphore("c_" + e))
            self.cnt[e] = 0
        self.known = {e: {} for e in self.eng}
        self.rings = {}
        self.dma_n = {}
        for q, n in {"sp": 24, "pool": 8, "act": 8}.items():
            self.rings[q] = [es.enter_context(nc.semaphore(f"r_{q}{i}")) for i in range(n)]
            self.dma_n[q] = 0

    def wait(self, e, ev):
        sem, val = ev
        kk = id(sem)
        if self.known[e].get(kk, 0) >= val:
            return
        self.eng[e].wait_ge(sem, val)
        self.known[e][kk] = val

    def _deps(self, reads, writes):
        evs = []
        for acc in reads:
            t, sl = _norm(acc)
            for s in sl:
                if t.lw[s] is not None:
                    evs.append(t.lw[s])
        for acc in writes:
            t, sl = _norm(acc)
            for s in sl:
                if t.lw[s] is not None:
                    evs.append(t.lw[s])
                evs.extend(t.rd[s].values())
        return evs

    def _record(self, ev, reads, writes):
        sem, val = ev
        kk = id(sem)
        for acc in reads:
            t, sl = _norm(acc)
            for s in sl:
                d = t.rd[s]
                if kk not in d or d[kk][1] < val:
                    d[kk] = ev
        for acc in writes:
            t, sl = _norm(acc)
            for s in sl:
                t.lw[s] = ev
                t.rd[s] = dict()

    def op(self, e, fn, reads=(), writes=(), inc=True):
        own = self.sem[e]
        for ev in self._deps(reads, writes):
            if ev[0] is own and e == "pe":
                continue
            self.wait(e, ev)
        ins = fn()
        if inc:
            self.cnt[e] += 1
            ins.then_inc(own, 1)
            ev = (own, self.cnt[e])
        else:
            assert e == "pe"
            ev = (own, self.cnt[e] + 1)
        self._record(ev, reads, writes)
        return ev

    def dma(self, q, out, in_, reads=(), writes=(), **kw):
        n = self.dma_n[q]
        ring = self.rings[q]
        K = len(ring)
        slot = n % K
        if n >= K:
            self.wait(q, (ring[slot], 16 * (n // K)))
        for ev in self._deps(reads, writes):
            self.wait(q, ev)
        self.eng[q].dma_start(out=out, in_=in_, **kw).then_inc(ring[slot], 16)
        self.dma_n[q] = n + 1
        ev = (ring[slot], 16 * (n // K + 1))
        self._record(ev, reads, writes)
        return ev

    def all_events(self):
        evs = []
        for e in ("pe", "act", "dve", "pool"):
            if self.cnt[e] > 0:
                evs.append((self.sem[e], self.cnt[e]))
        for q, ring in self.rings.items():
            n = self.dma_n[q]
            K = len(ring)
            for slot in range(min(n, K)):
                last = ((n - 1 - slot) // K) * K + slot
                evs.append((ring[slot], 16 * (last // K + 1)))
        return evs

    def barrier(self):
        evs = self.all_events()
        for e in self.eng:
            for ev in evs:
                self.wait(e, ev)

    def sbuf(self, name, shape, dt, nslots=1, es=None):
        self.uid = getattr(self, "uid", 0) + 1
        name = f"{name}_{self.uid}"
        h = (es or self.es).enter_context(self.nc.sbuf_tensor(name, list(shape), dt))
        return Tn(h, nslots, name)

    def psum(self, name, shape, dt, nslots=1, es=None):
        h = (es or self.es).enter_context(self.nc.psum_tensor(name, list(shape), dt))
        return Tn(h, nslots, name)

    def dram(self, name, shape, dt, nslots=1):
        h = self.nc.dram_tensor(name, list(shape), dt).ap()
        return Tn(h, nslots, name)


class Prog:
    def __init__(self, debug=()):
        self.debug = set(debug)
        self.nc = bass.Bass("TRN2", target_bir_lowering=False)
        self.dbg_out = {}

    def din(self, name, shape, dt=F32):
        return self.nc.dram_tensor(name, list(shape), dt, kind="ExternalInput").ap()

    def build(self, stages=("mla", "peer0", "dsa", "peer1")):
        nc = self.nc
        I = {}
        I["x"] = self.din("x", [S, D])
        I["ident"] = self.din("ident", [P, P])
        I["cos64"] = self.din("cos64", [P, S])
        I["sin64"] = self.din("sin64", [P, S])
        I["cos128"] = self.din("cos128", [P, S])
        I["sin128"] = self.din("sin128", [P, S])
        I["maskT"] = self.din("maskT", [P, P])
        I["negm"] = self.din("negm", [P, P])
        I["iota"] = self.din("iota", [P, P])
        I["m_win"] = self.din("m_win", [D, 896])
        I["m_wuq"] = self.din("m_wuq", [384, 2048])
        I["m_wukv"] = self.din("m_wukv", [256, 2048])
        I["m_wo"] = self.din("m_wo", [D, D])
        I["m_qn"] = self.din("m_qn", [P, 3])
        I["m_kvn"] = self.din("m_kvn", [P, 2])
        I["d_win"] = self.din("d_win", [D, 6408])
        I["d_wo"] = self.din("d_wo", [D, D])
        for l in range(2):
            I[f"p_wq{l}"] = self.din(f"p_wq{l}", [D, 2048])
            I[f"p_skT{l}"] = self.din(f"p_skT{l}", [16, P, P])
            I[f"p_wdT{l}"] = self.din(f"p_wdT{l}", [P, P, 8 * P])
            I[f"p_wu{l}"] = self.din(f"p_wu{l}", [P * P, D])
        I["ln_g"] = self.din("ln_g", [4, D])
        I["ln_b"] = self.din("ln_b", [4, D])
        self.I = I
        self.out = nc.dram_tensor("out", [S, D], F32, kind="ExternalOutput").ap()

        with ExitStack() as es:
            k = KB(nc, es)
            self.k = k
            self.setup_common(es)
            self.load_x()
            li = 0
            for st in stages:
                if st == "mla":
                    self.mla_layer()
                    self.mixer_out_ln(I["m_wo"], 0)
                elif st == "dsa":
                    self.dsa_layer()
                    self.mixer_out_ln(I["d_wo"], 2)
                    self.dsa_es.close()
                elif st.startswith("peer"):
                    l = int(st[4:])
                    self.peer_layer(l, 2 * l + 1)
            self.store_out()
            k.barrier()
        return nc

    def setup_common(self, es):
        k, nc, I = self.k, self.nc, self.I
        self.X = k.sbuf("X", [P, NT, D], F32, nslots=NT)
        self.xT = k.sbuf("xT", [P, 8, S], BF16, nslots=NT)
        self.ident_f = k.sbuf("ident_f", [P, P], F32)
        self.ident_b = k.sbuf("ident_b", [P, P], BF16)
        self.ones_b = k.sbuf("ones_b", [P, P], BF16)
        self.maskT_b = k.sbuf("maskT_b", [P, P], BF16)
        self.negm = k.sbuf("negm_s", [P, P], F32)
        self.iota_b = k.sbuf("iota_b", [P, P], BF16)
        self.iota_f = k.sbuf("iota_f", [P, P], F32)
        self.eps_ln = k.sbuf("eps_ln", [P, 1], F32)
        self.eps_rms = k.sbuf("eps_rms", [P, 1], F32)
        self.thr_c = k.sbuf("thr_c", [P, 1], F32)
        self.stg = [k.sbuf(f"stg{i}", [P, 2048], F32) for i in range(2)]
        self.stg_i = 0
        self.pb = [k.psum(f"pb{i}", [P, 512], F32) for i in range(7)]
        self.pb_i = 0
        self.ptb = k.psum("ptb", [P, 1024], BF16)
        k.dma("sp", self.ident_f[:], I["ident"], writes=[self.ident_f])
        k.op("dve", lambda: nc.vector.tensor_copy(self.ident_b[:], self.ident_f[:]), reads=[self.ident_f], writes=[self.ident_b])
        k.op("pool", lambda: nc.gpsimd.memset(self.ones_b[:], 1.0), writes=[self.ones_b])
        k.op("pool", lambda: nc.gpsimd.memset(self.eps_ln[:], LN_EPS), writes=[self.eps_ln])
        k.op("pool", lambda: nc.gpsimd.memset(self.eps_rms[:], RMS_EPS), writes=[self.eps_rms])
        k.op("pool", lambda: nc.gpsimd.memset(self.thr_c[:], -1.0e29), writes=[self.thr_c])
        st = self.stage()
        k.dma("sp", st[:, 0:P], I["maskT"], writes=[st])
        k.op("dve", lambda: nc.vector.tensor_copy(self.maskT_b[:], st[:, 0:P]), reads=[st], writes=[self.maskT_b])
        k.dma("sp", self.negm[:], I["negm"], writes=[self.negm])
        k.dma("sp", self.iota_f[:], I["iota"], writes=[self.iota_f])
        k.op("dve", lambda: nc.vector.tensor_copy(self.iota_b[:], self.iota_f[:]), reads=[self.iota_f], writes=[self.iota_b])
        self.ln_st = k.sbuf("ln_st", [P, NT, 12], F32, nslots=NT)
        self.ln_mv = k.sbuf("ln_mv", [P, NT, 4], F32, nslots=NT)
        self.xbf = [k.sbuf(f"xbf{i}", [P, D], BF16) for i in range(2)]
        self.qnD = k.dram("qnD", [8, P, S], BF16, nslots=8)
        self.knD = k.dram("knD", [8, P, S], BF16, nslots=8)
        self.qrD = k.dram("qrD", [8, 64, S], BF16, nslots=8)
        self.krD = k.dram("krD", [P, S], BF16)
        self.VD = k.dram("VD", [NT, P, D], BF16, nslots=NT)
        self.qiD = k.dram("qiD", [4, P, S], BF16, nslots=4)
        self.kiD = k.dram("kiD", [P, S], BF16)

    def stage(self):
        s = self.stg[self.stg_i % len(self.stg)]
        self.stg_i += 1
        return s

    def bank(self):
        b = self.pb[self.pb_i % len(self.pb)]
        self.pb_i += 1
        return b

    def dbg(self, name, t, shape, dt=F32, ap=None):
        if name not in self.debug:
            return
        o = self.nc.dram_tensor("dbg_" + name, list(shape), dt, kind="ExternalOutput").ap()
        self.k.dma("sp", o, ap if ap is not None else t[:], reads=[t])
        self.dbg_out[name] = (shape, dt)

    def make_xT(self, tt):
        k, nc = self.k, self.nc
        xb = self.xbf[tt % 2]
        k.op("act", lambda: nc.scalar.copy(xb[:], self.X[:, tt, :]), reads=[self.X.s(tt)], writes=[xb])
        for c in range(8):
            k.op("pe", lambda c=c: nc.tensor.transpose(self.ptb[:, c * P:(c + 1) * P], xb[:, c * P:(c + 1) * P], self.ident_b[:]),
                 reads=[xb, self.ident_b], writes=[self.ptb], inc=(c == 7))
        k.op("dve", lambda: nc.vector.tensor_copy(self.xT[:, :, tt * P:(tt + 1) * P],
                                                  self.ptb[:].rearrange("p (c t) -> p c t", c=8)),
             reads=[self.ptb], writes=[self.xT.s(tt)])

    def load_x(self):
        k, nc = self.k, self.nc
        for tt in range(NT):
            k.dma("sp", self.X[:, tt, :], self.I["x"][tt * P:(tt + 1) * P, :], writes=[self.X.s(tt)])
            self.make_xT(tt)

    def store_out(self):
        k = self.k
        for tt in range(NT):
            k.dma("sp", self.out[tt * P:(tt + 1) * P, :], self.X[:, tt, :], reads=[self.X.s(tt)])

    def ln_params(self, li, es):
        k, I = self.k, self.I
        self.lng = k.sbuf("lng", [P, D], F32, es=es)
        self.lnb = k.sbuf("lnb", [P, D], F32, es=es)
        k.dma("sp", self.lng[:], I["ln_g"][li:li + 1, :].partition_broadcast(P), writes=[self.lng])
        k.dma("sp", self.lnb[:], I["ln_b"][li:li + 1, :].partition_broadcast(P), writes=[self.lnb])

    def ln_tile(self, tt, li):
        k, nc = self.k, self.nc
        X = self.X
        xt = X[:, tt, :]
        st = self.ln_st
        mv = self.ln_mv
        k.op("dve", lambda: nc.vector.bn_stats(st[:, tt, 0:6], X[:, tt, 0:512]), reads=[X.s(tt)], writes=[st.s(tt)])
        k.op("dve", lambda: nc.vector.bn_stats(st[:, tt, 6:12], X[:, tt, 512:1024]), reads=[X.s(tt)], writes=[st.s(tt)])
        k.op("dve", lambda: nc.vector.bn_aggr(mv[:, tt, 0:2], st[:, tt, :].rearrange("p (a b) -> p a b", a=2)),
             reads=[st.s(tt)], writes=[mv.s(tt)])
        k.op("act", lambda: nc.scalar.activation(mv[:, tt, 2:3], mv[:, tt, 1:2], AF.Sqrt, bias=self.eps_ln[:, 0:1], scale=1.0),
             reads=[mv.s(tt), self.eps_ln], writes=[mv.s(tt)])
        k.op("dve", lambda: nc.vector.reciprocal(mv[:, tt, 2:3], mv[:, tt, 2:3]), reads=[mv.s(tt)], writes=[mv.s(tt)])
        k.op("dve", lambda: nc.vector.scalar_tensor_tensor(mv[:, tt, 3:4], mv[:, tt, 0:1], -1.0, mv[:, tt, 2:3], op0=ALU.mult, op1=ALU.mult),
             reads=[mv.s(tt)], writes=[mv.s(tt)])
        k.op("act", lambda: nc.scalar.activation(xt, xt, AF.Identity, bias=mv[:, tt, 3:4], scale=mv[:, tt, 2:3]),
             reads=[X.s(tt), mv.s(tt)], writes=[X.s(tt)])
        k.op("pool", lambda: nc.gpsimd.tensor_tensor(xt, xt, self.lng[:], op=ALU.mult), reads=[X.s(tt), self.lng], writes=[X.s(tt)])
        k.op("pool", lambda: nc.gpsimd.tensor_tensor(xt, xt, self.lnb[:], op=ALU.add), reads=[X.s(tt), self.lnb], writes=[X.s(tt)])
        self.make_xT(tt)

    def load_w(self, dst, dst_ap, src_ap, ncols, eng="pool", scale=None, dst_slots=None):
        k, nc = self.k, self.nc
        st = self.stage()
        w = [dst] if dst_slots is None else [dst.s(*dst_slots)]
        k.dma("sp", st[:, 0:ncols], src_ap, writes=[st])
        if scale is not None:
            k.op("dve", lambda: nc.vector.tensor_scalar(dst_ap, st[:, 0:ncols], scale[1], None, op0=ALU.mult), reads=[st, scale[0]], writes=w)
        elif eng == "pool":
            k.op("pool", lambda: nc.gpsimd.tensor_copy(dst_ap, st[:, 0:ncols]), reads=[st], writes=w)
        elif eng == "act":
            k.op("act", lambda: nc.scalar.copy(dst_ap, st[:, 0:ncols]), reads=[st], writes=w)
        else:
            k.op("dve", lambda: nc.vector.tensor_copy(dst_ap, st[:, 0:ncols]), reads=[st], writes=w)

    def rope(self, ps_n, ps_s, cos, sin, tb, out_t, out_ap, es_tmp):
        k, nc = self.k, self.nc
        t1, t2 = es_tmp
        sl = slice(tb * 512, (tb + 1) * 512)
        k.op("dve", lambda: nc.vector.tensor_tensor(t1[:], ps_n[:], cos[:, sl], op=ALU.mult), reads=[ps_n, cos], writes=[t1])
        k.op("dve", lambda: nc.vector.tensor_tensor(t2[:], ps_s[:], sin[:, sl], op=ALU.mult), reads=[ps_s, sin], writes=[t2])
        k.op("pool", lambda: nc.gpsimd.tensor_tensor(out_ap, t1[:], t2[:], op=ALU.add), reads=[t1, t2], writes=[out_t])

    def mla_layer(self):
        k, nc, I = self.k, self.nc, self.I
        xT = self.xT
        with ExitStack() as es:
            win = k.sbuf("m_win_b", [P, 8, 896], BF16, es=es)
            wuq = k.sbuf("m_wuq_b", [P, 3, 2048], BF16, es=es)
            wukv = k.sbuf("m_wukv_b", [P, 2, 2048], BF16, es=es)
            qn = k.sbuf("m_qn_s", [P, 3], F32, es=es)
            kvn = k.sbuf("m_kvn_s", [P, 2], F32, es=es)
            cos = k.sbuf("cos64_s", [P, S], F32, es=es)
            sin = k.sbuf("sin64_s", [P, S], F32, es=es)
            k.dma("sp", qn[:], I["m_qn"], writes=[qn])
            k.dma("sp", kvn[:], I["m_kvn"], writes=[kvn])
            k.dma("sp", cos[:], I["cos64"], writes=[cos])
            k.dma("sp", sin[:], I["sin64"], writes=[sin])
            for c in range(8):
                self.load_w(win, win[:, c, :], I["m_win"][c * P:(c + 1) * P, :], 896, eng=("pool", "act")[c % 2])
            for c in range(3):
                self.load_w(wuq, wuq[:, c, :], I["m_wuq"][c * P:(c + 1) * P, :], 2048, scale=(qn, qn[:, c:c + 1]))
            for c in range(2):
                self.load_w(wukv, wukv[:, c, :], I["m_wukv"][c * P:(c + 1) * P, :], 2048, scale=(kvn, kvn[:, c:c + 1]))
            lat_f = k.sbuf("lat_f", [P, 5, 512], F32, es=es)
            sq_b = k.sbuf("sq_b", [P, 5, 512], BF16, es=es)
            rs = k.sbuf("rs", [P, 2, 512], F32, es=es)
            lat_n = [k.sbuf(f"lat_n{i}", [P, 5, 512], BF16, es=es) for i in range(1)]
            t1 = k.sbuf("rp_t1", [P, 512], F32, es=es)
            t2 = k.sbuf("rp_t2", [P, 512], F32, es=es)
            ob = [k.sbuf(f"m_ob{i}", [P, 512], BF16, es=es) for i in range(4)]
            vb = [k.sbuf(f"m_vb{i}", [P, D], BF16, es=es) for i in range(2)]
            obi = 0
            for tb in range(4):
                tsl = slice(tb * 512, (tb + 1) * 512)
                tslots = tuple(range(tb * 4, tb * 4 + 4))
                ln = lat_n[0]
                for c in range(5):
                    ps = self.bank()
                    for dc in range(8):
                        k.op("pe", lambda dc=dc, c=c, ps=ps: nc.tensor.matmul(ps[:], win[:, dc, c * P:(c + 1) * P], xT[:, dc, tsl],
                                                                               start=(dc == 0), stop=(dc == 7)),
                             reads=[win, xT.s(*tslots)], writes=[ps], inc=(dc == 7))
                    k.op("act", lambda c=c, ps=ps: nc.scalar.copy(lat_f[:, c, :], ps[:]), reads=[ps], writes=[lat_f])
                    k.op("act", lambda c=c, ps=ps: nc.scalar.activation(sq_b[:, c, :], ps[:], AF.Square), reads=[ps], writes=[sq_b])
                for gi, (c0, nch, width) in enumerate(((0, 3, 384), (3, 2, 256))):
                    ps = self.bank()
                    for c in range(nch):
                        k.op("pe", lambda c=c, ps=ps, c0=c0, nch=nch: nc.tensor.matmul(ps[:], self.ones_b[:], sq_b[:, c0 + c, :],
                                                                                        start=(c == 0), stop=(c == nch - 1)),
                             reads=[self.ones_b, sq_b], writes=[ps], inc=(c == nch - 1))
                    k.op("act", lambda ps=ps, gi=gi, width=width: nc.scalar.activation(rs[:, gi, :], ps[:], AF.Sqrt, bias=self.eps_rms[:, 0:1],
                                                                                         scale=1.0 / width),
                         reads=[ps, self.eps_rms], writes=[rs])
                    k.op("dve", lambda gi=gi: nc.vector.reciprocal(rs[:, gi, :], rs[:, gi, :]), reads=[rs], writes=[rs])
                    k.op("dve", lambda gi=gi, c0=c0, nch=nch, ln=ln: nc.vector.tensor_tensor(
                        ln[:, c0:c0 + nch, :], lat_f[:, c0:c0 + nch, :], rs[:, gi, :].unsqueeze(1).to_broadcast([P, nch, 512]), op=ALU.mult),
                         reads=[lat_f, rs], writes=[ln])
                psn = self.bank()
                pss = self.bank()
                for (ps, c0) in ((psn, 640), (pss, 768)):
                    for dc in range(8):
                        k.op("pe", lambda dc=dc, ps=ps, c0=c0: nc.tensor.matmul(ps[:], win[:, dc, c0:c0 + P], xT[:, dc, tsl],
                                                                                 start=(dc == 0), stop=(dc == 7)),
                             reads=[win, xT.s(*tslots)], writes=[ps], inc=(dc == 7))
                o = ob[obi % 4]; obi += 1
                self.rope(psn, pss, cos, sin, tb, o, o[:], (t1, t2))
                k.dma("sp", self.krD[:, tsl], o[:], reads=[o], writes=[self.krD])
                for h in range(8):
                    ps = self.bank()
                    for rc in range(3):
                        k.op("pe", lambda rc=rc, ps=ps, h=h: nc.tensor.matmul(ps[:], wuq[:, rc, h * P:(h + 1) * P], ln[:, rc, :],
                                                                               start=(rc == 0), stop=(rc == 2)),
                             reads=[wuq, ln], writes=[ps], inc=(rc == 2))
                    o = ob[obi % 4]; obi += 1
                    k.op("act", lambda ps=ps, o=o: nc.scalar.copy(o[:], ps[:]), reads=[ps], writes=[o])
                    k.dma("sp", self.qnD[h, :, tsl], o[:], reads=[o], writes=[self.qnD.s(h)])
                for pr in range(4):
                    psn = self.bank()
                    pss = self.bank()
                    for (ps, c0) in ((psn, 1024 + pr * P), (pss, 1536 + pr * P)):
                        for rc in range(3):
                            k.op("pe", lambda rc=rc, ps=ps, c0=c0: nc.tensor.matmul(ps[:], wuq[:, rc, c0:c0 + P], ln[:, rc, :],
                                                                                     start=(rc == 0), stop=(rc == 2)),
                                 reads=[wuq, ln], writes=[ps], inc=(rc == 2))
                    o = ob[obi % 4]; obi += 1
                    self.rope(psn, pss, cos, sin, tb, o, o[:], (t1, t2))
                    k.dma("sp", self.qrD[2 * pr, :, tsl], o[0:64, :], reads=[o], writes=[self.qrD.s(2 * pr)])
                    k.dma("sp", self.qrD[2 * pr + 1, :, tsl], o[64:128, :], reads=[o], writes=[self.qrD.s(2 * pr + 1)])
                for h in range(8):
                    ps = self.bank()
                    for rc in range(2):
                        k.op("pe", lambda rc=rc, ps=ps, h=h: nc.tensor.matmul(ps[:], wukv[:, rc, h * P:(h + 1) * P], ln[:, 3 + rc, :],
                                                                               start=(rc == 0), stop=(rc == 1)),
                             reads=[wukv, ln], writes=[ps], inc=(rc == 1))
                    o = ob[obi % 4]; obi += 1
                    k.op("dve", lambda ps=ps, o=o: nc.vector.tensor_copy(o[:], ps[:]), reads=[ps], writes=[o])
                    k.dma("sp", self.knD[h, :, tsl], o[:], reads=[o], writes=[self.knD.s(h)])
                for t4 in range(4):
                    tt = tb * 4 + t4
                    v = vb[tt % 2]
                    for half in range(2):
                        ps = self.bank()
                        for rc in range(2):
                            k.op("pe", lambda rc=rc, ps=ps, half=half, t4=t4: nc.tensor.matmul(
                                ps[:], ln[:, 3 + rc, t4 * P:(t4 + 1) * P], wukv[:, rc, 1024 + half * 512:1024 + (half + 1) * 512],
                                start=(rc == 0), stop=(rc == 1)),
                                 reads=[wukv, ln], writes=[ps], inc=(rc == 1))
                        if half == 0:
                            k.op("act", lambda ps=ps, v=v: nc.scalar.copy(v[:, 0:512], ps[:]), reads=[ps], writes=[v])
                        else:
                            k.op("dve", lambda ps=ps, v=v: nc.vector.tensor_copy(v[:, 512:1024], ps[:]), reads=[ps], writes=[v])
                    k.dma("sp", self.VD[tt], v[:], reads=[v], writes=[self.VD.s(tt)])
            k.barrier()
        self.attention(mla=True)

    def attention(self, mla, selT=None, selT_t=None):
        k, nc = self.k, self.nc
        scale = (192.0 if mla else 128.0) ** -0.5
        self.att_es = ExitStack()
        es = self.att_es
        OT = k.sbuf("OT", [P, 8, S], BF16, nslots=8, es=es)
        self.OT = OT
        with ExitStack() as es2:
            kn = [Tn(self.stg[i][:].bitcast(BF16), 1, f"a_kn{i}") for i in range(2)]
            qn = [k.sbuf(f"a_qn{i}", [P, S], BF16, es=es2) for i in range(2)]
            vh = [k.sbuf(f"a_vh{i}", [P, NT, P], BF16, es=es2) for i in range(2)]
            pt = [k.sbuf(f"a_pt{i}", [P, 512], BF16, es=es2) for i in range(3)]
            rec = k.sbuf("a_rec", [P, 512], F32, es=es2)
            if mla:
                qr = [k.sbuf(f"a_qr{i}", [64, S], BF16, es=es2) for i in range(2)]
                kr = k.sbuf("a_kr", [64, S], BF16, es=es2)
                k.dma("sp", kr[:], self.krD[0:64, :], reads=[self.krD], writes=[kr])
            def load_head(h):
                b = h % 2
                k.dma("sp", kn[b][:, 0:S], self.knD[h], reads=[self.knD.s(h)], writes=[kn[b]])
                k.dma("sp", qn[b][:], self.qnD[h], reads=[self.qnD.s(h)], writes=[qn[b]])
                k.dma("sp", vh[b][:], self.VD[:, :, h * P:(h + 1) * P].rearrange("t p v -> p t v"), reads=[self.VD], writes=[vh[b]])
                if mla:
                    k.dma("sp", qr[b][:], self.qrD[h], reads=[self.qrD.s(h)], writes=[qr[b]])

            pairs = []
            for h in range(8):
                for QB in range(4):
                    for kc in range(4 * QB + 4):
                        pairs.append((h, QB, kc))
            info = {}

            def emit_qk(i):
                h, QB, kc = pairs[i]
                b = h % 2
                if QB == 0 and kc == 0 and h == 0:
                    load_head(0)
                qlo = max(kc, 4 * QB)
                c0 = (qlo - 4 * QB) * P
                qs = slice(QB * 512 + c0, (QB + 1) * 512)
                ks = slice(kc * P, (kc + 1) * P)
                st = self.pb[i % 3]
                k.op("pe", lambda: nc.tensor.matmul(st[:, c0:512], kn[b][:, ks], qn[b][:, qs], start=True, stop=(not mla)),
                     reads=[kn[b], qn[b]], writes=[st], inc=(not mla))
                if mla:
                    k.op("pe", lambda: nc.tensor.matmul(st[:, c0:512], kr[:, ks], qr[b][:, qs], start=False, stop=True),
                         reads=[kr, qr[b]], writes=[st])
                info[i] = (st, c0, qs)

            def emit_soft(i):
                h, QB, kc = pairs[i]
                st, c0, qs = info[i]
                p_ = pt[i % 3]
                k.op("act", lambda: nc.scalar.activation(p_[:, c0:512], st[:, c0:512], AF.Exp, scale=scale), reads=[st], writes=[p_])
                if mla:
                    if kc >= 4 * QB:
                        k.op("dve", lambda: nc.vector.tensor_tensor(p_[:, c0:c0 + P], p_[:, c0:c0 + P], self.maskT_b[:], op=ALU.mult),
                             reads=[p_, self.maskT_b], writes=[p_])
                else:
                    k.op("dve", lambda: nc.vector.tensor_tensor(p_[:, c0:512], p_[:, c0:512], selT(kc, qs.start, qs.stop), op=ALU.mult),
                         reads=[p_, selT_t], writes=[p_])

            def emit_pv(i):
                h, QB, kc = pairs[i]
                b = h % 2
                st, c0, qs = info.pop(i)
                p_ = pt[i % 3]
                oT = self.pb[3 + QB % 2]
                sm = self.pb[5 + QB % 2]
                last = 4 * QB + 3
                if QB == 0 and kc == 0 and h + 1 < 8:
                    load_head(h + 1)
                k.op("pe", lambda: nc.tensor.matmul(oT[:, c0:512], vh[b][:, kc, :], p_[:, c0:512], start=(kc == 0), stop=(kc == last), skip_group_check=True),
                     reads=[vh[b], p_], writes=[oT], inc=False)
                k.op("pe", lambda: nc.tensor.matmul(sm[:, c0:512], self.ones_b[:], p_[:, c0:512], start=(kc == 0), stop=(kc == last), skip_group_check=True),
                     reads=[self.ones_b, p_], writes=[sm])
                if kc == last:
                    k.op("dve", lambda: nc.vector.reciprocal(rec[:], sm[:]), reads=[sm], writes=[rec])
                    k.op("dve", lambda: nc.vector.tensor_tensor(OT[:, h, QB * 512:(QB + 1) * 512], oT[:], rec[:], op=ALU.mult),
                         reads=[oT, rec], writes=[OT.s(h)])

            npairs = len(pairs)
            LA = 2
            for i in range(min(LA, npairs)):
                emit_qk(i)
            for i in range(npairs):
                emit_soft(i)
                if i + LA < npairs:
                    emit_qk(i + LA)
                emit_pv(i)
            k.barrier()

    def mixer_out_ln(self, wo_ap, li):
        k, nc = self.k, self.nc
        OT = self.OT
        with ExitStack() as es:
            wo = k.sbuf("wo_b", [P, 8, 512], BF16, es=es)
            self.ln_params(li, es)
            for half in range(2):
                for c in range(8):
                    self.load_w(wo, wo[:, c, :], wo_ap[c * P:(c + 1) * P, half * 512:(half + 1) * 512], 512, eng=("pool", "act")[c % 2])
                for tt in range(NT):
                    ps = self.bank()
                    for h in range(8):
                        k.op("pe", lambda ps=ps, h=h, tt=tt: nc.tensor.matmul(
                            ps[:], OT[:, h, tt * P:(tt + 1) * P], wo[:, h, :], start=(h == 0), stop=(h == 7)),
                             reads=[OT, wo], writes=[ps], inc=(h == 7))
                    xs = self.X[:, tt, half * 512:(half + 1) * 512]
                    k.op("dve", lambda ps=ps, xs=xs: nc.vector.scalar_tensor_tensor(xs, xs, ALPHA, ps[:], op0=ALU.mult, op1=ALU.add),
                         reads=[ps, self.X.s(tt)], writes=[self.X.s(tt)])
            for tt in range(NT):
                self.ln_tile(tt, li)
            k.barrier()
        self.att_es.close()

    def dsa_layer(self):
        k, nc, I = self.k, self.nc, self.I
        xT = self.xT
        WI = I["d_win"].rearrange("(c p) n -> p c n", p=P)
        self.dsa_es = ExitStack()
        esD = self.dsa_es
        selT = k.sbuf("d_selT", [P, 136 * P], BF16, es=esD)
        wtok = k.sbuf("d_wtok", [P, NT, 8], F32, es=esD)

        def soff(kc):
            return P * (16 * kc - kc * (kc - 1) // 2)

        with ExitStack() as es:
            wb = [k.sbuf(f"d_wb{i}", [P, 8, 512], BF16, es=es) for i in range(2)]
            t1 = k.sbuf("d_t1", [P, 512], F32, es=es)
            t2 = k.sbuf("d_t2", [P, 512], F32, es=es)
            ob = [k.sbuf(f"d_ob{i}", [P, 512], BF16, es=es) for i in range(4)]
            cos = k.sbuf("d_cos", [P, S], F32, es=es)
            sin = k.sbuf("d_sin", [P, S], F32, es=es)
            gi = 0
            obi = 0

            def load_group(c0, ncols):
                nonlocal gi
                wb_ = wb[gi % 2]
                gi += 1
                for hf in range(2):
                    st_ = self.stage()
                    sv = st_[:].rearrange("p (c n) -> p c n", c=4)
                    k.dma("sp", sv[:, :, 0:ncols], WI[:, hf * 4:(hf + 1) * 4, c0:c0 + ncols], writes=[st_])
                    if hf == 0:
                        k.op("act", lambda sv=sv, hf=hf: nc.scalar.copy(wb_[:, hf * 4:(hf + 1) * 4, 0:ncols], sv[:, :, 0:ncols]), reads=[st_], writes=[wb_])
                    else:
                        k.op("pool", lambda sv=sv, hf=hf: nc.gpsimd.tensor_copy(wb_[:, hf * 4:(hf + 1) * 4, 0:ncols], sv[:, :, 0:ncols]), reads=[st_], writes=[wb_])
                return wb_

            def rope_pairs(c0, npairs, dst_fn):
                nonlocal obi
                wb_ = load_group(c0, npairs * 256)
                for tb in range(4):
                    tsl = slice(tb * 512, (tb + 1) * 512)
                    tslots = tuple(range(tb * 4, tb * 4 + 4))
                    for pi in range(npairs):
                        psn = self.bank()
                        pss = self.bank()
                        for (ps, cc) in ((psn, pi * 256), (pss, pi * 256 + P)):
                            for dc in range(8):
                                k.op("pe", lambda dc=dc, ps=ps, cc=cc: nc.tensor.matmul(ps[:], wb_[:, dc, cc:cc + P], xT[:, dc, tsl],
                                                                                         start=(dc == 0), stop=(dc == 7)),
                                     reads=[wb_, xT.s(*tslots)], writes=[ps], inc=(dc == 7))
                        o = ob[obi % 4]; obi += 1
                        self.rope(psn, pss, cos, sin, tb, o, o[:], (t1, t2))
                        dt_, dap = dst_fn(pi, tsl)
                        k.dma("sp", dap, o[:], reads=[o], writes=[dt_])

            k.dma("sp", cos[:], I["cos128"], writes=[cos])
            k.dma("sp", sin[:], I["sin128"], writes=[sin])
            for g in range(4):
                rope_pairs(g * 512, 2, lambda pi, tsl, g=g: (self.qnD.s(2 * g + pi), self.qnD[2 * g + pi, :, tsl]))
            for g in range(4):
                rope_pairs(2048 + g * 512, 2, lambda pi, tsl, g=g: (self.knD.s(2 * g + pi), self.knD[2 * g + pi, :, tsl]))
            k.dma("sp", cos[:], I["cos64"], writes=[cos])
            k.dma("sp", sin[:], I["sin64"], writes=[sin])
            for g in range(2):
                rope_pairs(4096 + g * 512, 2, lambda pi, tsl, g=g: (self.qiD.s(2 * g + pi), self.qiD[2 * g + pi, :, tsl]))
            rope_pairs(5120, 1, lambda pi, tsl: (self.kiD, self.kiD[:, tsl]))
            vb = [k.sbuf(f"d_vb{i}", [P, 512], BF16, es=es) for i in range(2)]
            vi = 0
            for half in range(2):
                wb_ = load_group(5376 + half * 512, 512)
                for tt in range(NT):
                    ps = self.bank()
                    for dc in range(8):
                        k.op("pe", lambda dc=dc, ps=ps, tt=tt: nc.tensor.matmul(ps[:], xT[:, dc, tt * P:(tt + 1) * P], wb_[:, dc, :],
                                                                                 start=(dc == 0), stop=(dc == 7)),
                             reads=[wb_, xT.s(tt)], writes=[ps], inc=(dc == 7))
                    v = vb[vi % 2]; vi += 1
                    if tt % 2 == 0:
                        k.op("act", lambda ps=ps, v=v: nc.scalar.copy(v[:], ps[:]), reads=[ps], writes=[v])
                    else:
                        k.op("dve", lambda ps=ps, v=v: nc.vector.tensor_copy(v[:], ps[:]), reads=[ps], writes=[v])
                    k.dma("sp", self.VD[tt, :, half * 512:(half + 1) * 512], v[:], reads=[v], writes=[self.VD.s(tt)])
            wb_ = load_group(6400, 8)
            wscale = float(8 ** -0.5 * 64 ** -0.5)
            for tt in range(NT):
                ps = self.bank()
                for dc in range(8):
                    k.op("pe", lambda dc=dc, ps=ps, tt=tt: nc.tensor.matmul(ps[:, 0:8], xT[:, dc, tt * P:(tt + 1) * P], wb_[:, dc, 0:8],
                                                                             start=(dc == 0), stop=(dc == 7)),
                         reads=[wb_, xT.s(tt)], writes=[ps], inc=(dc == 7))
                k.op("act", lambda ps=ps, tt=tt: nc.scalar.mul(wtok[:, tt, :], ps[:, 0:8], wscale), reads=[ps], writes=[wtok])
            k.barrier()
        with ExitStack() as es:
            kiT = k.sbuf("d_kiT", [P, S], BF16, es=es)
            qiT = k.sbuf("d_qiT", [P, 4, S], BF16, es=es)
            accs = [k.sbuf(f"d_acc{i}", [P, S], F32, es=es) for i in range(2)]
            rls = [k.sbuf(f"d_rl{i}", [P, 512], F32, es=es) for i in range(2)]
            sels = [k.sbuf(f"d_sel{i}", [P, S], BF16, es=es) for i in range(1)]
            mxs = [k.sbuf(f"d_mx{i}", [P, 8], F32, es=es) for i in range(4)]
            k.dma("sp", kiT[:], self.kiD[:], reads=[self.kiD], writes=[kiT])
            for pr in range(4):
                k.dma("sp", qiT[:, pr, :], self.qiD[pr], reads=[self.qiD.s(pr)], writes=[qiT])
            rli = 0
            tbi = 0
            sel2 = k.sbuf("d_sel2", [P, S], BF16, es=es)
            bss = [k.sbuf(f"d_bs{i}", [P, 8], F32, es=es) for i in range(2)]

            def q_chain(qi):
                nonlocal rli, tbi
                n = (qi + 1) * P
                acc = accs[qi % 2]
                mxs_ = mxs[2 * (qi % 2):2 * (qi % 2) + 2]
                qsl = slice(qi * P, (qi + 1) * P)
                for h in range(8):
                    pr, hp = h // 2, h % 2
                    prt = slice(hp * 64, (hp + 1) * 64)
                    for k0 in range(0, n, 512):
                        kw = min(512, n - k0)
                        ps = self.bank()
                        k.op("pe", lambda ps=ps, kw=kw, k0=k0, pr=pr, prt=prt, qsl=qsl: nc.tensor.matmul(
                            ps[:, 0:kw], qiT[prt, pr, qsl], kiT[prt, k0:k0 + kw], start=True, stop=True),
                             reads=[qiT, kiT], writes=[ps])
                        rl = rls[rli % 2]; rli += 1
                        k.op("act", lambda ps=ps, rl=rl, kw=kw: nc.scalar.activation(rl[:, 0:kw], ps[:, 0:kw], AF.Relu), reads=[ps], writes=[rl])
                        if h == 0:
                            k.op("dve", lambda rl=rl, kw=kw, k0=k0, acc=acc, qi=qi: nc.vector.tensor_scalar(
                                acc[:, k0:k0 + kw], rl[:, 0:kw], wtok[:, qi, 0:1], None, op0=ALU.mult), reads=[rl, wtok], writes=[acc])
                        else:
                            k.op("dve", lambda rl=rl, kw=kw, k0=k0, acc=acc, qi=qi, h=h: nc.vector.scalar_tensor_tensor(
                                acc[:, k0:k0 + kw], rl[:, 0:kw], wtok[:, qi, h:h + 1], acc[:, k0:k0 + kw], op0=ALU.mult, op1=ALU.add),
                                 reads=[rl, wtok, acc], writes=[acc])
                        yield
                sel = sels[0] if qi % 2 == 0 else sel2
                if qi >= 2:
                    bs = bss[qi % 2]
                    AXX_ = mybir.AxisListType.X
                    k.op("dve", lambda: nc.vector.tensor_reduce(bs[:, 0:1], acc[:, 0:n], AXX_, ALU.max), reads=[acc], writes=[bs])
                    yield
                    k.op("dve", lambda: nc.vector.tensor_reduce(bs[:, 1:2], acc[:, 0:n], AXX_, ALU.min), reads=[acc], writes=[bs])
                    yield
                    k.op("dve", lambda: nc.vector.scalar_tensor_tensor(bs[:, 2:3], bs[:, 0:1], 1.0, bs[:, 1:2], op0=ALU.mult, op1=ALU.subtract),
                         reads=[bs], writes=[bs])
                    yield
                    k.op("dve", lambda: nc.vector.tensor_scalar(bs[:, 2:3], bs[:, 2:3], 1.0009765625, 1e-20, op0=ALU.mult, op1=ALU.add),
                         reads=[bs], writes=[bs])
                    yield
                k.op("pool", lambda acc=acc, n=n: nc.gpsimd.tensor_tensor(acc[:, n - P:n], acc[:, n - P:n], self.negm[:], op=ALU.add),
                     reads=[acc, self.negm], writes=[acc])
                if qi >= 2:
                    bs = bss[qi % 2]
                    NIT = 20
                    k.op("dve", lambda: nc.vector.scalar_tensor_tensor(bs[:, 3:4], bs[:, 2:3], 0.5, bs[:, 1:2], op0=ALU.mult, op1=ALU.add),
                         reads=[bs], writes=[bs])
                    yield
                    for it in range(1, NIT + 1):
                        step = 0.5 ** it
                        k.op("dve", lambda: nc.vector.tensor_scalar(sel[:, 0:n], acc[:, 0:n], bs[:, 3:4], 0.0, op0=ALU.is_ge, op1=ALU.add,
                                                                    accum_out=bs[:, 4:5]),
                             reads=[acc, bs], writes=[sel, bs])
                        yield
                        k.op("dve", lambda step=step: nc.vector.tensor_scalar(bs[:, 5:6], bs[:, 4:5], 256.0, step, op0=ALU.is_ge, op1=ALU.mult),
                             reads=[bs], writes=[bs])
                        yield
                        k.op("dve", lambda: nc.vector.scalar_tensor_tensor(bs[:, 1:2], bs[:, 5:6], bs[:, 2:3], bs[:, 1:2], op0=ALU.mult, op1=ALU.add),
                             reads=[bs], writes=[bs])
                        yield
                        if it < NIT:
                            k.op("dve", lambda step=step: nc.vector.scalar_tensor_tensor(bs[:, 3:4], bs[:, 2:3], step * 0.5, bs[:, 1:2], op0=ALU.mult, op1=ALU.add),
                                 reads=[bs], writes=[bs])
                            yield
                    k.op("dve", lambda: nc.vector.tensor_scalar(sel[:, 0:n], acc[:, 0:n], bs[:, 1:2], None, op0=ALU.is_ge),
                         reads=[acc, bs], writes=[sel])
                    yield
                else:
                    thr_t, thr = self.thr_c, self.thr_c[:, 0:1]
                    k.op("dve", lambda sel=sel, acc=acc, n=n, thr=thr: nc.vector.tensor_scalar(sel[:, 0:n], acc[:, 0:n], thr, None, op0=ALU.is_ge),
                         reads=[acc, thr_t], writes=[sel])
                    yield
                for kc in range(qi + 1):
                    blk = tbi % 8; tbi += 1
                    k.op("pe", lambda sel=sel, kc=kc, blk=blk: nc.tensor.transpose(self.ptb[:, blk * P:(blk + 1) * P], sel[:, kc * P:(kc + 1) * P], self.ident_b[:]),
                         reads=[sel, self.ident_b], writes=[self.ptb])
                    o_ = soff(kc) + (qi - kc) * P
                    k.op("act", lambda blk=blk, o_=o_: nc.scalar.copy(selT[:, o_:o_ + P], self.ptb[:, blk * P:(blk + 1) * P]),
                         reads=[self.ptb], writes=[selT])

            for pa in range(8):
                gens = [q_chain(2 * pa + 1), q_chain(2 * pa)]
                while gens:
                    for g_ in list(gens):
                        try:
                            next(g_)
                        except StopIteration:
                            gens.remove(g_)
            self.dbg("bs1", bss[1], [P, 8])
            self.dbg("acc1", accs[1], [P, S])
            self.dbg("sel1", sel2, [P, S], BF16)
            k.barrier()
        self.dbg("selT", selT, [P, 136 * P], BF16)
        self.attention(mla=False, selT=lambda kc, q0, q1: selT[:, soff(kc) + q0 - kc * P: soff(kc) + q1 - kc * P], selT_t=selT)

    def peer_layer(self, l, li):
        k, nc, I = self.k, self.nc, self.I
        xT, X = self.xT, self.X
        AXX = mybir.AxisListType.X
        if not hasattr(self, "gateD"):
            self.gateD = k.dram("gateD", [P, P, S], BF16, nslots=P)
        gateD = self.gateD
        with ExitStack() as esL:
            LT = k.sbuf("p_LT", [P, 3, S], BF16, nslots=NT, es=esL)
            with ExitStack() as es:
                wq = k.sbuf("p_wq_b", [P, 8, 2048], BF16, es=es)
                skT = k.sbuf("p_skT_b", [P, 16, P], BF16, es=es)
                for c in range(8):
                    self.load_w(wq, wq[:, c, :], I[f"p_wq{l}"][c * P:(c + 1) * P, :], 2048, eng=("dve", "act")[c % 2])
                st = self.stage()
                k.dma("sp", st[:].rearrange("p (g n) -> p g n", g=16), I[f"p_skT{l}"].rearrange("g d n -> d g n"), writes=[st])
                k.op("dve", lambda: nc.vector.tensor_copy(skT[:].rearrange("p g n -> p (g n)"), st[:]), reads=[st], writes=[skT])
                qTbs = [k.sbuf(f"p_qTb{z}", [P, 16, 256], BF16, es=es) for z in range(2)]
                s_sbs = [Tn(self.stg[z].h, 16, f"p_s{z}") for z in range(2)]
                v16 = k.sbuf("p_v16", [P, 256], F32, nslots=16, es=es)
                i16 = k.sbuf("p_i16", [P, 256], U32, nslots=16, es=es)
                i16f = k.sbuf("p_i16f", [P, 256], F32, es=es)
                best = k.sbuf("p_best", [P, 128], F32, nslots=8, es=es)
                pos = k.sbuf("p_pos", [P, 128], U32, nslots=8, es=es)
                pab = k.sbuf("p_pab", [P, 2, 128], U32, es=es)
                pabf = k.sbuf("p_pabf", [P, 2, 128], F32, es=es)
                sm8 = k.sbuf("p_sm8", [P, 3, 8], F32, es=es)
                ex = k.sbuf("p_ex", [P, 128], F32, es=es)
                Lt = k.sbuf("p_Lt", [P, 3, 128], F32, es=es)
                k.barrier()

                def selfsync():
                    if k.cnt["dve"] > 0:
                        k.wait("dve", (k.sem["dve"], k.cnt["dve"]))

                def tile_scores(tt, t2, qTb):
                    s_sb = s_sbs[tt % 2]
                    for bq in range(4):
                        ps = self.bank()
                        for gg in range(4):
                            g = bq * 4 + gg
                            k.op("pe", lambda ps=ps, g=g, gg=gg: nc.tensor.matmul(
                                ps[:, gg * P:(gg + 1) * P], qTb[:, g, t2 * P:(t2 + 1) * P], skT[:, g, :], start=True, stop=True),
                                 reads=[qTb, skT], writes=[ps], inc=(gg == 3))
                        k.op("act", lambda ps=ps, bq=bq: nc.scalar.copy(s_sb[:, bq * 512:(bq + 1) * 512], ps[:]),
                             reads=[ps], writes=[s_sb.s(*range(bq * 4, bq * 4 + 4))])

                def tile_chain(tt, t2, qTb):
                    s_sb = s_sbs[tt % 2]
                    cand = s_sb
                    work = s_sb
                    G16 = range(16)
                    sg = lambda g: s_sb[:, g * P:(g + 1) * P]
                    va = lambda g: v16[:, g * 16:g * 16 + 8]
                    vb_ = lambda g: v16[:, g * 16 + 8:g * 16 + 16]
                    wk = lambda g: work[:, g * P:(g + 1) * P]
                    for g in G16:
                        k.op("dve", lambda g=g: nc.vector.max(out=va(g), in_=sg(g)), reads=[s_sb.s(g)], writes=[v16.s(g)])
                    selfsync()
                    for g in G16:
                        k.op("dve", lambda g=g: nc.vector.max_index(i16[:, g * 16:g * 16 + 8], va(g), sg(g)), reads=[s_sb.s(g), v16.s(g)], writes=[i16.s(g)])
                    for g in G16:
                        k.op("dve", lambda g=g: nc.vector.match_replace(out=sg(g), in_to_replace=va(g), in_values=sg(g), imm_value=NEG),
                             reads=[s_sb.s(g), v16.s(g)], writes=[s_sb.s(g)])
                    selfsync()
                    for g in G16:
                        k.op("dve", lambda g=g: nc.vector.max(out=vb_(g), in_=sg(g)), reads=[s_sb.s(g)], writes=[v16.s(g)])
                    selfsync()
                    for g in G16:
                        k.op("dve", lambda g=g: nc.vector.max_index(i16[:, g * 16 + 8:g * 16 + 16], vb_(g), sg(g)), reads=[s_sb.s(g), v16.s(g)], writes=[i16.s(g)])
                    k.op("dve", lambda: nc.vector.tensor_copy(i16f[:], i16[:]), reads=[i16], writes=[i16f])
                    v4 = v16[:].rearrange("p (h c a) -> p h c a", h=8, c=2)
                    i4 = i16f[:].rearrange("p (h c a) -> p h c a", h=8, c=2)
                    c4 = cand[:].rearrange("p (h a b) -> p h a b", h=8, a=16)
                    k.op("dve", lambda: nc.vector.tensor_tensor(c4, v4[:, :, 0, :].unsqueeze(3).to_broadcast([P, 8, 16, 16]),
                                                                v4[:, :, 1, :].unsqueeze(2).to_broadcast([P, 8, 16, 16]), op=ALU.add),
                         reads=[v16], writes=[cand])
                    H8 = range(8)
                    ch = lambda h: cand[:, h * 256:(h + 1) * 256]
                    cs = lambda h: cand.s(2 * h, 2 * h + 1)
                    ws = lambda h: work.s(2 * h, 2 * h + 1)
                    wh = lambda h: work[:, h * 256:(h + 1) * 256]
                    ba = lambda h: best[:, h * 16:h * 16 + 8]
                    bb = lambda h: best[:, h * 16 + 8:h * 16 + 16]
                    selfsync()
                    for h in H8:
                        k.op("dve", lambda h=h: nc.vector.max(out=ba(h), in_=ch(h)), reads=[cs(h)], writes=[best.s(h)])
                    selfsync()
                    for h in H8:
                        k.op("dve", lambda h=h: nc.vector.max_index(pos[:, h * 16:h * 16 + 8], ba(h), ch(h)), reads=[cs(h), best.s(h)], writes=[pos.s(h)])
                    for h in H8:
                        k.op("dve", lambda h=h: nc.vector.match_replace(out=ch(h), in_to_replace=ba(h), in_values=ch(h), imm_value=NEG),
                             reads=[cs(h), best.s(h)], writes=[cs(h)])
                    selfsync()
                    for h in H8:
                        k.op("dve", lambda h=h: nc.vector.max(out=bb(h), in_=ch(h)), reads=[cs(h)], writes=[best.s(h)])
                    selfsync()
                    for h in H8:
                        k.op("dve", lambda h=h: nc.vector.max_index(pos[:, h * 16 + 8:h * 16 + 16], bb(h), ch(h)), reads=[cs(h), best.s(h)], writes=[pos.s(h)])
                    b3 = best[:].rearrange("p (h r) -> p h r", h=8)
                    e3 = ex[:].rearrange("p (h r) -> p h r", h=8)
                    k.op("dve", lambda: nc.vector.tensor_tensor(e3, b3, b3[:, :, 0:1].to_broadcast([P, 8, 16]), op=ALU.subtract),
                         reads=[best], writes=[ex])
                    k.op("act", lambda: nc.scalar.activation(ex[:], ex[:], AF.Exp), reads=[ex], writes=[ex])
                    k.op("dve", lambda: nc.vector.tensor_single_scalar(pab[:, 0, :], pos[:], 4, op=ALU.logical_shift_right), reads=[pos], writes=[pab])
                    k.op("dve", lambda: nc.vector.tensor_single_scalar(pab[:, 1, :], pos[:], 15, op=ALU.bitwise_and), reads=[pos], writes=[pab])
                    k.op("dve", lambda: nc.vector.tensor_copy(pabf[:], pab[:]), reads=[pab], writes=[pabf])
                    k.op("dve", lambda: nc.vector.reduce_sum(sm8[:, 0, :], e3, axis=AXX), reads=[ex], writes=[sm8])
                    k.op("dve", lambda: nc.vector.reciprocal(sm8[:, 1, :], sm8[:, 0, :]), reads=[sm8], writes=[sm8])
                    k.op("dve", lambda: nc.vector.tensor_tensor(Lt[:, 0, :].rearrange("p (h r) -> p h r", h=8), e3,
                                                                sm8[:, 1, :].unsqueeze(2).to_broadcast([P, 8, 16]), op=ALU.mult),
                         reads=[ex, sm8], writes=[Lt])
                    for w_ in range(2):
                        abf = pabf[:, w_, :].rearrange("p (h r) -> p h r", h=8)
                        k.op("dve", lambda abf=abf: nc.vector.tensor_tensor(
                            c4, self.iota_f[:, 0:16].unsqueeze(1).unsqueeze(1).to_broadcast([P, 8, 16, 16]),
                            abf.unsqueeze(3).to_broadcast([P, 8, 16, 16]), op=ALU.is_equal),
                             reads=[self.iota_f, pabf], writes=[cand])
                        k.op("dve", lambda w_=w_: nc.vector.tensor_tensor(c4, c4, i4[:, :, w_, :].unsqueeze(2).to_broadcast([P, 8, 16, 16]), op=ALU.mult),
                             reads=[i16f, cand], writes=[cand])
                        k.op("dve", lambda w_=w_: nc.vector.reduce_sum(Lt[:, 1 + w_, :].rearrange("p (h r) -> p h r", h=8), c4, axis=AXX),
                             reads=[cand], writes=[Lt])
                    ps = self.bank()
                    for q3 in range(3):
                        k.op("pe", lambda q3=q3, ps=ps: nc.tensor.transpose(ps[:, q3 * P:(q3 + 1) * P], Lt[:, q3, :], self.ident_f[:]),
                             reads=[Lt, self.ident_f], writes=[ps], inc=(q3 == 2))
                    k.op("act", lambda ps=ps, tt=tt: nc.scalar.copy(LT[:, :, tt * P:(tt + 1) * P], ps[:, 0:384].rearrange("p (q t) -> p q t", q=3)),
                         reads=[ps], writes=[LT.s(tt)])

                def qproj(tb):
                    tsl = slice(tb * 256, (tb + 1) * 256)
                    tslots = (2 * tb, 2 * tb + 1)
                    qTb = qTbs[tb % 2]
                    for g in range(16):
                        ps = self.bank()
                        for dc in range(8):
                            k.op("pe", lambda dc=dc, g=g, ps=ps: nc.tensor.matmul(ps[:, 0:256], wq[:, dc, g * P:(g + 1) * P], xT[:, dc, tsl],
                                                                                   start=(dc == 0), stop=(dc == 7)),
                                 reads=[wq, xT.s(*tslots)], writes=[ps], inc=(dc == 7))
                        k.op("act", lambda g=g, ps=ps, qTb=qTb: nc.scalar.copy(qTb[:, g, :], ps[:, 0:256]), reads=[ps], writes=[qTb])

                qproj(0)
                tile_scores(0, 0, qTbs[0])
                for tt in range(NT):
                    tb, t2 = tt // 2, tt % 2
                    if t2 == 0 and tb + 1 < 8:
                        qproj(tb + 1)
                    if tt + 1 < NT:
                        tile_scores(tt + 1, (tt + 1) % 2, qTbs[((tt + 1) // 2) % 2])
                    tile_chain(tt, t2, qTbs[tb % 2])
                k.barrier()
            self.dbg("LT", LT, [P, 3, S], BF16)
            with ExitStack() as es:
                TBK = 8
                Ap = [k.sbuf(f"p_Ap{i}", [P, TBK, P], BF16, es=es) for i in range(2)]
                Bp = [k.sbuf(f"p_Bp{i}", [P, TBK, P], BF16, es=es) for i in range(2)]
                gt = [k.sbuf(f"p_gt{i}", [P, P, P], BF16, es=es) for i in range(2)]
                ev_i = 0
                for tt in range(NT):
                    g_ = gt[tt % 2]
                    for sub in range(P // TBK):
                        t0 = tt * P + sub * TBK
                        A_ = Ap[sub % 2]
                        B_ = Bp[sub % 2]
                        for tl_ in range(TBK):
                            t_g = t0 + tl_
                            k.op("dve", lambda A_=A_, tl_=tl_, t_g=t_g: nc.vector.tensor_scalar(
                                A_[:, tl_, :], self.iota_b[:], LT[:, 1, t_g:t_g + 1], LT[:, 0, t_g:t_g + 1], op0=ALU.is_equal, op1=ALU.mult),
                                 reads=[self.iota_b, LT.s(tt)], writes=[A_])
                            k.op("dve", lambda B_=B_, tl_=tl_, t_g=t_g: nc.vector.tensor_scalar(
                                B_[:, tl_, :], self.iota_b[:], LT[:, 2, t_g:t_g + 1], None, op0=ALU.is_equal),
                                 reads=[self.iota_b, LT.s(tt)], writes=[B_])
                        for q4 in range(TBK // 4):
                            ps = self.bank()
                            for tl in range(4):
                                t_ = q4 * 4 + tl
                                k.op("pe", lambda ps=ps, tl=tl, t_=t_, A_=A_, B_=B_: nc.tensor.matmul(
                                    ps[:, tl * P:(tl + 1) * P], B_[:, t_, :], A_[:, t_, :], start=True, stop=True),
                                     reads=[A_, B_], writes=[ps], inc=(tl == 3))
                            c_ = sub * TBK + q4 * 4
                            dst = g_[:, :, c_:c_ + 4]
                            src = ps[:].rearrange("p (t i) -> p i t", t=4)
                            k.op("act", lambda dst=dst, src=src: nc.scalar.copy(dst, src), reads=[ps], writes=[g_])
                            ev_i += 1
                    for i8 in range(8):
                        k.dma("sp", gateD[i8 * 16:(i8 + 1) * 16, :, tt * P:(tt + 1) * P].rearrange("i j t -> j i t"),
                              g_[:, i8 * 16:(i8 + 1) * 16, :], reads=[g_], writes=[gateD.s(*range(i8 * 16, (i8 + 1) * 16))])
                k.barrier()
        with ExitStack() as es:
            G = 4
            hg = [k.sbuf(f"p_hg{i}", [P, S], BF16, es=es) for i in range(2 * G)]
            wub = [k.sbuf(f"p_wub{i}", [P, D], BF16, es=es) for i in range(2 * G)]
            wdb = [k.sbuf(f"p_wdb{i}", [P, D], BF16, es=es) for i in range(2)]
            wst = [k.sbuf(f"p_wst{i}", [P, D], F32, es=es) for i in range(4)]
            ge = [k.sbuf(f"p_ge{i}", [P, 512], BF16, es=es) for i in range(3)]
            for tt in range(NT):
                k.op("pool", lambda tt=tt: nc.gpsimd.tensor_scalar(X[:, tt, :], X[:, tt, :], ALPHA, None, op0=ALU.mult),
                     reads=[X.s(tt)], writes=[X.s(tt)])
            gei = 0
            abank = 0
            ybank = 0
            wsi = 0
            for i in range(P):
                slot = i % (2 * G)
                h_ = hg[slot]
                wu_ = wub[slot]
                wd_ = wdb[i % 2]
                s1 = wst[wsi % 4]; wsi += 1
                s2 = wst[wsi % 4]; wsi += 1
                k.dma("sp", s1[:], I[f"p_wdT{l}"][i], writes=[s1])
                k.dma("sp", s2[:], I[f"p_wu{l}"][i * P:(i + 1) * P, :], writes=[s2])
                k.dma("sp", h_[:], gateD[i], reads=[gateD.s(i)], writes=[h_])
                k.op("pool", lambda wd_=wd_, s1=s1: nc.gpsimd.tensor_copy(wd_[:], s1[:]), reads=[s1], writes=[wd_])
                k.op("pool", lambda wu_=wu_, s2=s2: nc.gpsimd.tensor_copy(wu_[:], s2[:]), reads=[s2], writes=[wu_])
                for nb in range(4):
                    ps = self.pb[abank % 4]; abank += 1
                    for dc in range(8):
                        k.op("pe", lambda ps=ps, dc=dc, nb=nb, wd_=wd_: nc.tensor.matmul(
                            ps[:], wd_[:, dc * P:(dc + 1) * P], xT[:, dc, nb * 512:(nb + 1) * 512], start=(dc == 0), stop=(dc == 7)),
                             reads=[wd_, xT.s(*range(nb * 4, nb * 4 + 4))], writes=[ps], inc=(dc == 7))
                    g_ = ge[gei % 3]; gei += 1
                    k.op("act", lambda ps=ps, g_=g_: nc.scalar.activation(g_[:], ps[:], AF.Gelu), reads=[ps], writes=[g_])
                    k.op("dve", lambda g_=g_, h_=h_, nb=nb: nc.vector.tensor_tensor(
                        h_[:, nb * 512:(nb + 1) * 512], h_[:, nb * 512:(nb + 1) * 512], g_[:], op=ALU.mult),
                         reads=[g_, h_], writes=[h_])
                if i % G == G - 1:
                    base = slot - (G - 1)
                    for tt in range(NT):
                        for half in range(2):
                            ps = self.pb[4 + ybank % 3]; ybank += 1
                            for gg in range(G):
                                k.op("pe", lambda ps=ps, gg=gg, tt=tt, half=half, base=base: nc.tensor.matmul(
                                    ps[:], hg[base + gg][:, tt * P:(tt + 1) * P], wub[base + gg][:, half * 512:(half + 1) * 512],
                                    start=(gg == 0), stop=(gg == G - 1)),
                                     reads=[hg[base + gg], wub[base + gg]], writes=[ps], inc=(gg == G - 1))
                            xs = X[:, tt, half * 512:(half + 1) * 512]
                            k.op("dve", lambda ps=ps, xs=xs: nc.vector.tensor_tensor(xs, xs, ps[:], op=ALU.add),
                                 reads=[ps, X.s(tt)], writes=[X.s(tt)])
            self.ln_params(li, es)
            for tt in range(NT):
                self.ln_tile(tt, li)
            k.barrier()

def _rope_tab(dim):
    inv = (np.float32(10000.0) ** (-np.arange(0, dim, 2, dtype=np.float32) / np.float32(dim))).astype(np.float32)
    ang = (np.arange(S, dtype=np.float32)[:, None] * inv[None, :]).astype(np.float32)
    c = np.cos(ang).astype(np.float32).T
    s = np.sin(ang).astype(np.float32).T
    half = dim // 2
    reps = P // dim
    cos = np.concatenate([c, c] * reps, axis=0)
    sin = np.concatenate([-s, s] * reps, axis=0)
    return np.ascontiguousarray(cos), np.ascontiguousarray(sin)


def prep_shared(inp):
    f = np.float32
    sh = {}
    sh["ident"] = np.eye(P, dtype=f)
    sh["cos64"], sh["sin64"] = _rope_tab(64)
    sh["cos128"], sh["sin128"] = _rope_tab(128)
    kk = np.arange(P)[:, None]
    qq = np.arange(P)[None, :]
    sh["maskT"] = np.where((kk >= 64) & (qq < 64), 0.0, 1.0).astype(f)
    sh["negm"] = np.where((kk < 64) & (qq >= 64), NEG, 0.0).astype(f)
    sh["iota"] = np.broadcast_to(np.arange(P, dtype=f)[None, :], (P, P)).copy()
    w_in = inp["mla_w_in"][0]
    kr = w_in[:, 640:704]
    kr_sw = np.concatenate([kr[:, 32:64], kr[:, 0:32]], axis=1)
    sh["m_win"] = np.ascontiguousarray(np.concatenate([w_in[:, :640], kr, kr, kr_sw, kr_sw], axis=1))
    wuq = inp["mla_w_uq"][0]
    nope = wuq[:, :, :128].reshape(384, 1024)
    rp = wuq[:, :, 128:192]
    rp_sw = np.concatenate([rp[:, :, 32:64], rp[:, :, 0:32]], axis=2)
    sh["m_wuq"] = np.ascontiguousarray(np.concatenate([nope, rp.reshape(384, 512), rp_sw.reshape(384, 512)], axis=1))
    wukv = inp["mla_w_ukv"][0]
    sh["m_wukv"] = np.ascontiguousarray(np.concatenate([wukv[:, :, :128].reshape(256, 1024), wukv[:, :, 128:].reshape(256, 1024)], axis=1))
    sh["m_wo"] = np.ascontiguousarray(inp["mla_w_o"][0])
    sh["m_qn"] = np.ascontiguousarray(inp["mla_q_norm"][0].reshape(3, P).T)
    sh["m_kvn"] = np.ascontiguousarray(inp["mla_kv_norm"][0].reshape(2, P).T)
    dw = inp["dsa_w_in"][0]

    def sw(w, hd):
        n = w.shape[1] // hd
        w3 = w.reshape(w.shape[0], n, hd)
        return np.concatenate([w3[:, :, hd // 2:], w3[:, :, :hd // 2]], axis=2).reshape(w.shape[0], n * hd)

    q, kx, v = dw[:, 0:1024], dw[:, 1024:2048], dw[:, 2048:3072]
    qi, ki, wi = dw[:, 3072:3584], dw[:, 3584:3648], dw[:, 3648:3656]
    cols = []
    for h in range(8):
        cols += [q[:, h * P:(h + 1) * P], sw(q[:, h * P:(h + 1) * P], P)]
    for h in range(8):
        cols += [kx[:, h * P:(h + 1) * P], sw(kx[:, h * P:(h + 1) * P], P)]
    qis = sw(qi, 64)
    for pr in range(4):
        cols += [qi[:, pr * P:(pr + 1) * P], qis[:, pr * P:(pr + 1) * P]]
    kis = sw(ki, 64)
    cols += [ki, ki, kis, kis]
    cols += [v, wi]
    sh["d_win"] = np.ascontiguousarray(np.concatenate(cols, axis=1))
    assert sh["d_win"].shape[1] == 6408
    sh["d_wo"] = np.ascontiguousarray(inp["dsa_w_o"][0])
    for l in range(2):
        sh[f"p_wq{l}"] = np.ascontiguousarray(inp["peer_w_q"][l])
        sk = inp["peer_sub_keys"][l]
        sh[f"p_skT{l}"] = np.ascontiguousarray(sk.transpose(1, 0, 3, 2).reshape(16, P, P))
        wd = inp["peer_w_down"][l]
        sh[f"p_wdT{l}"] = np.ascontiguousarray(wd.reshape(P, P, 8, P).transpose(0, 3, 2, 1).reshape(P, P, 8 * P))
        sh[f"p_wu{l}"] = np.ascontiguousarray(inp["peer_w_up"][l])
    sh["ln_g"] = np.ascontiguousarray(inp["ln_gain"].reshape(4, D))
    sh["ln_b"] = np.ascontiguousarray(inp["ln_bias"].reshape(4, D))
    return sh


def kernel(**inputs):
    inp = {k_: np.asarray(v) for k_, v in inputs.items()}
    sh = prep_shared(inp)
    prog = Prog()
    nc = prog.build()
    x = np.ascontiguousarray(inp["x"], dtype=np.float32)
    in_maps = []
    for c in range(8):
        m = dict(sh)
        m["x"] = x[c]
        in_maps.append(m)
    res = run_bass_kernel_spmd(nc, in_maps, core_ids=list(range(8)))
    return np.stack([res.results[c]["out"] for c in range(8)], axis=0).astype(np.float32)
```

```python
from contextlib import ExitStack
import math
import numpy as np
import concourse.bass as bass
import concourse.mybir as mybir
from concourse.bass_utils import run_bass_kernel_spmd

F32 = mybir.dt.float32
BF16 = mybir.dt.bfloat16
U32 = mybir.dt.uint32
AF = mybir.ActivationFunctionType
ALU = mybir.AluOpType

S = 2048
D = 1024
NT = 16
P = 128
ALPHA = float((2 * 2) ** 0.25)
LN_EPS = 1e-5
RMS_EPS = 1e-6
NEG = -1.0e30
BATCH_B = False
ACT_SHARE = 0


class Tn:
    def __init__(self, h, nslots=1, name=""):
        self.h = h
        self.n = nslots
        self.name = name
        self.lw = [None] * nslots
        self.rd = [dict() for _ in range(nslots)]

    def __getitem__(self, key):
        return self.h[key]

    def s(self, *slots):
        return (self, slots)


def _norm(acc):
    if isinstance(acc, Tn):
        return acc, range(acc.n)
    return acc


class KB:
    def __init__(self, nc, es):
        self.nc = nc
        self.es = es
        self.eng = {"pe": nc.tensor, "act": nc.scalar, "dve": nc.vector, "pool": nc.gpsimd, "sp": nc.sync}
        self.sem = {}
        self.cnt = {}
        for e in ("pe", "act", "dve", "pool"):
            self.sem[e] = es.enter_context(nc.semaphore("c_" + e))
            self.cnt[e] = 0
        self.known = {e: {} for e in self.eng}
        self.rings = {}
        self.dma_n = {}
        for q, n in {"sp": 24, "pool": 8, "act": 8}.items():
            self.rings[q] = [es.enter_context(nc.semaphore(f"r_{q}{i}")) for i in range(n)]
            self.dma_n[q] = 0

    def wait(self, e, ev):
        sem, val = ev
        kk = id(sem)
        if self.known[e].get(kk, 0) >= val:
            return
        self.eng[e].wait_ge(sem, val)
        self.known[e][kk] = val

    def _deps(self, reads, writes):
        evs = []
        for acc in reads:
            t, sl = _norm(acc)
            for s in sl:
                if t.lw[s] is not None:
                    evs.append(t.lw[s])
        for acc in writes:
            t, sl = _norm(acc)
            for s in sl:
                if t.lw[s] is not None:
                    evs.append(t.lw[s])
                evs.extend(t.rd[s].values())
        return evs

    def _record(self, ev, reads, writes):
        sem, val = ev
        kk = id(sem)
        for acc in reads:
            t, sl = _norm(acc)
            for s in sl:
                d = t.rd[s]
                if kk not in d or d[kk][1] < val:
                    d[kk] = ev
        for acc in writes:
            t, sl = _norm(acc)
            for s in sl:
                t.lw[s] = ev
                t.rd[s] = dict()

    def op(self, e, fn, reads=(), writes=(), inc=True):
        own = self.sem[e]
        for ev in self._deps(reads, writes):
            if ev[0] is own and e == "pe":
                continue
            self.wait(e, ev)
        ins = fn()
        if inc:
            self.cnt[e] += 1
            ins.then_inc(own, 1)
            ev = (own, self.cnt[e])
        else:
            assert e == "pe"
            ev = (own, self.cnt[e] + 1)
        self._record(ev, reads, writes)
        return ev

    def dma(self, q, out, in_, reads=(), writes=(), **kw):
        n = self.dma_n[q]
        ring = self.rings[q]
        K = len(ring)
        slot = n % K
        if n >= K:
            self.wait(q, (ring[slot], 16 * (n // K)))
        for ev in self._deps(reads, writes):
            self.wait(q, ev)
        self.eng[q].dma_start(out=out, in_=in_, **kw).then_inc(ring[slot], 16)
        self.dma_n[q] = n + 1
        ev = (ring[slot], 16 * (n // K + 1))
        self._record(ev, reads, writes)
        return ev

    def all_events(self):
        evs = []
        for e in ("pe", "act", "dve", "pool"):
            if self.cnt[e] > 0:
                evs.append((self.sem[e], self.cnt[e]))
        for q, ring in self.rings.items():
            n = self.dma_n[q]
            K = len(ring)
            for slot in range(min(n, K)):
                last = ((n - 1 - slot) // K) * K + slot
                evs.append((ring[slot], 16 * (last // K + 1)))
        return evs

    def barrier(self):
        evs = self.all_events()
        for e in self.eng:
            for ev in evs:
                self.wait(e, ev)

    def sbuf(self, name, shape, dt, nslots=1, es=None):
        self.uid = getattr(self, "uid", 0) + 1
        name = f"{name}_{self.uid}"
        h = (es or self.es).enter_context(self.nc.sbuf_tensor(name, list(shape), dt))
        return Tn(h, nslots, name)

    def psum(self, name, shape, dt, nslots=1, es=None):
        h = (es or self.es).enter_context(self.nc.psum_tensor(name, list(shape), dt))
        return Tn(h, nslots, name)

    def dram(self, name, shape, dt, nslots=1):
        h = self.nc.dram_tensor(name, list(shape), dt).ap()
        return Tn(h, nslots, name)


class Prog:
    def __init__(self, debug=()):
        self.debug = set(debug)
        self.nc = bass.Bass("TRN2", target_bir_lowering=False)
        self.dbg_out = {}

    def din(self, name, shape, dt=F32):
        return self.nc.dram_tensor(name, list(shape), dt, kind="ExternalInput").ap()

    def build(self, stages=("mla", "peer0", "dsa", "peer1")):
        nc = self.nc
        I = {}
        I["x"] = self.din("x", [S, D])
        I["ident"] = self.din("ident", [P, P])
        I["cos64"] = self.din("cos64", [P, S])
        I["sin64"] = self.din("sin64", [P, S])
        I["cos128"] = self.din("cos128", [P, S])
        I["sin128"] = self.din("sin128", [P, S])
        I["maskT"] = self.din("maskT", [P, P])
        I["negm"] = self.din("negm", [P, P])
        I["iota"] = self.din("iota", [P, P])
        I["m_win"] = self.din("m_win", [D, 896])
        I["m_wuq"] = self.din("m_wuq", [384, 2048])
        I["m_wukv"] = self.din("m_wukv", [256, 2048])
        I["m_wo"] = self.din("m_wo", [D, D])
        I["m_qn"] = self.din("m_qn", [P, 3])
        I["m_kvn"] = self.din("m_kvn", [P, 2])
        I["d_win"] = self.din("d_win", [D, 6408])
        I["d_wo"] = self.din("d_wo", [D, D])
        for l in range(2):
            I[f"p_wq{l}"] = self.din(f"p_wq{l}", [D, 2048])
            I[f"p_skT{l}"] = self.din(f"p_skT{l}", [16, P, P])
            I[f"p_wdT{l}"] = self.din(f"p_wdT{l}", [P, P, 8 * P])
            I[f"p_wu{l}"] = self.din(f"p_wu{l}", [P * P, D])
        I["ln_g"] = self.din("ln_g", [4, D])
        I["ln_b"] = self.din("ln_b", [4, D])
        self.I = I
        self.out = nc.dram_tensor("out", [S, D], F32, kind="ExternalOutput").ap()

        with ExitStack() as es:
            k = KB(nc, es)
            self.k = k
            self.setup_common(es)
            self.load_x()
            li = 0
            for st in stages:
                if st == "mla":
                    self.mla_layer()
                    self.mixer_out_ln(I["m_wo"], 0)
                elif st == "dsa":
                    self.dsa_layer()
                    self.mixer_out_ln(I["d_wo"], 2)
                    self.dsa_es.close()
                elif st.startswith("peer"):
                    l = int(st[4:])
                    self.peer_layer(l, 2 * l + 1)
            self.store_out()
            k.barrier()
        return nc

    def setup_common(self, es):
        k, nc, I = self.k, self.nc, self.I
        self.X = k.sbuf("X", [P, NT, D], F32, nslots=NT)
        self.xT = k.sbuf("xT", [P, 8, S], BF16, nslots=NT)
        self.ident_f = k.sbuf("ident_f", [P, P], F32)
        self.ident_b = k.sbuf("ident_b", [P, P], BF16)
        self.ones_b = k.sbuf("ones_b", [P, P], BF16)
        self.maskT_b = k.sbuf("maskT_b", [P, P], BF16)
        self.negm = k.sbuf("negm_s", [P, P], F32)
        self.iota_b = k.sbuf("iota_b", [P, P], BF16)
        self.iota_f = k.sbuf("iota_f", [P, P], F32)
        self.eps_ln = k.sbuf("eps_ln", [P, 1], F32)
        self.eps_rms = k.sbuf("eps_rms", [P, 1], F32)
        self.thr_c = k.sbuf("thr_c", [P, 1], F32)
        self.stg = [k.sbuf(f"stg{i}", [P, 2048], F32) for i in range(2)]
        self.stg_i = 0
        self.pb = [k.psum(f"pb{i}", [P, 512], F32) for i in range(7)]
        self.pb_i = 0
        self.ptb = k.psum("ptb", [P, 1024], BF16)
        k.dma("sp", self.ident_f[:], I["ident"], writes=[self.ident_f])
        k.op("dve", lambda: nc.vector.tensor_copy(self.ident_b[:], self.ident_f[:]), reads=[self.ident_f], writes=[self.ident_b])
        k.op("pool", lambda: nc.gpsimd.memset(self.ones_b[:], 1.0), writes=[self.ones_b])
        k.op("pool", lambda: nc.gpsimd.memset(self.eps_ln[:], LN_EPS), writes=[self.eps_ln])
        k.op("pool", lambda: nc.gpsimd.memset(self.eps_rms[:], RMS_EPS), writes=[self.eps_rms])
        k.op("pool", lambda: nc.gpsimd.memset(self.thr_c[:], -1.0e29), writes=[self.thr_c])
        st = self.stage()
        k.dma("sp", st[:, 0:P], I["maskT"], writes=[st])
        k.op("dve", lambda: nc.vector.tensor_copy(self.maskT_b[:], st[:, 0:P]), reads=[st], writes=[self.maskT_b])
        k.dma("sp", self.negm[:], I["negm"], writes=[self.negm])
        k.dma("sp", self.iota_f[:], I["iota"], writes=[self.iota_f])
        k.op("dve", lambda: nc.vector.tensor_copy(self.iota_b[:], self.iota_f[:]), reads=[self.iota_f], writes=[self.iota_b])
        self.ln_st = k.sbuf("ln_st", [P, NT, 12], F32, nslots=NT)
        self.ln_mv = k.sbuf("ln_mv", [P, NT, 4], F32, nslots=NT)
        self.xbf = [k.sbuf(f"xbf{i}", [P, D], BF16) for i in range(2)]
        self.qnD = k.dram("qnD", [8, P, S], BF16, nslots=8)
        self.knD = k.dram("knD", [8, P, S], BF16, nslots=8)
        self.qrD = k.dram("qrD", [8, 64, S], BF16, nslots=8)
        self.krD = k.dram("krD", [P, S], BF16)
        self.VD = k.dram("VD", [NT, P, D], BF16, nslots=NT)
        self.qiD = k.dram("qiD", [4, P, S], BF16, nslots=4)
        self.kiD = k.dram("kiD", [P, S], BF16)

    def stage(self):
        s = self.stg[self.stg_i % len(self.stg)]
        self.stg_i += 1
        return s

    def bank(self):
        b = self.pb[self.pb_i % len(self.pb)]
        self.pb_i += 1
        return b

    def dbg(self, name, t, shape, dt=F32, ap=None):
        if name not in self.debug:
            return
        o = self.nc.dram_tensor("dbg_" + name, list(shape), dt, kind="ExternalOutput").ap()
        self.k.dma("sp", o, ap if ap is not None else t[:], reads=[t])
        self.dbg_out[name] = (shape, dt)

    def make_xT(self, tt):
        k, nc = self.k, self.nc
        xb = self.xbf[tt % 2]
        k.op("act", lambda: nc.scalar.copy(xb[:], self.X[:, tt, :]), reads=[self.X.s(tt)], writes=[xb])
        for c in range(8):
            k.op("pe", lambda c=c: nc.tensor.transpose(self.ptb[:, c * P:(c + 1) * P], xb[:, c * P:(c + 1) * P], self.ident_b[:]),
                 reads=[xb, self.ident_b], writes=[self.ptb], inc=(c == 7))
        k.op("dve", lambda: nc.vector.tensor_copy(self.xT[:, :, tt * P:(tt + 1) * P],
                                                  self.ptb[:].rearrange("p (c t) -> p c t", c=8)),
             reads=[self.ptb], writes=[self.xT.s(tt)])

    def load_x(self):
        k, nc = self.k, self.nc
        for tt in range(NT):
            k.dma("sp", self.X[:, tt, :], self.I["x"][tt * P:(tt + 1) * P, :], writes=[self.X.s(tt)])
            self.make_xT(tt)

    def store_out(self):
        k = self.k
        for tt in range(NT):
            k.dma("sp", self.out[tt * P:(tt + 1) * P, :], self.X[:, tt, :], reads=[self.X.s(tt)])

    def ln_params(self, li, es):
        k, I = self.k, self.I
        self.lng = k.sbuf("lng", [P, D], F32, es=es)
        self.lnb = k.sbuf("lnb", [P, D], F32, es=es)
        k.dma("sp", self.lng[:], I["ln_g"][li:li + 1, :].partition_broadcast(P), writes=[self.lng])
        k.dma("sp", self.lnb[:], I["ln_b"][li:li + 1, :].partition_broadcast(P), writes=[self.lnb])

    def ln_tile(self, tt, li):
        k, nc = self.k, self.nc
        X = self.X
        xt = X[:, tt, :]
        st = self.ln_st
        mv = self.ln_mv
        k.op("dve", lambda: nc.vector.bn_stats(st[:, tt, 0:6], X[:, tt, 0:512]), reads=[X.s(tt)], writes=[st.s(tt)])
        k.op("dve", lambda: nc.vector.bn_stats(st[:, tt, 6:12], X[:, tt, 512:1024]), reads=[X.s(tt)], writes=[st.s(tt)])
        k.op("dve", lambda: nc.vector.bn_aggr(mv[:, tt, 0:2], st[:, tt, :].rearrange("p (a b) -> p a b", a=2)),
             reads=[st.s(tt)], writes=[mv.s(tt)])
        k.op("act", lambda: nc.scalar.activation(mv[:, tt, 2:3], mv[:, tt, 1:2], AF.Sqrt, bias=self.eps_ln[:, 0:1], scale=1.0),
             reads=[mv.s(tt), self.eps_ln], writes=[mv.s(tt)])
        k.op("dve", lambda: nc.vector.reciprocal(mv[:, tt, 2:3], mv[:, tt, 2:3]), reads=[mv.s(tt)], writes=[mv.s(tt)])
        k.op("dve", lambda: nc.vector.scalar_tensor_tensor(mv[:, tt, 3:4], mv[:, tt, 0:1], -1.0, mv[:, tt, 2:3], op0=ALU.mult, op1=ALU.mult),
             reads=[mv.s(tt)], writes=[mv.s(tt)])
        k.op("act", lambda: nc.scalar.activation(xt, xt, AF.Identity, bias=mv[:, tt, 3:4], scale=mv[:, tt, 2:3]),
             reads=[X.s(tt), mv.s(tt)], writes=[X.s(tt)])
        k.op("pool", lambda: nc.gpsimd.tensor_tensor(xt, xt, self.lng[:], op=ALU.mult), reads=[X.s(tt), self.lng], writes=[X.s(tt)])
        k.op("pool", lambda: nc.gpsimd.tensor_tensor(xt, xt, self.lnb[:], op=ALU.add), reads=[X.s(tt), self.lnb], writes=[X.s(tt)])
        self.make_xT(tt)

    def load_w(self, dst, dst_ap, src_ap, ncols, eng="pool", scale=None, dst_slots=None):
        k, nc = self.k, self.nc
        st = self.stage()
        w = [dst] if dst_slots is None else [dst.s(*dst_slots)]
        k.dma("sp", st[:, 0:ncols], src_ap, writes=[st])
        if scale is not None:
            k.op("dve", lambda: nc.vector.tensor_scalar(dst_ap, st[:, 0:ncols], scale[1], None, op0=ALU.mult), reads=[st, scale[0]], writes=w)
        elif eng == "pool":
            k.op("pool", lambda: nc.gpsimd.tensor_copy(dst_ap, st[:, 0:ncols]), reads=[st], writes=w)
        elif eng == "act":
            k.op("act", lambda: nc.scalar.copy(dst_ap, st[:, 0:ncols]), reads=[st], writes=w)
        else:
            k.op("dve", lambda: nc.vector.tensor_copy(dst_ap, st[:, 0:ncols]), reads=[st], writes=w)

    def rope(self, ps_n, ps_s, cos, sin, tb, out_t, out_ap, es_tmp):
        k, nc = self.k, self.nc
        t1, t2 = es_tmp
        sl = slice(tb * 512, (tb + 1) * 512)
        k.op("dve", lambda: nc.vector.tensor_tensor(t1[:], ps_n[:], cos[:, sl], op=ALU.mult), reads=[ps_n, cos], writes=[t1])
        k.op("dve", lambda: nc.vector.tensor_tensor(t2[:], ps_s[:], sin[:, sl], op=ALU.mult), reads=[ps_s, sin], writes=[t2])
        k.op("pool", lambda: nc.gpsimd.tensor_tensor(out_ap, t1[:], t2[:], op=ALU.add), reads=[t1, t2], writes=[out_t])

    def mla_layer(self):
        k, nc, I = self.k, self.nc, self.I
        xT = self.xT
        with ExitStack() as es:
            win = k.sbuf("m_win_b", [P, 8, 896], BF16, es=es)
            wuq = k.sbuf("m_wuq_b", [P, 3, 2048], BF16, es=es)
            wukv = k.sbuf("m_wukv_b", [P, 2, 2048], BF16, es=es)
            qn = k.sbuf("m_qn_s", [P, 3], F32, es=es)
            kvn = k.sbuf("m_kvn_s", [P, 2], F32, es=es)
            cos = k.sbuf("cos64_s", [P, S], F32, es=es)
            sin = k.sbuf("sin64_s", [P, S], F32, es=es)
            k.dma("sp", qn[:], I["m_qn"], writes=[qn])
            k.dma("sp", kvn[:], I["m_kvn"], writes=[kvn])
            k.dma("sp", cos[:], I["cos64"], writes=[cos])
            k.dma("sp", sin[:], I["sin64"], writes=[sin])
            for c in range(8):
                self.load_w(win, win[:, c, :], I["m_win"][c * P:(c + 1) * P, :], 896, eng=("pool", "act")[c % 2])
            for c in range(3):
                self.load_w(wuq, wuq[:, c, :], I["m_wuq"][c * P:(c + 1) * P, :], 2048, scale=(qn, qn[:, c:c + 1]))
            for c in range(2):
                self.load_w(wukv, wukv[:, c, :], I["m_wukv"][c * P:(c + 1) * P, :], 2048, scale=(kvn, kvn[:, c:c + 1]))
            lat_f = k.sbuf("lat_f", [P, 5, 512], F32, es=es)
            sq_b = k.sbuf("sq_b", [P, 5, 512], BF16, es=es)
            rs = k.sbuf("rs", [P, 2, 512], F32, es=es)
            lat_n = [k.sbuf(f"lat_n{i}", [P, 5, 512], BF16, es=es) for i in range(1)]
            t1 = k.sbuf("rp_t1", [P, 512], F32, es=es)
            t2 = k.sbuf("rp_t2", [P, 512], F32, es=es)
            ob = [k.sbuf(f"m_ob{i}", [P, 512], BF16, es=es) for i in range(4)]
            vb = [k.sbuf(f"m_vb{i}", [P, D], BF16, es=es) for i in range(2)]
            obi = 0
            for tb in range(4):
                tsl = slice(tb * 512, (tb + 1) * 512)
                tslots = tuple(range(tb * 4, tb * 4 + 4))
                ln = lat_n[0]
                for c in range(5):
                    ps = self.bank()
                    for dc in range(8):
                        k.op("pe", lambda dc=dc, c=c, ps=ps: nc.tensor.matmul(ps[:], win[:, dc, c * P:(c + 1) * P], xT[:, dc, tsl],
                                                                               start=(dc == 0), stop=(dc == 7)),
                             reads=[win, xT.s(*tslots)], writes=[ps], inc=(dc == 7))
                    k.op("act", lambda c=c, ps=ps: nc.scalar.copy(lat_f[:, c, :], ps[:]), reads=[ps], writes=[lat_f])
                    k.op("act", lambda c=c, ps=ps: nc.scalar.activation(sq_b[:, c, :], ps[:], AF.Square), reads=[ps], writes=[sq_b])
                for gi, (c0, nch, width) in enumerate(((0, 3, 384), (3, 2, 256))):
                    ps = self.bank()
                    for c in range(nch):
                        k.op("pe", lambda c=c, ps=ps, c0=c0, nch=nch: nc.tensor.matmul(ps[:], self.ones_b[:], sq_b[:, c0 + c, :],
                                                                                        start=(c == 0), stop=(c == nch - 1)),
                             reads=[self.ones_b, sq_b], writes=[ps], inc=(c == nch - 1))
                    k.op("act", lambda ps=ps, gi=gi, width=width: nc.scalar.activation(rs[:, gi, :], ps[:], AF.Sqrt, bias=self.eps_rms[:, 0:1],
                                                                                         scale=1.0 / width),
                         reads=[ps, self.eps_rms], writes=[rs])
                    k.op("dve", lambda gi=gi: nc.vector.reciprocal(rs[:, gi, :], rs[:, gi, :]), reads=[rs], writes=[rs])
                    k.op("dve", lambda gi=gi, c0=c0, nch=nch, ln=ln: nc.vector.tensor_tensor(
                        ln[:, c0:c0 + nch, :], lat_f[:, c0:c0 + nch, :], rs[:, gi, :].unsqueeze(1).to_broadcast([P, nch, 512]), op=ALU.mult),
                         reads=[lat_f, rs], writes=[ln])
                psn = self.bank()
                pss = self.bank()
                for (ps, c0) in ((psn, 640), (pss, 768)):
                    for dc in range(8):
                        k.op("pe", lambda dc=dc, ps=ps, c0=c0: nc.tensor.matmul(ps[:], win[:, dc, c0:c0 + P], xT[:, dc, tsl],
                                                                                 start=(dc == 0), stop=(dc == 7)),
                             reads=[win, xT.s(*tslots)], writes=[ps], inc=(dc == 7))
                o = ob[obi % 4]; obi += 1
                self.rope(psn, pss, cos, sin, tb, o, o[:], (t1, t2))
                k.dma("sp", self.krD[:, tsl], o[:], reads=[o], writes=[self.krD])
                for h in range(8):
                    ps = self.bank()
                    for rc in range(3):
                        k.op("pe", lambda rc=rc, ps=ps, h=h: nc.tensor.matmul(ps[:], wuq[:, rc, h * P:(h + 1) * P], ln[:, rc, :],
                                                                               start=(rc == 0), stop=(rc == 2)),
                             reads=[wuq, ln], writes=[ps], inc=(rc == 2))
                    o = ob[obi % 4]; obi += 1
                    k.op("act", lambda ps=ps, o=o: nc.scalar.copy(o[:], ps[:]), reads=[ps], writes=[o])
                    k.dma("sp", self.qnD[h, :, tsl], o[:], reads=[o], writes=[self.qnD.s(h)])
                for pr in range(4):
                    psn = self.bank()
                    pss = self.bank()
                    for (ps, c0) in ((psn, 1024 + pr * P), (pss, 1536 + pr * P)):
                        for rc in range(3):
                            k.op("pe", lambda rc=rc, ps=ps, c0=c0: nc.tensor.matmul(ps[:], wuq[:, rc, c0:c0 + P], ln[:, rc, :],
                                                                                     start=(rc == 0), stop=(rc == 2)),
                                 reads=[wuq, ln], writes=[ps], inc=(rc == 2))
                    o = ob[obi % 4]; obi += 1
                    self.rope(psn, pss, cos, sin, tb, o, o[:], (t1, t2))
                    k.dma("sp", self.qrD[2 * pr, :, tsl], o[0:64, :], reads=[o], writes=[self.qrD.s(2 * pr)])
                    k.dma("sp", self.qrD[2 * pr + 1, :, tsl], o[64:128, :], reads=[o], writes=[self.qrD.s(2 * pr + 1)])
                for h in range(8):
                    ps = self.bank()
                    for rc in range(2):
                        k.op("pe", lambda rc=rc, ps=ps, h=h: nc.tensor.matmul(ps[:], wukv[:, rc, h * P:(h + 1) * P], ln[:, 3 + rc, :],
                                                                               start=(rc == 0), stop=(rc == 1)),
                             reads=[wukv, ln], writes=[ps], inc=(rc == 1))
                    o = ob[obi % 4]; obi += 1
                    k.op("dve", lambda ps=ps, o=o: nc.vector.tensor_copy(o[:], ps[:]), reads=[ps], writes=[o])
                    k.dma("sp", self.knD[h, :, tsl], o[:], reads=[o], writes=[self.knD.s(h)])
                for t4 in range(4):
                    tt = tb * 4 + t4
                    v = vb[tt % 2]
                    for half in range(2):
                        ps = self.bank()
                        for rc in range(2):
                            k.op("pe", lambda rc=rc, ps=ps, half=half, t4=t4: nc.tensor.matmul(
                                ps[:], ln[:, 3 + rc, t4 * P:(t4 + 1) * P], wukv[:, rc, 1024 + half * 512:1024 + (half + 1) * 512],
                                start=(rc == 0), stop=(rc == 1)),
                                 reads=[wukv, ln], writes=[ps], inc=(rc == 1))
                        if half == 0:
                            k.op("act", lambda ps=ps, v=v: nc.scalar.copy(v[:, 0:512], ps[:]), reads=[ps], writes=[v])
                        else:
                            k.op("dve", lambda ps=ps, v=v: nc.vector.tensor_copy(v[:, 512:1024], ps[:]), reads=[ps], writes=[v])
                    k.dma("sp", self.VD[tt], v[:], reads=[v], writes=[self.VD.s(tt)])
            k.barrier()
        self.attention(mla=True)

    def attention(self, mla, selT=None, selT_t=None):
        k, nc = self.k, self.nc
        scale = (192.0 if mla else 128.0) ** -0.5
        self.att_es = ExitStack()
        es = self.att_es
        OT = k.sbuf("OT", [P, 8, S], BF16, nslots=8, es=es)
        self.OT = OT
        with ExitStack() as es2:
            kn = [Tn(self.stg[i][:].bitcast(BF16), 1, f"a_kn{i}") for i in range(2)]
            qn = [k.sbuf(f"a_qn{i}", [P, S], BF16, es=es2) for i in range(2)]
            vh = [k.sbuf(f"a_vh{i}", [P, NT, P], BF16, es=es2) for i in range(2)]
            pt = [k.sbuf(f"a_pt{i}", [P, 512], BF16, es=es2) for i in range(3)]
            rec = k.sbuf("a_rec", [P, 512], F32, es=es2)
            if mla:
                qr = [k.sbuf(f"a_qr{i}", [64, S], BF16, es=es2) for i in range(2)]
                kr = k.sbuf("a_kr", [64, S], BF16, es=es2)
                k.dma("sp", kr[:], self.krD[0:64, :], reads=[self.krD], writes=[kr])
            def load_head(h):
                b = h % 2
                k.dma("sp", kn[b][:, 0:S], self.knD[h], reads=[self.knD.s(h)], writes=[kn[b]])
                k.dma("sp", qn[b][:], self.qnD[h], reads=[self.qnD.s(h)], writes=[qn[b]])
                k.dma("sp", vh[b][:], self.VD[:, :, h * P:(h + 1) * P].rearrange("t p v -> p t v"), reads=[self.VD], writes=[vh[b]])
                if mla:
                    k.dma("sp", qr[b][:], self.qrD[h], reads=[self.qrD.s(h)], writes=[qr[b]])

            pairs = []
            for h in range(8):
                for QB in range(4):
                    for kc in range(4 * QB + 4):
                        pairs.append((h, QB, kc))
            info = {}

            def emit_qk(i):
                h, QB, kc = pairs[i]
                b = h % 2
                if QB == 0 and kc == 0 and h == 0:
                    load_head(0)
                qlo = max(kc, 4 * QB)
                c0 = (qlo - 4 * QB) * P
                qs = slice(QB * 512 + c0, (QB + 1) * 512)
                ks = slice(kc * P, (kc + 1) * P)
                st = self.pb[i % 3]
                k.op("pe", lambda: nc.tensor.matmul(st[:, c0:512], kn[b][:, ks], qn[b][:, qs], start=True, stop=(not mla)),
                     reads=[kn[b], qn[b]], writes=[st], inc=(not mla))
                if mla:
                    k.op("pe", lambda: nc.tensor.matmul(st[:, c0:512], kr[:, ks], qr[b][:, qs], start=False, stop=True),
                         reads=[kr, qr[b]], writes=[st])
                info[i] = (st, c0, qs)

            def emit_soft(i):
                h, QB, kc = pairs[i]
                st, c0, qs = info[i]
                p_ = pt[i % 3]
                k.op("act", lambda: nc.scalar.activation(p_[:, c0:512], st[:, c0:512], AF.Exp, scale=scale), reads=[st], writes=[p_])
                if mla:
                    if kc >= 4 * QB:
                        k.op("dve", lambda: nc.vector.tensor_tensor(p_[:, c0:c0 + P], p_[:, c0:c0 + P], self.maskT_b[:], op=ALU.mult),
                             reads=[p_, self.maskT_b], writes=[p_])
                else:
                    k.op("dve", lambda: nc.vector.tensor_tensor(p_[:, c0:512], p_[:, c0:512], selT(kc, qs.start, qs.stop), op=ALU.mult),
                         reads=[p_, selT_t], writes=[p_])

            def emit_pv(i):
                h, QB, kc = pairs[i]
                b = h % 2
                st, c0, qs = info.pop(i)
                p_ = pt[i % 3]
                oT = self.pb[3 + QB % 2]
                sm = self.pb[5 + QB % 2]
                last = 4 * QB + 3
                if QB == 0 and kc == 0 and h + 1 < 8:
                    load_head(h + 1)
                k.op("pe", lambda: nc.tensor.matmul(oT[:, c0:512], vh[b][:, kc, :], p_[:, c0:512], start=(kc == 0), stop=(kc == last), skip_group_check=True),
                     reads=[vh[b], p_], writes=[oT], inc=False)
                k.op("pe", lambda: nc.tensor.matmul(sm[:, c0:512], self.ones_b[:], p_[:, c0:512], start=(kc == 0), stop=(kc == last), skip_group_check=True),
                     reads=[self.ones_b, p_], writes=[sm])
                if kc == last:
                    k.op("dve", lambda: nc.vector.reciprocal(rec[:], sm[:]), reads=[sm], writes=[rec])
                    k.op("dve", lambda: nc.vector.tensor_tensor(OT[:, h, QB * 512:(QB + 1) * 512], oT[:], rec[:], op=ALU.mult),
                         reads=[oT, rec], writes=[OT.s(h)])

            npairs = len(pairs)
            LA = 2
            for i in range(min(LA, npairs)):
                emit_qk(i)
            for i in range(npairs):
                emit_soft(i)
                if i + LA < npairs:
                    emit_qk(i + LA)
                emit_pv(i)
            k.barrier()

    def mixer_out_ln(self, wo_ap, li):
        k, nc = self.k, self.nc
        OT = self.OT
        with ExitStack() as es:
            wo = k.sbuf("wo_b", [P, 8, 512], BF16, es=es)
            self.ln_params(li, es)
            for half in range(2):
                for c in range(8):
                    self.load_w(wo, wo[:, c, :], wo_ap[c * P:(c + 1) * P, half * 512:(half + 1) * 512], 512, eng=("pool", "act")[c % 2])
                for tt in range(NT):
                    ps = self.bank()
                    for h in range(8):
                        k.op("pe", lambda ps=ps, h=h, tt=tt: nc.tensor.matmul(
                            ps[:], OT[:, h, tt * P:(tt + 1) * P], wo[:, h, :], start=(h == 0), stop=(h == 7)),
                             reads=[OT, wo], writes=[ps], inc=(h == 7))
                    xs = self.X[:, tt, half * 512:(half + 1) * 512]
                    k.op("dve", lambda ps=ps, xs=xs: nc.vector.scalar_tensor_tensor(xs, xs, ALPHA, ps[:], op0=ALU.mult, op1=ALU.add),
                         reads=[ps, self.X.s(tt)], writes=[self.X.s(tt)])
            for tt in range(NT):
                self.ln_tile(tt, li)
            k.barrier()
        self.att_es.close()

    def dsa_layer(self):
        k, nc, I = self.k, self.nc, self.I
        xT = self.xT
        WI = I["d_win"].rearrange("(c p) n -> p c n", p=P)
        self.dsa_es = ExitStack()
        esD = self.dsa_es
        selT = k.sbuf("d_selT", [P, 136 * P], BF16, es=esD)
        wtok = k.sbuf("d_wtok", [P, NT, 8], F32, es=esD)

        def soff(kc):
            return P * (16 * kc - kc * (kc - 1) // 2)

        with ExitStack() as es:
            wb = [k.sbuf(f"d_wb{i}", [P, 8, 512], BF16, es=es) for i in range(2)]
            t1 = k.sbuf("d_t1", [P, 512], F32, es=es)
            t2 = k.sbuf("d_t2", [P, 512], F32, es=es)
            ob = [k.sbuf(f"d_ob{i}", [P, 512], BF16, es=es) for i in range(4)]
            cos = k.sbuf("d_cos", [P, S], F32, es=es)
            sin = k.sbuf("d_sin", [P, S], F32, es=es)
            gi = 0
            obi = 0

            def load_group(c0, ncols):
                nonlocal gi
                wb_ = wb[gi % 2]
                gi += 1
                for hf in range(2):
                    st_ = self.stage()
                    sv = st_[:].rearrange("p (c n) -> p c n", c=4)
                    k.dma("sp", sv[:, :, 0:ncols], WI[:, hf * 4:(hf + 1) * 4, c0:c0 + ncols], writes=[st_])
                    if hf == 0:
                        k.op("act", lambda sv=sv, hf=hf: nc.scalar.copy(wb_[:, hf * 4:(hf + 1) * 4, 0:ncols], sv[:, :, 0:ncols]), reads=[st_], writes=[wb_])
                    else:
                        k.op("pool", lambda sv=sv, hf=hf: nc.gpsimd.tensor_copy(wb_[:, hf * 4:(hf + 1) * 4, 0:ncols], sv[:, :, 0:ncols]), reads=[st_], writes=[wb_])
                return wb_

            def rope_pairs(c0, npairs, dst_fn):
                nonlocal obi
                wb_ = load_group(c0, npairs * 256)
                for tb in range(4):
                    tsl = slice(tb * 512, (tb + 1) * 512)
                    tslots = tuple(range(tb * 4, tb * 4 + 4))
                    for pi in range(npairs):
                        psn = self.bank()
                        pss = self.bank()
                        for (ps, cc) in ((psn, pi * 256), (pss, pi * 256 + P)):
                            for dc in range(8):
                                k.op("pe", lambda dc=dc, ps=ps, cc=cc: nc.tensor.matmul(ps[:], wb_[:, dc, cc:cc + P], xT[:, dc, tsl],
                                                                                         start=(dc == 0), stop=(dc == 7)),
                                     reads=[wb_, xT.s(*tslots)], writes=[ps], inc=(dc == 7))
                        o = ob[obi % 4]; obi += 1
                        self.rope(psn, pss, cos, sin, tb, o, o[:], (t1, t2))
                        dt_, dap = dst_fn(pi, tsl)
                        k.dma("sp", dap, o[:], reads=[o], writes=[dt_])

            k.dma("sp", cos[:], I["cos128"], writes=[cos])
            k.dma("sp", sin[:], I["sin128"], writes=[sin])
            for g in range(4):
                rope_pairs(g * 512, 2, lambda pi, tsl, g=g: (self.qnD.s(2 * g + pi), self.qnD[2 * g + pi, :, tsl]))
            for g in range(4):
                rope_pairs(2048 + g * 512, 2, lambda pi, tsl, g=g: (self.knD.s(2 * g + pi), self.knD[2 * g + pi, :, tsl]))
            k.dma("sp", cos[:], I["cos64"], writes=[cos])
            k.dma("sp", sin[:], I["sin64"], writes=[sin])
            for g in range(2):
                rope_pairs(4096 + g * 512, 2, lambda pi, tsl, g=g: (self.qiD.s(2 * g + pi), self.qiD[2 * g + pi, :, tsl]))
            rope_pairs(5120, 1, lambda pi, tsl: (self.kiD, self.kiD[:, tsl]))
            vb = [k.sbuf(f"d_vb{i}", [P, 512], BF16, es=es) for i in range(2)]
            vi = 0
            for half in range(2):
                wb_ = load_group(5376 + half * 512, 512)
                for tt in range(NT):
                    ps = self.bank()
                    for dc in range(8):
                        k.op("pe", lambda dc=dc, ps=ps, tt=tt: nc.tensor.matmul(ps[:], xT[:, dc, tt * P:(tt + 1) * P], wb_[:, dc, :],
                                                                                 start=(dc == 0), stop=(dc == 7)),
                             reads=[wb_, xT.s(tt)], writes=[ps], inc=(dc == 7))
                    v = vb[vi % 2]; vi += 1
                    if tt % 2 == 0:
                        k.op("act", lambda ps=ps, v=v: nc.scalar.copy(v[:], ps[:]), reads=[ps], writes=[v])
                    else:
                        k.op("dve", lambda ps=ps, v=v: nc.vector.tensor_copy(v[:], ps[:]), reads=[ps], writes=[v])
                    k.dma("sp", self.VD[tt, :, half * 512:(half + 1) * 512], v[:], reads=[v], writes=[self.VD.s(tt)])
            wb_ = load_group(6400, 8)
            wscale = float(8 ** -0.5 * 64 ** -0.5)
            for tt in range(NT):
                ps = self.bank()
                for dc in range(8):
                    k.op("pe", lambda dc=dc, ps=ps, tt=tt: nc.tensor.matmul(ps[:, 0:8], xT[:, dc, tt * P:(tt + 1) * P], wb_[:, dc, 0:8],
                                                                             start=(dc == 0), stop=(dc == 7)),
                         reads=[wb_, xT.s(tt)], writes=[ps], inc=(dc == 7))
                k.op("act", lambda ps=ps, tt=tt: nc.scalar.mul(wtok[:, tt, :], ps[:, 0:8], wscale), reads=[ps], writes=[wtok])
            k.barrier()
        with ExitStack() as es:
            kiT = k.sbuf("d_kiT", [P, S], BF16, es=es)
            qiT = k.sbuf("d_qiT", [P, 4, S], BF16, es=es)
            accs = [k.sbuf(f"d_acc{i}", [P, S], F32, es=es) for i in range(2)]
            rls = [k.sbuf(f"d_rl{i}", [P, 512], F32, es=es) for i in range(2)]
            sels = [k.sbuf(f"d_sel{i}", [P, S], BF16, es=es) for i in range(1)]
            mxs = [k.sbuf(f"d_mx{i}", [P, 8], F32, es=es) for i in range(4)]
            k.dma("sp", kiT[:], self.kiD[:], reads=[self.kiD], writes=[kiT])
            for pr in range(4):
                k.dma("sp", qiT[:, pr, :], self.qiD[pr], reads=[self.qiD.s(pr)], writes=[qiT])
            rli = 0
            tbi = 0
            sel2 = k.sbuf("d_sel2", [P, S], BF16, es=es)
            bss = [k.sbuf(f"d_bs{i}", [P, 8], F32, es=es) for i in range(2)]

            def q_chain(qi):
                nonlocal rli, tbi
                n = (qi + 1) * P
                acc = accs[qi % 2]
                mxs_ = mxs[2 * (qi % 2):2 * (qi % 2) + 2]
                qsl = slice(qi * P, (qi + 1) * P)
                for h in range(8):
                    pr, hp = h // 2, h % 2
                    prt = slice(hp * 64, (hp + 1) * 64)
                    for k0 in range(0, n, 512):
                        kw = min(512, n - k0)
                        ps = self.bank()
                        k.op("pe", lambda ps=ps, kw=kw, k0=k0, pr=pr, prt=prt, qsl=qsl: nc.tensor.matmul(
                            ps[:, 0:kw], qiT[prt, pr, qsl], kiT[prt, k0:k0 + kw], start=True, stop=True),
                             reads=[qiT, kiT], writes=[ps])
                        rl = rls[rli % 2]; rli += 1
                        k.op("act", lambda ps=ps, rl=rl, kw=kw: nc.scalar.activation(rl[:, 0:kw], ps[:, 0:kw], AF.Relu), reads=[ps], writes=[rl])
                        if h == 0:
                            k.op("dve", lambda rl=rl, kw=kw, k0=k0, acc=acc, qi=qi: nc.vector.tensor_scalar(
                                acc[:, k0:k0 + kw], rl[:, 0:kw], wtok[:, qi, 0:1], None, op0=ALU.mult), reads=[rl, wtok], writes=[acc])
                        else:
                            k.op("dve", lambda rl=rl, kw=kw, k0=k0, acc=acc, qi=qi, h=h: nc.vector.scalar_tensor_tensor(
                                acc[:, k0:k0 + kw], rl[:, 0:kw], wtok[:, qi, h:h + 1], acc[:, k0:k0 + kw], op0=ALU.mult, op1=ALU.add),
                                 reads=[rl, wtok, acc], writes=[acc])
                        yield
                sel = sels[0] if qi % 2 == 0 else sel2
                if qi >= 2:
                    bs = bss[qi % 2]
                    AXX_ = mybir.AxisListType.X
                    k.op("dve", lambda: nc.vector.tensor_reduce(bs[:, 0:1], acc[:, 0:n], AXX_, ALU.max), reads=[acc], writes=[bs])
                    yield
                    k.op("dve", lambda: nc.vector.tensor_reduce(bs[:, 1:2], acc[:, 0:n], AXX_, ALU.min), reads=[acc], writes=[bs])
                    yield
                    k.op("dve", lambda: nc.vector.scalar_tensor_tensor(bs[:, 2:3], bs[:, 0:1], 1.0, bs[:, 1:2], op0=ALU.mult, op1=ALU.subtract),
                         reads=[bs], writes=[bs])
                    yield
                    k.op("dve", lambda: nc.vector.tensor_scalar(bs[:, 2:3], bs[:, 2:3], 1.0009765625, 1e-20, op0=ALU.mult, op1=ALU.add),
                         reads=[bs], writes=[bs])
                    yield
                k.op("pool", lambda acc=acc, n=n: nc.gpsimd.tensor_tensor(acc[:, n - P:n], acc[:, n - P:n], self.negm[:], op=ALU.add),
                     reads=[acc, self.negm], writes=[acc])
                if qi >= 2:
                    bs = bss[qi % 2]
                    NIT = 20
                    k.op("dve", lambda: nc.vector.scalar_tensor_tensor(bs[:, 3:4], bs[:, 2:3], 0.5, bs[:, 1:2], op0=ALU.mult, op1=ALU.add),
                         reads=[bs], writes=[bs])
                    yield
                    for it in range(1, NIT + 1):
                        step = 0.5 ** it
                        k.op("dve", lambda: nc.vector.tensor_scalar(sel[:, 0:n], acc[:, 0:n], bs[:, 3:4], 0.0, op0=ALU.is_ge, op1=ALU.add,
                                                                    accum_out=bs[:, 4:5]),
                             reads=[acc, bs], writes=[sel, bs])
                        yield
                        k.op("dve", lambda step=step: nc.vector.tensor_scalar(bs[:, 5:6], bs[:, 4:5], 256.0, step, op0=ALU.is_ge, op1=ALU.mult),
                             reads=[bs], writes=[bs])
                        yield
                        k.op("dve", lambda: nc.vector.scalar_tensor_tensor(bs[:, 1:2], bs[:, 5:6], bs[:, 2:3], bs[:, 1:2], op0=ALU.mult, op1=ALU.add),
                             reads=[bs], writes=[bs])
                        yield
                        if it < NIT:
                            k.op("dve", lambda step=step: nc.vector.scalar_tensor_tensor(bs[:, 3:4], bs[:, 2:3], step * 0.5, bs[:, 1:2], op0=ALU.mult, op1=ALU.add),
                                 reads=[bs], writes=[bs])
                            yield
                    k.op("dve", lambda: nc.vector.tensor_scalar(sel[:, 0:n], acc[:, 0:n], bs[:, 1:2], None, op0=ALU.is_ge),
                         reads=[acc, bs], writes=[sel])
                    yield
                else:
                    thr_t, thr = self.thr_c, self.thr_c[:, 0:1]
                    k.op("dve", lambda sel=sel, acc=acc, n=n, thr=thr: nc.vector.tensor_scalar(sel[:, 0:n], acc[:, 0:n], thr, None, op0=ALU.is_ge),
                         reads=[acc, thr_t], writes=[sel])
                    yield
                for kc in range(qi + 1):
                    blk = tbi % 8; tbi += 1
                    k.op("pe", lambda sel=sel, kc=kc, blk=blk: nc.tensor.transpose(self.ptb[:, blk * P:(blk + 1) * P], sel[:, kc * P:(kc + 1) * P], self.ident_b[:]),
                         reads=[sel, self.ident_b], writes=[self.ptb])
                    o_ = soff(kc) + (qi - kc) * P
                    k.op("act", lambda blk=blk, o_=o_: nc.scalar.copy(selT[:, o_:o_ + P], self.ptb[:, blk * P:(blk + 1) * P]),
                         reads=[self.ptb], writes=[selT])

            for pa in range(8):
                gens = [q_chain(2 * pa + 1), q_chain(2 * pa)]
                while gens:
                    for g_ in list(gens):
                        try:
                            next(g_)
                        except StopIteration:
                            gens.remove(g_)
            self.dbg("bs1", bss[1], [P, 8])
            self.dbg("acc1", accs[1], [P, S])
            self.dbg("sel1", sel2, [P, S], BF16)
            k.barrier()
        self.dbg("selT", selT, [P, 136 * P], BF16)
        self.attention(mla=False, selT=lambda kc, q0, q1: selT[:, soff(kc) + q0 - kc * P: soff(kc) + q1 - kc * P], selT_t=selT)

    def peer_layer(self, l, li):
        k, nc, I = self.k, self.nc, self.I
        xT, X = self.xT, self.X
        AXX = mybir.AxisListType.X
        if not hasattr(self, "gateD"):
            self.gateD = k.dram("gateD", [P, P, S], BF16, nslots=P)
        gateD = self.gateD
        with ExitStack() as esL:
            LT = k.sbuf("p_LT", [P, 3, S], BF16, nslots=NT, es=esL)
            with ExitStack() as es:
                wq = k.sbuf("p_wq_b", [P, 8, 2048], BF16, es=es)
                skT = k.sbuf("p_skT_b", [P, 16, P], BF16, es=es)
                for c in range(8):
                    self.load_w(wq, wq[:, c, :], I[f"p_wq{l}"][c * P:(c + 1) * P, :], 2048, eng=("dve", "act")[c % 2])
                st = self.stage()
                k.dma("sp", st[:].rearrange("p (g n) -> p g n", g=16), I[f"p_skT{l}"].rearrange("g d n -> d g n"), writes=[st])
                k.op("dve", lambda: nc.vector.tensor_copy(skT[:].rearrange("p g n -> p (g n)"), st[:]), reads=[st], writes=[skT])
                qTbs = [k.sbuf(f"p_qTb{z}", [P, 16, 256], BF16, es=es) for z in range(2)]
                s_sbs = [Tn(self.stg[z].h, 16, f"p_s{z}") for z in range(2)]
                v16 = k.sbuf("p_v16", [P, 256], F32, nslots=16, es=es)
                i16 = k.sbuf("p_i16", [P, 256], U32, nslots=16, es=es)
                i16f = k.sbuf("p_i16f", [P, 256], F32, es=es)
                best = k.sbuf("p_best", [P, 128], F32, nslots=8, es=es)
                pos = k.sbuf("p_pos", [P, 128], U32, nslots=8, es=es)
                pab = k.sbuf("p_pab", [P, 2, 128], U32, es=es)
                pabf = k.sbuf("p_pabf", [P, 2, 128], F32, es=es)
                sm8 = k.sbuf("p_sm8", [P, 3, 8], F32, es=es)
                ex = k.sbuf("p_ex", [P, 128], F32, es=es)
                Lt = k.sbuf("p_Lt", [P, 3, 128], F32, es=es)
                k.barrier()

                def selfsync():
                    if k.cnt["dve"] > 0:
                        k.wait("dve", (k.sem["dve"], k.cnt["dve"]))

                def tile_scores(tt, t2, qTb):
                    s_sb = s_sbs[tt % 2]
                    for bq in range(4):
                        ps = self.bank()
                        for gg in range(4):
                            g = bq * 4 + gg
                            k.op("pe", lambda ps=ps, g=g, gg=gg: nc.tensor.matmul(
                                ps[:, gg * P:(gg + 1) * P], qTb[:, g, t2 * P:(t2 + 1) * P], skT[:, g, :], start=True, stop=True),
                                 reads=[qTb, skT], writes=[ps], inc=(gg == 3))
                        k.op("act", lambda ps=ps, bq=bq: nc.scalar.copy(s_sb[:, bq * 512:(bq + 1) * 512], ps[:]),
                             reads=[ps], writes=[s_sb.s(*range(bq * 4, bq * 4 + 4))])

                def tile_chain(tt, t2, qTb):
                    s_sb = s_sbs[tt % 2]
                    cand = s_sb
                    work = s_sb
                    G16 = range(16)
                    sg = lambda g: s_sb[:, g * P:(g + 1) * P]
                    va = lambda g: v16[:, g * 16:g * 16 + 8]
                    vb_ = lambda g: v16[:, g * 16 + 8:g * 16 + 16]
                    wk = lambda g: work[:, g * P:(g + 1) * P]
                    for g in G16:
                        k.op("dve", lambda g=g: nc.vector.max(out=va(g), in_=sg(g)), reads=[s_sb.s(g)], writes=[v16.s(g)])
                    selfsync()
                    for g in G16:
                        k.op("dve", lambda g=g: nc.vector.max_index(i16[:, g * 16:g * 16 + 8], va(g), sg(g)), reads=[s_sb.s(g), v16.s(g)], writes=[i16.s(g)])
                    for g in G16:
                        k.op("dve", lambda g=g: nc.vector.match_replace(out=sg(g), in_to_replace=va(g), in_values=sg(g), imm_value=NEG),
                             reads=[s_sb.s(g), v16.s(g)], writes=[s_sb.s(g)])
                    selfsync()
                    for g in G16:
                        k.op("dve", lambda g=g: nc.vector.max(out=vb_(g), in_=sg(g)), reads=[s_sb.s(g)], writes=[v16.s(g)])
                    selfsync()
                    for g in G16:
                        k.op("dve", lambda g=g: nc.vector.max_index(i16[:, g * 16 + 8:g * 16 + 16], vb_(g), sg(g)), reads=[s_sb.s(g), v16.s(g)], writes=[i16.s(g)])
                    k.op("dve", lambda: nc.vector.tensor_copy(i16f[:], i16[:]), reads=[i16], writes=[i16f])
                    v4 = v16[:].rearrange("p (h c a) -> p h c a", h=8, c=2)
                    i4 = i16f[:].rearrange("p (h c a) -> p h c a", h=8, c=2)
                    c4 = cand[:].rearrange("p (h a b) -> p h a b", h=8, a=16)
                    k.op("dve", lambda: nc.vector.tensor_tensor(c4, v4[:, :, 0, :].unsqueeze(3).to_broadcast([P, 8, 16, 16]),
                                                                v4[:, :, 1, :].unsqueeze(2).to_broadcast([P, 8, 16, 16]), op=ALU.add),
                         reads=[v16], writes=[cand])
                    H8 = range(8)
                    ch = lambda h: cand[:, h * 256:(h + 1) * 256]
                    cs = lambda h: cand.s(2 * h, 2 * h + 1)
                    ws = lambda h: work.s(2 * h, 2 * h + 1)
                    wh = lambda h: work[:, h * 256:(h + 1) * 256]
                    ba = lambda h: best[:, h * 16:h * 16 + 8]
                    bb = lambda h: best[:, h * 16 + 8:h * 16 + 16]
                    selfsync()
                    for h in H8:
                        k.op("dve", lambda h=h: nc.vector.max(out=ba(h), in_=ch(h)), reads=[cs(h)], writes=[best.s(h)])
                    selfsync()
                    for h in H8:
                        k.op("dve", lambda h=h: nc.vector.max_index(pos[:, h * 16:h * 16 + 8], ba(h), ch(h)), reads=[cs(h), best.s(h)], writes=[pos.s(h)])
                    for h in H8:
                        k.op("dve", lambda h=h: nc.vector.match_replace(out=ch(h), in_to_replace=ba(h), in_values=ch(h), imm_value=NEG),
                             reads=[cs(h), best.s(h)], writes=[cs(h)])
                    selfsync()
                    for h in H8:
                        k.op("dve", lambda h=h: nc.vector.max(out=bb(h), in_=ch(h)), reads=[cs(h)], writes=[best.s(h)])
                    selfsync()
                    for h in H8:
                        k.op("dve", lambda h=h: nc.vector.max_index(pos[:, h * 16 + 8:h * 16 + 16], bb(h), ch(h)), reads=[cs(h), best.s(h)], writes=[pos.s(h)])
                    b3 = best[:].rearrange("p (h r) -> p h r", h=8)
                    e3 = ex[:].rearrange("p (h r) -> p h r", h=8)
                    k.op("dve", lambda: nc.vector.tensor_tensor(e3, b3, b3[:, :, 0:1].to_broadcast([P, 8, 16]), op=ALU.subtract),
                         reads=[best], writes=[ex])
                    k.op("act", lambda: nc.scalar.activation(ex[:], ex[:], AF.Exp), reads=[ex], writes=[ex])
                    k.op("dve", lambda: nc.vector.tensor_single_scalar(pab[:, 0, :], pos[:], 4, op=ALU.logical_shift_right), reads=[pos], writes=[pab])
                    k.op("dve", lambda: nc.vector.tensor_single_scalar(pab[:, 1, :], pos[:], 15, op=ALU.bitwise_and), reads=[pos], writes=[pab])
                    k.op("dve", lambda: nc.vector.tensor_copy(pabf[:], pab[:]), reads=[pab], writes=[pabf])
                    k.op("dve", lambda: nc.vector.reduce_sum(sm8[:, 0, :], e3, axis=AXX), reads=[ex], writes=[sm8])
                    k.op("dve", lambda: nc.vector.reciprocal(sm8[:, 1, :], sm8[:, 0, :]), reads=[sm8], writes=[sm8])
                    k.op("dve", lambda: nc.vector.tensor_tensor(Lt[:, 0, :].rearrange("p (h r) -> p h r", h=8), e3,
                                                                sm8[:, 1, :].unsqueeze(2).to_broadcast([P, 8, 16]), op=ALU.mult),
                         reads=[ex, sm8], writes=[Lt])
                    for w_ in range(2):
                        abf = pabf[:, w_, :].rearrange("p (h r) -> p h r", h=8)
                        k.op("dve", lambda abf=abf: nc.vector.tensor_tensor(
                            c4, self.iota_f[:, 0:16].unsqueeze(1).unsqueeze(1).to_broadcast([P, 8, 16, 16]),
                            abf.unsqueeze(3).to_broadcast([P, 8, 16, 16]), op=ALU.is_equal),
                             reads=[self.iota_f, pabf], writes=[cand])
                        k.op("dve", lambda w_=w_: nc.vector.tensor_tensor(c4, c4, i4[:, :, w_, :].unsqueeze(2).to_broadcast([P, 8, 16, 16]), op=ALU.mult),
                             reads=[i16f, cand], writes=[cand])
                        k.op("dve", lambda w_=w_: nc.vector.reduce_sum(Lt[:, 1 + w_, :].rearrange("p (h r) -> p h r", h=8), c4, axis=AXX),
                             reads=[cand], writes=[Lt])
                    ps = self.bank()
                    for q3 in range(3):
                        k.op("pe", lambda q3=q3, ps=ps: nc.tensor.transpose(ps[:, q3 * P:(q3 + 1) * P], Lt[:, q3, :], self.ident_f[:]),
                             reads=[Lt, self.ident_f], writes=[ps], inc=(q3 == 2))
                    k.op("act", lambda ps=ps, tt=tt: nc.scalar.copy(LT[:, :, tt * P:(tt + 1) * P], ps[:, 0:384].rearrange("p (q t) -> p q t", q=3)),
                         reads=[ps], writes=[LT.s(tt)])

                def qproj(tb):
                    tsl = slice(tb * 256, (tb + 1) * 256)
                    tslots = (2 * tb, 2 * tb + 1)
                    qTb = qTbs[tb % 2]
                    for g in range(16):
                        ps = self.bank()
                        for dc in range(8):
                            k.op("pe", lambda dc=dc, g=g, ps=ps: nc.tensor.matmul(ps[:, 0:256], wq[:, dc, g * P:(g + 1) * P], xT[:, dc, tsl],
                                                                                   start=(dc == 0), stop=(dc == 7)),
                                 reads=[wq, xT.s(*tslots)], writes=[ps], inc=(dc == 7))
                        k.op("act", lambda g=g, ps=ps, qTb=qTb: nc.scalar.copy(qTb[:, g, :], ps[:, 0:256]), reads=[ps], writes=[qTb])

                qproj(0)
                tile_scores(0, 0, qTbs[0])
                for tt in range(NT):
                    tb, t2 = tt // 2, tt % 2
                    if t2 == 0 and tb + 1 < 8:
                        qproj(tb + 1)
                    if tt + 1 < NT:
                        tile_scores(tt + 1, (tt + 1) % 2, qTbs[((tt + 1) // 2) % 2])
                    tile_chain(tt, t2, qTbs[tb % 2])
                k.barrier()
            self.dbg("LT", LT, [P, 3, S], BF16)
            with ExitStack() as es:
                TBK = 8
                Ap = [k.sbuf(f"p_Ap{i}", [P, TBK, P], BF16, es=es) for i in range(2)]
                Bp = [k.sbuf(f"p_Bp{i}", [P, TBK, P], BF16, es=es) for i in range(2)]
                gt = [k.sbuf(f"p_gt{i}", [P, P, P], BF16, es=es) for i in range(2)]
                if ACT_SHARE:
                    nLj = Tn(self.stg[0].h, 1, "p_nLj")
                    one_t = k.sbuf("p_one", [P, 1], F32, es=es)
                    abt = [Tn(self.stg[1][:, i * P:(i + 1) * P], 1, f"p_abt{i}") for i in range(2)]
                    k.op("pool", lambda: nc.gpsimd.memset(one_t[:], 1.0), writes=[one_t])
                    k.op("pool", lambda: nc.gpsimd.tensor_scalar(nLj[:], LT[:, 2, :], -1.0, None, op0=ALU.mult), reads=[LT], writes=[nLj])
                abi = 0
                ev_i = 0
                for tt in range(NT):
                    g_ = gt[tt % 2]
                    for sub in range(P // TBK):
                        t0 = tt * P + sub * TBK
                        A_ = Ap[sub % 2]
                        B_ = Bp[sub % 2]
                        if BATCH_B:
                            k.op("dve", lambda B_=B_, t0=t0: nc.vector.tensor_tensor(
                                B_[:], self.iota_b[:].unsqueeze(1).to_broadcast([P, TBK, P]),
                                LT[:, 2, t0:t0 + TBK].unsqueeze(2).to_broadcast([P, TBK, P]), op=ALU.is_equal),
                                 reads=[self.iota_b, LT.s(tt)], writes=[B_])
                        for tl_ in range(TBK):
                            t_g = t0 + tl_
                            k.op("dve", lambda A_=A_, tl_=tl_, t_g=t_g: nc.vector.tensor_scalar(
                                A_[:, tl_, :], self.iota_b[:], LT[:, 1, t_g:t_g + 1], LT[:, 0, t_g:t_g + 1], op0=ALU.is_equal, op1=ALU.mult),
                                 reads=[self.iota_b, LT.s(tt)], writes=[A_])
                            if ACT_SHARE and tl_ < ACT_SHARE:
                                ab = abt[abi % 2]; abi += 1
                                k.op("act", lambda ab=ab, t_g=t_g: nc.scalar.activation(ab[:], self.iota_f[:], AF.Abs, bias=nLj[:, t_g:t_g + 1], scale=1.0),
                                     reads=[self.iota_f, nLj], writes=[ab])
                                k.op("act", lambda ab=ab, B_=B_, tl_=tl_: nc.scalar.activation(B_[:, tl_, :], ab[:], AF.Relu, bias=one_t[:, 0:1], scale=-1.0),
                                     reads=[ab, one_t], writes=[B_])
                            elif not BATCH_B:
                                k.op("dve", lambda B_=B_, tl_=tl_, t_g=t_g: nc.vector.tensor_scalar(
                                    B_[:, tl_, :], self.iota_b[:], LT[:, 2, t_g:t_g + 1], None, op0=ALU.is_equal),
                                     reads=[self.iota_b, LT.s(tt)], writes=[B_])
                        for q4 in range(TBK // 4):
                            ps = self.bank()
                            for tl in range(4):
                                t_ = q4 * 4 + tl
                                k.op("pe", lambda ps=ps, tl=tl, t_=t_, A_=A_, B_=B_: nc.tensor.matmul(
                                    ps[:, tl * P:(tl + 1) * P], B_[:, t_, :], A_[:, t_, :], start=True, stop=True),
                                     reads=[A_, B_], writes=[ps], inc=(tl == 3))
                            c_ = sub * TBK + q4 * 4
                            dst = g_[:, :, c_:c_ + 4]
                            src = ps[:].rearrange("p (t i) -> p i t", t=4)
                            k.op("act", lambda dst=dst, src=src: nc.scalar.copy(dst, src), reads=[ps], writes=[g_])
                            ev_i += 1
                    for i8 in range(8):
                        k.dma("sp", gateD[i8 * 16:(i8 + 1) * 16, :, tt * P:(tt + 1) * P].rearrange("i j t -> j i t"),
                              g_[:, i8 * 16:(i8 + 1) * 16, :], reads=[g_], writes=[gateD.s(*range(i8 * 16, (i8 + 1) * 16))])
                k.barrier()
        with ExitStack() as es:
            G = 4
            hg = [k.sbuf(f"p_hg{i}", [P, S], BF16, es=es) for i in range(2 * G)]
            wub = [k.sbuf(f"p_wub{i}", [P, D], BF16, es=es) for i in range(2 * G)]
            wdb = [k.sbuf(f"p_wdb{i}", [P, D], BF16, es=es) for i in range(2)]
            wst = [k.sbuf(f"p_wst{i}", [P, D], F32, es=es) for i in range(4)]
            ge = [k.sbuf(f"p_ge{i}", [P, 512], BF16, es=es) for i in range(3)]
            gei = 0
            abank = 0
            ybank = 0
            wsi = 0
            for i in range(P):
                slot = i % (2 * G)
                h_ = hg[slot]
                wu_ = wub[slot]
                wd_ = wdb[i % 2]
                s1 = wst[wsi % 4]; wsi += 1
                s2 = wst[wsi % 4]; wsi += 1
                k.dma("sp", s1[:], I[f"p_wdT{l}"][i], writes=[s1])
                k.dma("sp", s2[:], I[f"p_wu{l}"][i * P:(i + 1) * P, :], writes=[s2])
                k.dma("sp", h_[:], gateD[i], reads=[gateD.s(i)], writes=[h_])
                k.op("pool", lambda wd_=wd_, s1=s1: nc.gpsimd.tensor_copy(wd_[:], s1[:]), reads=[s1], writes=[wd_])
                k.op("pool", lambda wu_=wu_, s2=s2: nc.gpsimd.tensor_copy(wu_[:], s2[:]), reads=[s2], writes=[wu_])
                for nb in range(4):
                    ps = self.pb[abank % 4]; abank += 1
                    for dc in range(8):
                        k.op("pe", lambda ps=ps, dc=dc, nb=nb, wd_=wd_: nc.tensor.matmul(
                            ps[:], wd_[:, dc * P:(dc + 1) * P], xT[:, dc, nb * 512:(nb + 1) * 512], start=(dc == 0), stop=(dc == 7)),
                             reads=[wd_, xT.s(*range(nb * 4, nb * 4 + 4))], writes=[ps], inc=(dc == 7))
                    g_ = ge[gei % 3]; gei += 1
                    k.op("act", lambda ps=ps, g_=g_: nc.scalar.activation(g_[:], ps[:], AF.Gelu), reads=[ps], writes=[g_])
                    k.op("dve", lambda g_=g_, h_=h_, nb=nb: nc.vector.tensor_tensor(
                        h_[:, nb * 512:(nb + 1) * 512], h_[:, nb * 512:(nb + 1) * 512], g_[:], op=ALU.mult),
                         reads=[g_, h_], writes=[h_])
                if i % G == G - 1:
                    base = slot - (G - 1)
                    for tt in range(NT):
                        for half in range(2):
                            ps = self.pb[4 + ybank % 3]; ybank += 1
                            for gg in range(G):
                                k.op("pe", lambda ps=ps, gg=gg, tt=tt, half=half, base=base: nc.tensor.matmul(
                                    ps[:], hg[base + gg][:, tt * P:(tt + 1) * P], wub[base + gg][:, half * 512:(half + 1) * 512],
                                    start=(gg == 0), stop=(gg == G - 1)),
                                     reads=[hg[base + gg], wub[base + gg]], writes=[ps], inc=(gg == G - 1))
                            xs = X[:, tt, half * 512:(half + 1) * 512]
                            if i == G - 1:
                                k.op("dve", lambda ps=ps, xs=xs: nc.vector.scalar_tensor_tensor(xs, xs, ALPHA, ps[:], op0=ALU.mult, op1=ALU.add),
                                     reads=[ps, X.s(tt)], writes=[X.s(tt)])
                            else:
                                k.op("dve", lambda ps=ps, xs=xs: nc.vector.tensor_tensor(xs, xs, ps[:], op=ALU.add),
                                     reads=[ps, X.s(tt)], writes=[X.s(tt)])
            self.ln_params(li, es)
            for tt in range(NT):
                self.ln_tile(tt, li)
            k.barrier()

def _rope_tab(dim):
    inv = (np.float32(10000.0) ** (-np.arange(0, dim, 2, dtype=np.float32) / np.float32(dim))).astype(np.float32)
    ang = (np.arange(S, dtype=np.float32)[:, None] * inv[None, :]).astype(np.float32)
    c = np.cos(ang).astype(np.float32).T
    s = np.sin(ang).astype(np.float32).T
    half = dim // 2
    reps = P // dim
    cos = np.concatenate([c, c] * reps, axis=0)
    sin = np.concatenate([-s, s] * reps, axis=0)
    return np.ascontiguousarray(cos), np.ascontiguousarray(sin)


def prep_shared(inp):
    f = np.float32
    sh = {}
    sh["ident"] = np.eye(P, dtype=f)
    sh["cos64"], sh["sin64"] = _rope_tab(64)
    sh["cos128"], sh["sin128"] = _rope_tab(128)
    kk = np.arange(P)[:, None]
    qq = np.arange(P)[None, :]
    sh["maskT"] = np.where((kk >= 64) & (qq < 64), 0.0, 1.0).astype(f)
    sh["negm"] = np.where((kk < 64) & (qq >= 64), NEG, 0.0).astype(f)
    sh["iota"] = np.broadcast_to(np.arange(P, dtype=f)[None, :], (P, P)).copy()
    w_in = inp["mla_w_in"][0]
    kr = w_in[:, 640:704]
    kr_sw = np.concatenate([kr[:, 32:64], kr[:, 0:32]], axis=1)
    sh["m_win"] = np.ascontiguousarray(np.concatenate([w_in[:, :640], kr, kr, kr_sw, kr_sw], axis=1))
    wuq = inp["mla_w_uq"][0]
    nope = wuq[:, :, :128].reshape(384, 1024)
    rp = wuq[:, :, 128:192]
    rp_sw = np.concatenate([rp[:, :, 32:64], rp[:, :, 0:32]], axis=2)
    sh["m_wuq"] = np.ascontiguousarray(np.concatenate([nope, rp.reshape(384, 512), rp_sw.reshape(384, 512)], axis=1))
    wukv = inp["mla_w_ukv"][0]
    sh["m_wukv"] = np.ascontiguousarray(np.concatenate([wukv[:, :, :128].reshape(256, 1024), wukv[:, :, 128:].reshape(256, 1024)], axis=1))
    sh["m_wo"] = np.ascontiguousarray(inp["mla_w_o"][0])
    sh["m_qn"] = np.ascontiguousarray(inp["mla_q_norm"][0].reshape(3, P).T)
    sh["m_kvn"] = np.ascontiguousarray(inp["mla_kv_norm"][0].reshape(2, P).T)
    dw = inp["dsa_w_in"][0]

    def sw(w, hd):
        n = w.shape[1] // hd
        w3 = w.reshape(w.shape[0], n, hd)
        return np.concatenate([w3[:, :, hd // 2:], w3[:, :, :hd // 2]], axis=2).reshape(w.shape[0], n * hd)

    q, kx, v = dw[:, 0:1024], dw[:, 1024:2048], dw[:, 2048:3072]
    qi, ki, wi = dw[:, 3072:3584], dw[:, 3584:3648], dw[:, 3648:3656]
    cols = []
    for h in range(8):
        cols += [q[:, h * P:(h + 1) * P], sw(q[:, h * P:(h + 1) * P], P)]
    for h in range(8):
        cols += [kx[:, h * P:(h + 1) * P], sw(kx[:, h * P:(h + 1) * P], P)]
    qis = sw(qi, 64)
    for pr in range(4):
        cols += [qi[:, pr * P:(pr + 1) * P], qis[:, pr * P:(pr + 1) * P]]
    kis = sw(ki, 64)
    cols += [ki, ki, kis, kis]
    cols += [v, wi]
    sh["d_win"] = np.ascontiguousarray(np.concatenate(cols, axis=1))
    assert sh["d_win"].shape[1] == 6408
    sh["d_wo"] = np.ascontiguousarray(inp["dsa_w_o"][0])
    for l in range(2):
        sh[f"p_wq{l}"] = np.ascontiguousarray(inp["peer_w_q"][l])
        sk = inp["peer_sub_keys"][l]
        sh[f"p_skT{l}"] = np.ascontiguousarray(sk.transpose(1, 0, 3, 2).reshape(16, P, P))
        wd = inp["peer_w_down"][l]
        sh[f"p_wdT{l}"] = np.ascontiguousarray(wd.reshape(P, P, 8, P).transpose(0, 3, 2, 1).reshape(P, P, 8 * P))
        sh[f"p_wu{l}"] = np.ascontiguousarray(inp["peer_w_up"][l])
    sh["ln_g"] = np.ascontiguousarray(inp["ln_gain"].reshape(4, D))
    sh["ln_b"] = np.ascontiguousarray(inp["ln_bias"].reshape(4, D))
    return sh


def kernel(**inputs):
    inp = {k_: np.asarray(v) for k_, v in inputs.items()}
    sh = prep_shared(inp)
    prog = Prog()
    nc = prog.build()
    x = np.ascontiguousarray(inp["x"], dtype=np.float32)
    in_maps = []
    for c in range(8):
        m = dict(sh)
        m["x"] = x[c]
        in_maps.append(m)
    res = run_bass_kernel_spmd(nc, in_maps, core_ids=list(range(8)))
    return np.stack([res.results[c]["out"] for c in range(8)], axis=0).astype(np.float32)
```

```python
from contextlib import ExitStack
import math
import numpy as np
import concourse.bass as bass
import concourse.mybir as mybir
from concourse.bass_utils import run_bass_kernel_spmd

F32 = mybir.dt.float32
BF16 = mybir.dt.bfloat16
U32 = mybir.dt.uint32
AF = mybir.ActivationFunctionType
ALU = mybir.AluOpType

S = 2048
D = 1024
NT = 16
P = 128
ALPHA = float((2 * 2) ** 0.25)
LN_EPS = 1e-5
RMS_EPS = 1e-6
NEG = -1.0e30
BATCH_B = False
ACT_SHARE = 0


class Tn:
    def __init__(self, h, nslots=1, name=""):
        self.h = h
        self.n = nslots
        self.name = name
        self.lw = [None] * nslots
        self.rd = [dict() for _ in range(nslots)]

    def __getitem__(self, key):
        return self.h[key]

    def s(self, *slots):
        return (self, slots)


def _norm(acc):
    if isinstance(acc, Tn):
        return acc, range(acc.n)
    return acc


class KB:
    def __init__(self, nc, es):
        self.nc = nc
        self.es = es
        self.eng = {"pe": nc.tensor, "act": nc.scalar, "dve": nc.vector, "pool": nc.gpsimd, "sp": nc.sync}
        self.sem = {}
        self.cnt = {}
        for e in ("pe", "act", "dve", "pool"):
            self.sem[e] = es.enter_context(nc.semaphore("c_" + e))
            self.cnt[e] = 0
        self.known = {e: {} for e in self.eng}
        self.rings = {}
        self.dma_n = {}
        for q, n in {"sp": 24, "pool": 8, "act": 8}.items():
            self.rings[q] = [es.enter_context(nc.semaphore(f"r_{q}{i}")) for i in range(n)]
            self.dma_n[q] = 0

    def wait(self, e, ev):
        sem, val = ev
        kk = id(sem)
        if self.known[e].get(kk, 0) >= val:
            return
        self.eng[e].wait_ge(sem, val)
        self.known[e][kk] = val

    def _deps(self, reads, writes):
        evs = []
        for acc in reads:
            t, sl = _norm(acc)
            for s in sl:
                if t.lw[s] is not None:
                    evs.append(t.lw[s])
        for acc in writes:
            t, sl = _norm(acc)
            for s in sl:
                if t.lw[s] is not None:
                    evs.append(t.lw[s])
                evs.extend(t.rd[s].values())
        return evs

    def _record(self, ev, reads, writes):
        sem, val = ev
        kk = id(sem)
        for acc in reads:
            t, sl = _norm(acc)
            for s in sl:
                d = t.rd[s]
                if kk not in d or d[kk][1] < val:
                    d[kk] = ev
        for acc in writes:
            t, sl = _norm(acc)
            for s in sl:
                t.lw[s] = ev
                t.rd[s] = dict()

    def op(self, e, fn, reads=(), writes=(), inc=True):
        own = self.sem[e]
        for ev in self._deps(reads, writes):
            if ev[0] is own and e == "pe":
                continue
            self.wait(e, ev)
        ins = fn()
        if inc:
            self.cnt[e] += 1
            ins.then_inc(own, 1)
            ev = (own, self.cnt[e])
        else:
            assert e == "pe"
            ev = (own, self.cnt[e] + 1)
        self._record(ev, reads, writes)
        return ev

    def dma(self, q, out, in_, reads=(), writes=(), **kw):
        n = self.dma_n[q]
        ring = self.rings[q]
        K = len(ring)
        slot = n % K
        if n >= K:
            self.wait(q, (ring[slot], 16 * (n // K)))
        for ev in self._deps(reads, writes):
            self.wait(q, ev)
        self.eng[q].dma_start(out=out, in_=in_, **kw).then_inc(ring[slot], 16)
        self.dma_n[q] = n + 1
        ev = (ring[slot], 16 * (n // K + 1))
        self._record(ev, reads, writes)
        return ev

    def all_events(self):
        evs = []
        for e in ("pe", "act", "dve", "pool"):
            if self.cnt[e] > 0:
                evs.append((self.sem[e], self.cnt[e]))
        for q, ring in self.rings.items():
            n = self.dma_n[q]
            K = len(ring)
            for slot in range(min(n, K)):
                last = ((n - 1 - slot) // K) * K + slot
                evs.append((ring[slot], 16 * (last // K + 1)))
        return evs

    def barrier(self):
        evs = self.all_events()
        for e in self.eng:
            for ev in evs:
                self.wait(e, ev)

    def sbuf(self, name, shape, dt, nslots=1, es=None):
        self.uid = getattr(self, "uid", 0) + 1
        name = f"{name}_{self.uid}"
        h = (es or self.es).enter_context(self.nc.sbuf_tensor(name, list(shape), dt))
        return Tn(h, nslots, name)

    def psum(self, name, shape, dt, nslots=1, es=None):
        h = (es or self.es).enter_context(self.nc.psum_tensor(name, list(shape), dt))
        return Tn(h, nslots, name)

    def dram(self, name, shape, dt, nslots=1):
        h = self.nc.dram_tensor(name, list(shape), dt).ap()
        return Tn(h, nslots, name)


class Prog:
    def __init__(self, debug=()):
        self.debug = set(debug)
        self.nc = bass.Bass("TRN2", target_bir_lowering=False)
        self.dbg_out = {}

    def din(self, name, shape, dt=F32):
        return self.nc.dram_tensor(name, list(shape), dt, kind="ExternalInput").ap()

    def build(self, stages=("mla", "peer0", "dsa", "peer1")):
        nc = self.nc
        I = {}
        I["x"] = self.din("x", [S, D])
        I["ident"] = self.din("ident", [P, P])
        I["cos64"] = self.din("cos64", [P, S])
        I["sin64"] = self.din("sin64", [P, S])
        I["cos128"] = self.din("cos128", [P, S])
        I["sin128"] = self.din("sin128", [P, S])
        I["maskT"] = self.din("maskT", [P, P])
        I["negm"] = self.din("negm", [P, P])
        I["iota"] = self.din("iota", [P, P])
        I["m_win"] = self.din("m_win", [D, 896])
        I["m_wuq"] = self.din("m_wuq", [384, 2048])
        I["m_wukv"] = self.din("m_wukv", [256, 2048])
        I["m_wo"] = self.din("m_wo", [D, D])
        I["m_qn"] = self.din("m_qn", [P, 3])
        I["m_kvn"] = self.din("m_kvn", [P, 2])
        I["d_win"] = self.din("d_win", [D, 6408])
        I["d_wo"] = self.din("d_wo", [D, D])
        for l in range(2):
            I[f"p_wq{l}"] = self.din(f"p_wq{l}", [D, 2048])
            I[f"p_skT{l}"] = self.din(f"p_skT{l}", [16, P, P])
            I[f"p_wdT{l}"] = self.din(f"p_wdT{l}", [P, P, 8 * P])
            I[f"p_wu{l}"] = self.din(f"p_wu{l}", [P * P, D])
        I["ln_g"] = self.din("ln_g", [4, D])
        I["ln_b"] = self.din("ln_b", [4, D])
        self.I = I
        self.out = nc.dram_tensor("out", [S, D], F32, kind="ExternalOutput").ap()

        with ExitStack() as es:
            k = KB(nc, es)
            self.k = k
            self.setup_common(es)
            self.load_x()
            li = 0
            for st in stages:
                if st == "mla":
                    self.mla_layer()
                    self.mixer_out_ln(I["m_wo"], 0)
                elif st == "dsa":
                    self.dsa_layer()
                    self.mixer_out_ln(I["d_wo"], 2)
                    self.dsa_es.close()
                elif st.startswith("peer"):
                    l = int(st[4:])
                    self.peer_layer(l, 2 * l + 1)
            self.store_out()
            k.barrier()
        return nc

    def setup_common(self, es):
        k, nc, I = self.k, self.nc, self.I
        self.X = k.sbuf("X", [P, NT, D], F32, nslots=NT)
        self.xT = k.sbuf("xT", [P, 8, S], BF16, nslots=NT)
        self.ident_f = k.sbuf("ident_f", [P, P], F32)
        self.ident_b = k.sbuf("ident_b", [P, P], BF16)
        self.ones_b = k.sbuf("ones_b", [P, P], BF16)
        self.maskT_b = k.sbuf("maskT_b", [P, P], BF16)
        self.negm = k.sbuf("negm_s", [P, P], F32)
        self.iota_b = k.sbuf("iota_b", [P, P], BF16)
        self.iota_f = k.sbuf("iota_f", [P, P], F32)
        self.eps_ln = k.sbuf("eps_ln", [P, 1], F32)
        self.eps_rms = k.sbuf("eps_rms", [P, 1], F32)
        self.thr_c = k.sbuf("thr_c", [P, 1], F32)
        self.stg = [k.sbuf(f"stg{i}", [P, 2048], F32) for i in range(2)]
        self.stg_i = 0
        self.pb = [k.psum(f"pb{i}", [P, 512], F32) for i in range(7)]
        self.pb_i = 0
        self.ptb = k.psum("ptb", [P, 1024], BF16)
        k.dma("sp", self.ident_f[:], I["ident"], writes=[self.ident_f])
        k.op("dve", lambda: nc.vector.tensor_copy(self.ident_b[:], self.ident_f[:]), reads=[self.ident_f], writes=[self.ident_b])
        k.op("pool", lambda: nc.gpsimd.memset(self.ones_b[:], 1.0), writes=[self.ones_b])
        k.op("pool", lambda: nc.gpsimd.memset(self.eps_ln[:], LN_EPS), writes=[self.eps_ln])
        k.op("pool", lambda: nc.gpsimd.memset(self.eps_rms[:], RMS_EPS), writes=[self.eps_rms])
        k.op("pool", lambda: nc.gpsimd.memset(self.thr_c[:], -1.0e29), writes=[self.thr_c])
        st = self.stage()
        k.dma("sp", st[:, 0:P], I["maskT"], writes=[st])
        k.op("dve", lambda: nc.vector.tensor_copy(self.maskT_b[:], st[:, 0:P]), reads=[st], writes=[self.maskT_b])
        k.dma("sp", self.negm[:], I["negm"], writes=[self.negm])
        k.dma("sp", self.iota_f[:], I["iota"], writes=[self.iota_f])
        k.op("dve", lambda: nc.vector.tensor_copy(self.iota_b[:], self.iota_f[:]), reads=[self.iota_f], writes=[self.iota_b])
        self.ln_st = k.sbuf("ln_st", [P, NT, 12], F32, nslots=NT)
        self.ln_mv = k.sbuf("ln_mv", [P, NT, 4], F32, nslots=NT)
        self.xbf = [k.sbuf(f"xbf{i}", [P, D], BF16) for i in range(2)]
        self.qnD = k.dram("qnD", [8, P, S], BF16, nslots=8)
        self.knD = k.dram("knD", [8, P, S], BF16, nslots=8)
        self.qrD = k.dram("qrD", [8, 64, S], BF16, nslots=8)
        self.krD = k.dram("krD", [P, S], BF16)
        self.VD = k.dram("VD", [NT, P, D], BF16, nslots=NT)
        self.qiD = k.dram("qiD", [4, P, S], BF16, nslots=4)
        self.kiD = k.dram("kiD", [P, S], BF16)

    def stage(self):
        s = self.stg[self.stg_i % len(self.stg)]
        self.stg_i += 1
        return s

    def bank(self):
        b = self.pb[self.pb_i % len(self.pb)]
        self.pb_i += 1
        return b

    def dbg(self, name, t, shape, dt=F32, ap=None):
        if name not in self.debug:
            return
        o = self.nc.dram_tensor("dbg_" + name, list(shape), dt, kind="ExternalOutput").ap()
        self.k.dma("sp", o, ap if ap is not None else t[:], reads=[t])
        self.dbg_out[name] = (shape, dt)

    def make_xT(self, tt):
        k, nc = self.k, self.nc
        xb = self.xbf[tt % 2]
        k.op("act", lambda: nc.scalar.copy(xb[:], self.X[:, tt, :]), reads=[self.X.s(tt)], writes=[xb])
        for c in range(8):
            k.op("pe", lambda c=c: nc.tensor.transpose(self.ptb[:, c * P:(c + 1) * P], xb[:, c * P:(c + 1) * P], self.ident_b[:]),
                 reads=[xb, self.ident_b], writes=[self.ptb], inc=(c == 7))
        k.op("dve", lambda: nc.vector.tensor_copy(self.xT[:, :, tt * P:(tt + 1) * P],
                                                  self.ptb[:].rearrange("p (c t) -> p c t", c=8)),
             reads=[self.ptb], writes=[self.xT.s(tt)])

    def load_x(self):
        k, nc = self.k, self.nc
        for tt in range(NT):
            k.dma("sp", self.X[:, tt, :], self.I["x"][tt * P:(tt + 1) * P, :], writes=[self.X.s(tt)])
            self.make_xT(tt)

    def store_out(self):
        k = self.k
        for tt in range(NT):
            k.dma("sp", self.out[tt * P:(tt + 1) * P, :], self.X[:, tt, :], reads=[self.X.s(tt)])

    def ln_params(self, li, es):
        k, I = self.k, self.I
        self.lng = k.sbuf("lng", [P, D], F32, es=es)
        self.lnb = k.sbuf("lnb", [P, D], F32, es=es)
        k.dma("sp", self.lng[:], I["ln_g"][li:li + 1, :].partition_broadcast(P), writes=[self.lng])
        k.dma("sp", self.lnb[:], I["ln_b"][li:li + 1, :].partition_broadcast(P), writes=[self.lnb])

    def ln_tile(self, tt, li):
        k, nc = self.k, self.nc
        X = self.X
        xt = X[:, tt, :]
        st = self.ln_st
        mv = self.ln_mv
        k.op("dve", lambda: nc.vector.bn_stats(st[:, tt, 0:6], X[:, tt, 0:512]), reads=[X.s(tt)], writes=[st.s(tt)])
        k.op("dve", lambda: nc.vector.bn_stats(st[:, tt, 6:12], X[:, tt, 512:1024]), reads=[X.s(tt)], writes=[st.s(tt)])
        k.op("dve", lambda: nc.vector.bn_aggr(mv[:, tt, 0:2], st[:, tt, :].rearrange("p (a b) -> p a b", a=2)),
             reads=[st.s(tt)], writes=[mv.s(tt)])
        k.op("act", lambda: nc.scalar.activation(mv[:, tt, 2:3], mv[:, tt, 1:2], AF.Sqrt, bias=self.eps_ln[:, 0:1], scale=1.0),
             reads=[mv.s(tt), self.eps_ln], writes=[mv.s(tt)])
        k.op("dve", lambda: nc.vector.reciprocal(mv[:, tt, 2:3], mv[:, tt, 2:3]), reads=[mv.s(tt)], writes=[mv.s(tt)])
        k.op("dve", lambda: nc.vector.scalar_tensor_tensor(mv[:, tt, 3:4], mv[:, tt, 0:1], -1.0, mv[:, tt, 2:3], op0=ALU.mult, op1=ALU.mult),
             reads=[mv.s(tt)], writes=[mv.s(tt)])
        k.op("act", lambda: nc.scalar.activation(xt, xt, AF.Identity, bias=mv[:, tt, 3:4], scale=mv[:, tt, 2:3]),
             reads=[X.s(tt), mv.s(tt)], writes=[X.s(tt)])
        k.op("dve", lambda: nc.vector.tensor_tensor(xt, xt, self.lng[:], op=ALU.mult), reads=[X.s(tt), self.lng], writes=[X.s(tt)])
        k.op("dve", lambda: nc.vector.tensor_tensor(xt, xt, self.lnb[:], op=ALU.add), reads=[X.s(tt), self.lnb], writes=[X.s(tt)])
        self.make_xT(tt)

    def load_w(self, dst, dst_ap, src_ap, ncols, eng="pool", scale=None, dst_slots=None):
        k, nc = self.k, self.nc
        st = self.stage()
        w = [dst] if dst_slots is None else [dst.s(*dst_slots)]
        k.dma("sp", st[:, 0:ncols], src_ap, writes=[st])
        if scale is not None:
            k.op("dve", lambda: nc.vector.tensor_scalar(dst_ap, st[:, 0:ncols], scale[1], None, op0=ALU.mult), reads=[st, scale[0]], writes=w)
        elif eng == "pool":
            k.op("pool", lambda: nc.gpsimd.tensor_copy(dst_ap, st[:, 0:ncols]), reads=[st], writes=w)
        elif eng == "act":
            k.op("act", lambda: nc.scalar.copy(dst_ap, st[:, 0:ncols]), reads=[st], writes=w)
        else:
            k.op("dve", lambda: nc.vector.tensor_copy(dst_ap, st[:, 0:ncols]), reads=[st], writes=w)

    def rope(self, ps_n, ps_s, cos, sin, tb, out_t, out_ap, es_tmp):
        k, nc = self.k, self.nc
        t1, t2 = es_tmp
        sl = slice(tb * 512, (tb + 1) * 512)
        k.op("dve", lambda: nc.vector.tensor_tensor(t1[:], ps_n[:], cos[:, sl], op=ALU.mult), reads=[ps_n, cos], writes=[t1])
        k.op("dve", lambda: nc.vector.tensor_tensor(t2[:], ps_s[:], sin[:, sl], op=ALU.mult), reads=[ps_s, sin], writes=[t2])
        k.op("dve", lambda: nc.vector.tensor_tensor(out_ap, t1[:], t2[:], op=ALU.add), reads=[t1, t2], writes=[out_t])

    def mla_layer(self):
        k, nc, I = self.k, self.nc, self.I
        xT = self.xT
        with ExitStack() as es:
            win = k.sbuf("m_win_b", [P, 8, 896], BF16, es=es)
            wuq = k.sbuf("m_wuq_b", [P, 3, 2048], BF16, es=es)
            wukv = k.sbuf("m_wukv_b", [P, 2, 2048], BF16, es=es)
            qn = k.sbuf("m_qn_s", [P, 3], F32, es=es)
            kvn = k.sbuf("m_kvn_s", [P, 2], F32, es=es)
            cos = k.sbuf("cos64_s", [P, S], F32, es=es)
            sin = k.sbuf("sin64_s", [P, S], F32, es=es)
            k.dma("sp", qn[:], I["m_qn"], writes=[qn])
            k.dma("sp", kvn[:], I["m_kvn"], writes=[kvn])
            k.dma("sp", cos[:], I["cos64"], writes=[cos])
            k.dma("sp", sin[:], I["sin64"], writes=[sin])
            for c in range(8):
                self.load_w(win, win[:, c, :], I["m_win"][c * P:(c + 1) * P, :], 896, eng=("dve", "act")[c % 2])
            for c in range(3):
                self.load_w(wuq, wuq[:, c, :], I["m_wuq"][c * P:(c + 1) * P, :], 2048, scale=(qn, qn[:, c:c + 1]))
            for c in range(2):
                self.load_w(wukv, wukv[:, c, :], I["m_wukv"][c * P:(c + 1) * P, :], 2048, scale=(kvn, kvn[:, c:c + 1]))
            lat_f = k.sbuf("lat_f", [P, 5, 512], F32, es=es)
            sq_b = k.sbuf("sq_b", [P, 5, 512], BF16, es=es)
            rs = k.sbuf("rs", [P, 2, 512], F32, es=es)
            lat_n = [k.sbuf(f"lat_n{i}", [P, 5, 512], BF16, es=es) for i in range(1)]
            t1 = k.sbuf("rp_t1", [P, 512], F32, es=es)
            t2 = k.sbuf("rp_t2", [P, 512], F32, es=es)
            ob = [k.sbuf(f"m_ob{i}", [P, 512], BF16, es=es) for i in range(4)]
            vb = [k.sbuf(f"m_vb{i}", [P, D], BF16, es=es) for i in range(2)]
            obi = 0
            for tb in range(4):
                tsl = slice(tb * 512, (tb + 1) * 512)
                tslots = tuple(range(tb * 4, tb * 4 + 4))
                ln = lat_n[0]
                for c in range(5):
                    ps = self.bank()
                    for dc in range(8):
                        k.op("pe", lambda dc=dc, c=c, ps=ps: nc.tensor.matmul(ps[:], win[:, dc, c * P:(c + 1) * P], xT[:, dc, tsl],
                                                                               start=(dc == 0), stop=(dc == 7)),
                             reads=[win, xT.s(*tslots)], writes=[ps], inc=(dc == 7))
                    k.op("act", lambda c=c, ps=ps: nc.scalar.copy(lat_f[:, c, :], ps[:]), reads=[ps], writes=[lat_f])
                    k.op("act", lambda c=c, ps=ps: nc.scalar.activation(sq_b[:, c, :], ps[:], AF.Square), reads=[ps], writes=[sq_b])
                for gi, (c0, nch, width) in enumerate(((0, 3, 384), (3, 2, 256))):
                    ps = self.bank()
                    for c in range(nch):
                        k.op("pe", lambda c=c, ps=ps, c0=c0, nch=nch: nc.tensor.matmul(ps[:], self.ones_b[:], sq_b[:, c0 + c, :],
                                                                                        start=(c == 0), stop=(c == nch - 1)),
                             reads=[self.ones_b, sq_b], writes=[ps], inc=(c == nch - 1))
                    k.op("act", lambda ps=ps, gi=gi, width=width: nc.scalar.activation(rs[:, gi, :], ps[:], AF.Sqrt, bias=self.eps_rms[:, 0:1],
                                                                                         scale=1.0 / width),
                         reads=[ps, self.eps_rms], writes=[rs])
                    k.op("dve", lambda gi=gi: nc.vector.reciprocal(rs[:, gi, :], rs[:, gi, :]), reads=[rs], writes=[rs])
                    k.op("dve", lambda gi=gi, c0=c0, nch=nch, ln=ln: nc.vector.tensor_tensor(
                        ln[:, c0:c0 + nch, :], lat_f[:, c0:c0 + nch, :], rs[:, gi, :].unsqueeze(1).to_broadcast([P, nch, 512]), op=ALU.mult),
                         reads=[lat_f, rs], writes=[ln])
                psn = self.bank()
                pss = self.bank()
                for (ps, c0) in ((psn, 640), (pss, 768)):
                    for dc in range(8):
                        k.op("pe", lambda dc=dc, ps=ps, c0=c0: nc.tensor.matmul(ps[:], win[:, dc, c0:c0 + P], xT[:, dc, tsl],
                                                                                 start=(dc == 0), stop=(dc == 7)),
                             reads=[win, xT.s(*tslots)], writes=[ps], inc=(dc == 7))
                o = ob[obi % 4]; obi += 1
                self.rope(psn, pss, cos, sin, tb, o, o[:], (t1, t2))
                k.dma("sp", self.krD[:, tsl], o[:], reads=[o], writes=[self.krD])
                for h in range(8):
                    ps = self.bank()
                    for rc in range(3):
                        k.op("pe", lambda rc=rc, ps=ps, h=h: nc.tensor.matmul(ps[:], wuq[:, rc, h * P:(h + 1) * P], ln[:, rc, :],
                                                                               start=(rc == 0), stop=(rc == 2)),
                             reads=[wuq, ln], writes=[ps], inc=(rc == 2))
                    o = ob[obi % 4]; obi += 1
                    k.op("act", lambda ps=ps, o=o: nc.scalar.copy(o[:], ps[:]), reads=[ps], writes=[o])
                    k.dma("sp", self.qnD[h, :, tsl], o[:], reads=[o], writes=[self.qnD.s(h)])
                for pr in range(4):
                    psn = self.bank()
                    pss = self.bank()
                    for (ps, c0) in ((psn, 1024 + pr * P), (pss, 1536 + pr * P)):
                        for rc in range(3):
                            k.op("pe", lambda rc=rc, ps=ps, c0=c0: nc.tensor.matmul(ps[:], wuq[:, rc, c0:c0 + P], ln[:, rc, :],
                                                                                     start=(rc == 0), stop=(rc == 2)),
                                 reads=[wuq, ln], writes=[ps], inc=(rc == 2))
                    o = ob[obi % 4]; obi += 1
                    self.rope(psn, pss, cos, sin, tb, o, o[:], (t1, t2))
                    k.dma("sp", self.qrD[2 * pr, :, tsl], o[0:64, :], reads=[o], writes=[self.qrD.s(2 * pr)])
                    k.dma("sp", self.qrD[2 * pr + 1, :, tsl], o[64:128, :], reads=[o], writes=[self.qrD.s(2 * pr + 1)])
                for h in range(8):
                    ps = self.bank()
                    for rc in range(2):
                        k.op("pe", lambda rc=rc, ps=ps, h=h: nc.tensor.matmul(ps[:], wukv[:, rc, h * P:(h + 1) * P], ln[:, 3 + rc, :],
                                                                               start=(rc == 0), stop=(rc == 1)),
                             reads=[wukv, ln], writes=[ps], inc=(rc == 1))
                    o = ob[obi % 4]; obi += 1
                    k.op("dve", lambda ps=ps, o=o: nc.vector.tensor_copy(o[:], ps[:]), reads=[ps], writes=[o])
                    k.dma("sp", self.knD[h, :, tsl], o[:], reads=[o], writes=[self.knD.s(h)])
                for t4 in range(4):
                    tt = tb * 4 + t4
                    v = vb[tt % 2]
                    for half in range(2):
                        ps = self.bank()
                        for rc in range(2):
                            k.op("pe", lambda rc=rc, ps=ps, half=half, t4=t4: nc.tensor.matmul(
                                ps[:], ln[:, 3 + rc, t4 * P:(t4 + 1) * P], wukv[:, rc, 1024 + half * 512:1024 + (half + 1) * 512],
                                start=(rc == 0), stop=(rc == 1)),
                                 reads=[wukv, ln], writes=[ps], inc=(rc == 1))
                        if half == 0:
                            k.op("act", lambda ps=ps, v=v: nc.scalar.copy(v[:, 0:512], ps[:]), reads=[ps], writes=[v])
                        else:
                            k.op("dve", lambda ps=ps, v=v: nc.vector.tensor_copy(v[:, 512:1024], ps[:]), reads=[ps], writes=[v])
                    k.dma("sp", self.VD[tt], v[:], reads=[v], writes=[self.VD.s(tt)])
            k.barrier()
        self.attention(mla=True)

    def attention(self, mla, selT=None, selT_t=None):
        k, nc = self.k, self.nc
        scale = (192.0 if mla else 128.0) ** -0.5
        self.att_es = ExitStack()
        es = self.att_es
        OT = k.sbuf("OT", [P, 8, S], BF16, nslots=8, es=es)
        self.OT = OT
        with ExitStack() as es2:
            kn = [Tn(self.stg[i][:].bitcast(BF16), 1, f"a_kn{i}") for i in range(2)]
            qn = [k.sbuf(f"a_qn{i}", [P, S], BF16, es=es2) for i in range(2)]
            vh = [k.sbuf(f"a_vh{i}", [P, NT, P], BF16, es=es2) for i in range(2)]
            pt = [k.sbuf(f"a_pt{i}", [P, 512], BF16, es=es2) for i in range(3)]
            rec = k.sbuf("a_rec", [P, 512], F32, es=es2)
            if mla:
                qr = [k.sbuf(f"a_qr{i}", [64, S], BF16, es=es2) for i in range(2)]
                kr = k.sbuf("a_kr", [64, S], BF16, es=es2)
                k.dma("sp", kr[:], self.krD[0:64, :], reads=[self.krD], writes=[kr])
            def load_head(h):
                b = h % 2
                k.dma("sp", kn[b][:, 0:S], self.knD[h], reads=[self.knD.s(h)], writes=[kn[b]])
                k.dma("sp", qn[b][:], self.qnD[h], reads=[self.qnD.s(h)], writes=[qn[b]])
                k.dma("sp", vh[b][:], self.VD[:, :, h * P:(h + 1) * P].rearrange("t p v -> p t v"), reads=[self.VD], writes=[vh[b]])
                if mla:
                    k.dma("sp", qr[b][:], self.qrD[h], reads=[self.qrD.s(h)], writes=[qr[b]])

            pairs = []
            for h in range(8):
                for QB in range(4):
                    for kc in range(4 * QB + 4):
                        pairs.append((h, QB, kc))
            info = {}

            def emit_qk(i):
                h, QB, kc = pairs[i]
                b = h % 2
                if QB == 0 and kc == 0 and h == 0:
                    load_head(0)
                qlo = max(kc, 4 * QB)
                c0 = (qlo - 4 * QB) * P
                qs = slice(QB * 512 + c0, (QB + 1) * 512)
                ks = slice(kc * P, (kc + 1) * P)
                st = self.pb[i % 3]
                k.op("pe", lambda: nc.tensor.matmul(st[:, c0:512], kn[b][:, ks], qn[b][:, qs], start=True, stop=(not mla)),
                     reads=[kn[b], qn[b]], writes=[st], inc=(not mla))
                if mla:
                    k.op("pe", lambda: nc.tensor.matmul(st[:, c0:512], kr[:, ks], qr[b][:, qs], start=False, stop=True),
                         reads=[kr, qr[b]], writes=[st])
                info[i] = (st, c0, qs)

            def emit_soft(i):
                h, QB, kc = pairs[i]
                st, c0, qs = info[i]
                p_ = pt[i % 3]
                k.op("act", lambda: nc.scalar.activation(p_[:, c0:512], st[:, c0:512], AF.Exp, scale=scale), reads=[st], writes=[p_])
                if mla:
                    if kc >= 4 * QB:
                        k.op("dve", lambda: nc.vector.tensor_tensor(p_[:, c0:c0 + P], p_[:, c0:c0 + P], self.maskT_b[:], op=ALU.mult),
                             reads=[p_, self.maskT_b], writes=[p_])
                else:
                    k.op("dve", lambda: nc.vector.tensor_tensor(p_[:, c0:512], p_[:, c0:512], selT(kc, qs.start, qs.stop), op=ALU.mult),
                         reads=[p_, selT_t], writes=[p_])

            def emit_pv(i):
                h, QB, kc = pairs[i]
                b = h % 2
                st, c0, qs = info.pop(i)
                p_ = pt[i % 3]
                oT = self.pb[3 + QB % 2]
                sm = self.pb[5 + QB % 2]
                last = 4 * QB + 3
                if QB == 0 and kc == 0 and h + 1 < 8:
                    load_head(h + 1)
                k.op("pe", lambda: nc.tensor.matmul(oT[:, c0:512], vh[b][:, kc, :], p_[:, c0:512], start=(kc == 0), stop=(kc == last), skip_group_check=True),
                     reads=[vh[b], p_], writes=[oT], inc=False)
                k.op("pe", lambda: nc.tensor.matmul(sm[:, c0:512], self.ones_b[:], p_[:, c0:512], start=(kc == 0), stop=(kc == last), skip_group_check=True),
                     reads=[self.ones_b, p_], writes=[sm])
                if kc == last:
                    k.op("dve", lambda: nc.vector.reciprocal(rec[:], sm[:]), reads=[sm], writes=[rec])
                    k.op("dve", lambda: nc.vector.tensor_tensor(OT[:, h, QB * 512:(QB + 1) * 512], oT[:], rec[:], op=ALU.mult),
                         reads=[oT, rec], writes=[OT.s(h)])

            npairs = len(pairs)
            LA = 2
            for i in range(min(LA, npairs)):
                emit_qk(i)
            for i in range(npairs):
                emit_soft(i)
                if i + LA < npairs:
                    emit_qk(i + LA)
                emit_pv(i)
            k.barrier()

    def mixer_out_ln(self, wo_ap, li):
        k, nc = self.k, self.nc
        OT = self.OT
        with ExitStack() as es:
            wo = k.sbuf("wo_b", [P, 8, 512], BF16, es=es)
            self.ln_params(li, es)
            for half in range(2):
                for c in range(8):
                    self.load_w(wo, wo[:, c, :], wo_ap[c * P:(c + 1) * P, half * 512:(half + 1) * 512], 512, eng=("dve", "act")[c % 2])
                for tt in range(NT):
                    ps = self.bank()
                    for h in range(8):
                        k.op("pe", lambda ps=ps, h=h, tt=tt: nc.tensor.matmul(
                            ps[:], OT[:, h, tt * P:(tt + 1) * P], wo[:, h, :], start=(h == 0), stop=(h == 7)),
                             reads=[OT, wo], writes=[ps], inc=(h == 7))
                    xs = self.X[:, tt, half * 512:(half + 1) * 512]
                    k.op("dve", lambda ps=ps, xs=xs: nc.vector.scalar_tensor_tensor(xs, xs, ALPHA, ps[:], op0=ALU.mult, op1=ALU.add),
                         reads=[ps, self.X.s(tt)], writes=[self.X.s(tt)])
            for tt in range(NT):
                self.ln_tile(tt, li)
            k.barrier()
        self.att_es.close()

    def dsa_layer(self):
        k, nc, I = self.k, self.nc, self.I
        xT = self.xT
        WI = I["d_win"].rearrange("(c p) n -> p c n", p=P)
        self.dsa_es = ExitStack()
        esD = self.dsa_es
        selT = k.sbuf("d_selT", [P, 136 * P], BF16, es=esD)
        wtok = k.sbuf("d_wtok", [P, NT, 8], F32, es=esD)

        def soff(kc):
            return P * (16 * kc - kc * (kc - 1) // 2)

        with ExitStack() as es:
            wb = [k.sbuf(f"d_wb{i}", [P, 8, 512], BF16, es=es) for i in range(2)]
            t1 = k.sbuf("d_t1", [P, 512], F32, es=es)
            t2 = k.sbuf("d_t2", [P, 512], F32, es=es)
            ob = [k.sbuf(f"d_ob{i}", [P, 512], BF16, es=es) for i in range(4)]
            cos = k.sbuf("d_cos", [P, S], F32, es=es)
            sin = k.sbuf("d_sin", [P, S], F32, es=es)
            gi = 0
            obi = 0

            def load_group(c0, ncols):
                nonlocal gi
                wb_ = wb[gi % 2]
                gi += 1
                for hf in range(2):
                    st_ = self.stage()
                    sv = st_[:].rearrange("p (c n) -> p c n", c=4)
                    k.dma("sp", sv[:, :, 0:ncols], WI[:, hf * 4:(hf + 1) * 4, c0:c0 + ncols], writes=[st_])
                    if hf == 0:
                        k.op("act", lambda sv=sv, hf=hf: nc.scalar.copy(wb_[:, hf * 4:(hf + 1) * 4, 0:ncols], sv[:, :, 0:ncols]), reads=[st_], writes=[wb_])
                    else:
                        k.op("dve", lambda sv=sv, hf=hf: nc.vector.tensor_copy(wb_[:, hf * 4:(hf + 1) * 4, 0:ncols], sv[:, :, 0:ncols]), reads=[st_], writes=[wb_])
                return wb_

            def rope_pairs(c0, npairs, dst_fn):
                nonlocal obi
                wb_ = load_group(c0, npairs * 256)
                for tb in range(4):
                    tsl = slice(tb * 512, (tb + 1) * 512)
                    tslots = tuple(range(tb * 4, tb * 4 + 4))
                    for pi in range(npairs):
                        psn = self.bank()
                        pss = self.bank()
                        for (ps, cc) in ((psn, pi * 256), (pss, pi * 256 + P)):
                            for dc in range(8):
                                k.op("pe", lambda dc=dc, ps=ps, cc=cc: nc.tensor.matmul(ps[:], wb_[:, dc, cc:cc + P], xT[:, dc, tsl],
                                                                                         start=(dc == 0), stop=(dc == 7)),
                                     reads=[wb_, xT.s(*tslots)], writes=[ps], inc=(dc == 7))
                        o = ob[obi % 4]; obi += 1
                        self.rope(psn, pss, cos, sin, tb, o, o[:], (t1, t2))
                        dt_, dap = dst_fn(pi, tsl)
                        k.dma("sp", dap, o[:], reads=[o], writes=[dt_])

            k.dma("sp", cos[:], I["cos128"], writes=[cos])
            k.dma("sp", sin[:], I["sin128"], writes=[sin])
            for g in range(4):
                rope_pairs(g * 512, 2, lambda pi, tsl, g=g: (self.qnD.s(2 * g + pi), self.qnD[2 * g + pi, :, tsl]))
            for g in range(4):
                rope_pairs(2048 + g * 512, 2, lambda pi, tsl, g=g: (self.knD.s(2 * g + pi), self.knD[2 * g + pi, :, tsl]))
            k.dma("sp", cos[:], I["cos64"], writes=[cos])
            k.dma("sp", sin[:], I["sin64"], writes=[sin])
            for g in range(2):
                rope_pairs(4096 + g * 512, 2, lambda pi, tsl, g=g: (self.qiD.s(2 * g + pi), self.qiD[2 * g + pi, :, tsl]))
            rope_pairs(5120, 1, lambda pi, tsl: (self.kiD, self.kiD[:, tsl]))
            vb = [k.sbuf(f"d_vb{i}", [P, 512], BF16, es=es) for i in range(2)]
            vi = 0
            for half in range(2):
                wb_ = load_group(5376 + half * 512, 512)
                for tt in range(NT):
                    ps = self.bank()
                    for dc in range(8):
                        k.op("pe", lambda dc=dc, ps=ps, tt=tt: nc.tensor.matmul(ps[:], xT[:, dc, tt * P:(tt + 1) * P], wb_[:, dc, :],
                                                                                 start=(dc == 0), stop=(dc == 7)),
                             reads=[wb_, xT.s(tt)], writes=[ps], inc=(dc == 7))
                    v = vb[vi % 2]; vi += 1
                    if tt % 2 == 0:
                        k.op("act", lambda ps=ps, v=v: nc.scalar.copy(v[:], ps[:]), reads=[ps], writes=[v])
                    else:
                        k.op("dve", lambda ps=ps, v=v: nc.vector.tensor_copy(v[:], ps[:]), reads=[ps], writes=[v])
                    k.dma("sp", self.VD[tt, :, half * 512:(half + 1) * 512], v[:], reads=[v], writes=[self.VD.s(tt)])
            wb_ = load_group(6400, 8)
            wscale = float(8 ** -0.5 * 64 ** -0.5)
            for tt in range(NT):
                ps = self.bank()
                for dc in range(8):
                    k.op("pe", lambda dc=dc, ps=ps, tt=tt: nc.tensor.matmul(ps[:, 0:8], xT[:, dc, tt * P:(tt + 1) * P], wb_[:, dc, 0:8],
                                                                             start=(dc == 0), stop=(dc == 7)),
                         reads=[wb_, xT.s(tt)], writes=[ps], inc=(dc == 7))
                k.op("act", lambda ps=ps, tt=tt: nc.scalar.mul(wtok[:, tt, :], ps[:, 0:8], wscale), reads=[ps], writes=[wtok])
            k.barrier()
        with ExitStack() as es:
            kiT = k.sbuf("d_kiT", [P, S], BF16, es=es)
            qiT = k.sbuf("d_qiT", [P, 4, S], BF16, es=es)
            accs = [k.sbuf(f"d_acc{i}", [P, S], F32, es=es) for i in range(2)]
            rls = [k.sbuf(f"d_rl{i}", [P, 512], F32, es=es) for i in range(2)]
            sels = [k.sbuf(f"d_sel{i}", [P, S], BF16, es=es) for i in range(1)]
            mxs = [k.sbuf(f"d_mx{i}", [P, 8], F32, es=es) for i in range(4)]
            k.dma("sp", kiT[:], self.kiD[:], reads=[self.kiD], writes=[kiT])
            for pr in range(4):
                k.dma("sp", qiT[:, pr, :], self.qiD[pr], reads=[self.qiD.s(pr)], writes=[qiT])
            rli = 0
            tbi = 0
            sel2 = k.sbuf("d_sel2", [P, S], BF16, es=es)
            bss = [k.sbuf(f"d_bs{i}", [P, 8], F32, es=es) for i in range(2)]

            def q_chain(qi):
                nonlocal rli, tbi
                n = (qi + 1) * P
                acc = accs[qi % 2]
                mxs_ = mxs[2 * (qi % 2):2 * (qi % 2) + 2]
                qsl = slice(qi * P, (qi + 1) * P)
                for h in range(8):
                    pr, hp = h // 2, h % 2
                    prt = slice(hp * 64, (hp + 1) * 64)
                    for k0 in range(0, n, 512):
                        kw = min(512, n - k0)
                        ps = self.bank()
                        k.op("pe", lambda ps=ps, kw=kw, k0=k0, pr=pr, prt=prt, qsl=qsl: nc.tensor.matmul(
                            ps[:, 0:kw], qiT[prt, pr, qsl], kiT[prt, k0:k0 + kw], start=True, stop=True),
                             reads=[qiT, kiT], writes=[ps])
                        rl = rls[rli % 2]; rli += 1
                        k.op("act", lambda ps=ps, rl=rl, kw=kw: nc.scalar.activation(rl[:, 0:kw], ps[:, 0:kw], AF.Relu), reads=[ps], writes=[rl])
                        if h == 0:
                            k.op("dve", lambda rl=rl, kw=kw, k0=k0, acc=acc, qi=qi: nc.vector.tensor_scalar(
                                acc[:, k0:k0 + kw], rl[:, 0:kw], wtok[:, qi, 0:1], None, op0=ALU.mult), reads=[rl, wtok], writes=[acc])
                        else:
                            k.op("dve", lambda rl=rl, kw=kw, k0=k0, acc=acc, qi=qi, h=h: nc.vector.scalar_tensor_tensor(
                                acc[:, k0:k0 + kw], rl[:, 0:kw], wtok[:, qi, h:h + 1], acc[:, k0:k0 + kw], op0=ALU.mult, op1=ALU.add),
                                 reads=[rl, wtok, acc], writes=[acc])
                        yield
                sel = sels[0] if qi % 2 == 0 else sel2
                if qi >= 2:
                    bs = bss[qi % 2]
                    AXX_ = mybir.AxisListType.X
                    k.op("dve", lambda: nc.vector.tensor_reduce(bs[:, 0:1], acc[:, 0:n], AXX_, ALU.max), reads=[acc], writes=[bs])
                    yield
                    k.op("dve", lambda: nc.vector.tensor_reduce(bs[:, 1:2], acc[:, 0:n], AXX_, ALU.min), reads=[acc], writes=[bs])
                    yield
                    k.op("dve", lambda: nc.vector.scalar_tensor_tensor(bs[:, 2:3], bs[:, 0:1], 1.0, bs[:, 1:2], op0=ALU.mult, op1=ALU.subtract),
                         reads=[bs], writes=[bs])
                    yield
                    k.op("dve", lambda: nc.vector.tensor_scalar(bs[:, 2:3], bs[:, 2:3], 1.0009765625, 1e-20, op0=ALU.mult, op1=ALU.add),
                         reads=[bs], writes=[bs])
                    yield
                k.op("pool", lambda acc=acc, n=n: nc.gpsimd.tensor_tensor(acc[:, n - P:n], acc[:, n - P:n], self.negm[:], op=ALU.add),
                     reads=[acc, self.negm], writes=[acc])
                if qi >= 2:
                    bs = bss[qi % 2]
                    NIT = 20
                    k.op("dve", lambda: nc.vector.scalar_tensor_tensor(bs[:, 3:4], bs[:, 2:3], 0.5, bs[:, 1:2], op0=ALU.mult, op1=ALU.add),
                         reads=[bs], writes=[bs])
                    yield
                    for it in range(1, NIT + 1):
                        step = 0.5 ** it
                        k.op("dve", lambda: nc.vector.tensor_scalar(sel[:, 0:n], acc[:, 0:n], bs[:, 3:4], 0.0, op0=ALU.is_ge, op1=ALU.add,
                                                                    accum_out=bs[:, 4:5]),
                             reads=[acc, bs], writes=[sel, bs])
                        yield
                        k.op("dve", lambda step=step: nc.vector.tensor_scalar(bs[:, 5:6], bs[:, 4:5], 256.0, step, op0=ALU.is_ge, op1=ALU.mult),
                             reads=[bs], writes=[bs])
                        yield
                        k.op("dve", lambda: nc.vector.scalar_tensor_tensor(bs[:, 1:2], bs[:, 5:6], bs[:, 2:3], bs[:, 1:2], op0=ALU.mult, op1=ALU.add),
                             reads=[bs], writes=[bs])
                        yield
                        if it < NIT:
                            k.op("dve", lambda step=step: nc.vector.scalar_tensor_tensor(bs[:, 3:4], bs[:, 2:3], step * 0.5, bs[:, 1:2], op0=ALU.mult, op1=ALU.add),
                                 reads=[bs], writes=[bs])
                            yield
                    k.op("dve", lambda: nc.vector.tensor_scalar(sel[:, 0:n], acc[:, 0:n], bs[:, 1:2], None, op0=ALU.is_ge),
                         reads=[acc, bs], writes=[sel])
                    yield
                else:
                    thr_t, thr = self.thr_c, self.thr_c[:, 0:1]
                    k.op("dve", lambda sel=sel, acc=acc, n=n, thr=thr: nc.vector.tensor_scalar(sel[:, 0:n], acc[:, 0:n], thr, None, op0=ALU.is_ge),
                         reads=[acc, thr_t], writes=[sel])
                    yield
                for kc in range(qi + 1):
                    blk = tbi % 8; tbi += 1
                    k.op("pe", lambda sel=sel, kc=kc, blk=blk: nc.tensor.transpose(self.ptb[:, blk * P:(blk + 1) * P], sel[:, kc * P:(kc + 1) * P], self.ident_b[:]),
                         reads=[sel, self.ident_b], writes=[self.ptb])
                    o_ = soff(kc) + (qi - kc) * P
                    k.op("act", lambda blk=blk, o_=o_: nc.scalar.copy(selT[:, o_:o_ + P], self.ptb[:, blk * P:(blk + 1) * P]),
                         reads=[self.ptb], writes=[selT])

            for pa in range(8):
                gens = [q_chain(2 * pa + 1), q_chain(2 * pa)]
                while gens:
                    for g_ in list(gens):
                        try:
                            next(g_)
                        except StopIteration:
                            gens.remove(g_)
            self.dbg("bs1", bss[1], [P, 8])
            self.dbg("acc1", accs[1], [P, S])
            self.dbg("sel1", sel2, [P, S], BF16)
            k.barrier()
        self.dbg("selT", selT, [P, 136 * P], BF16)
        self.attention(mla=False, selT=lambda kc, q0, q1: selT[:, soff(kc) + q0 - kc * P: soff(kc) + q1 - kc * P], selT_t=selT)

    def peer_layer(self, l, li):
        k, nc, I = self.k, self.nc, self.I
        xT, X = self.xT, self.X
        AXX = mybir.AxisListType.X
        if not hasattr(self, "gateD"):
            self.gateD = k.dram("gateD", [P, P, S], BF16, nslots=P)
        gateD = self.gateD
        with ExitStack() as esL:
            LT = k.sbuf("p_LT", [P, 3, S], BF16, nslots=NT, es=esL)
            with ExitStack() as es:
                wq = k.sbuf("p_wq_b", [P, 8, 2048], BF16, es=es)
                skT = k.sbuf("p_skT_b", [P, 16, P], BF16, es=es)
                for c in range(8):
                    self.load_w(wq, wq[:, c, :], I[f"p_wq{l}"][c * P:(c + 1) * P, :], 2048, eng=("dve", "act")[c % 2])
                st = self.stage()
                k.dma("sp", st[:].rearrange("p (g n) -> p g n", g=16), I[f"p_skT{l}"].rearrange("g d n -> d g n"), writes=[st])
                k.op("dve", lambda: nc.vector.tensor_copy(skT[:].rearrange("p g n -> p (g n)"), st[:]), reads=[st], writes=[skT])
                qTbs = [k.sbuf(f"p_qTb{z}", [P, 16, 256], BF16, es=es) for z in range(2)]
                s_sbs = [Tn(self.stg[z].h, 16, f"p_s{z}") for z in range(2)]
                v16 = k.sbuf("p_v16", [P, 256], F32, nslots=16, es=es)
                i16 = k.sbuf("p_i16", [P, 256], U32, nslots=16, es=es)
                i16f = k.sbuf("p_i16f", [P, 256], F32, es=es)
                best = k.sbuf("p_best", [P, 128], F32, nslots=8, es=es)
                pos = k.sbuf("p_pos", [P, 128], U32, nslots=8, es=es)
                pab = k.sbuf("p_pab", [P, 2, 128], U32, es=es)
                pabf = k.sbuf("p_pabf", [P, 2, 128], F32, es=es)
                sm8 = k.sbuf("p_sm8", [P, 3, 8], F32, es=es)
                ex = k.sbuf("p_ex", [P, 128], F32, es=es)
                Lt = k.sbuf("p_Lt", [P, 3, 128], F32, es=es)
                k.barrier()

                def selfsync():
                    if k.cnt["dve"] > 0:
                        k.wait("dve", (k.sem["dve"], k.cnt["dve"]))

                def tile_scores(tt, t2, qTb):
                    s_sb = s_sbs[tt % 2]
                    for bq in range(4):
                        ps = self.bank()
                        for gg in range(4):
                            g = bq * 4 + gg
                            k.op("pe", lambda ps=ps, g=g, gg=gg: nc.tensor.matmul(
                                ps[:, gg * P:(gg + 1) * P], qTb[:, g, t2 * P:(t2 + 1) * P], skT[:, g, :], start=True, stop=True),
                                 reads=[qTb, skT], writes=[ps], inc=(gg == 3))
                        k.op("act", lambda ps=ps, bq=bq: nc.scalar.copy(s_sb[:, bq * 512:(bq + 1) * 512], ps[:]),
                             reads=[ps], writes=[s_sb.s(*range(bq * 4, bq * 4 + 4))])

                def tile_chain(tt, t2, qTb):
                    s_sb = s_sbs[tt % 2]
                    cand = s_sb
                    work = s_sb
                    G16 = range(16)
                    sg = lambda g: s_sb[:, g * P:(g + 1) * P]
                    va = lambda g: v16[:, g * 16:g * 16 + 8]
                    vb_ = lambda g: v16[:, g * 16 + 8:g * 16 + 16]
                    wk = lambda g: work[:, g * P:(g + 1) * P]
                    for g in G16:
                        k.op("dve", lambda g=g: nc.vector.max(out=va(g), in_=sg(g)), reads=[s_sb.s(g)], writes=[v16.s(g)])
                    selfsync()
                    for g in G16:
                        k.op("dve", lambda g=g: nc.vector.max_index(i16[:, g * 16:g * 16 + 8], va(g), sg(g)), reads=[s_sb.s(g), v16.s(g)], writes=[i16.s(g)])
                    for g in G16:
                        k.op("dve", lambda g=g: nc.vector.match_replace(out=sg(g), in_to_replace=va(g), in_values=sg(g), imm_value=NEG),
                             reads=[s_sb.s(g), v16.s(g)], writes=[s_sb.s(g)])
                    selfsync()
                    for g in G16:
                        k.op("dve", lambda g=g: nc.vector.max(out=vb_(g), in_=sg(g)), reads=[s_sb.s(g)], writes=[v16.s(g)])
                    selfsync()
                    for g in G16:
                        k.op("dve", lambda g=g: nc.vector.max_index(i16[:, g * 16 + 8:g * 16 + 16], vb_(g), sg(g)), reads=[s_sb.s(g), v16.s(g)], writes=[i16.s(g)])
                    k.op("dve", lambda: nc.vector.tensor_copy(i16f[:], i16[:]), reads=[i16], writes=[i16f])
                    v4 = v16[:].rearrange("p (h c a) -> p h c a", h=8, c=2)
                    i4 = i16f[:].rearrange("p (h c a) -> p h c a", h=8, c=2)
                    c4 = cand[:].rearrange("p (h a b) -> p h a b", h=8, a=16)
                    k.op("dve", lambda: nc.vector.tensor_tensor(c4, v4[:, :, 0, :].unsqueeze(3).to_broadcast([P, 8, 16, 16]),
                                                                v4[:, :, 1, :].unsqueeze(2).to_broadcast([P, 8, 16, 16]), op=ALU.add),
                         reads=[v16], writes=[cand])
                    H8 = range(8)
                    ch = lambda h: cand[:, h * 256:(h + 1) * 256]
                    cs = lambda h: cand.s(2 * h, 2 * h + 1)
                    ws = lambda h: work.s(2 * h, 2 * h + 1)
                    wh = lambda h: work[:, h * 256:(h + 1) * 256]
                    ba = lambda h: best[:, h * 16:h * 16 + 8]
                    bb = lambda h: best[:, h * 16 + 8:h * 16 + 16]
                    selfsync()
                    for h in H8:
                        k.op("dve", lambda h=h: nc.vector.max(out=ba(h), in_=ch(h)), reads=[cs(h)], writes=[best.s(h)])
                    selfsync()
                    for h in H8:
                        k.op("dve", lambda h=h: nc.vector.max_index(pos[:, h * 16:h * 16 + 8], ba(h), ch(h)), reads=[cs(h), best.s(h)], writes=[pos.s(h)])
                    for h in H8:
                        k.op("dve", lambda h=h: nc.vector.match_replace(out=ch(h), in_to_replace=ba(h), in_values=ch(h), imm_value=NEG),
                             reads=[cs(h), best.s(h)], writes=[cs(h)])
                    selfsync()
                    for h in H8:
                        k.op("dve", lambda h=h: nc.vector.max(out=bb(h), in_=ch(h)), reads=[cs(h)], writes=[best.s(h)])
                    selfsync()
                    for h in H8:
                        k.op("dve", lambda h=h: nc.vector.max_index(pos[:, h * 16 + 8:h * 16 + 16], bb(h), ch(h)), reads=[cs(h), best.s(h)], writes=[pos.s(h)])
                    b3 = best[:].rearrange("p (h r) -> p h r", h=8)
                    e3 = ex[:].rearrange("p (h r) -> p h r", h=8)
                    k.op("dve", lambda: nc.vector.tensor_tensor(e3, b3, b3[:, :, 0:1].to_broadcast([P, 8, 16]), op=ALU.subtract),
                         reads=[best], writes=[ex])
                    k.op("act", lambda: nc.scalar.activation(ex[:], ex[:], AF.Exp), reads=[ex], writes=[ex])
                    k.op("dve", lambda: nc.vector.tensor_single_scalar(pab[:, 0, :], pos[:], 4, op=ALU.logical_shift_right), reads=[pos], writes=[pab])
                    k.op("dve", lambda: nc.vector.tensor_single_scalar(pab[:, 1, :], pos[:], 15, op=ALU.bitwise_and), reads=[pos], writes=[pab])
                    k.op("dve", lambda: nc.vector.tensor_copy(pabf[:], pab[:]), reads=[pab], writes=[pabf])
                    k.op("dve", lambda: nc.vector.reduce_sum(sm8[:, 0, :], e3, axis=AXX), reads=[ex], writes=[sm8])
                    k.op("dve", lambda: nc.vector.reciprocal(sm8[:, 1, :], sm8[:, 0, :]), reads=[sm8], writes=[sm8])
                    k.op("dve", lambda: nc.vector.tensor_tensor(Lt[:, 0, :].rearrange("p (h r) -> p h r", h=8), e3,
                                                                sm8[:, 1, :].unsqueeze(2).to_broadcast([P, 8, 16]), op=ALU.mult),
                         reads=[ex, sm8], writes=[Lt])
                    for w_ in range(2):
                        abf = pabf[:, w_, :].rearrange("p (h r) -> p h r", h=8)
                        k.op("dve", lambda abf=abf: nc.vector.tensor_tensor(
                            c4, self.iota_f[:, 0:16].unsqueeze(1).unsqueeze(1).to_broadcast([P, 8, 16, 16]),
                            abf.unsqueeze(3).to_broadcast([P, 8, 16, 16]), op=ALU.is_equal),
                             reads=[self.iota_f, pabf], writes=[cand])
                        k.op("dve", lambda w_=w_: nc.vector.tensor_tensor(c4, c4, i4[:, :, w_, :].unsqueeze(2).to_broadcast([P, 8, 16, 16]), op=ALU.mult),
                             reads=[i16f, cand], writes=[cand])
                        k.op("dve", lambda w_=w_: nc.vector.reduce_sum(Lt[:, 1 + w_, :].rearrange("p (h r) -> p h r", h=8), c4, axis=AXX),
                             reads=[cand], writes=[Lt])
                    ps = self.bank()
                    for q3 in range(3):
                        k.op("pe", lambda q3=q3, ps=ps: nc.tensor.transpose(ps[:, q3 * P:(q3 + 1) * P], Lt[:, q3, :], self.ident_f[:]),
                             reads=[Lt, self.ident_f], writes=[ps], inc=(q3 == 2))
                    k.op("act", lambda ps=ps, tt=tt: nc.scalar.copy(LT[:, :, tt * P:(tt + 1) * P], ps[:, 0:384].rearrange("p (q t) -> p q t", q=3)),
                         reads=[ps], writes=[LT.s(tt)])

                def qproj(tb):
                    tsl = slice(tb * 256, (tb + 1) * 256)
                    tslots = (2 * tb, 2 * tb + 1)
                    qTb = qTbs[tb % 2]
                    for g in range(16):
                        ps = self.bank()
                        for dc in range(8):
                            k.op("pe", lambda dc=dc, g=g, ps=ps: nc.tensor.matmul(ps[:, 0:256], wq[:, dc, g * P:(g + 1) * P], xT[:, dc, tsl],
                                                                                   start=(dc == 0), stop=(dc == 7)),
                                 reads=[wq, xT.s(*tslots)], writes=[ps], inc=(dc == 7))
                        k.op("act", lambda g=g, ps=ps, qTb=qTb: nc.scalar.copy(qTb[:, g, :], ps[:, 0:256]), reads=[ps], writes=[qTb])

                qproj(0)
                tile_scores(0, 0, qTbs[0])
                for tt in range(NT):
                    tb, t2 = tt // 2, tt % 2
                    if t2 == 0 and tb + 1 < 8:
                        qproj(tb + 1)
                    if tt + 1 < NT:
                        tile_scores(tt + 1, (tt + 1) % 2, qTbs[((tt + 1) // 2) % 2])
                    tile_chain(tt, t2, qTbs[tb % 2])
                k.barrier()
            self.dbg("LT", LT, [P, 3, S], BF16)
            with ExitStack() as es:
                TBK = 8
                Ap = [k.sbuf(f"p_Ap{i}", [P, TBK, P], BF16, es=es) for i in range(2)]
                Bp = [k.sbuf(f"p_Bp{i}", [P, TBK, P], BF16, es=es) for i in range(2)]
                gt = [k.sbuf(f"p_gt{i}", [P, P, P], BF16, es=es) for i in range(2)]
                if ACT_SHARE:
                    nLj = Tn(self.stg[0].h, 1, "p_nLj")
                    one_t = k.sbuf("p_one", [P, 1], F32, es=es)
                    abt = [Tn(self.stg[1][:, i * P:(i + 1) * P], 1, f"p_abt{i}") for i in range(2)]
                    k.op("pool", lambda: nc.gpsimd.memset(one_t[:], 1.0), writes=[one_t])
                    k.op("pool", lambda: nc.gpsimd.tensor_scalar(nLj[:], LT[:, 2, :], -1.0, None, op0=ALU.mult), reads=[LT], writes=[nLj])
                abi = 0
                ev_i = 0
                for tt in range(NT):
                    g_ = gt[tt % 2]
                    for sub in range(P // TBK):
                        t0 = tt * P + sub * TBK
                        A_ = Ap[sub % 2]
                        B_ = Bp[sub % 2]
                        if BATCH_B:
                            k.op("dve", lambda B_=B_, t0=t0: nc.vector.tensor_tensor(
                                B_[:], self.iota_b[:].unsqueeze(1).to_broadcast([P, TBK, P]),
                                LT[:, 2, t0:t0 + TBK].unsqueeze(2).to_broadcast([P, TBK, P]), op=ALU.is_equal),
                                 reads=[self.iota_b, LT.s(tt)], writes=[B_])
                        for tl_ in range(TBK):
                            t_g = t0 + tl_
                            k.op("dve", lambda A_=A_, tl_=tl_, t_g=t_g: nc.vector.tensor_scalar(
                                A_[:, tl_, :], self.iota_b[:], LT[:, 1, t_g:t_g + 1], LT[:, 0, t_g:t_g + 1], op0=ALU.is_equal, op1=ALU.mult),
                                 reads=[self.iota_b, LT.s(tt)], writes=[A_])
                            if ACT_SHARE and tl_ < ACT_SHARE:
                                ab = abt[abi % 2]; abi += 1
                                k.op("act", lambda ab=ab, t_g=t_g: nc.scalar.activation(ab[:], self.iota_f[:], AF.Abs, bias=nLj[:, t_g:t_g + 1], scale=1.0),
                                     reads=[self.iota_f, nLj], writes=[ab])
                                k.op("act", lambda ab=ab, B_=B_, tl_=tl_: nc.scalar.activation(B_[:, tl_, :], ab[:], AF.Relu, bias=one_t[:, 0:1], scale=-1.0),
                                     reads=[ab, one_t], writes=[B_])
                            elif not BATCH_B:
                                k.op("dve", lambda B_=B_, tl_=tl_, t_g=t_g: nc.vector.tensor_scalar(
                                    B_[:, tl_, :], self.iota_b[:], LT[:, 2, t_g:t_g + 1], None, op0=ALU.is_equal),
                                     reads=[self.iota_b, LT.s(tt)], writes=[B_])
                        for q4 in range(TBK // 4):
                            ps = self.bank()
                            for tl in range(4):
                                t_ = q4 * 4 + tl
                                k.op("pe", lambda ps=ps, tl=tl, t_=t_, A_=A_, B_=B_: nc.tensor.matmul(
                                    ps[:, tl * P:(tl + 1) * P], B_[:, t_, :], A_[:, t_, :], start=True, stop=True),
                                     reads=[A_, B_], writes=[ps], inc=(tl == 3))
                            c_ = sub * TBK + q4 * 4
                            dst = g_[:, :, c_:c_ + 4]
                            src = ps[:].rearrange("p (t i) -> p i t", t=4)
                            k.op("act", lambda dst=dst, src=src: nc.scalar.copy(dst, src), reads=[ps], writes=[g_])
                            ev_i += 1
                    for i8 in range(8):
                        k.dma("sp", gateD[i8 * 16:(i8 + 1) * 16, :, tt * P:(tt + 1) * P].rearrange("i j t -> j i t"),
                              g_[:, i8 * 16:(i8 + 1) * 16, :], reads=[g_], writes=[gateD.s(*range(i8 * 16, (i8 + 1) * 16))])
                k.barrier()
        with ExitStack() as es:
            G = 4
            hg = [k.sbuf(f"p_hg{i}", [P, S], BF16, es=es) for i in range(2 * G)]
            wub = [k.sbuf(f"p_wub{i}", [P, D], BF16, es=es) for i in range(2 * G)]
            wdb = [k.sbuf(f"p_wdb{i}", [P, D], BF16, es=es) for i in range(2)]
            wst = [k.sbuf(f"p_wst{i}", [P, D], F32, es=es) for i in range(4)]
            ge = [k.sbuf(f"p_ge{i}", [P, 512], BF16, es=es) for i in range(3)]
            gei = 0
            abank = 0
            ybank = 0
            wsi = 0
            for i in range(P):
                slot = i % (2 * G)
                h_ = hg[slot]
                wu_ = wub[slot]
                wd_ = wdb[i % 2]
                s1 = wst[wsi % 4]; wsi += 1
                s2 = wst[wsi % 4]; wsi += 1
                k.dma("sp", s1[:], I[f"p_wdT{l}"][i], writes=[s1])
                k.dma("sp", s2[:], I[f"p_wu{l}"][i * P:(i + 1) * P, :], writes=[s2])
                k.dma("sp", h_[:], gateD[i], reads=[gateD.s(i)], writes=[h_])
                k.op("pool", lambda wd_=wd_, s1=s1: nc.gpsimd.tensor_copy(wd_[:], s1[:]), reads=[s1], writes=[wd_])
                k.op("pool", lambda wu_=wu_, s2=s2: nc.gpsimd.tensor_copy(wu_[:], s2[:]), reads=[s2], writes=[wu_])
                for nb in range(4):
                    ps = self.pb[abank % 4]; abank += 1
                    for dc in range(8):
                        k.op("pe", lambda ps=ps, dc=dc, nb=nb, wd_=wd_: nc.tensor.matmul(
                            ps[:], wd_[:, dc * P:(dc + 1) * P], xT[:, dc, nb * 512:(nb + 1) * 512], start=(dc == 0), stop=(dc == 7)),
                             reads=[wd_, xT.s(*range(nb * 4, nb * 4 + 4))], writes=[ps], inc=(dc == 7))
                    g_ = ge[gei % 3]; gei += 1
                    k.op("act", lambda ps=ps, g_=g_: nc.scalar.activation(g_[:], ps[:], AF.Gelu), reads=[ps], writes=[g_])
                    k.op("dve", lambda g_=g_, h_=h_, nb=nb: nc.vector.tensor_tensor(
                        h_[:, nb * 512:(nb + 1) * 512], h_[:, nb * 512:(nb + 1) * 512], g_[:], op=ALU.mult),
                         reads=[g_, h_], writes=[h_])
                if i % G == G - 1:
                    base = slot - (G - 1)
                    for tt in range(NT):
                        for half in range(2):
                            ps = self.pb[4 + ybank % 3]; ybank += 1
                            for gg in range(G):
                                k.op("pe", lambda ps=ps, gg=gg, tt=tt, half=half, base=base: nc.tensor.matmul(
                                    ps[:], hg[base + gg][:, tt * P:(tt + 1) * P], wub[base + gg][:, half * 512:(half + 1) * 512],
                                    start=(gg == 0), stop=(gg == G - 1)),
                                     reads=[hg[base + gg], wub[base + gg]], writes=[ps], inc=(gg == G - 1))
                            xs = X[:, tt, half * 512:(half + 1) * 512]
                            if i == G - 1:
                                k.op("dve", lambda ps=ps, xs=xs: nc.vector.scalar_tensor_tensor(xs, xs, ALPHA, ps[:], op0=ALU.mult, op1=ALU.add),
                                     reads=[ps, X.s(tt)], writes=[X.s(tt)])
                            else:
                                k.op("dve", lambda ps=ps, xs=xs: nc.vector.tensor_tensor(xs, xs, ps[:], op=ALU.add),
                                     reads=[ps, X.s(tt)], writes=[X.s(tt)])
            self.ln_params(li, es)
            for tt in range(NT):
                self.ln_tile(tt, li)
            k.barrier()

def _rope_tab(dim):
    inv = (np.float32(10000.0) ** (-np.arange(0, dim, 2, dtype=np.float32) / np.float32(dim))).astype(np.float32)
    ang = (np.arange(S, dtype=np.float32)[:, None] * inv[None, :]).astype(np.float32)
    c = np.cos(ang).astype(np.float32).T
    s = np.sin(ang).astype(np.float32).T
    half = dim // 2
    reps = P // dim
    cos = np.concatenate([c, c] * reps, axis=0)
    sin = np.concatenate([-s, s] * reps, axis=0)
    return np.ascontiguousarray(cos), np.ascontiguousarray(sin)


def prep_shared(inp):
    f = np.float32
    sh = {}
    sh["ident"] = np.eye(P, dtype=f)
    sh["cos64"], sh["sin64"] = _rope_tab(64)
    sh["cos128"], sh["sin128"] = _rope_tab(128)
    kk = np.arange(P)[:, None]
    qq = np.arange(P)[None, :]
    sh["maskT"] = np.where((kk >= 64) & (qq < 64), 0.0, 1.0).astype(f)
    sh["negm"] = np.where((kk < 64) & (qq >= 64), NEG, 0.0).astype(f)
    sh["iota"] = np.broadcast_to(np.arange(P, dtype=f)[None, :], (P, P)).copy()
    w_in = inp["mla_w_in"][0]
    kr = w_in[:, 640:704]
    kr_sw = np.concatenate([kr[:, 32:64], kr[:, 0:32]], axis=1)
    sh["m_win"] = np.ascontiguousarray(np.concatenate([w_in[:, :640], kr, kr, kr_sw, kr_sw], axis=1))
    wuq = inp["mla_w_uq"][0]
    nope = wuq[:, :, :128].reshape(384, 1024)
    rp = wuq[:, :, 128:192]
    rp_sw = np.concatenate([rp[:, :, 32:64], rp[:, :, 0:32]], axis=2)
    sh["m_wuq"] = np.ascontiguousarray(np.concatenate([nope, rp.reshape(384, 512), rp_sw.reshape(384, 512)], axis=1))
    wukv = inp["mla_w_ukv"][0]
    sh["m_wukv"] = np.ascontiguousarray(np.concatenate([wukv[:, :, :128].reshape(256, 1024), wukv[:, :, 128:].reshape(256, 1024)], axis=1))
    sh["m_wo"] = np.ascontiguousarray(inp["mla_w_o"][0])
    sh["m_qn"] = np.ascontiguousarray(inp["mla_q_norm"][0].reshape(3, P).T)
    sh["m_kvn"] = np.ascontiguousarray(inp["mla_kv_norm"][0].reshape(2, P).T)
    dw = inp["dsa_w_in"][0]

    def sw(w, hd):
        n = w.shape[1] // hd
        w3 = w.reshape(w.shape[0], n, hd)
        return np.concatenate([w3[:, :, hd // 2:], w3[:, :, :hd // 2]], axis=2).reshape(w.shape[0], n * hd)

    q, kx, v = dw[:, 0:1024], dw[:, 1024:2048], dw[:, 2048:3072]
    qi, ki, wi = dw[:, 3072:3584], dw[:, 3584:3648], dw[:, 3648:3656]
    cols = []
    for h in range(8):
        cols += [q[:, h * P:(h + 1) * P], sw(q[:, h * P:(h + 1) * P], P)]
    for h in range(8):
        cols += [kx[:, h * P:(h + 1) * P], sw(kx[:, h * P:(h + 1) * P], P)]
    qis = sw(qi, 64)
    for pr in range(4):
        cols += [qi[:, pr * P:(pr + 1) * P], qis[:, pr * P:(pr + 1) * P]]
    kis = sw(ki, 64)
    cols += [ki, ki, kis, kis]
    cols += [v, wi]
    sh["d_win"] = np.ascontiguousarray(np.concatenate(cols, axis=1))
    assert sh["d_win"].shape[1] == 6408
    sh["d_wo"] = np.ascontiguousarray(inp["dsa_w_o"][0])
    for l in range(2):
        sh[f"p_wq{l}"] = np.ascontiguousarray(inp["peer_w_q"][l])
        sk = inp["peer_sub_keys"][l]
        sh[f"p_skT{l}"] = np.ascontiguousarray(sk.transpose(1, 0, 3, 2).reshape(16, P, P))
        wd = inp["peer_w_down"][l]
        sh[f"p_wdT{l}"] = np.ascontiguousarray(wd.reshape(P, P, 8, P).transpose(0, 3, 2, 1).reshape(P, P, 8 * P))
        sh[f"p_wu{l}"] = np.ascontiguousarray(inp["peer_w_up"][l])
    sh["ln_g"] = np.ascontiguousarray(inp["ln_gain"].reshape(4, D))
    sh["ln_b"] = np.ascontiguousarray(inp["ln_bias"].reshape(4, D))
    return sh


def kernel(**inputs):
    inp = {k_: np.asarray(v) for k_, v in inputs.items()}
    sh = prep_shared(inp)
    prog = Prog()
    nc = prog.build()
    x = np.ascontiguousarray(inp["x"], dtype=np.float32)
    in_maps = []
    for c in range(8):
        m = dict(sh)
        m["x"] = x[c]
        in_maps.append(m)
    res = run_bass_kernel_spmd(nc, in_maps, core_ids=list(range(8)))
    return np.stack([res.results[c]["out"] for c in range(8)], axis=0).astype(np.float32)
```

```python
from contextlib import ExitStack
import math
import numpy as np
import concourse.bass as bass
import concourse.mybir as mybir
from concourse.bass_utils import run_bass_kernel_spmd

F32 = mybir.dt.float32
BF16 = mybir.dt.bfloat16
U32 = mybir.dt.uint32
AF = mybir.ActivationFunctionType
ALU = mybir.AluOpType

S = 2048
D = 1024
NT = 16
P = 128
ALPHA = float((2 * 2) ** 0.25)
LN_EPS = 1e-5
RMS_EPS = 1e-6
NEG = -1.0e30
BATCH_B = False
ACT_SHARE = 0


class Tn:
    def __init__(self, h, nslots=1, name=""):
        self.h = h
        self.n = nslots
        self.name = name
        self.lw = [None] * nslots
        self.rd = [dict() for _ in range(nslots)]

    def __getitem__(self, key):
        return self.h[key]

    def s(self, *slots):
        return (self, slots)


def _norm(acc):
    if isinstance(acc, Tn):
        return acc, range(acc.n)
    return acc


class KB:
    def __init__(self, nc, es):
        self.nc = nc
        self.es = es
        self.eng = {"pe": nc.tensor, "act": nc.scalar, "dve": nc.vector, "pool": nc.gpsimd, "sp": nc.sync}
        self.sem = {}
        self.cnt = {}
        for e in ("pe", "act", "dve", "pool"):
            self.sem[e] = es.enter_context(nc.semaphore("c_" + e))
            self.cnt[e] = 0
        self.known = {e: {} for e in self.eng}
        self.rings = {}
        self.dma_n = {}
        for q, n in {"sp": 24, "pool": 8, "act": 8}.items():
            self.rings[q] = [es.enter_context(nc.semaphore(f"r_{q}{i}")) for i in range(n)]
            self.dma_n[q] = 0

    def wait(self, e, ev):
        sem, val = ev
        kk = id(sem)
        if self.known[e].get(kk, 0) >= val:
            return
        self.eng[e].wait_ge(sem, val)
        self.known[e][kk] = val

    def _deps(self, reads, writes):
        evs = []
        for acc in reads:
            t, sl = _norm(acc)
            for s in sl:
                if t.lw[s] is not None:
                    evs.append(t.lw[s])
        for acc in writes:
            t, sl = _norm(acc)
            for s in sl:
                if t.lw[s] is not None:
                    evs.append(t.lw[s])
                evs.extend(t.rd[s].values())
        return evs

    def _record(self, ev, reads, writes):
        sem, val = ev
        kk = id(sem)
        for acc in reads:
            t, sl = _norm(acc)
            for s in sl:
                d = t.rd[s]
                if kk not in d or d[kk][1] < val:
                    d[kk] = ev
        for acc in writes:
            t, sl = _norm(acc)
            for s in sl:
                t.lw[s] = ev
                t.rd[s] = dict()

    def op(self, e, fn, reads=(), writes=(), inc=True):
        own = self.sem[e]
        for ev in self._deps(reads, writes):
            if ev[0] is own and e == "pe":
                continue
            self.wait(e, ev)
        ins = fn()
        if inc:
            self.cnt[e] += 1
            ins.then_inc(own, 1)
            ev = (own, self.cnt[e])
        else:
            assert e == "pe"
            ev = (own, self.cnt[e] + 1)
        self._record(ev, reads, writes)
        return ev

    def dma(self, q, out, in_, reads=(), writes=(), **kw):
        n = self.dma_n[q]
        ring = self.rings[q]
        K = len(ring)
        slot = n % K
        if n >= K:
            self.wait(q, (ring[slot], 16 * (n // K)))
        for ev in self._deps(reads, writes):
            self.wait(q, ev)
        self.eng[q].dma_start(out=out, in_=in_, **kw).then_inc(ring[slot], 16)
        self.dma_n[q] = n + 1
        ev = (ring[slot], 16 * (n // K + 1))
        self._record(ev, reads, writes)
        return ev

    def all_events(self):
        evs = []
        for e in ("pe", "act", "dve", "pool"):
            if self.cnt[e] > 0:
                evs.append((self.sem[e], self.cnt[e]))
        for q, ring in self.rings.items():
            n = self.dma_n[q]
            K = len(ring)
            for slot in range(min(n, K)):
                last = ((n - 1 - slot) // K) * K + slot
                evs.append((ring[slot], 16 * (last // K + 1)))
        return evs

    def barrier(self):
        evs = self.all_events()
        for e in self.eng:
            for ev in evs:
                self.wait(e, ev)

    def sbuf(self, name, shape, dt, nslots=1, es=None):
        self.uid = getattr(self, "uid", 0) + 1
        name = f"{name}_{self.uid}"
        h = (es or self.es).enter_context(self.nc.sbuf_tensor(name, list(shape), dt))
        return Tn(h, nslots, name)

    def psum(self, name, shape, dt, nslots=1, es=None):
        h = (es or self.es).enter_context(self.nc.psum_tensor(name, list(shape), dt))
        return Tn(h, nslots, name)

    def dram(self, name, shape, dt, nslots=1):
        h = self.nc.dram_tensor(name, list(shape), dt).ap()
        return Tn(h, nslots, name)


class Prog:
    def __init__(self, debug=()):
        self.debug = set(debug)
        self.nc = bass.Bass("TRN2", target_bir_lowering=False)
        self.dbg_out = {}

    def din(self, name, shape, dt=F32):
        return self.nc.dram_tensor(name, list(shape), dt, kind="ExternalInput").ap()

    def build(self, stages=("mla", "peer0", "dsa", "peer1")):
        nc = self.nc
        I = {}
        I["x"] = self.din("x", [S, D])
        I["ident"] = self.din("ident", [P, P])
        I["cos64"] = self.din("cos64", [P, S])
        I["sin64"] = self.din("sin64", [P, S])
        I["cos128"] = self.din("cos128", [P, S])
        I["sin128"] = self.din("sin128", [P, S])
        I["maskT"] = self.din("maskT", [P, P])
        I["negm"] = self.din("negm", [P, P])
        I["iota"] = self.din("iota", [P, P])
        I["m_win"] = self.din("m_win", [D, 896])
        I["m_wuq"] = self.din("m_wuq", [384, 2048])
        I["m_wukv"] = self.din("m_wukv", [256, 2048])
        I["m_wo"] = self.din("m_wo", [D, D])
        I["m_qn"] = self.din("m_qn", [P, 3])
        I["m_kvn"] = self.din("m_kvn", [P, 2])
        I["d_win"] = self.din("d_win", [D, 6408])
        I["d_wo"] = self.din("d_wo", [D, D])
        for l in range(2):
            I[f"p_wq{l}"] = self.din(f"p_wq{l}", [D, 2048])
            I[f"p_skT{l}"] = self.din(f"p_skT{l}", [16, P, P])
            I[f"p_wdT{l}"] = self.din(f"p_wdT{l}", [P, P, 8 * P])
            I[f"p_wu{l}"] = self.din(f"p_wu{l}", [P * P, D])
        I["ln_g"] = self.din("ln_g", [4, D])
        I["ln_b"] = self.din("ln_b", [4, D])
        self.I = I
        self.out = nc.dram_tensor("out", [S, D], F32, kind="ExternalOutput").ap()

        with ExitStack() as es:
            k = KB(nc, es)
            self.k = k
            self.setup_common(es)
            self.load_x()
            li = 0
            for st in stages:
                if st == "mla":
                    self.mla_layer()
                    self.mixer_out_ln(I["m_wo"], 0)
                elif st == "dsa":
                    self.dsa_layer()
                    self.mixer_out_ln(I["d_wo"], 2)
                    self.dsa_es.close()
                elif st.startswith("peer"):
                    l = int(st[4:])
                    self.peer_layer(l, 2 * l + 1)
            self.store_out()
            k.barrier()
        return nc

    def setup_common(self, es):
        k, nc, I = self.k, self.nc, self.I
        self.X = k.sbuf("X", [P, NT, D], F32, nslots=NT)
        self.xT = k.sbuf("xT", [P, 8, S], BF16, nslots=NT)
        self.ident_f = k.sbuf("ident_f", [P, P], F32)
        self.ident_b = k.sbuf("ident_b", [P, P], BF16)
        self.ones_b = k.sbuf("ones_b", [P, P], BF16)
        self.maskT_b = k.sbuf("maskT_b", [P, P], BF16)
        self.negm = k.sbuf("negm_s", [P, P], F32)
        self.iota_b = k.sbuf("iota_b", [P, P], BF16)
        self.iota_f = k.sbuf("iota_f", [P, P], F32)
        self.eps_ln = k.sbuf("eps_ln", [P, 1], F32)
        self.eps_rms = k.sbuf("eps_rms", [P, 1], F32)
        self.thr_c = k.sbuf("thr_c", [P, 1], F32)
        self.stg = [k.sbuf(f"stg{i}", [P, 2048], F32) for i in range(2)]
        self.stg_i = 0
        self.pb = [k.psum(f"pb{i}", [P, 512], F32) for i in range(7)]
        self.pb_i = 0
        self.ptb = k.psum("ptb", [P, 1024], BF16)
        k.dma("sp", self.ident_f[:], I["ident"], writes=[self.ident_f])
        k.op("dve", lambda: nc.vector.tensor_copy(self.ident_b[:], self.ident_f[:]), reads=[self.ident_f], writes=[self.ident_b])
        k.op("pool", lambda: nc.gpsimd.memset(self.ones_b[:], 1.0), writes=[self.ones_b])
        k.op("pool", lambda: nc.gpsimd.memset(self.eps_ln[:], LN_EPS), writes=[self.eps_ln])
        k.op("pool", lambda: nc.gpsimd.memset(self.eps_rms[:], RMS_EPS), writes=[self.eps_rms])
        k.op("pool", lambda: nc.gpsimd.memset(self.thr_c[:], -1.0e29), writes=[self.thr_c])
        st = self.stage()
        k.dma("sp", st[:, 0:P], I["maskT"], writes=[st])
        k.op("dve", lambda: nc.vector.tensor_copy(self.maskT_b[:], st[:, 0:P]), reads=[st], writes=[self.maskT_b])
        k.dma("sp", self.negm[:], I["negm"], writes=[self.negm])
        k.dma("sp", self.iota_f[:], I["iota"], writes=[self.iota_f])
        k.op("dve", lambda: nc.vector.tensor_copy(self.iota_b[:], self.iota_f[:]), reads=[self.iota_f], writes=[self.iota_b])
        self.ln_st = k.sbuf("ln_st", [P, NT, 12], F32, nslots=NT)
        self.ln_mv = k.sbuf("ln_mv", [P, NT, 4], F32, nslots=NT)
        self.xbf = [k.sbuf(f"xbf{i}", [P, D], BF16) for i in range(2)]
        self.qnD = k.dram("qnD", [8, P, S], BF16, nslots=8)
        self.knD = k.dram("knD", [8, P, S], BF16, nslots=8)
        self.qrD = k.dram("qrD", [8, 64, S], BF16, nslots=8)
        self.krD = k.dram("krD", [P, S], BF16)
        self.VD = k.dram("VD", [NT, P, D], BF16, nslots=NT)
        self.qiD = k.dram("qiD", [4, P, S], BF16, nslots=4)
        self.kiD = k.dram("kiD", [P, S], BF16)

    def stage(self):
        s = self.stg[self.stg_i % len(self.stg)]
        self.stg_i += 1
        return s

    def bank(self):
        b = self.pb[self.pb_i % len(self.pb)]
        self.pb_i += 1
        return b

    def dbg(self, name, t, shape, dt=F32, ap=None):
        if name not in self.debug:
            return
        o = self.nc.dram_tensor("dbg_" + name, list(shape), dt, kind="ExternalOutput").ap()
        self.k.dma("sp", o, ap if ap is not None else t[:], reads=[t])
        self.dbg_out[name] = (shape, dt)

    def make_xT(self, tt):
        k, nc = self.k, self.nc
        xb = self.xbf[tt % 2]
        k.op("act", lambda: nc.scalar.copy(xb[:], self.X[:, tt, :]), reads=[self.X.s(tt)], writes=[xb])
        for c in range(8):
            k.op("pe", lambda c=c: nc.tensor.transpose(self.ptb[:, c * P:(c + 1) * P], xb[:, c * P:(c + 1) * P], self.ident_b[:]),
                 reads=[xb, self.ident_b], writes=[self.ptb], inc=(c == 7))
        k.op("dve", lambda: nc.vector.tensor_copy(self.xT[:, :, tt * P:(tt + 1) * P],
                                                  self.ptb[:].rearrange("p (c t) -> p c t", c=8)),
             reads=[self.ptb], writes=[self.xT.s(tt)])

    def load_x(self):
        k, nc = self.k, self.nc
        for tt in range(NT):
            k.dma("sp", self.X[:, tt, :], self.I["x"][tt * P:(tt + 1) * P, :], writes=[self.X.s(tt)])
            self.make_xT(tt)

    def store_out(self):
        k = self.k
        for tt in range(NT):
            k.dma("sp", self.out[tt * P:(tt + 1) * P, :], self.X[:, tt, :], reads=[self.X.s(tt)])

    def ln_params(self, li, es):
        k, I = self.k, self.I
        self.lng = k.sbuf("lng", [P, D], F32, es=es)
        self.lnb = k.sbuf("lnb", [P, D], F32, es=es)
        k.dma("sp", self.lng[:], I["ln_g"][li:li + 1, :].partition_broadcast(P), writes=[self.lng])
        k.dma("sp", self.lnb[:], I["ln_b"][li:li + 1, :].partition_broadcast(P), writes=[self.lnb])

    def ln_tile(self, tt, li):
        k, nc = self.k, self.nc
        X = self.X
        xt = X[:, tt, :]
        st = self.ln_st
        mv = self.ln_mv
        k.op("dve", lambda: nc.vector.bn_stats(st[:, tt, 0:6], X[:, tt, 0:512]), reads=[X.s(tt)], writes=[st.s(tt)])
        k.op("dve", lambda: nc.vector.bn_stats(st[:, tt, 6:12], X[:, tt, 512:1024]), reads=[X.s(tt)], writes=[st.s(tt)])
        k.op("dve", lambda: nc.vector.bn_aggr(mv[:, tt, 0:2], st[:, tt, :].rearrange("p (a b) -> p a b", a=2)),
             reads=[st.s(tt)], writes=[mv.s(tt)])
        k.op("act", lambda: nc.scalar.activation(mv[:, tt, 2:3], mv[:, tt, 1:2], AF.Sqrt, bias=self.eps_ln[:, 0:1], scale=1.0),
             reads=[mv.s(tt), self.eps_ln], writes=[mv.s(tt)])
        k.op("dve", lambda: nc.vector.reciprocal(mv[:, tt, 2:3], mv[:, tt, 2:3]), reads=[mv.s(tt)], writes=[mv.s(tt)])
        k.op("dve", lambda: nc.vector.scalar_tensor_tensor(mv[:, tt, 3:4], mv[:, tt, 0:1], -1.0, mv[:, tt, 2:3], op0=ALU.mult, op1=ALU.mult),
             reads=[mv.s(tt)], writes=[mv.s(tt)])
        k.op("act", lambda: nc.scalar.activation(xt, xt, AF.Identity, bias=mv[:, tt, 3:4], scale=mv[:, tt, 2:3]),
             reads=[X.s(tt), mv.s(tt)], writes=[X.s(tt)])
        k.op("dve", lambda: nc.vector.tensor_tensor(xt, xt, self.lng[:], op=ALU.mult), reads=[X.s(tt), self.lng], writes=[X.s(tt)])
        k.op("dve", lambda: nc.vector.tensor_tensor(xt, xt, self.lnb[:], op=ALU.add), reads=[X.s(tt), self.lnb], writes=[X.s(tt)])
        self.make_xT(tt)

    def load_w(self, dst, dst_ap, src_ap, ncols, eng="pool", scale=None, dst_slots=None):
        k, nc = self.k, self.nc
        st = self.stage()
        w = [dst] if dst_slots is None else [dst.s(*dst_slots)]
        k.dma("sp", st[:, 0:ncols], src_ap, writes=[st])
        if scale is not None:
            k.op("dve", lambda: nc.vector.tensor_scalar(dst_ap, st[:, 0:ncols], scale[1], None, op0=ALU.mult), reads=[st, scale[0]], writes=w)
        elif eng == "pool":
            k.op("pool", lambda: nc.gpsimd.tensor_copy(dst_ap, st[:, 0:ncols]), reads=[st], writes=w)
        elif eng == "act":
            k.op("act", lambda: nc.scalar.copy(dst_ap, st[:, 0:ncols]), reads=[st], writes=w)
        else:
            k.op("dve", lambda: nc.vector.tensor_copy(dst_ap, st[:, 0:ncols]), reads=[st], writes=w)

    def rope(self, ps_n, ps_s, cos, sin, tb, out_t, out_ap, es_tmp):
        k, nc = self.k, self.nc
        t1, t2 = es_tmp
        sl = slice(tb * 512, (tb + 1) * 512)
        k.op("dve", lambda: nc.vector.tensor_tensor(t1[:], ps_n[:], cos[:, sl], op=ALU.mult), reads=[ps_n, cos], writes=[t1])
        k.op("dve", lambda: nc.vector.tensor_tensor(t2[:], ps_s[:], sin[:, sl], op=ALU.mult), reads=[ps_s, sin], writes=[t2])
        k.op("dve", lambda: nc.vector.tensor_tensor(out_ap, t1[:], t2[:], op=ALU.add), reads=[t1, t2], writes=[out_t])

    def mla_layer(self):
        k, nc, I = self.k, self.nc, self.I
        xT = self.xT
        with ExitStack() as es:
            win = k.sbuf("m_win_b", [P, 8, 896], BF16, es=es)
            wuq = k.sbuf("m_wuq_b", [P, 3, 2048], BF16, es=es)
            wukv = k.sbuf("m_wukv_b", [P, 2, 2048], BF16, es=es)
            qn = k.sbuf("m_qn_s", [P, 3], F32, es=es)
            kvn = k.sbuf("m_kvn_s", [P, 2], F32, es=es)
            cos = k.sbuf("cos64_s", [P, S], F32, es=es)
            sin = k.sbuf("sin64_s", [P, S], F32, es=es)
            k.dma("sp", qn[:], I["m_qn"], writes=[qn])
            k.dma("sp", kvn[:], I["m_kvn"], writes=[kvn])
            k.dma("sp", cos[:], I["cos64"], writes=[cos])
            k.dma("sp", sin[:], I["sin64"], writes=[sin])
            for c in range(8):
                self.load_w(win, win[:, c, :], I["m_win"][c * P:(c + 1) * P, :], 896, eng=("dve", "act")[c % 2])
            for c in range(3):
                self.load_w(wuq, wuq[:, c, :], I["m_wuq"][c * P:(c + 1) * P, :], 2048, scale=(qn, qn[:, c:c + 1]))
            for c in range(2):
                self.load_w(wukv, wukv[:, c, :], I["m_wukv"][c * P:(c + 1) * P, :], 2048, scale=(kvn, kvn[:, c:c + 1]))
            lat_f = k.sbuf("lat_f", [P, 5, 512], F32, es=es)
            sq_b = k.sbuf("sq_b", [P, 5, 512], BF16, es=es)
            rs = k.sbuf("rs", [P, 2, 512], F32, es=es)
            lat_n = [k.sbuf(f"lat_n{i}", [P, 5, 512], BF16, es=es) for i in range(1)]
            t1 = k.sbuf("rp_t1", [P, 512], F32, es=es)
            t2 = k.sbuf("rp_t2", [P, 512], F32, es=es)
            ob = [k.sbuf(f"m_ob{i}", [P, 512], BF16, es=es) for i in range(4)]
            vb = [k.sbuf(f"m_vb{i}", [P, D], BF16, es=es) for i in range(2)]
            obi = 0
            for tb in range(4):
                tsl = slice(tb * 512, (tb + 1) * 512)
                tslots = tuple(range(tb * 4, tb * 4 + 4))
                ln = lat_n[0]
                for c in range(5):
                    ps = self.bank()
                    for dc in range(8):
                        k.op("pe", lambda dc=dc, c=c, ps=ps: nc.tensor.matmul(ps[:], win[:, dc, c * P:(c + 1) * P], xT[:, dc, tsl],
                                                                               start=(dc == 0), stop=(dc == 7)),
                             reads=[win, xT.s(*tslots)], writes=[ps], inc=(dc == 7))
                    k.op("act", lambda c=c, ps=ps: nc.scalar.copy(lat_f[:, c, :], ps[:]), reads=[ps], writes=[lat_f])
                    k.op("act", lambda c=c, ps=ps: nc.scalar.activation(sq_b[:, c, :], ps[:], AF.Square), reads=[ps], writes=[sq_b])
                for gi, (c0, nch, width) in enumerate(((0, 3, 384), (3, 2, 256))):
                    ps = self.bank()
                    for c in range(nch):
                        k.op("pe", lambda c=c, ps=ps, c0=c0, nch=nch: nc.tensor.matmul(ps[:], self.ones_b[:], sq_b[:, c0 + c, :],
                                                                                        start=(c == 0), stop=(c == nch - 1)),
                             reads=[self.ones_b, sq_b], writes=[ps], inc=(c == nch - 1))
                    k.op("act", lambda ps=ps, gi=gi, width=width: nc.scalar.activation(rs[:, gi, :], ps[:], AF.Sqrt, bias=self.eps_rms[:, 0:1],
                                                                                         scale=1.0 / width),
                         reads=[ps, self.eps_rms], writes=[rs])
                    k.op("dve", lambda gi=gi: nc.vector.reciprocal(rs[:, gi, :], rs[:, gi, :]), reads=[rs], writes=[rs])
                    k.op("dve", lambda gi=gi, c0=c0, nch=nch, ln=ln: nc.vector.tensor_tensor(
                        ln[:, c0:c0 + nch, :], lat_f[:, c0:c0 + nch, :], rs[:, gi, :].unsqueeze(1).to_broadcast([P, nch, 512]), op=ALU.mult),
                         reads=[lat_f, rs], writes=[ln])
                psn = self.bank()
                pss = self.bank()
                for (ps, c0) in ((psn, 640), (pss, 768)):
                    for dc in range(8):
                        k.op("pe", lambda dc=dc, ps=ps, c0=c0: nc.tensor.matmul(ps[:], win[:, dc, c0:c0 + P], xT[:, dc, tsl],
                                                                                 start=(dc == 0), stop=(dc == 7)),
                             reads=[win, xT.s(*tslots)], writes=[ps], inc=(dc == 7))
                o = ob[obi % 4]; obi += 1
                self.rope(psn, pss, cos, sin, tb, o, o[:], (t1, t2))
                k.dma("sp", self.krD[:, tsl], o[:], reads=[o], writes=[self.krD])
                for h in range(8):
                    ps = self.bank()
                    for rc in range(3):
                        k.op("pe", lambda rc=rc, ps=ps, h=h: nc.tensor.matmul(ps[:], wuq[:, rc, h * P:(h + 1) * P], ln[:, rc, :],
                                                                               start=(rc == 0), stop=(rc == 2)),
                             reads=[wuq, ln], writes=[ps], inc=(rc == 2))
                    o = ob[obi % 4]; obi += 1
                    k.op("act", lambda ps=ps, o=o: nc.scalar.copy(o[:], ps[:]), reads=[ps], writes=[o])
                    k.dma("sp", self.qnD[h, :, tsl], o[:], reads=[o], writes=[self.qnD.s(h)])
                for pr in range(4):
                    psn = self.bank()
                    pss = self.bank()
                    for (ps, c0) in ((psn, 1024 + pr * P), (pss, 1536 + pr * P)):
                        for rc in range(3):
                            k.op("pe", lambda rc=rc, ps=ps, c0=c0: nc.tensor.matmul(ps[:], wuq[:, rc, c0:c0 + P], ln[:, rc, :],
                                                                                     start=(rc == 0), stop=(rc == 2)),
                                 reads=[wuq, ln], writes=[ps], inc=(rc == 2))
                    o = ob[obi % 4]; obi += 1
                    self.rope(psn, pss, cos, sin, tb, o, o[:], (t1, t2))
                    k.dma("sp", self.qrD[2 * pr, :, tsl], o[0:64, :], reads=[o], writes=[self.qrD.s(2 * pr)])
                    k.dma("sp", self.qrD[2 * pr + 1, :, tsl], o[64:128, :], reads=[o], writes=[self.qrD.s(2 * pr + 1)])
                for h in range(8):
                    ps = self.bank()
                    for rc in range(2):
                        k.op("pe", lambda rc=rc, ps=ps, h=h: nc.tensor.matmul(ps[:], wukv[:, rc, h * P:(h + 1) * P], ln[:, 3 + rc, :],
                                                                               start=(rc == 0), stop=(rc == 1)),
                             reads=[wukv, ln], writes=[ps], inc=(rc == 1))
                    o = ob[obi % 4]; obi += 1
                    k.op("dve", lambda ps=ps, o=o: nc.vector.tensor_copy(o[:], ps[:]), reads=[ps], writes=[o])
                    k.dma("sp", self.knD[h, :, tsl], o[:], reads=[o], writes=[self.knD.s(h)])
                for t4 in range(4):
                    tt = tb * 4 + t4
                    v = vb[tt % 2]
                    for half in range(2):
                        ps = self.bank()
                        for rc in range(2):
                            k.op("pe", lambda rc=rc, ps=ps, half=half, t4=t4: nc.tensor.matmul(
                                ps[:], ln[:, 3 + rc, t4 * P:(t4 + 1) * P], wukv[:, rc, 1024 + half * 512:1024 + (half + 1) * 512],
                                start=(rc == 0), stop=(rc == 1)),
                                 reads=[wukv, ln], writes=[ps], inc=(rc == 1))
                        if half == 0:
                            k.op("act", lambda ps=ps, v=v: nc.scalar.copy(v[:, 0:512], ps[:]), reads=[ps], writes=[v])
                        else:
                            k.op("dve", lambda ps=ps, v=v: nc.vector.tensor_copy(v[:, 512:1024], ps[:]), reads=[ps], writes=[v])
                    k.dma("sp", self.VD[tt], v[:], reads=[v], writes=[self.VD.s(tt)])
            k.barrier()
        self.attention(mla=True)

    def attention(self, mla, selT=None, selT_t=None):
        k, nc = self.k, self.nc
        scale = (192.0 if mla else 128.0) ** -0.5
        self.att_es = ExitStack()
        es = self.att_es
        OT = k.sbuf("OT", [P, 8, S], BF16, nslots=8, es=es)
        self.OT = OT
        with ExitStack() as es2:
            kn = [Tn(self.stg[i][:].bitcast(BF16), 1, f"a_kn{i}") for i in range(2)]
            qn = [k.sbuf(f"a_qn{i}", [P, S], BF16, es=es2) for i in range(2)]
            vh = [k.sbuf(f"a_vh{i}", [P, NT, P], BF16, es=es2) for i in range(2)]
            pt = [k.sbuf(f"a_pt{i}", [P, 512], BF16, es=es2) for i in range(3)]
            rec = k.sbuf("a_rec", [P, 512], F32, es=es2)
            if mla:
                qr = [k.sbuf(f"a_qr{i}", [64, S], BF16, es=es2) for i in range(2)]
                kr = k.sbuf("a_kr", [64, S], BF16, es=es2)
                k.dma("sp", kr[:], self.krD[0:64, :], reads=[self.krD], writes=[kr])
            def load_head(h):
                b = h % 2
                k.dma("sp", kn[b][:, 0:S], self.knD[h], reads=[self.knD.s(h)], writes=[kn[b]])
                k.dma("sp", qn[b][:], self.qnD[h], reads=[self.qnD.s(h)], writes=[qn[b]])
                k.dma("sp", vh[b][:], self.VD[:, :, h * P:(h + 1) * P].rearrange("t p v -> p t v"), reads=[self.VD], writes=[vh[b]])
                if mla:
                    k.dma("sp", qr[b][:], self.qrD[h], reads=[self.qrD.s(h)], writes=[qr[b]])

            pairs = []
            for h in range(8):
                for QB in range(4):
                    for kc in range(4 * QB + 4):
                        pairs.append((h, QB, kc))
            info = {}

            def emit_qk(i):
                h, QB, kc = pairs[i]
                b = h % 2
                if QB == 0 and kc == 0 and h == 0:
                    load_head(0)
                qlo = max(kc, 4 * QB)
                c0 = (qlo - 4 * QB) * P
                qs = slice(QB * 512 + c0, (QB + 1) * 512)
                ks = slice(kc * P, (kc + 1) * P)
                st = self.pb[i % 3]
                k.op("pe", lambda: nc.tensor.matmul(st[:, c0:512], kn[b][:, ks], qn[b][:, qs], start=True, stop=(not mla)),
                     reads=[kn[b], qn[b]], writes=[st], inc=(not mla))
                if mla:
                    k.op("pe", lambda: nc.tensor.matmul(st[:, c0:512], kr[:, ks], qr[b][:, qs], start=False, stop=True),
                         reads=[kr, qr[b]], writes=[st])
                info[i] = (st, c0, qs)

            def emit_soft(i):
                h, QB, kc = pairs[i]
                st, c0, qs = info[i]
                p_ = pt[i % 3]
                k.op("act", lambda: nc.scalar.activation(p_[:, c0:512], st[:, c0:512], AF.Exp, scale=scale), reads=[st], writes=[p_])
                if mla:
                    if kc >= 4 * QB:
                        k.op("dve", lambda: nc.vector.tensor_tensor(p_[:, c0:c0 + P], p_[:, c0:c0 + P], self.maskT_b[:], op=ALU.mult),
                             reads=[p_, self.maskT_b], writes=[p_])
                else:
                    k.op("dve", lambda: nc.vector.tensor_tensor(p_[:, c0:512], p_[:, c0:512], selT(kc, qs.start, qs.stop), op=ALU.mult),
                         reads=[p_, selT_t], writes=[p_])

            def emit_pv(i):
                h, QB, kc = pairs[i]
                b = h % 2
                st, c0, qs = info.pop(i)
                p_ = pt[i % 3]
                oT = self.pb[3 + QB % 2]
                sm = self.pb[5 + QB % 2]
                last = 4 * QB + 3
                if QB == 0 and kc == 0 and h + 1 < 8:
                    load_head(h + 1)
                k.op("pe", lambda: nc.tensor.matmul(oT[:, c0:512], vh[b][:, kc, :], p_[:, c0:512], start=(kc == 0), stop=(kc == last), skip_group_check=True),
                     reads=[vh[b], p_], writes=[oT], inc=False)
                k.op("pe", lambda: nc.tensor.matmul(sm[:, c0:512], self.ones_b[:], p_[:, c0:512], start=(kc == 0), stop=(kc == last), skip_group_check=True),
                     reads=[self.ones_b, p_], writes=[sm])
                if kc == last:
                    k.op("dve", lambda: nc.vector.reciprocal(rec[:], sm[:]), reads=[sm], writes=[rec])
                    k.op("dve", lambda: nc.vector.tensor_tensor(OT[:, h, QB * 512:(QB + 1) * 512], oT[:], rec[:], op=ALU.mult),
                         reads=[oT, rec], writes=[OT.s(h)])

            npairs = len(pairs)
            LA = 2
            for i in range(min(LA, npairs)):
                emit_qk(i)
            for i in range(npairs):
                emit_soft(i)
                if i + LA < npairs:
                    emit_qk(i + LA)
                emit_pv(i)
            k.barrier()

    def mixer_out_ln(self, wo_ap, li):
        k, nc = self.k, self.nc
        OT = self.OT
        with ExitStack() as es:
            wo = k.sbuf("wo_b", [P, 8, 512], BF16, es=es)
            self.ln_params(li, es)
            for half in range(2):
                for c in range(8):
                    self.load_w(wo, wo[:, c, :], wo_ap[c * P:(c + 1) * P, half * 512:(half + 1) * 512], 512, eng=("dve", "act")[c % 2])
                for tt in range(NT):
                    ps = self.bank()
                    for h in range(8):
                        k.op("pe", lambda ps=ps, h=h, tt=tt: nc.tensor.matmul(
                            ps[:], OT[:, h, tt * P:(tt + 1) * P], wo[:, h, :], start=(h == 0), stop=(h == 7)),
                             reads=[OT, wo], writes=[ps], inc=(h == 7))
                    xs = self.X[:, tt, half * 512:(half + 1) * 512]
                    k.op("dve", lambda ps=ps, xs=xs: nc.vector.scalar_tensor_tensor(xs, xs, ALPHA, ps[:], op0=ALU.mult, op1=ALU.add),
                         reads=[ps, self.X.s(tt)], writes=[self.X.s(tt)])
            for tt in range(NT):
                self.ln_tile(tt, li)
            k.barrier()
        self.att_es.close()

    def dsa_layer(self):
        k, nc, I = self.k, self.nc, self.I
        xT = self.xT
        WI = I["d_win"].rearrange("(c p) n -> p c n", p=P)
        self.dsa_es = ExitStack()
        esD = self.dsa_es
        selT = k.sbuf("d_selT", [P, 136 * P], BF16, es=esD)
        wtok = k.sbuf("d_wtok", [P, NT, 8], F32, es=esD)

        def soff(kc):
            return P * (16 * kc - kc * (kc - 1) // 2)

        with ExitStack() as es:
            wb = [k.sbuf(f"d_wb{i}", [P, 8, 512], BF16, es=es) for i in range(2)]
            t1 = k.sbuf("d_t1", [P, 512], F32, es=es)
            t2 = k.sbuf("d_t2", [P, 512], F32, es=es)
            ob = [k.sbuf(f"d_ob{i}", [P, 512], BF16, es=es) for i in range(4)]
            cos = k.sbuf("d_cos", [P, S], F32, es=es)
            sin = k.sbuf("d_sin", [P, S], F32, es=es)
            gi = 0
            obi = 0

            def load_group(c0, ncols):
                nonlocal gi
                wb_ = wb[gi % 2]
                gi += 1
                for hf in range(2):
                    st_ = self.stage()
                    sv = st_[:].rearrange("p (c n) -> p c n", c=4)
                    k.dma("sp", sv[:, :, 0:ncols], WI[:, hf * 4:(hf + 1) * 4, c0:c0 + ncols], writes=[st_])
                    if hf == 0:
                        k.op("act", lambda sv=sv, hf=hf: nc.scalar.copy(wb_[:, hf * 4:(hf + 1) * 4, 0:ncols], sv[:, :, 0:ncols]), reads=[st_], writes=[wb_])
                    else:
                        k.op("dve", lambda sv=sv, hf=hf: nc.vector.tensor_copy(wb_[:, hf * 4:(hf + 1) * 4, 0:ncols], sv[:, :, 0:ncols]), reads=[st_], writes=[wb_])
                return wb_

            def rope_pairs(c0, npairs, dst_fn):
                nonlocal obi
                wb_ = load_group(c0, npairs * 256)
                for tb in range(4):
                    tsl = slice(tb * 512, (tb + 1) * 512)
                    tslots = tuple(range(tb * 4, tb * 4 + 4))
                    for pi in range(npairs):
                        psn = self.bank()
                        pss = self.bank()
                        for (ps, cc) in ((psn, pi * 256), (pss, pi * 256 + P)):
                            for dc in range(8):
                                k.op("pe", lambda dc=dc, ps=ps, cc=cc: nc.tensor.matmul(ps[:], wb_[:, dc, cc:cc + P], xT[:, dc, tsl],
                                                                                         start=(dc == 0), stop=(dc == 7)),
                                     reads=[wb_, xT.s(*tslots)], writes=[ps], inc=(dc == 7))
                        o = ob[obi % 4]; obi += 1
                        self.rope(psn, pss, cos, sin, tb, o, o[:], (t1, t2))
                        dt_, dap = dst_fn(pi, tsl)
                        k.dma("sp", dap, o[:], reads=[o], writes=[dt_])

            k.dma("sp", cos[:], I["cos128"], writes=[cos])
            k.dma("sp", sin[:], I["sin128"], writes=[sin])
            for g in range(4):
                rope_pairs(g * 512, 2, lambda pi, tsl, g=g: (self.qnD.s(2 * g + pi), self.qnD[2 * g + pi, :, tsl]))
            for g in range(4):
                rope_pairs(2048 + g * 512, 2, lambda pi, tsl, g=g: (self.knD.s(2 * g + pi), self.knD[2 * g + pi, :, tsl]))
            k.dma("sp", cos[:], I["cos64"], writes=[cos])
            k.dma("sp", sin[:], I["sin64"], writes=[sin])
            for g in range(2):
                rope_pairs(4096 + g * 512, 2, lambda pi, tsl, g=g: (self.qiD.s(2 * g + pi), self.qiD[2 * g + pi, :, tsl]))
            rope_pairs(5120, 1, lambda pi, tsl: (self.kiD, self.kiD[:, tsl]))
            vb = [k.sbuf(f"d_vb{i}", [P, 512], BF16, es=es) for i in range(2)]
            vi = 0
            for half in range(2):
                wb_ = load_group(5376 + half * 512, 512)
                for tt in range(NT):
                    ps = self.bank()
                    for dc in range(8):
                        k.op("pe", lambda dc=dc, ps=ps, tt=tt: nc.tensor.matmul(ps[:], xT[:, dc, tt * P:(tt + 1) * P], wb_[:, dc, :],
                                                                                 start=(dc == 0), stop=(dc == 7)),
                             reads=[wb_, xT.s(tt)], writes=[ps], inc=(dc == 7))
                    v = vb[vi % 2]; vi += 1
                    if tt % 2 == 0:
                        k.op("act", lambda ps=ps, v=v: nc.scalar.copy(v[:], ps[:]), reads=[ps], writes=[v])
                    else:
                        k.op("dve", lambda ps=ps, v=v: nc.vector.tensor_copy(v[:], ps[:]), reads=[ps], writes=[v])
                    k.dma("sp", self.VD[tt, :, half * 512:(half + 1) * 512], v[:], reads=[v], writes=[self.VD.s(tt)])
            wb_ = load_group(6400, 8)
            wscale = float(8 ** -0.5 * 64 ** -0.5)
            for tt in range(NT):
                ps = self.bank()
                for dc in range(8):
                    k.op("pe", lambda dc=dc, ps=ps, tt=tt: nc.tensor.matmul(ps[:, 0:8], xT[:, dc, tt * P:(tt + 1) * P], wb_[:, dc, 0:8],
                                                                             start=(dc == 0), stop=(dc == 7)),
                         reads=[wb_, xT.s(tt)], writes=[ps], inc=(dc == 7))
                k.op("act", lambda ps=ps, tt=tt: nc.scalar.mul(wtok[:, tt, :], ps[:, 0:8], wscale), reads=[ps], writes=[wtok])
            k.barrier()
        with ExitStack() as es:
            kiT = k.sbuf("d_kiT", [P, S], BF16, es=es)
            qiT = k.sbuf("d_qiT", [P, 4, S], BF16, es=es)
            accs = [k.sbuf(f"d_acc{i}", [P, S], F32, es=es) for i in range(2)]
            rls = [k.sbuf(f"d_rl{i}", [P, 512], F32, es=es) for i in range(2)]
            sels = [k.sbuf(f"d_sel{i}", [P, S], BF16, es=es) for i in range(1)]
            mxs = [k.sbuf(f"d_mx{i}", [P, 8], F32, es=es) for i in range(4)]
            k.dma("sp", kiT[:], self.kiD[:], reads=[self.kiD], writes=[kiT])
            for pr in range(4):
                k.dma("sp", qiT[:, pr, :], self.qiD[pr], reads=[self.qiD.s(pr)], writes=[qiT])
            rli = 0
            tbi = 0
            sel2 = k.sbuf("d_sel2", [P, S], BF16, es=es)
            bss = [k.sbuf(f"d_bs{i}", [P, 8], F32, es=es) for i in range(2)]

            def q_chain(qi):
                nonlocal rli, tbi
                n = (qi + 1) * P
                acc = accs[qi % 2]
                mxs_ = mxs[2 * (qi % 2):2 * (qi % 2) + 2]
                qsl = slice(qi * P, (qi + 1) * P)
                for h in range(8):
                    pr, hp = h // 2, h % 2
                    prt = slice(hp * 64, (hp + 1) * 64)
                    for k0 in range(0, n, 512):
                        kw = min(512, n - k0)
                        ps = self.bank()
                        k.op("pe", lambda ps=ps, kw=kw, k0=k0, pr=pr, prt=prt, qsl=qsl: nc.tensor.matmul(
                            ps[:, 0:kw], qiT[prt, pr, qsl], kiT[prt, k0:k0 + kw], start=True, stop=True),
                             reads=[qiT, kiT], writes=[ps])
                        rl = rls[rli % 2]; rli += 1
                        k.op("act", lambda ps=ps, rl=rl, kw=kw: nc.scalar.activation(rl[:, 0:kw], ps[:, 0:kw], AF.Relu), reads=[ps], writes=[rl])
                        if h == 0:
                            k.op("dve", lambda rl=rl, kw=kw, k0=k0, acc=acc, qi=qi: nc.vector.tensor_scalar(
                                acc[:, k0:k0 + kw], rl[:, 0:kw], wtok[:, qi, 0:1], None, op0=ALU.mult), reads=[rl, wtok], writes=[acc])
                        else:
                            k.op("dve", lambda rl=rl, kw=kw, k0=k0, acc=acc, qi=qi, h=h: nc.vector.scalar_tensor_tensor(
                                acc[:, k0:k0 + kw], rl[:, 0:kw], wtok[:, qi, h:h + 1], acc[:, k0:k0 + kw], op0=ALU.mult, op1=ALU.add),
                                 reads=[rl, wtok, acc], writes=[acc])
                        yield
                sel = sels[0] if qi % 2 == 0 else sel2
                if qi >= 2:
                    bs = bss[qi % 2]
                    AXX_ = mybir.AxisListType.X
                    k.op("dve", lambda: nc.vector.tensor_reduce(bs[:, 0:1], acc[:, 0:n], AXX_, ALU.max), reads=[acc], writes=[bs])
                    yield
                    k.op("dve", lambda: nc.vector.tensor_reduce(bs[:, 1:2], acc[:, 0:n], AXX_, ALU.min), reads=[acc], writes=[bs])
                    yield
                    k.op("dve", lambda: nc.vector.scalar_tensor_tensor(bs[:, 2:3], bs[:, 0:1], 1.0, bs[:, 1:2], op0=ALU.mult, op1=ALU.subtract),
                         reads=[bs], writes=[bs])
                    yield
                    k.op("dve", lambda: nc.vector.tensor_scalar(bs[:, 2:3], bs[:, 2:3], 1.0009765625, 1e-20, op0=ALU.mult, op1=ALU.add),
                         reads=[bs], writes=[bs])
                    yield
                k.op("dve", lambda acc=acc, n=n: nc.vector.tensor_tensor(acc[:, n - P:n], acc[:, n - P:n], self.negm[:], op=ALU.add),
                     reads=[acc, self.negm], writes=[acc])
                if qi >= 2:
                    bs = bss[qi % 2]
                    NIT = 20
                    k.op("dve", lambda: nc.vector.scalar_tensor_tensor(bs[:, 3:4], bs[:, 2:3], 0.5, bs[:, 1:2], op0=ALU.mult, op1=ALU.add),
                         reads=[bs], writes=[bs])
                    yield
                    for it in range(1, NIT + 1):
                        step = 0.5 ** it
                        k.op("dve", lambda: nc.vector.tensor_scalar(sel[:, 0:n], acc[:, 0:n], bs[:, 3:4], 0.0, op0=ALU.is_ge, op1=ALU.add,
                                                                    accum_out=bs[:, 4:5]),
                             reads=[acc, bs], writes=[sel, bs])
                        yield
                        k.op("dve", lambda step=step: nc.vector.tensor_scalar(bs[:, 5:6], bs[:, 4:5], 256.0, step, op0=ALU.is_ge, op1=ALU.mult),
                             reads=[bs], writes=[bs])
                        yield
                        k.op("dve", lambda: nc.vector.scalar_tensor_tensor(bs[:, 1:2], bs[:, 5:6], bs[:, 2:3], bs[:, 1:2], op0=ALU.mult, op1=ALU.add),
                             reads=[bs], writes=[bs])
                        yield
                        if it < NIT:
                            k.op("dve", lambda step=step: nc.vector.scalar_tensor_tensor(bs[:, 3:4], bs[:, 2:3], step * 0.5, bs[:, 1:2], op0=ALU.mult, op1=ALU.add),
                                 reads=[bs], writes=[bs])
                            yield
                    k.op("dve", lambda: nc.vector.tensor_scalar(sel[:, 0:n], acc[:, 0:n], bs[:, 1:2], None, op0=ALU.is_ge),
                         reads=[acc, bs], writes=[sel])
                    yield
                else:
                    thr_t, thr = self.thr_c, self.thr_c[:, 0:1]
                    k.op("dve", lambda sel=sel, acc=acc, n=n, thr=thr: nc.vector.tensor_scalar(sel[:, 0:n], acc[:, 0:n], thr, None, op0=ALU.is_ge),
                         reads=[acc, thr_t], writes=[sel])
                    yield
                for kc in range(qi + 1):
                    blk = tbi % 8; tbi += 1
                    k.op("pe", lambda sel=sel, kc=kc, blk=blk: nc.tensor.transpose(self.ptb[:, blk * P:(blk + 1) * P], sel[:, kc * P:(kc + 1) * P], self.ident_b[:]),
                         reads=[sel, self.ident_b], writes=[self.ptb])
                    o_ = soff(kc) + (qi - kc) * P
                    k.op("act", lambda blk=blk, o_=o_: nc.scalar.copy(selT[:, o_:o_ + P], self.ptb[:, blk * P:(blk + 1) * P]),
                         reads=[self.ptb], writes=[selT])

            for pa in range(8):
                gens = [q_chain(2 * pa + 1), q_chain(2 * pa)]
                while gens:
                    for g_ in list(gens):
                        try:
                            next(g_)
                        except StopIteration:
                            gens.remove(g_)
            self.dbg("bs1", bss[1], [P, 8])
            self.dbg("acc1", accs[1], [P, S])
            self.dbg("sel1", sel2, [P, S], BF16)
            k.barrier()
        self.dbg("selT", selT, [P, 136 * P], BF16)
        self.attention(mla=False, selT=lambda kc, q0, q1: selT[:, soff(kc) + q0 - kc * P: soff(kc) + q1 - kc * P], selT_t=selT)

    def peer_layer(self, l, li):
        k, nc, I = self.k, self.nc, self.I
        xT, X = self.xT, self.X
        AXX = mybir.AxisListType.X
        if not hasattr(self, "gateD"):
            self.gateD = k.dram("gateD", [P, P, S], BF16, nslots=P)
        gateD = self.gateD
        with ExitStack() as esL:
            LT = k.sbuf("p_LT", [P, 3, S], BF16, nslots=NT, es=esL)
            with ExitStack() as es:
                wq = k.sbuf("p_wq_b", [P, 8, 2048], BF16, es=es)
                skT = k.sbuf("p_skT_b", [P, 16, P], BF16, es=es)
                for c in range(8):
                    self.load_w(wq, wq[:, c, :], I[f"p_wq{l}"][c * P:(c + 1) * P, :], 2048, eng=("dve", "act")[c % 2])
                st = self.stage()
                k.dma("sp", st[:].rearrange("p (g n) -> p g n", g=16), I[f"p_skT{l}"].rearrange("g d n -> d g n"), writes=[st])
                k.op("dve", lambda: nc.vector.tensor_copy(skT[:].rearrange("p g n -> p (g n)"), st[:]), reads=[st], writes=[skT])
                qTbs = [k.sbuf(f"p_qTb{z}", [P, 16, 256], BF16, es=es) for z in range(2)]
                s_sbs = [Tn(self.stg[z].h, 16, f"p_s{z}") for z in range(2)]
                v16 = k.sbuf("p_v16", [P, 256], F32, nslots=16, es=es)
                i16 = k.sbuf("p_i16", [P, 256], U32, nslots=16, es=es)
                i16f = k.sbuf("p_i16f", [P, 256], F32, es=es)
                best = k.sbuf("p_best", [P, 128], F32, nslots=8, es=es)
                pos = k.sbuf("p_pos", [P, 128], U32, nslots=8, es=es)
                pab = k.sbuf("p_pab", [P, 2, 128], U32, es=es)
                pabf = k.sbuf("p_pabf", [P, 2, 128], F32, es=es)
                sm8 = k.sbuf("p_sm8", [P, 3, 8], F32, es=es)
                ex = k.sbuf("p_ex", [P, 128], F32, es=es)
                Lt = k.sbuf("p_Lt", [P, 3, 128], F32, es=es)
                k.barrier()

                def selfsync():
                    if k.cnt["dve"] > 0:
                        k.wait("dve", (k.sem["dve"], k.cnt["dve"]))

                def tile_scores(tt, t2, qTb):
                    s_sb = s_sbs[tt % 2]
                    for bq in range(4):
                        ps = self.bank()
                        for gg in range(4):
                            g = bq * 4 + gg
                            k.op("pe", lambda ps=ps, g=g, gg=gg: nc.tensor.matmul(
                                ps[:, gg * P:(gg + 1) * P], qTb[:, g, t2 * P:(t2 + 1) * P], skT[:, g, :], start=True, stop=True),
                                 reads=[qTb, skT], writes=[ps], inc=(gg == 3))
                        k.op("act", lambda ps=ps, bq=bq: nc.scalar.copy(s_sb[:, bq * 512:(bq + 1) * 512], ps[:]),
                             reads=[ps], writes=[s_sb.s(*range(bq * 4, bq * 4 + 4))])

                def tile_chain(tt, t2, qTb):
                    s_sb = s_sbs[tt % 2]
                    cand = s_sb
                    work = s_sb
                    G16 = range(16)
                    sg = lambda g: s_sb[:, g * P:(g + 1) * P]
                    va = lambda g: v16[:, g * 16:g * 16 + 8]
                    vb_ = lambda g: v16[:, g * 16 + 8:g * 16 + 16]
                    wk = lambda g: work[:, g * P:(g + 1) * P]
                    for g in G16:
                        k.op("dve", lambda g=g: nc.vector.max(out=va(g), in_=sg(g)), reads=[s_sb.s(g)], writes=[v16.s(g)])
                    selfsync()
                    for g in G16:
                        k.op("dve", lambda g=g: nc.vector.max_index(i16[:, g * 16:g * 16 + 8], va(g), sg(g)), reads=[s_sb.s(g), v16.s(g)], writes=[i16.s(g)])
                    for g in G16:
                        k.op("dve", lambda g=g: nc.vector.match_replace(out=sg(g), in_to_replace=va(g), in_values=sg(g), imm_value=NEG),
                             reads=[s_sb.s(g), v16.s(g)], writes=[s_sb.s(g)])
                    selfsync()
                    for g in G16:
                        k.op("dve", lambda g=g: nc.vector.max(out=vb_(g), in_=sg(g)), reads=[s_sb.s(g)], writes=[v16.s(g)])
                    selfsync()
                    for g in G16:
                        k.op("dve", lambda g=g: nc.vector.max_index(i16[:, g * 16 + 8:g * 16 + 16], vb_(g), sg(g)), reads=[s_sb.s(g), v16.s(g)], writes=[i16.s(g)])
                    k.op("dve", lambda: nc.vector.tensor_copy(i16f[:], i16[:]), reads=[i16], writes=[i16f])
                    v4 = v16[:].rearrange("p (h c a) -> p h c a", h=8, c=2)
                    i4 = i16f[:].rearrange("p (h c a) -> p h c a", h=8, c=2)
                    c4 = cand[:].rearrange("p (h a b) -> p h a b", h=8, a=16)
                    k.op("dve", lambda: nc.vector.tensor_tensor(c4, v4[:, :, 0, :].unsqueeze(3).to_broadcast([P, 8, 16, 16]),
                                                                v4[:, :, 1, :].unsqueeze(2).to_broadcast([P, 8, 16, 16]), op=ALU.add),
                         reads=[v16], writes=[cand])
                    H8 = range(8)
                    ch = lambda h: cand[:, h * 256:(h + 1) * 256]
                    cs = lambda h: cand.s(2 * h, 2 * h + 1)
                    ws = lambda h: work.s(2 * h, 2 * h + 1)
                    wh = lambda h: work[:, h * 256:(h + 1) * 256]
                    ba = lambda h: best[:, h * 16:h * 16 + 8]
                    bb = lambda h: best[:, h * 16 + 8:h * 16 + 16]
                    selfsync()
                    for h in H8:
                        k.op("dve", lambda h=h: nc.vector.max(out=ba(h), in_=ch(h)), reads=[cs(h)], writes=[best.s(h)])
                    selfsync()
                    for h in H8:
                        k.op("dve", lambda h=h: nc.vector.max_index(pos[:, h * 16:h * 16 + 8], ba(h), ch(h)), reads=[cs(h), best.s(h)], writes=[pos.s(h)])
                    for h in H8:
                        k.op("dve", lambda h=h: nc.vector.match_replace(out=ch(h), in_to_replace=ba(h), in_values=ch(h), imm_value=NEG),
                             reads=[cs(h), best.s(h)], writes=[cs(h)])
                    selfsync()
                    for h in H8:
                        k.op("dve", lambda h=h: nc.vector.max(out=bb(h), in_=ch(h)), reads=[cs(h)], writes=[best.s(h)])
                    selfsync()
                    for h in H8:
                        k.op("dve", lambda h=h: nc.vector.max_index(pos[:, h * 16 + 8:h * 16 + 16], bb(h), ch(h)), reads=[cs(h), best.s(h)], writes=[pos.s(h)])
                    b3 = best[:].rearrange("p (h r) -> p h r", h=8)
                    e3 = ex[:].rearrange("p (h r) -> p h r", h=8)
                    k.op("dve", lambda: nc.vector.tensor_tensor(e3, b3, b3[:, :, 0:1].to_broadcast([P, 8, 16]), op=ALU.subtract),
                         reads=[best], writes=[ex])
                    k.op("act", lambda: nc.scalar.activation(ex[:], ex[:], AF.Exp), reads=[ex], writes=[ex])
                    k.op("dve", lambda: nc.vector.tensor_single_scalar(pab[:, 0, :], pos[:], 4, op=ALU.logical_shift_right), reads=[pos], writes=[pab])
                    k.op("dve", lambda: nc.vector.tensor_single_scalar(pab[:, 1, :], pos[:], 15, op=ALU.bitwise_and), reads=[pos], writes=[pab])
                    k.op("dve", lambda: nc.vector.tensor_copy(pabf[:], pab[:]), reads=[pab], writes=[pabf])
                    k.op("dve", lambda: nc.vector.reduce_sum(sm8[:, 0, :], e3, axis=AXX), reads=[ex], writes=[sm8])
                    k.op("dve", lambda: nc.vector.reciprocal(sm8[:, 1, :], sm8[:, 0, :]), reads=[sm8], writes=[sm8])
                    k.op("dve", lambda: nc.vector.tensor_tensor(Lt[:, 0, :].rearrange("p (h r) -> p h r", h=8), e3,
                                                                sm8[:, 1, :].unsqueeze(2).to_broadcast([P, 8, 16]), op=ALU.mult),
                         reads=[ex, sm8], writes=[Lt])
                    for w_ in range(2):
                        abf = pabf[:, w_, :].rearrange("p (h r) -> p h r", h=8)
                        k.op("dve", lambda abf=abf: nc.vector.tensor_tensor(
                            c4, self.iota_f[:, 0:16].unsqueeze(1).unsqueeze(1).to_broadcast([P, 8, 16, 16]),
                            abf.unsqueeze(3).to_broadcast([P, 8, 16, 16]), op=ALU.is_equal),
                             reads=[self.iota_f, pabf], writes=[cand])
                        k.op("dve", lambda w_=w_: nc.vector.tensor_tensor(c4, c4, i4[:, :, w_, :].unsqueeze(2).to_broadcast([P, 8, 16, 16]), op=ALU.mult),
                             reads=[i16f, cand], writes=[cand])
                        k.op("dve", lambda w_=w_: nc.vector.reduce_sum(Lt[:, 1 + w_, :].rearrange("p (h r) -> p h r", h=8), c4, axis=AXX),
                             reads=[cand], writes=[Lt])
                    ps = self.bank()
                    for q3 in range(3):
                        k.op("pe", lambda q3=q3, ps=ps: nc.tensor.transpose(ps[:, q3 * P:(q3 + 1) * P], Lt[:, q3, :], self.ident_f[:]),
                             reads=[Lt, self.ident_f], writes=[ps], inc=(q3 == 2))
                    k.op("act", lambda ps=ps, tt=tt: nc.scalar.copy(LT[:, :, tt * P:(tt + 1) * P], ps[:, 0:384].rearrange("p (q t) -> p q t", q=3)),
                         reads=[ps], writes=[LT.s(tt)])

                def qproj(tb):
                    tsl = slice(tb * 256, (tb + 1) * 256)
                    tslots = (2 * tb, 2 * tb + 1)
                    qTb = qTbs[tb % 2]
                    for g in range(16):
                        ps = self.bank()
                        for dc in range(8):
                            k.op("pe", lambda dc=dc, g=g, ps=ps: nc.tensor.matmul(ps[:, 0:256], wq[:, dc, g * P:(g + 1) * P], xT[:, dc, tsl],
                                                                                   start=(dc == 0), stop=(dc == 7)),
                                 reads=[wq, xT.s(*tslots)], writes=[ps], inc=(dc == 7))
                        k.op("act", lambda g=g, ps=ps, qTb=qTb: nc.scalar.copy(qTb[:, g, :], ps[:, 0:256]), reads=[ps], writes=[qTb])

                qproj(0)
                tile_scores(0, 0, qTbs[0])
                for tt in range(NT):
                    tb, t2 = tt // 2, tt % 2
                    if t2 == 0 and tb + 1 < 8:
                        qproj(tb + 1)
                    if tt + 1 < NT:
                        tile_scores(tt + 1, (tt + 1) % 2, qTbs[((tt + 1) // 2) % 2])
                    tile_chain(tt, t2, qTbs[tb % 2])
                k.barrier()
            self.dbg("LT", LT, [P, 3, S], BF16)
            with ExitStack() as es:
                TBK = 8
                Ap = [k.sbuf(f"p_Ap{i}", [P, TBK, P], BF16, es=es) for i in range(2)]
                Bp = [k.sbuf(f"p_Bp{i}", [P, TBK, P], BF16, es=es) for i in range(2)]
                gt = [k.sbuf(f"p_gt{i}", [P, P, P], BF16, es=es) for i in range(2)]
                if ACT_SHARE:
                    nLj = Tn(self.stg[0].h, 1, "p_nLj")
                    one_t = k.sbuf("p_one", [P, 1], F32, es=es)
                    abt = [Tn(self.stg[1][:, i * P:(i + 1) * P], 1, f"p_abt{i}") for i in range(2)]
                    k.op("pool", lambda: nc.gpsimd.memset(one_t[:], 1.0), writes=[one_t])
                    k.op("pool", lambda: nc.gpsimd.tensor_scalar(nLj[:], LT[:, 2, :], -1.0, None, op0=ALU.mult), reads=[LT], writes=[nLj])
                abi = 0
                ev_i = 0
                for tt in range(NT):
                    g_ = gt[tt % 2]
                    for sub in range(P // TBK):
                        t0 = tt * P + sub * TBK
                        A_ = Ap[sub % 2]
                        B_ = Bp[sub % 2]
                        if BATCH_B:
                            k.op("dve", lambda B_=B_, t0=t0: nc.vector.tensor_tensor(
                                B_[:], self.iota_b[:].unsqueeze(1).to_broadcast([P, TBK, P]),
                                LT[:, 2, t0:t0 + TBK].unsqueeze(2).to_broadcast([P, TBK, P]), op=ALU.is_equal),
                                 reads=[self.iota_b, LT.s(tt)], writes=[B_])
                        for tl_ in range(TBK):
                            t_g = t0 + tl_
                            k.op("dve", lambda A_=A_, tl_=tl_, t_g=t_g: nc.vector.tensor_scalar(
                                A_[:, tl_, :], self.iota_b[:], LT[:, 1, t_g:t_g + 1], LT[:, 0, t_g:t_g + 1], op0=ALU.is_equal, op1=ALU.mult),
                                 reads=[self.iota_b, LT.s(tt)], writes=[A_])
                            if ACT_SHARE and tl_ < ACT_SHARE:
                                ab = abt[abi % 2]; abi += 1
                                k.op("act", lambda ab=ab, t_g=t_g: nc.scalar.activation(ab[:], self.iota_f[:], AF.Abs, bias=nLj[:, t_g:t_g + 1], scale=1.0),
                                     reads=[self.iota_f, nLj], writes=[ab])
                                k.op("act", lambda ab=ab, B_=B_, tl_=tl_: nc.scalar.activation(B_[:, tl_, :], ab[:], AF.Relu, bias=one_t[:, 0:1], scale=-1.0),
                                     reads=[ab, one_t], writes=[B_])
                            elif not BATCH_B:
                                k.op("dve", lambda B_=B_, tl_=tl_, t_g=t_g: nc.vector.tensor_scalar(
                                    B_[:, tl_, :], self.iota_b[:], LT[:, 2, t_g:t_g + 1], None, op0=ALU.is_equal),
                                     reads=[self.iota_b, LT.s(tt)], writes=[B_])
                        for q4 in range(TBK // 4):
                            ps = self.bank()
                            for tl in range(4):
                                t_ = q4 * 4 + tl
                                k.op("pe", lambda ps=ps, tl=tl, t_=t_, A_=A_, B_=B_: nc.tensor.matmul(
                                    ps[:, tl * P:(tl + 1) * P], B_[:, t_, :], A_[:, t_, :], start=True, stop=True),
                                     reads=[A_, B_], writes=[ps], inc=(tl == 3))
                            c_ = sub * TBK + q4 * 4
                            dst = g_[:, :, c_:c_ + 4]
                            src = ps[:].rearrange("p (t i) -> p i t", t=4)
                            k.op("act", lambda dst=dst, src=src: nc.scalar.copy(dst, src), reads=[ps], writes=[g_])
                            ev_i += 1
                    for i8 in range(8):
                        k.dma("sp", gateD[i8 * 16:(i8 + 1) * 16, :, tt * P:(tt + 1) * P].rearrange("i j t -> j i t"),
                              g_[:, i8 * 16:(i8 + 1) * 16, :], reads=[g_], writes=[gateD.s(*range(i8 * 16, (i8 + 1) * 16))])
                k.barrier()
        with ExitStack() as es:
            G = 4
            hg = [k.sbuf(f"p_hg{i}", [P, S], BF16, es=es) for i in range(2 * G)]
            wub = [k.sbuf(f"p_wub{i}", [P, D], BF16, es=es) for i in range(2 * G)]
            wdb = [k.sbuf(f"p_wdb{i}", [P, D], BF16, es=es) for i in range(2)]
            wst = [k.sbuf(f"p_wst{i}", [P, D], F32, es=es) for i in range(4)]
            ge = [k.sbuf(f"p_ge{i}", [P, 512], BF16, es=es) for i in range(3)]
            gei = 0
            abank = 0
            ybank = 0
            wsi = 0
            for i in range(P):
                slot = i % (2 * G)
                h_ = hg[slot]
                wu_ = wub[slot]
                wd_ = wdb[i % 2]
                s1 = wst[wsi % 4]; wsi += 1
                s2 = wst[wsi % 4]; wsi += 1
                k.dma("sp", s1[:], I[f"p_wdT{l}"][i], writes=[s1])
                k.dma("sp", s2[:], I[f"p_wu{l}"][i * P:(i + 1) * P, :], writes=[s2])
                k.dma("sp", h_[:], gateD[i], reads=[gateD.s(i)], writes=[h_])
                k.op("pool", lambda wd_=wd_, s1=s1: nc.gpsimd.tensor_copy(wd_[:], s1[:]), reads=[s1], writes=[wd_])
                k.op("pool", lambda wu_=wu_, s2=s2: nc.gpsimd.tensor_copy(wu_[:], s2[:]), reads=[s2], writes=[wu_])
                for nb in range(4):
                    ps = self.pb[abank % 4]; abank += 1
                    for dc in range(8):
                        k.op("pe", lambda ps=ps, dc=dc, nb=nb, wd_=wd_: nc.tensor.matmul(
                            ps[:], wd_[:, dc * P:(dc + 1) * P], xT[:, dc, nb * 512:(nb + 1) * 512], start=(dc == 0), stop=(dc == 7)),
                             reads=[wd_, xT.s(*range(nb * 4, nb * 4 + 4))], writes=[ps], inc=(dc == 7))
                    g_ = ge[gei % 3]; gei += 1
                    k.op("act", lambda ps=ps, g_=g_: nc.scalar.activation(g_[:], ps[:], AF.Gelu), reads=[ps], writes=[g_])
                    k.op("dve", lambda g_=g_, h_=h_, nb=nb: nc.vector.tensor_tensor(
                        h_[:, nb * 512:(nb + 1) * 512], h_[:, nb * 512:(nb + 1) * 512], g_[:], op=ALU.mult),
                         reads=[g_, h_], writes=[h_])
                if i % G == G - 1:
                    base = slot - (G - 1)
                    for tt in range(NT):
                        for half in range(2):
                            ps = self.pb[4 + ybank % 3]; ybank += 1
                            for gg in range(G):
                                k.op("pe", lambda ps=ps, gg=gg, tt=tt, half=half, base=base: nc.tensor.matmul(
                                    ps[:], hg[base + gg][:, tt * P:(tt + 1) * P], wub[base + gg][:, half * 512:(half + 1) * 512],
                                    start=(gg == 0), stop=(gg == G - 1)),
                                     reads=[hg[base + gg], wub[base + gg]], writes=[ps], inc=(gg == G - 1))
                            xs = X[:, tt, half * 512:(half + 1) * 512]
                            if i == G - 1:
                                k.op("dve", lambda ps=ps, xs=xs: nc.vector.scalar_tensor_tensor(xs, xs, ALPHA, ps[:], op0=ALU.mult, op1=ALU.add),
                                     reads=[ps, X.s(tt)], writes=[X.s(tt)])
                            else:
                                k.op("dve", lambda ps=ps, xs=xs: nc.vector.tensor_tensor(xs, xs, ps[:], op=ALU.add),
                                     reads=[ps, X.s(tt)], writes=[X.s(tt)])
            self.ln_params(li, es)
            for tt in range(NT):
                self.ln_tile(tt, li)
            k.barrier()

def _rope_tab(dim):
    inv = (np.float32(10000.0) ** (-np.arange(0, dim, 2, dtype=np.float32) / np.float32(dim))).astype(np.float32)
    ang = (np.arange(S, dtype=np.float32)[:, None] * inv[None, :]).astype(np.float32)
    c = np.cos(ang).astype(np.float32).T
    s = np.sin(ang).astype(np.float32).T
    half = dim // 2
    reps = P // dim
    cos = np.concatenate([c, c] * reps, axis=0)
    sin = np.concatenate([-s, s] * reps, axis=0)
    return np.ascontiguousarray(cos), np.ascontiguousarray(sin)


def prep_shared(inp):
    f = np.float32
    sh = {}
    sh["ident"] = np.eye(P, dtype=f)
    sh["cos64"], sh["sin64"] = _rope_tab(64)
    sh["cos128"], sh["sin128"] = _rope_tab(128)
    kk = np.arange(P)[:, None]
    qq = np.arange(P)[None, :]
    sh["maskT"] = np.where((kk >= 64) & (qq < 64), 0.0, 1.0).astype(f)
    sh["negm"] = np.where((kk < 64) & (qq >= 64), NEG, 0.0).astype(f)
    sh["iota"] = np.broadcast_to(np.arange(P, dtype=f)[None, :], (P, P)).copy()
    w_in = inp["mla_w_in"][0]
    kr = w_in[:, 640:704]
    kr_sw = np.concatenate([kr[:, 32:64], kr[:, 0:32]], axis=1)
    sh["m_win"] = np.ascontiguousarray(np.concatenate([w_in[:, :640], kr, kr, kr_sw, kr_sw], axis=1))
    wuq = inp["mla_w_uq"][0]
    nope = wuq[:, :, :128].reshape(384, 1024)
    rp = wuq[:, :, 128:192]
    rp_sw = np.concatenate([rp[:, :, 32:64], rp[:, :, 0:32]], axis=2)
    sh["m_wuq"] = np.ascontiguousarray(np.concatenate([nope, rp.reshape(384, 512), rp_sw.reshape(384, 512)], axis=1))
    wukv = inp["mla_w_ukv"][0]
    sh["m_wukv"] = np.ascontiguousarray(np.concatenate([wukv[:, :, :128].reshape(256, 1024), wukv[:, :, 128:].reshape(256, 1024)], axis=1))
    sh["m_wo"] = np.ascontiguousarray(inp["mla_w_o"][0])
    sh["m_qn"] = np.ascontiguousarray(inp["mla_q_norm"][0].reshape(3, P).T)
    sh["m_kvn"] = np.ascontiguousarray(inp["mla_kv_norm"][0].reshape(2, P).T)
    dw = inp["dsa_w_in"][0]

    def sw(w, hd):
        n = w.shape[1] // hd
        w3 = w.reshape(w.shape[0], n, hd)
        return np.concatenate([w3[:, :, hd // 2:], w3[:, :, :hd // 2]], axis=2).reshape(w.shape[0], n * hd)

    q, kx, v = dw[:, 0:1024], dw[:, 1024:2048], dw[:, 2048:3072]
    qi, ki, wi = dw[:, 3072:3584], dw[:, 3584:3648], dw[:, 3648:3656]
    cols = []
    for h in range(8):
        cols += [q[:, h * P:(h + 1) * P], sw(q[:, h * P:(h + 1) * P], P)]
    for h in range(8):
        cols += [kx[:, h * P:(h + 1) * P], sw(kx[:, h * P:(h + 1) * P], P)]
    qis = sw(qi, 64)
    for pr in range(4):
        cols += [qi[:, pr * P:(pr + 1) * P], qis[:, pr * P:(pr + 1) * P]]
    kis = sw(ki, 64)
    cols += [ki, ki, kis, kis]
    cols += [v, wi]
    sh["d_win"] = np.ascontiguousarray(np.concatenate(cols, axis=1))
    assert sh["d_win"].shape[1] == 6408
    sh["d_wo"] = np.ascontiguousarray(inp["dsa_w_o"][0])
    for l in range(2):
        sh[f"p_wq{l}"] = np.ascontiguousarray(inp["peer_w_q"][l])
        sk = inp["peer_sub_keys"][l]
        sh[f"p_skT{l}"] = np.ascontiguousarray(sk.transpose(1, 0, 3, 2).reshape(16, P, P))
        wd = inp["peer_w_down"][l]
        sh[f"p_wdT{l}"] = np.ascontiguousarray(wd.reshape(P, P, 8, P).transpose(0, 3, 2, 1).reshape(P, P, 8 * P))
        sh[f"p_wu{l}"] = np.ascontiguousarray(inp["peer_w_up"][l])
    sh["ln_g"] = np.ascontiguousarray(inp["ln_gain"].reshape(4, D))
    sh["ln_b"] = np.ascontiguousarray(inp["ln_bias"].reshape(4, D))
    return sh


def kernel(**inputs):
    inp = {k_: np.asarray(v) for k_, v in inputs.items()}
    sh = prep_shared(inp)
    prog = Prog()
    nc = prog.build()
    x = np.ascontiguousarray(inp["x"], dtype=np.float32)
    in_maps = []
    for c in range(8):
        m = dict(sh)
        m["x"] = x[c]
        in_maps.append(m)
    res = run_bass_kernel_spmd(nc, in_maps, core_ids=list(range(8)))
    return np.stack([res.results[c]["out"] for c in range(8)], axis=0).astype(np.float32)
```
